# Optimizing a Trainium2 kernel written in Bass

```python
import jax, jax.numpy as jnp
from jax import lax
import numpy as np

D_MODEL = 2048
BATCH = 2
SEQ = 4096
DEPTH = 1
DEC_BATCH = 8
DEC_SEQ = 1
PAST_LEN = 16384
PAGE_SIZE = 128

N_HEADS = 8
N_KV_HEADS = 2
HEAD_DIM = 128
Q_PER_KV = N_HEADS // N_KV_HEADS
ATTN_WIDTH = N_HEADS * HEAD_DIM
KV_WIDTH = N_KV_HEADS * HEAD_DIM
IDX_HEADS = 16
IDX_DIM = 128
TOPK_MAX = 256
Q_BLOCK = 128
GMLP_WIDTH = 1024
GMLP_GROUPS = 8
GMLP_GROUP_DIM = GMLP_WIDTH // GMLP_GROUPS
CHUNK = 128
RMS_EPS = 1e-6
LN_EPS = 1e-5
SPLIT_SIZES = (ATTN_WIDTH, KV_WIDTH, KV_WIDTH, IDX_HEADS * IDX_DIM, IDX_HEADS, IDX_DIM, ATTN_WIDTH,
               GMLP_WIDTH, GMLP_WIDTH, GMLP_WIDTH, D_MODEL, D_MODEL)
IN_WIDTH = sum(SPLIT_SIZES)

kernel_name = "dsa_gmlp_gated_hybrid_step"


def rmsnorm(x, g):
    x32 = x.astype(jnp.float32)
    y = x32 * lax.rsqrt(jnp.mean(x32 * x32, axis=-1, keepdims=True) + RMS_EPS)
    return (y * g.astype(jnp.float32)).astype(x.dtype)


def layernorm(x, g, b):
    x32 = x.astype(jnp.float32)
    mu = jnp.mean(x32, axis=-1, keepdims=True)
    xc = x32 - mu
    y = xc * lax.rsqrt(jnp.mean(xc * xc, axis=-1, keepdims=True) + LN_EPS)
    return (y * g.astype(jnp.float32) + b.astype(jnp.float32)).astype(x.dtype)


def in_projection(x, norm_g, w_in):
    h = rmsnorm(x, norm_g) @ w_in
    offs = [int(o) for o in np.cumsum(SPLIT_SIZES)[:-1]]
    return jnp.split(h, offs, axis=-1)


def attn_heads(q, k, v, q_idx, w_idx):
    B, T = q.shape[:2]
    q = q.reshape(B, T, N_KV_HEADS, Q_PER_KV, HEAD_DIM)
    k = k.reshape(B, T, N_KV_HEADS, HEAD_DIM)
    v = v.reshape(B, T, N_KV_HEADS, HEAD_DIM)
    q_idx = q_idx.reshape(B, T, IDX_HEADS, IDX_DIM)
    w_idx = w_idx * (IDX_HEADS ** -0.5)
    return q, k, v, q_idx, w_idx


def dsa_select(q_idx, w_idx, k_idx, q_pos, top_k):
    s = jnp.einsum('bthd,bsd->bths', q_idx, k_idx, preferred_element_type=jnp.float32) * (IDX_DIM ** -0.5)
    score = jnp.einsum('bths,bth->bts', jax.nn.relu(s), w_idx.astype(jnp.float32))
    L = k_idx.shape[1]
    admissible = jnp.arange(L, dtype=jnp.int32)[None, :] <= q_pos[:, None]
    score = jnp.where(admissible[None], score, -jnp.inf)
    _, idx = lax.top_k(score, top_k)
    valid = idx <= q_pos[None, :, None]
    return idx, valid


def sparse_attend(q, k_sel, v_sel, valid):
    B, T = q.shape[:2]
    s = jnp.einsum('btkgd,btskd->btkgs', q, k_sel, preferred_element_type=jnp.float32) * (HEAD_DIM ** -0.5)
    s = jnp.where(valid[:, :, None, None, :], s, -jnp.inf)
    p = jax.nn.softmax(s, axis=-1).astype(v_sel.dtype)
    o = jnp.einsum('btkgs,btskd->btkgd', p, v_sel)
    return o.reshape(B, T, ATTN_WIDTH)


def prompt_attention(q, k, v, q_idx, w_idx, k_idx):
    B, S = q.shape[:2]
    top_k = min(TOPK_MAX, S // 4)
    nb = S // Q_BLOCK

    def to_blocks(a):
        return jnp.moveaxis(a.reshape(B, nb, Q_BLOCK, *a.shape[2:]), 1, 0)

    def one_block(args):
        qb, qib, wb, start = args
        pos = start + jnp.arange(Q_BLOCK, dtype=jnp.int32)
        idx, valid = dsa_select(qib, wb, k_idx, pos, top_k)
        k_sel = jax.vmap(lambda kb, ib: kb[ib])(k, idx)
        v_sel = jax.vmap(lambda vb, ib: vb[ib])(v, idx)
        return sparse_attend(qb, k_sel, v_sel, valid)

    starts = jnp.arange(nb, dtype=jnp.int32) * Q_BLOCK
    out = lax.map(one_block, (to_blocks(q), to_blocks(q_idx), to_blocks(w_idx), starts))
    return jnp.moveaxis(out, 0, 1).reshape(B, S, ATTN_WIDTH)


def gather_rows(pool, new, idx, page_table):
    DB = idx.shape[0]
    past = page_table.shape[1] * PAGE_SIZE
    bidx = jnp.arange(DB)[:, None, None]
    pidx = jnp.minimum(idx, past - 1)
    phys = page_table[bidx, pidx // PAGE_SIZE]
    past_rows = pool[phys, pidx % PAGE_SIZE]
    new_rows = new[bidx, jnp.clip(idx - past, 0, new.shape[1] - 1)]
    is_past = (idx < past).reshape(idx.shape + (1,) * (pool.ndim - 2))
    return jnp.where(is_past, past_rows, new_rows)


def sample_attention(q, k_new, v_new, q_idx, w_idx, k_idx_new, pool_k, pool_v, pool_k_idx, page_table):
    DB, T = q.shape[:2]
    past = page_table.shape[1] * PAGE_SIZE
    k_idx_past = pool_k_idx[page_table].reshape(DB, past, IDX_DIM)
    k_idx_all = jnp.concatenate([k_idx_past, k_idx_new.astype(k_idx_past.dtype)], axis=1)
    top_k = min(TOPK_MAX, (past + T) // 4)
    pos = past + jnp.arange(T, dtype=jnp.int32)
    idx, valid = dsa_select(q_idx, w_idx, k_idx_all, pos, top_k)
    k_sel = gather_rows(pool_k, k_new.astype(pool_k.dtype), idx, page_table)
    v_sel = gather_rows(pool_v, v_new.astype(pool_v.dtype), idx, page_table)
    return sparse_attend(q, k_sel, v_sel, valid)


def gmlp_mix(u, v, ln_g, ln_b, w_s, b_s):
    B, T, W = u.shape
    u = jax.nn.gelu(u, approximate=False)
    vn = layernorm(jax.nn.gelu(v, approximate=False), ln_g, ln_b)
    c = min(T, CHUNK)
    vc = vn.reshape(B, T // c, c, GMLP_GROUPS, GMLP_GROUP_DIM)
    wm = jnp.tril(w_s[:, :c, :c])
    mixed = jnp.einsum('gts,bcsgd->bctgd', wm, vc) + b_s[:, :c].T[None, None, :, :, None]
    return u * mixed.reshape(B, T, W), vn


def merge_out(x, o_a, z_a, o_b, z_b, g_a, g_b, w_proj_a, w_proj_b, w_out):
    y_a = (o_a * jax.nn.silu(z_a)) @ w_proj_a
    y_b = (o_b * jax.nn.silu(z_b)) @ w_proj_b
    mix = jax.nn.sigmoid(g_a) * y_a + jax.nn.sigmoid(g_b) * y_b
    return x + mix @ w_out


def setup_inputs(seed: int = 0) -> dict:
    key = jax.random.key(seed)
    ks = jax.random.split(key, 20)
    n_pages = PAST_LEN // PAGE_SIZE
    used = DEC_BATCH * n_pages
    n_pool = used + (used + 3) // 4
    f = jnp.float32
    x_prompt = jax.random.normal(ks[0], (BATCH, SEQ, D_MODEL), f)
    x_sample = jax.random.normal(ks[1], (DEC_BATCH, DEC_SEQ, D_MODEL), f)
    cache_k = jax.random.normal(ks[2], (DEPTH, n_pool, PAGE_SIZE, N_KV_HEADS, HEAD_DIM), f)
    cache_v = jax.random.normal(ks[3], (DEPTH, n_pool, PAGE_SIZE, N_KV_HEADS, HEAD_DIM), f)
    cache_k_idx = jax.random.normal(ks[4], (DEPTH, n_pool, PAGE_SIZE, IDX_DIM), f)
    page_table = jax.random.permutation(ks[5], n_pool)[:used].reshape(DEC_BATCH, n_pages).astype(jnp.int32)
    norm_in_g = 1.0 + 0.02 * jax.random.normal(ks[6], (DEPTH, D_MODEL), f)
    w_in = jax.random.normal(ks[7], (DEPTH, D_MODEL, IN_WIDTH), f) * D_MODEL ** -0.5
    w_proj_a = jax.random.normal(ks[8], (DEPTH, ATTN_WIDTH, D_MODEL), f) * ATTN_WIDTH ** -0.5
    w_proj_b = jax.random.normal(ks[9], (DEPTH, GMLP_WIDTH, D_MODEL), f) * GMLP_WIDTH ** -0.5
    w_out = jax.random.normal(ks[10], (DEPTH, D_MODEL, D_MODEL), f) * D_MODEL ** -0.5
    ln_g = 1.0 + 0.02 * jax.random.normal(ks[11], (DEPTH, GMLP_WIDTH), f)
    ln_b = 0.02 * jax.random.normal(ks[12], (DEPTH, GMLP_WIDTH), f)
    w_spatial = jax.random.normal(ks[13], (DEPTH, GMLP_GROUPS, CHUNK, CHUNK), f) * CHUNK ** -0.5
    b_spatial = 1.0 + 0.01 * jax.random.normal(ks[14], (DEPTH, GMLP_GROUPS, CHUNK), f)
    norm_f_g = 1.0 + 0.02 * jax.random.normal(ks[15], (D_MODEL,), f)
    return {"x_prompt": x_prompt, "x_sample": x_sample, "cache_k": cache_k, "cache_v": cache_v,
            "cache_k_idx": cache_k_idx, "page_table": page_table, "norm_in_g": norm_in_g, "w_in": w_in,
            "w_proj_a": w_proj_a, "w_proj_b": w_proj_b, "w_out": w_out, "ln_g": ln_g, "ln_b": ln_b,
            "w_spatial": w_spatial, "b_spatial": b_spatial, "norm_f_g": norm_f_g}


def reference(x_prompt, x_sample, cache_k, cache_v, cache_k_idx, page_table, norm_in_g, w_in,
              w_proj_a, w_proj_b, w_out, ln_g, ln_b, w_spatial, b_spatial, norm_f_g):
    xp, xs = x_prompt, x_sample
    kp_l, vp_l, kip_l, ks_l, vs_l, kis_l, gv_l = [], [], [], [], [], [], []
    for l in range(DEPTH):
        q, k, v, qi, wi, ki, za, u, vb, zb, ga, gb = in_projection(xp, norm_in_g[l], w_in[l])
        q, k, v, qi, wi = attn_heads(q, k, v, qi, wi)
        o_a = prompt_attention(q, k, v, qi, wi, ki)
        o_b, _ = gmlp_mix(u, vb, ln_g[l], ln_b[l], w_spatial[l], b_spatial[l])
        xp = merge_out(xp, o_a, za, o_b, zb, ga, gb, w_proj_a[l], w_proj_b[l], w_out[l])
        kp_l.append(k)
        vp_l.append(v)
        kip_l.append(ki)
        q, k, v, qi, wi, ki, za, u, vb, zb, ga, gb = in_projection(xs, norm_in_g[l], w_in[l])
        q, k, v, qi, wi = attn_heads(q, k, v, qi, wi)
        o_a = sample_attention(q, k, v, qi, wi, ki, cache_k[l], cache_v[l], cache_k_idx[l], page_table)
        o_b, vn = gmlp_mix(u, vb, ln_g[l], ln_b[l], w_spatial[l], b_spatial[l])
        xs = merge_out(xs, o_a, za, o_b, zb, ga, gb, w_proj_a[l], w_proj_b[l], w_out[l])
        ks_l.append(k)
        vs_l.append(v)
        kis_l.append(ki)
        gv_l.append(vn)
    y_prompt = rmsnorm(xp, norm_f_g)
    y_sample = rmsnorm(xs, norm_f_g)
    new_k_prompt = jnp.stack(kp_l)
    new_v_prompt = jnp.stack(vp_l)
    new_k_idx_prompt = jnp.stack(kip_l)
    new_k_sample = jnp.stack(ks_l)
    new_v_sample = jnp.stack(vs_l)
    new_k_idx_sample = jnp.stack(kis_l)
    new_gmlp_v_sample = jnp.stack(gv_l)
    return (y_prompt, y_sample, new_k_prompt, new_v_prompt, new_k_idx_prompt,
            new_k_sample, new_v_sample, new_k_idx_sample, new_gmlp_v_sample)
```

```python
import numpy as np
from contextlib import ExitStack
import concourse.bass as bass
import concourse.mybir as mybir
from concourse.bass_utils import run_bass_kernel_spmd

F32 = mybir.dt.float32
BF16 = mybir.dt.bfloat16
I32 = mybir.dt.int32
AF = mybir.ActivationFunctionType
ALU = mybir.AluOpType
AX = mybir.AxisListType

NCORES = 8
D = 2048
NCH = 16
SEQ = 4096
NB = 32
OWN = 8
TOK = 1024
BIG = 30000.0
CBIG = 1.0e6
NIT = 16
TOPK = 256
COMPUTE = ("pe", "act", "dve", "pool")

C_Q, C_K, C_V, C_QI, C_WI, C_KI, C_ZA, C_U, C_VB, C_ZB, C_GA, C_GB = (
    0, 1024, 1280, 1536, 3584, 3600, 3728, 4752, 5776, 6800, 7824, 9872)
FM_Q, FM_QI, FM_ZA, FM_U, FM_ZB, FM_G = 0, 8, 24, 32, 40, 48
N_FM = 80


class _Op:
    __slots__ = ("eng", "fn", "deps", "is_dma", "slot", "marked", "mark_idx", "cum", "idx")


class Prog:
    def __init__(self, nc, stack):
        self.nc = nc
        self.stack = stack
        self.ops = []
        self.last_w = {}
        self.readers = {}
        self.slot_cum = {}
        self.bar = []
        self.psx = {}
        self.engs = {"pe": nc.tensor, "act": nc.scalar, "dve": nc.vector, "pool": nc.gpsimd, "sp": nc.sync}

    def add(self, eng, fn, reads=(), writes=(), slot=None):
        op = _Op()
        op.eng = eng
        op.fn = fn
        op.is_dma = slot is not None
        op.slot = slot
        op.marked = False
        op.mark_idx = None
        op.idx = len(self.ops)
        op.deps = set()
        if op.is_dma:
            self.slot_cum[slot] = self.slot_cum.get(slot, 0) + 16
            op.cum = self.slot_cum[slot]
        for y in self.bar:
            self._dep(op, y, "raw")
        for r in reads:
            lw = self.last_w.get(r)
            if lw is not None:
                self._dep(op, lw, "raw")
        for w in writes:
            lw = self.last_w.get(w)
            if lw is not None:
                self._dep(op, lw, "waw")
            for rd in self.readers.get(w, ()):
                self._dep(op, rd, "war")
        for r in reads:
            self.readers.setdefault(r, []).append(op)
        for w in writes:
            self.last_w[w] = op
            self.readers[w] = []
        for res in list(reads) + list(writes):
            if isinstance(res, tuple) and len(res) >= 2 and res[0] == "ps":
                lastx = self.psx.get(res[1])
                if lastx is not None and lastx is not op and lastx.eng != op.eng:
                    op.deps.add(lastx.idx)
                    if not lastx.is_dma:
                        lastx.marked = True
                self.psx[res[1]] = op
        self.ops.append(op)
        return op

    def _dep(self, x, y, kind):
        if y is x:
            return
        if not y.is_dma and not x.is_dma and y.eng == x.eng:
            if kind != "raw" or x.eng == "pe":
                return
        x.deps.add(y.idx)
        if not y.is_dma:
            y.marked = True

    def barrier(self):
        last = {}
        for op in self.ops:
            if op.fn is None:
                continue
            key = ("slot", op.slot) if op.is_dma else ("eng", op.eng)
            last[key] = op
        self.bar = list(last.values())

    def emit(self):
        nc = self.nc
        sems = {}
        for e in COMPUTE:
            sems[("eng", e)] = self.stack.enter_context(nc.semaphore("c_" + e))
        for i, s in enumerate(self.slot_cum):
            sems[("slot", s)] = self.stack.enter_context(nc.semaphore("d%d" % i))
        cnt = {e: 0 for e in COMPUTE}
        for op in self.ops:
            if op.marked:
                cnt[op.eng] += 1
                op.mark_idx = cnt[op.eng]
        waited = {}
        nw = 0
        for op in self.ops:
            eng = self.engs[op.eng]
            need = {}
            for di in op.deps:
                y = self.ops[di]
                if y.is_dma:
                    k, v = ("slot", y.slot), y.cum
                else:
                    k, v = ("eng", y.eng), y.mark_idx
                if need.get(k, 0) < v:
                    need[k] = v
            wd = waited.setdefault(op.eng, {})
            for k, v in need.items():
                if wd.get(k, 0) >= v:
                    continue
                eng.wait_ge(sems[k], v)
                wd[k] = v
                nw += 1
            if op.fn is None:
                continue
            ins = op.fn(eng)
            if op.is_dma:
                ins.then_inc(sems[("slot", op.slot)], 16)
            elif op.marked:
                ins.then_inc(sems[("eng", op.eng)], 1)
        return dict(nops=len(self.ops), nwaits=nw, marks=cnt, nsems=len(sems))


def build_program(debug=False, upto=99):
    nc = bass.Bass("TRN2", target_bir_lowering=False)

    def din(name, shape, dtype=F32):
        return nc.dram_tensor(name, list(shape), dtype, kind="ExternalInput").ap()

    def dout(name, shape, dtype=F32):
        return nc.dram_tensor(name, list(shape), dtype, kind="ExternalOutput").ap()

    x_all = din("x_all", [SEQ, D])
    x_own = din("x_own", [TOK, D])
    qrel_d = din("qrel", [128, OWN])
    x_s = din("x_s", [1, D])
    ptab = din("ptab", [1, 128], I32)
    pool_ki8 = din("pool_ki8", [1280 * 8, 2048])
    pool_k16 = din("pool_k16", [1280 * 16, 2048])
    pool_v16 = din("pool_v16", [1280 * 16, 2048])
    g_in_d = din("g_in", [1, D])
    g_f_d = din("g_f", [1, D])
    ln_g_d = din("ln_g", [1, 1024])
    ln_b_d = din("ln_b", [1, 1024])
    bsp_d = din("bsp", [1, 1024])
    ws00_d = din("ws00", [1, 1024])
    bs0_d = din("bs0", [1, 1024])
    wspT_d = din("wspT", [128, 8 * 128])
    wkvki_d = din("wkvki", [128, NCH * 640])
    wfm_d = din("wfm", [N_FM, 128, NCH * 128])
    wwi_d = din("wwi", [128, NCH * 16])
    wvb_d = din("wvb", [128, NCH * 1024])
    wpa_d = din("wpa", [16, 128, 8 * 128])
    wpb_d = din("wpb", [16, 128, 8 * 128])
    wout_d = din("wout", [4, 128, NCH * 512])

    y_own = dout("y_own", [TOK, D])
    knew = dout("knew", [SEQ, 256])
    vnew = dout("vnew", [SEQ, 256])
    kinew = dout("kinew", [SEQ, 128])
    ys_o = dout("ys", [1, D])
    ks_o = dout("ks", [1, 256])
    vs_o = dout("vs", [1, 256])
    kis_o = dout("kis", [1, 128])
    gvs_o = dout("gvs", [1, 1024])
    dbg = {}
    if debug:
        dbg["qT"] = dout("dbg_qT", [128, 8 * TOK], BF16)
        dbg["kT"] = dout("dbg_kT", [128, 2 * SEQ], BF16)
        dbg["kiT"] = dout("dbg_kiT", [128, SEQ], BF16)
        dbg["qiT"] = dout("dbg_qiT", [128, 16 * TOK], BF16)
        dbg["wabs"] = dout("dbg_wabs", [128, 128])
        dbg["acc"] = dout("dbg_acc", [128, OWN * 4096])
        dbg["thr"] = dout("dbg_thr", [128, OWN])
        dbg["oaT"] = dout("dbg_oaT", [128, 8 * TOK], BF16)
        dbg["obT"] = dout("dbg_obT", [128, 8 * TOK], BF16)
        dbg["mixT"] = dout("dbg_mixT", [128, 16 * TOK], BF16)

    st = ExitStack()
    P = Prog(nc, st)
    ARENA_B = 207 * 1024
    arena = st.enter_context(nc.sbuf_tensor("arena", [128, ARENA_B // 2], BF16))
    ps = [st.enter_context(nc.psum_tensor("ps%d" % i, [128, 512], F32)) for i in range(8)]
    psb = [p[:, :].bitcast(BF16) for p in ps]


    def finish():
        fin = P.add("sp", None)
        lastd = {}
        for op in P.ops:
            if op.is_dma:
                lastd[op.slot] = op
        for op in lastd.values():
            fin.deps.add(op.idx)
        info = P.emit()
        st.close()
        return nc, info

    KB = 1024

    def buf(off_b, nbytes, dtype=BF16, parts=128):
        if off_b >= 20 * KB:
            off_b += KB
        assert off_b + nbytes <= ARENA_B, (off_b, nbytes)
        a = arena[0:parts, off_b // 2:(off_b + nbytes) // 2]
        if dtype != BF16:
            a = a.bitcast(dtype)
        return a

    def v3(ap, a):
        return ap.rearrange("p (a b) -> p a b", a=a)

    o = 0
    ident = buf(o, 256); o += 256
    identB4 = buf(o, 1024); o += 1024
    ones_bf = buf(o, 256); o += 256
    identf = buf(o, 512, F32); o += 512
    iota_f = buf(o, 2048, F32); o += 2048
    pow2 = buf(o, 128, F32); o += 128
    qrel = buf(o, 32, F32); o += 32
    epsc = buf(o, 16, F32); o += 16
    stat = buf(o, 256, F32); o += 256
    wabs = buf(o, 512, F32); o += 512
    wsgn = buf(o, 512, F32); o += 512
    sm = buf(o, 2048, F32); o += 2048
    iota_i = sm.bitcast(I32)
    smb = buf(o, 1024, BF16); o += 1024
    WsT = buf(o, 2048); o += 2048
    gbc = buf(o, 8192, F32); o += 8192
    ksv = buf(o, 2560, F32, parts=1); o += 2560
    assert o <= 21 * KB, o
    hsT = smb[:, 0:16]
    oazTs = smb[:, 16:24]
    obzTs = smb[:, 24:32]
    mixTs = smb[:, 32:48]

    kT_all = buf(20 * KB, 16 * KB)
    kiT_all = buf(36 * KB, 8 * KB)
    v_all = buf(44 * KB, 16 * KB)
    qT = buf(60 * KB, 16 * KB)
    qiT = buf(76 * KB, 32 * KB)
    hT_own = buf(108 * KB, 32 * KB)
    oaT = buf(140 * KB, 16 * KB)
    mixT = buf(156 * KB, 32 * KB)
    kT3 = v3(kT_all, 2)
    v3a = v3(v_all, 32)
    qT3 = v3(qT, 8)
    qiT3 = v3(qiT, 16)
    hT3 = v3(hT_own, 16)
    oaT3 = v3(oaT, 8)
    mixT3 = v3(mixT, 16)

    P.add("pool", lambda e: e.memset(identf, 1.0), writes=["identf"])
    P.add("pool", lambda e: e.affine_select(out=identf, in_=identf, pattern=[[-1, 128]], compare_op=ALU.is_equal,
                                            fill=0.0, base=0, channel_multiplier=1), reads=["identf"], writes=["identf"])
    P.add("dve", lambda e: e.tensor_copy(out=ident, in_=identf), reads=["identf"], writes=["ident"])
    for j in range(4):
        P.add("dve", lambda e, j=j: e.tensor_scalar(out=identB4[:, j * 128:(j + 1) * 128], in0=identf, scalar1=BIG, scalar2=None,
                                                    op0=ALU.mult), reads=["identf"], writes=["identB4"])
    P.add("pool", lambda e: e.memset(ones_bf, 1.0), writes=["ones"])
    P.add("pool", lambda e: e.iota(iota_i, pattern=[[1, 512]], base=0, channel_multiplier=0), writes=["iota_i"])
    P.add("dve", lambda e: e.tensor_copy(out=iota_f, in_=iota_i), reads=["iota_i"], writes=["iota"])
    for k in range(NIT + 1):
        P.add("pool", lambda e, k=k: e.memset(pow2[:, k:k + 1], float(2.0 ** -(k + 1))), writes=["pow2"])
    P.add("pool", lambda e: e.memset(epsc[:, 0:1], 1e-6), writes=["epsc"])
    P.add("pool", lambda e: e.memset(epsc[:, 1:2], 1e-5), writes=["epsc"])
    P.add("sp", lambda e: e.dma_start(out=qrel, in_=qrel_d), writes=["qrel"], slot="c_qrel")
    P.add("sp", lambda e: e.dma_start(out=gbc, in_=g_in_d.partition_broadcast(128)), writes=["gbc"], slot="c_gbc")

    def rms_block(src_ap, xb, xres, xslot, hb, hres, junk, sidx, gb_ap, nparts=128):
        ss = stat[0:nparts, sidx:sidx + 1]
        sd = stat[0:nparts, sidx + 1:sidx + 2]
        rs = stat[0:nparts, sidx + 2:sidx + 3]
        P.add("sp", lambda e: e.dma_start(out=xb, in_=src_ap), writes=[xres], slot=xslot)
        P.add("act", lambda e: e.activation(out=junk, in_=xb, func=AF.Square, accum_out=ss),
              reads=[xres], writes=["junk", ("st", sidx)])
        P.add("act", lambda e: e.activation(out=sd, in_=ss, func=AF.Sqrt, scale=1.0 / D, bias=epsc[0:nparts, 0:1]),
              reads=[("st", sidx), "epsc"], writes=[("st", sidx + 1)])
        P.add("dve", lambda e: e.reciprocal(out=rs, in_=sd), reads=[("st", sidx + 1)], writes=[("st", sidx + 2)])
        P.add("dve", lambda e: e.scalar_tensor_tensor(out=hb, in0=xb, scalar=rs, in1=gb_ap, op0=ALU.mult, op1=ALU.mult),
              reads=[xres, ("st", sidx + 2), "gbc"], writes=[hres])

    def transposes16(hb, hres, bankA, bankB):
        for c in range(16):
            bk = bankA if c < 8 else bankB
            cc = c % 8
            P.add("pe", lambda e, c=c, bk=bk, cc=cc: e.transpose(out=psb[bk][:, cc * 128:(cc + 1) * 128],
                                                                 in_=hb[:, c * 128:(c + 1) * 128], identity=ident),
                  reads=[hres, "ident"], writes=[("ps", bk)])

    if upto <= 0:
        return finish()
    S1 = 60 * KB
    xbuf = [buf(S1 + i * 8 * KB, 8 * KB, F32) for i in range(2)]
    hbuf = [buf(S1 + 16 * KB + i * 4 * KB, 4 * KB) for i in range(2)]
    junk = buf(S1 + 24 * KB, 4 * KB)
    hTblk = [buf(S1 + 28 * KB + i * 4 * KB, 4 * KB) for i in range(2)]
    Wkvki = buf(S1 + 36 * KB, 20 * KB)
    kvf = [buf(S1 + 56 * KB + i * 2560, 2560, F32) for i in range(2)]
    kb16 = [buf(S1 + 62 * KB + i * 768, 768) for i in range(2)]
    Wk3 = v3(Wkvki, 16)
    for q4 in range(8):
        P.add("pool", lambda e, q4=q4: e.dma_start(out=Wk3[:, q4 * 2:(q4 + 1) * 2, :],
                                                  in_=v3(wkvki_d, 16)[:, q4 * 2:(q4 + 1) * 2, :]),
              writes=["Wkvki"], slot="wkvki%d" % (q4 % 4))
    xs_f = buf(S1 + 64 * KB, 8 * KB, F32, parts=1)
    hs_f = buf(S1 + 72 * KB, 8 * KB, F32, parts=1)
    kvs = buf(S1 + 80 * KB, 2560, F32, parts=1)
    import os as _os
    if _os.environ.get('SKIP_S1'):
        P.add('pool', lambda e: e.memset(hsT, 0.0), writes=['hsT'])
    else:
        rms_block(x_s, xs_f, "xs_f", "xs_f", hs_f, "hs_f", junk[0:1, :], 8, gbc[0:1, :], nparts=1)
        for c in range(16):
            P.add("pe", lambda e, c=c: e.transpose(out=ps[4][:, c:c + 1], in_=hs_f[0:1, c * 128:(c + 1) * 128], identity=identf[0:1, 0:1]),
                  reads=["hs_f", "identf"], writes=[("ps", 4)])
        P.add("act", lambda e: e.copy(out=hsT, in_=ps[4][:, 0:16]), reads=[("ps", 4)], writes=["hsT"])
        for c in range(16):
            P.add("pe", lambda e, c=c: e.matmul(ps[6][0:1, 0:512], lhsT=hsT[:, c:c + 1], rhs=Wk3[:, c, 0:512], start=(c == 0), stop=(c == 15)),
                  reads=["hsT", "Wkvki"], writes=[("ps", 6)])
            P.add("pe", lambda e, c=c: e.matmul(ps[7][0:1, 0:128], lhsT=hsT[:, c:c + 1], rhs=Wk3[:, c, 512:640], start=(c == 0), stop=(c == 15)),
                  reads=["hsT", "Wkvki"], writes=[("ps", 7)])
        P.add("dve", lambda e: e.tensor_copy(out=kvs[:, 0:512], in_=ps[6][0:1, 0:512]), reads=[("ps", 6)], writes=["kvs0"])
        P.add("dve", lambda e: e.tensor_copy(out=kvs[:, 512:640], in_=ps[7][0:1, 0:128]), reads=[("ps", 7)], writes=["kvs1"])
        P.add("dve", lambda e: e.tensor_copy(out=ksv, in_=kvs), reads=["kvs0", "kvs1"], writes=["ksv"])
        P.add("sp", lambda e: e.dma_start(out=ks_o, in_=kvs[:, 0:256]), reads=["kvs0"], writes=["o_ks"], slot="o_ks")
        P.add("sp", lambda e: e.dma_start(out=vs_o, in_=kvs[:, 256:512]), reads=["kvs0"], writes=["o_vs"], slot="o_vs")
        P.add("sp", lambda e: e.dma_start(out=kis_o, in_=kvs[:, 512:640]), reads=["kvs1"], writes=["o_kis"], slot="o_kis")
    S1CUT = float(_os.environ.get('S1CUT', '99'))
    for b in range(NB):
        pb = b % 2
        xb, hb, htb = xbuf[pb], hbuf[pb], hTblk[pb]
        rms_block(x_all[b * 128:(b + 1) * 128, :], xb, ("xbuf", pb), "xbuf%d" % pb, hb, ("hbuf", pb), junk, 4 * pb, gbc)
        bA, bB = (0, 1) if pb == 0 else (2, 3)
        bKV, bKI = (4, 5) if pb == 0 else (6, 7)
        if S1CUT <= 1:
            continue
        transposes16(hb, ("hbuf", pb), bA, bB)
        P.add("act", lambda e, bA=bA, htb=htb: e.copy(out=htb[:, 0:1024], in_=psb[bA][:, 0:1024]),
              reads=[("ps", bA)], writes=[("htb", pb, 0)])
        P.add("dve", lambda e, bB=bB, htb=htb: e.tensor_copy(out=htb[:, 1024:2048], in_=psb[bB][:, 0:1024]),
              reads=[("ps", bB)], writes=[("htb", pb, 1)])
        htb3 = v3(htb, 16)
        if S1CUT <= 2:
            continue
        for c in range(16):
            P.add("pe", lambda e, c=c, htb3=htb3, bKV=bKV: e.matmul(ps[bKV][:, 0:512], lhsT=htb3[:, c, :], rhs=Wk3[:, c, 0:512],
                                                                   start=(c == 0), stop=(c == 15)),
                  reads=[("htb", pb, c // 8), "Wkvki"], writes=[("ps", bKV)])
            P.add("pe", lambda e, c=c, htb3=htb3, bKI=bKI: e.matmul(ps[bKI][:, 0:128], lhsT=htb3[:, c, :], rhs=Wk3[:, c, 512:640],
                                                                   start=(c == 0), stop=(c == 15)),
                  reads=[("htb", pb, c // 8), "Wkvki"], writes=[("ps", bKI)])
        kf = kvf[pb]
        k16 = kb16[pb]
        if S1CUT <= 2.3:
            continue
        P.add("dve", lambda e, kf=kf, bKV=bKV: e.tensor_copy(out=kf[:, 0:512], in_=ps[bKV][:, 0:512]),
              reads=[("ps", bKV)], writes=[("kvf", pb, 0)])
        P.add("act", lambda e, kf=kf, bKI=bKI: e.copy(out=kf[:, 512:640], in_=ps[bKI][:, 0:128]),
              reads=[("ps", bKI)], writes=[("kvf", pb, 1)])
        P.add("dve", lambda e, b=b, bKV=bKV: e.tensor_copy(out=v3a[:, b, :], in_=ps[bKV][:, 256:512]),
              reads=[("ps", bKV)], writes=[("v_all", b)])
        P.add("dve", lambda e, k16=k16, bKV=bKV: e.tensor_copy(out=k16[:, 0:256], in_=ps[bKV][:, 0:256]),
              reads=[("ps", bKV)], writes=[("kb16", pb)])
        P.add("act", lambda e, k16=k16, bKI=bKI: e.copy(out=k16[:, 256:384], in_=ps[bKI][:, 0:128]),
              reads=[("ps", bKI)], writes=[("kb16", pb)])
        rows = slice(b * 128, (b + 1) * 128)
        if S1CUT <= 2.6:
            continue
        P.add("sp", lambda e, kf=kf, rows=rows: e.dma_start(out=knew[rows, :], in_=kf[:, 0:256]),
              reads=[("kvf", pb, 0)], writes=[("o_k", pb)], slot="o_k%d" % pb)
        P.add("sp", lambda e, kf=kf, rows=rows: e.dma_start(out=vnew[rows, :], in_=kf[:, 256:512]),
              reads=[("kvf", pb, 0)], writes=[("o_v", pb)], slot="o_v%d" % pb)
        P.add("sp", lambda e, kf=kf, rows=rows: e.dma_start(out=kinew[rows, :], in_=kf[:, 512:640]),
              reads=[("kvf", pb, 1)], writes=[("o_ki", pb)], slot="o_ki%d" % pb)
        if S1CUT <= 3:
            continue
        for j in range(3):
            P.add("pe", lambda e, j=j, k16=k16, bKI=bKI: e.transpose(out=psb[bKI][:, 256 + j * 128:256 + (j + 1) * 128],
                                                                    in_=k16[:, j * 128:(j + 1) * 128], identity=ident),
                  reads=[("kb16", pb), "ident"], writes=[("ps", bKI)])
        P.add("act", lambda e, b=b, bKI=bKI: e.copy(out=kT3[:, :, b * 128:(b + 1) * 128],
                                                   in_=v3(psb[bKI][:, 256:512], 2)),
              reads=[("ps", bKI)], writes=[("kT", b)])
        P.add("act", lambda e, b=b, bKI=bKI: e.copy(out=kiT_all[:, b * 128:(b + 1) * 128], in_=psb[bKI][:, 512:640]),
              reads=[("ps", bKI)], writes=[("kiT", b)])

    if upto <= 1:
        return finish()
    P.barrier()
    S2 = 140 * KB
    xbuf = [buf(S2 + i * 8 * KB, 8 * KB, F32) for i in range(2)]
    hbuf = [buf(S2 + 16 * KB + i * 4 * KB, 4 * KB) for i in range(2)]
    junk = buf(S2 + 24 * KB, 4 * KB)
    wbuf = [buf(S2 + 28 * KB + i * 4 * KB, 4 * KB) for i in range(3)]
    wwi = buf(S2 + 40 * KB, 512)

    def own_pass(xbuf, hbuf, junk):
        for i in range(OWN):
            pb = i % 2
            xb, hb = xbuf[pb], hbuf[pb]
            rms_block(x_own[i * 128:(i + 1) * 128, :], xb, ("xbuf", pb), "xbuf%d" % pb, hb, ("hbuf", pb), junk, 4 * pb, gbc)
            bA, bB = (0, 1) if pb == 0 else (2, 3)
            transposes16(hb, ("hbuf", pb), bA, bB)
            P.add("act", lambda e, bA=bA, i=i: e.copy(out=hT3[:, 0:8, i * 128:(i + 1) * 128], in_=v3(psb[bA][:, 0:1024], 8)),
                  reads=[("ps", bA)], writes=[("hT", i, 0)])
            P.add("dve", lambda e, bB=bB, i=i: e.tensor_copy(out=hT3[:, 8:16, i * 128:(i + 1) * 128], in_=v3(psb[bB][:, 0:1024], 8)),
                  reads=[("ps", bB)], writes=[("hT", i, 1)])

    own_pass(xbuf, hbuf, junk)

    hT_reads = [("hT", i, h) for i in range(OWN) for h in range(2)]
    fm_state = {"n": 0}

    def proj_fm(cb, scol, evac, pa=None):
        n = fm_state["n"]
        fm_state["n"] += 1
        ws = n % 3
        wsl = wbuf[ws]
        w3 = v3(wsl, 16)
        P.add("pool", lambda e: e.dma_start(out=wsl, in_=wfm_d[cb]), writes=[("wbuf", ws)], slot="wbuf%d" % ws)
        if pa is None:
            pa = 2 * (n % 3)
        for c in range(16):
            for hf in range(2):
                P.add("pe", lambda e, c=c, hf=hf: e.matmul(ps[pa + hf][:, 0:512], lhsT=w3[:, c, :], rhs=hT3[:, c, hf * 512:(hf + 1) * 512],
                                                          start=(c == 0), stop=(c == 15)),
                      reads=[("wbuf", ws)] + hT_reads, writes=[("ps", pa + hf)])
            P.add("pe", lambda e, c=c: e.matmul(ps[6][:, scol:scol + 1], lhsT=w3[:, c, :], rhs=hsT[:, c:c + 1],
                                                start=(c == 0), stop=(c == 15)),
                  reads=[("wbuf", ws), "hsT"], writes=[("ps", 6)])
        evac(pa)

    def evac_copy(dst3, h):
        def f(pa):
            P.add("act", lambda e: e.copy(out=dst3[:, h, 0:512], in_=ps[pa][:, 0:512]), reads=[("ps", pa)], writes=[(id(dst3), h, 0)])
            P.add("dve", lambda e: e.tensor_copy(out=dst3[:, h, 512:1024], in_=ps[pa + 1][:, 0:512]), reads=[("ps", pa + 1)],
                  writes=[(id(dst3), h, 1)])
        return f

    for h in range(8):
        proj_fm(FM_Q + h, h, evac_copy(qT3, h))
    for h in range(16):
        proj_fm(FM_QI + h, 8 + h, evac_copy(qiT3, h))
    P.add("dve", lambda e: e.tensor_copy(out=sm[:, 0:24], in_=ps[6][:, 0:24]), reads=[("ps", 6)], writes=["sm_q"])
    P.add("pool", lambda e: e.dma_start(out=wwi, in_=wwi_d), writes=["wwi"], slot="wwi")
    wwi3 = v3(wwi, 16)
    for i in range(OWN):
        for c in range(16):
            P.add("pe", lambda e, i=i, c=c: e.matmul(ps[7][:, i * 16:(i + 1) * 16], lhsT=hT3[:, c, i * 128:(i + 1) * 128], rhs=wwi3[:, c, :],
                                                    start=(c == 0), stop=(c == 15)),
                  reads=["wwi"] + hT_reads, writes=[("ps", 7)])
    P.add("act", lambda e: e.activation(out=wabs, in_=ps[7][:, 0:128], func=AF.Abs), reads=[("ps", 7)], writes=["wabs"])
    P.add("act", lambda e: e.activation(out=wsgn, in_=ps[7][:, 0:128], func=AF.Sign), reads=[("ps", 7)], writes=["wsgn"])
    for c in range(16):
        P.add("pe", lambda e, c=c: e.matmul(ps[7][0:16, 128:129], lhsT=wwi3[:, c, :], rhs=hsT[:, c:c + 1], start=(c == 0), stop=(c == 15)),
              reads=["wwi", "hsT"], writes=[("ps", 7)])
    P.add("dve", lambda e: e.tensor_copy(out=sm[0:16, 120:121], in_=ps[7][0:16, 128:129]), reads=[("ps", 7)], writes=["sm_w"])
    if debug:
        P.add("sp", lambda e: e.dma_start(out=dbg["qT"], in_=qT), reads=[(id(qT3), h, j) for h in range(8) for j in range(2)], writes=["dbg_dq"], slot="dbg0")
        P.add("sp", lambda e: e.dma_start(out=dbg["qiT"], in_=qiT), reads=[(id(qiT3), h, j) for h in range(16) for j in range(2)], writes=["dbg_dqi"], slot="dbg1")
        P.add("sp", lambda e: e.dma_start(out=dbg["kT"], in_=kT_all), reads=[("kT", b) for b in range(NB)], writes=["dbg_dk"], slot="dbg2")
        P.add("sp", lambda e: e.dma_start(out=dbg["kiT"], in_=kiT_all), reads=[("kiT", b) for b in range(NB)], writes=["dbg_dki"], slot="dbg3")
        P.add("sp", lambda e: e.dma_start(out=dbg["wabs"], in_=wabs), reads=["wabs"], writes=["dbg_dwa"], slot="dbg4")

    if upto <= 2:
        return finish()
    P.barrier()
    acc = buf(108 * KB, 16 * KB, F32)
    negm = buf(124 * KB, 8 * KB)
    jnk3 = buf(132 * KB, 8 * KB)
    S3 = 156 * KB
    Rb = [buf(S3 + i * 2 * KB, 2 * KB, F32) for i in range(3)]
    PT = [buf(S3 + 6 * KB + i * KB, KB) for i in range(3)]
    pen = buf(S3 + 9 * KB, 2 * KB, F32)
    tmn = buf(S3 + 11 * KB, 2 * KB, F32)
    rden = buf(S3 + 13 * KB, 2 * KB, F32)
    bs = buf(S3 + 15 * KB, 512, F32)
    HK = bs[:, 0:NIT + 1]
    TC = bs[:, 32:32 + NIT + 2]
    cnt = bs[:, 64:65]
    tmpc = bs[:, 65:66]
    rmax = bs[:, 66:67]
    rmin = bs[:, 67:68]
    rmin2 = bs[:, 68:69]
    lo0 = bs[:, 69:70]
    w0 = bs[:, 70:71]
    thr = bs[:, 72:80]
    SCALE = float(128 ** -0.5)
    rcount = {"s": 0, "st": 0, "pt": 0}
    for i in range(OWN):
        span = 512 * (i + 1)
        tok = slice(i * 128, (i + 1) * 128)
        for kg in range(i + 1):
            cols = slice(kg * 512, (kg + 1) * 512)
            for h in range(16):
                n = rcount["s"]
                rcount["s"] += 1
                bk = n % 3
                rb = Rb[n % 3]
                P.add("pe", lambda e, h=h, bk=bk, cols=cols, tok=tok: e.matmul(ps[bk][:, 0:512], lhsT=qiT3[:, h, tok], rhs=kiT_all[:, cols],
                                                                            start=True, stop=True),
                      reads=["qiT", "kiT"], writes=[("ps", bk)])
                P.add("act", lambda e, h=h, bk=bk, rb=rb, i=i: e.activation(out=rb, in_=ps[bk][:, 0:512], func=AF.Relu,
                                                                         scale=wabs[:, i * 16 + h:i * 16 + h + 1]),
                      reads=[("ps", bk), "wabs"], writes=[("Rb", n % 3)])
                sg = wsgn[:, i * 16 + h:i * 16 + h + 1]
                if h == 0:
                    P.add("dve", lambda e, rb=rb, sg=sg, cols=cols: e.tensor_scalar(out=acc[:, cols], in0=rb, scalar1=sg, scalar2=None, op0=ALU.mult),
                          reads=[("Rb", n % 3), "wsgn"], writes=[("acc", kg)])
                else:
                    P.add("dve", lambda e, rb=rb, sg=sg, cols=cols: e.scalar_tensor_tensor(out=acc[:, cols], in0=rb, scalar=sg, in1=acc[:, cols],
                                                                                        op0=ALU.mult, op1=ALU.add),
                          reads=[("Rb", n % 3), "wsgn", ("acc", kg)], writes=[("acc", kg)])
        accr = [("acc", kg) for kg in range(i + 1)]
        last = slice(i * 512, (i + 1) * 512)
        P.add("dve", lambda e, i=i: e.tensor_scalar(out=pen, in0=iota_f, scalar1=qrel[:, i:i + 1], scalar2=CBIG, op0=ALU.is_gt, op1=ALU.mult),
              reads=["iota", "qrel"], writes=["pen"])
        P.add("dve", lambda e, last=last: e.tensor_tensor(out=tmn, in0=acc[:, last], in1=pen, op=ALU.add), reads=["pen", ("acc", i)], writes=["tmn"])
        P.add("dve", lambda e: e.tensor_reduce(out=rmin, in_=tmn, axis=AX.X, op=ALU.min), reads=["tmn"], writes=["rmin"])
        P.add("dve", lambda e, last=last: e.tensor_tensor(out=acc[:, last], in0=acc[:, last], in1=pen, op=ALU.subtract),
              reads=["pen", ("acc", i)], writes=[("acc", i)])
        P.add("dve", lambda e, span=span: e.tensor_reduce(out=rmax, in_=acc[:, 0:span], axis=AX.X, op=ALU.max), reads=accr, writes=["rmax"])
        if i > 0:
            P.add("dve", lambda e, i=i: e.tensor_reduce(out=rmin2, in_=acc[:, 0:i * 512], axis=AX.X, op=ALU.min), reads=accr, writes=["rmin2"])
            P.add("dve", lambda e: e.tensor_tensor(out=rmin, in0=rmin, in1=rmin2, op=ALU.min), reads=["rmin", "rmin2"], writes=["rmin"])
        P.add("dve", lambda e: e.tensor_tensor(out=w0, in0=rmax, in1=rmin, op=ALU.subtract), reads=["rmax", "rmin"], writes=["w0"])
        P.add("dve", lambda e: e.tensor_scalar(out=w0, in0=w0, scalar1=float(2.0 ** -10), scalar2=1e-3, op0=ALU.mult, op1=ALU.add),
              reads=["w0"], writes=["w0b"])
        P.add("dve", lambda e: e.tensor_tensor(out=lo0, in0=rmin, in1=w0, op=ALU.subtract), reads=["rmin", "w0b"], writes=["lo0"])
        P.add("dve", lambda e: e.tensor_tensor(out=w0, in0=rmax, in1=lo0, op=ALU.subtract), reads=["rmax", "lo0"], writes=["w0c"])
        P.add("dve", lambda e: e.tensor_scalar(out=HK, in0=pow2[:, 0:NIT + 1], scalar1=w0, scalar2=None, op0=ALU.mult),
              reads=["pow2", "w0c"], writes=["HK"])
        P.add("dve", lambda e: e.tensor_tensor(out=TC[:, 0:1], in0=lo0, in1=HK[:, 0:1], op=ALU.add), reads=["lo0", "HK"], writes=[("TC", 0)])
        for k in range(NIT):
            P.add("dve", lambda e, k=k, span=span: e.tensor_scalar(out=jnk3[:, 0:span], in0=acc[:, 0:span], scalar1=TC[:, k:k + 1], scalar2=None,
                                                                  op0=ALU.is_ge, op1=ALU.add, accum_out=cnt),
                  reads=accr + [("TC", k)], writes=["jnk3", "cnt"])
            P.add("dve", lambda e, k=k: e.scalar_tensor_tensor(out=tmpc, in0=cnt, scalar=TOPK - 0.5, in1=HK[:, k:k + 1], op0=ALU.is_ge, op1=ALU.mult),
                  reads=["cnt", "HK"], writes=["tmpc"])
            P.add("dve", lambda e, k=k: e.scalar_tensor_tensor(out=TC[:, k + 1:k + 2], in0=tmpc, scalar=HK[:, k + 1:k + 2], in1=TC[:, k:k + 1],
                                                               op0=ALU.subtract, op1=ALU.add),
                  reads=["tmpc", "HK", ("TC", k)], writes=[("TC", k + 1)])
        P.add("dve", lambda e, i=i: e.tensor_tensor(out=thr[:, i:i + 1], in0=TC[:, NIT:NIT + 1], in1=HK[:, NIT:NIT + 1], op=ALU.subtract),
              reads=[("TC", NIT), "HK"], writes=[("thr", i)])
        P.add("dve", lambda e, i=i, span=span: e.tensor_scalar(out=negm[:, 0:span], in0=acc[:, 0:span], scalar1=thr[:, i:i + 1], scalar2=1.0,
                                                              op0=ALU.is_ge, op1=ALU.subtract),
              reads=accr + [("thr", i)], writes=["negm"])
        if debug:
            P.add("sp", lambda e, i=i, span=span: e.dma_start(out=dbg["acc"][:, i * 4096:i * 4096 + span], in_=acc[:, 0:span]),
                  reads=accr, writes=["dbg_dacc"], slot="dbg5")
        nkb = 4 * (i + 1)
        for g in range(2):
            qrhs = qT3[:, 4 * g:4 * g + 4, tok]

            def emit_st(kb, g=g, qrhs=qrhs):
                n = rcount["st"]
                rcount["st"] += 1
                bk = 3 + n % 2
                P.add("pe", lambda e: e.matmul(ps[bk][:, 0:512], lhsT=kT3[:, g, kb * 128:(kb + 1) * 128], rhs=qrhs, start=True, stop=False),
                      reads=["qT", "kT"], writes=[("ps", bk)])
                P.add("pe", lambda e: e.matmul(ps[bk][:, 0:512], lhsT=negm[:, kb * 128:(kb + 1) * 128], rhs=identB4, start=False, stop=True),
                      reads=["negm", "identB4"], writes=[("ps", bk)])
                return bk

            def emit_pv(kb, bk, g=g, nkb=nkb):
                n = rcount["pt"]
                rcount["pt"] += 1
                pt = PT[n % 3]
                P.add("act", lambda e: e.activation(out=pt, in_=ps[bk][:, 0:512], func=AF.Exp, scale=SCALE), reads=[("ps", bk)], writes=[("PT", n % 3)])
                P.add("pe", lambda e: e.matmul(ps[5][:, 0:512], lhsT=v3a[:, kb, g * 128:(g + 1) * 128], rhs=pt, start=(kb == 0), stop=(kb == nkb - 1)),
                      reads=[("PT", n % 3), "v_all"], writes=[("ps", 5)])
                P.add("pe", lambda e: e.matmul(ps[6][:, 0:512], lhsT=ones_bf, rhs=pt, start=(kb == 0), stop=(kb == nkb - 1)),
                      reads=[("PT", n % 3), "ones"], writes=[("ps", 6)])

            prev = None
            for kb in range(nkb):
                bk = emit_st(kb)
                if prev is not None:
                    emit_pv(*prev)
                prev = (kb, bk)
            emit_pv(*prev)
            P.add("dve", lambda e: e.reciprocal(out=rden, in_=ps[6][:, 0:512]), reads=[("ps", 6)], writes=["rden"])
            P.add("dve", lambda e, g=g, tok=tok: e.tensor_tensor(out=oaT3[:, 4 * g:4 * g + 4, tok], in0=v3(ps[5][:, 0:512], 4), in1=v3(rden, 4), op=ALU.mult),
                  reads=[("ps", 5), "rden"], writes=[("oaT", i, g)])
    if debug:
        P.add("sp", lambda e: e.dma_start(out=dbg["thr"], in_=thr), reads=[("thr", i) for i in range(OWN)], writes=["dbg_dthr"], slot="dbg6")
        P.add("sp", lambda e: e.dma_start(out=dbg["oaT"], in_=oaT), reads=[("oaT", i, g) for i in range(OWN) for g in range(2)], writes=["dbg_doa"], slot="dbg7")

    if upto <= 3:
        return finish()

    P.barrier()
    oaTs = sm[:, 112:120]
    B3 = 20 * KB
    idxp_i = buf(B3, 64, I32)
    idx8 = buf(B3 + 64, 64, I32)
    idx16 = buf(B3 + 128, 64, I32)
    pgf = buf(B3 + 192, 64, F32)
    ones_f = buf(B3 + 512, 512, F32)
    scb = buf(B3 + 1024, 1024, F32)
    sc = scb[:, 0:129]
    vld = buf(B3 + 2048, 1024, F32)[:, 0:129]
    kib = [buf(B3 + 4 * KB + i * 4 * KB, 4 * KB) for i in range(2)]
    kidT = [buf(B3 + 12 * KB + i * KB, KB) for i in range(2)]
    Rs = [buf(B3 + 14 * KB + i * 2 * KB, 2 * KB, F32) for i in range(2)]
    Kc = [buf(B3 + 18 * KB + i * 4 * KB, 4 * KB) for i in range(2)]
    KTb = [buf(B3 + 26 * KB + i * 4 * KB, 4 * KB) for i in range(2)]
    Kx = buf(B3 + 34 * KB, 512)
    Vx = buf(B3 + 35 * KB, 512)
    Psb = buf(B3 + 36 * KB, 4224, F32)
    Pbf = buf(B3 + 41 * KB, 2112)
    Pred = buf(B3 + 44 * KB, 64, F32)
    smx = buf(B3 + 45 * KB, 256, F32)
    osb = buf(B3 + 46 * KB, 1024, F32)
    qiTs_bf = smb[:, 56:72]
    qTs_bf = smb[:, 48:56]
    w_s = sm[0:16, 120:121]
    IOA = bass.IndirectOffsetOnAxis
    P.add("sp", lambda e: e.dma_start(out=idxp_i[:, 0:1], in_=ptab.rearrange("o p -> p o")), writes=["idxp"], slot="s_idx")
    P.add("dve", lambda e: e.tensor_copy(out=pgf[:, 0:1], in_=idxp_i[:, 0:1]), reads=["idxp"], writes=["pgf0"])
    P.add("dve", lambda e: e.tensor_scalar(out=pgf[:, 1:2], in0=pgf[:, 0:1], scalar1=8.0, scalar2=None, op0=ALU.mult), reads=["pgf0"], writes=["pgf1"])
    P.add("dve", lambda e: e.tensor_scalar(out=pgf[:, 2:3], in0=pgf[:, 0:1], scalar1=16.0, scalar2=None, op0=ALU.mult), reads=["pgf0"], writes=["pgf2"])
    P.add("dve", lambda e: e.tensor_scalar(out=idx8, in0=iota_f[:, 0:16], scalar1=pgf[:, 1:2], scalar2=None, op0=ALU.add), reads=["pgf1", "iota"], writes=["idx8"])
    P.add("dve", lambda e: e.tensor_scalar(out=idx16, in0=iota_f[:, 0:16], scalar1=pgf[:, 2:3], scalar2=None, op0=ALU.add), reads=["pgf2", "iota"], writes=["idx16"])
    P.add("pool", lambda e: e.memset(ones_f, 1.0), writes=["ones_f"])
    P.add("dve", lambda e: e.tensor_copy(out=qiTs_bf, in_=sm[:, 8:24]), reads=["sm_q"], writes=["qis_bf"])
    P.add("dve", lambda e: e.tensor_copy(out=qTs_bf, in_=sm[:, 0:8]), reads=["sm_q"], writes=["qs_bf"])
    ng = 0
    for ch in range(8):
        kb_ = kib[ch % 2]
        P.add("pool", lambda e, ch=ch, kb_=kb_: e.indirect_dma_start(out=kb_, out_offset=None, in_=pool_ki8,
                                                                    in_offset=IOA(ap=idx8[:, ch:ch + 1], axis=0)),
              reads=["idx8"], writes=[("kib", ch % 2)], slot="s_kib%d" % (ch % 2))
        for grp in range(4):
            tb_ = ng % 2
            for j in range(4):
                pos_l = 4 * grp + j
                P.add("pe", lambda e, kb_=kb_, pos_l=pos_l, j=j, tb_=tb_: e.transpose(out=psb[tb_][:, j * 128:(j + 1) * 128],
                                                                                 in_=kb_[:, pos_l * 128:(pos_l + 1) * 128], identity=ident),
                      reads=[("kib", ch % 2), "ident"], writes=[("ps", tb_)])
            kt_ = kidT[ng % 2]
            P.add("act", lambda e, kt_=kt_, tb_=tb_: e.copy(out=kt_, in_=psb[tb_][:, 0:512]), reads=[("ps", tb_)], writes=[("kidT", ng % 2)])
            P.add("pe", lambda e, kt_=kt_, tb_=tb_: e.matmul(ps[2 + tb_][0:16, 0:512], lhsT=qiTs_bf, rhs=kt_, start=True, stop=True),
                  reads=[("kidT", ng % 2), "qis_bf"], writes=[("ps", 2 + tb_)])
            rs_ = Rs[ng % 2]
            P.add("act", lambda e, rs_=rs_, tb_=tb_: e.activation(out=rs_[0:16, :], in_=ps[2 + tb_][0:16, 0:512], func=AF.Relu),
                  reads=[("ps", 2 + tb_)], writes=[("Rs", ng % 2)])
            for j in range(4):
                pos = ch * 16 + grp * 4 + j
                P.add("pe", lambda e, rs_=rs_, j=j, pos=pos: e.matmul(ps[4][:, pos:pos + 1], lhsT=rs_[0:16, j * 128:(j + 1) * 128], rhs=w_s,
                                                                   start=True, stop=True),
                      reads=[("Rs", ng % 2), "sm_w"], writes=[("ps", 4)])
            ng += 1
    P.add("pe", lambda e: e.transpose(out=ps[5][:, 0:1], in_=ksv[0:1, 512:640], identity=identf[0:1, 0:1]), reads=["ksv", "identf"], writes=[("ps", 5)])
    P.add("act", lambda e: e.copy(out=smb[:, 72:73], in_=ps[5][:, 0:1]), reads=[("ps", 5)], writes=["kisT"])
    P.add("pe", lambda e: e.matmul(ps[5][0:16, 8:9], lhsT=qiTs_bf, rhs=smb[:, 72:73], start=True, stop=True), reads=["kisT", "qis_bf"], writes=[("ps", 5)])
    P.add("act", lambda e: e.activation(out=smx[0:16, 0:1], in_=ps[5][0:16, 8:9], func=AF.Relu), reads=[("ps", 5)], writes=["Rn"])
    P.add("pe", lambda e: e.matmul(ps[5][0:1, 16:17], lhsT=smx[0:16, 0:1], rhs=w_s, start=True, stop=True), reads=["Rn", "sm_w"], writes=[("ps", 5)])
    P.add("pool", lambda e: e.memset(scb[:, 128:129], -CBIG), writes=["sc_x"])
    P.add("dve", lambda e: e.tensor_copy(out=scb[0:1, 128:129], in_=ps[5][0:1, 16:17]), reads=[("ps", 5), "sc_x"], writes=["sc_x"])
    P.add("dve", lambda e: e.tensor_copy(out=scb[:, 0:128], in_=ps[4][:, 0:128]), reads=[("ps", 4)], writes=["sc_m"])
    pm = smx[:, 2:4]
    P.add("dve", lambda e: e.tensor_reduce(out=smx[:, 2:3], in_=sc, axis=AX.X, op=ALU.max), reads=["sc_x", "sc_m"], writes=["pm0"])
    P.add("dve", lambda e: e.tensor_reduce(out=smx[:, 3:4], in_=scb[:, 0:128], axis=AX.X, op=ALU.min, negate=True), reads=["sc_m"], writes=["pm1"])
    P.add("pe", lambda e: e.transpose(out=ps[5][0:2, 32:160], in_=pm, identity=identf), reads=["pm0", "pm1", "identf"], writes=[("ps", 5)])
    P.add("dve", lambda e: e.tensor_reduce(out=smx[0:2, 4:5], in_=ps[5][0:2, 32:160], axis=AX.X, op=ALU.max), reads=[("ps", 5)], writes=["g2"])
    P.add("dve", lambda e: e.tensor_scalar(out=smx[0:2, 6:8], in0=identf[0:2, 0:2], scalar1=smx[0:2, 4:5], scalar2=None, op0=ALU.mult),
          reads=["g2", "identf"], writes=["g2d"])
    P.add("pe", lambda e: e.matmul(ps[5][:, 200:202], lhsT=ones_f[0:2, :], rhs=smx[0:2, 6:8], start=True, stop=True), reads=["g2d", "ones_f"], writes=[("ps", 5)])
    P.add("dve", lambda e: e.tensor_copy(out=smx[:, 8:10], in_=ps[5][:, 200:202]), reads=[("ps", 5)], writes=["gmm"])
    gmax = smx[:, 8:9]
    ngmin = smx[:, 9:10]
    P.add("dve", lambda e: e.tensor_tensor(out=w0, in0=gmax, in1=ngmin, op=ALU.add), reads=["gmm"], writes=["w0"])
    P.add("dve", lambda e: e.tensor_scalar(out=w0, in0=w0, scalar1=float(2.0 ** -10), scalar2=1e-3, op0=ALU.mult, op1=ALU.add), reads=["w0"], writes=["w0b"])
    P.add("dve", lambda e: e.tensor_tensor(out=lo0, in0=ngmin, in1=w0, op=ALU.add), reads=["gmm", "w0b"], writes=["lo0n"])
    P.add("dve", lambda e: e.tensor_scalar(out=lo0, in0=lo0, scalar1=-1.0, scalar2=None, op0=ALU.mult), reads=["lo0n"], writes=["lo0"])
    P.add("dve", lambda e: e.tensor_tensor(out=w0, in0=gmax, in1=lo0, op=ALU.subtract), reads=["gmm", "lo0"], writes=["w0c"])
    P.add("dve", lambda e: e.tensor_scalar(out=HK, in0=pow2[:, 0:NIT + 1], scalar1=w0, scalar2=None, op0=ALU.mult), reads=["pow2", "w0c"], writes=["HK"])
    P.add("dve", lambda e: e.tensor_tensor(out=TC[:, 0:1], in0=lo0, in1=HK[:, 0:1], op=ALU.add), reads=["lo0", "HK"], writes=[("TC", 0)])
    for k in range(NIT):
        P.add("dve", lambda e, k=k: e.tensor_scalar(out=vld, in0=sc, scalar1=TC[:, k:k + 1], scalar2=None, op0=ALU.is_ge, op1=ALU.add, accum_out=cnt),
              reads=["sc_x", "sc_m", ("TC", k)], writes=["vld", "cnt"])
        P.add("pe", lambda e: e.matmul(ps[6][:, 0:1], lhsT=ones_f, rhs=cnt, start=True, stop=True), reads=["cnt", "ones_f"], writes=[("ps", 6)])
        P.add("dve", lambda e, k=k: e.scalar_tensor_tensor(out=tmpc, in0=ps[6][:, 0:1], scalar=TOPK - 0.5, in1=HK[:, k:k + 1], op0=ALU.is_ge, op1=ALU.mult),
              reads=[("ps", 6), "HK"], writes=["tmpc"])
        P.add("dve", lambda e, k=k: e.scalar_tensor_tensor(out=TC[:, k + 1:k + 2], in0=tmpc, scalar=HK[:, k + 1:k + 2], in1=TC[:, k:k + 1],
                                                           op0=ALU.subtract, op1=ALU.add),
              reads=["tmpc", "HK", ("TC", k)], writes=[("TC", k + 1)])
    thr_s = smx[:, 12:13]
    P.add("dve", lambda e: e.tensor_tensor(out=thr_s, in0=TC[:, NIT:NIT + 1], in1=HK[:, NIT:NIT + 1], op=ALU.subtract), reads=[("TC", NIT), "HK"], writes=["thr_s"])
    P.add("dve", lambda e: e.tensor_scalar(out=vld, in0=sc, scalar1=thr_s, scalar2=None, op0=ALU.is_ge), reads=["sc_x", "sc_m", "thr_s"], writes=["vld"])
    P.add("pool", lambda e: e.memset(Kx, 0.0), writes=["Kx"])
    P.add("pool", lambda e: e.memset(Vx, 0.0), writes=["Vx"])
    P.add("dve", lambda e: e.tensor_copy(out=Kx[0:1, :], in_=ksv[0:1, 0:256]), reads=["ksv", "Kx"], writes=["Kx"])
    P.add("dve", lambda e: e.tensor_copy(out=Vx[0:1, :], in_=ksv[0:1, 256:512]), reads=["ksv", "Vx"], writes=["Vx"])

    def s_col(pos):
        return (4 + pos // 64, (pos % 64) * 8) if pos < 128 else (6, 0)

    for ch in range(17):
        npos = 8 if ch < 16 else 1
        if ch < 16:
            kc_ = Kc[ch % 2]
            P.add("pool", lambda e, ch=ch, kc_=kc_: e.indirect_dma_start(out=kc_, out_offset=None, in_=pool_k16,
                                                                        in_offset=IOA(ap=idx16[:, ch:ch + 1], axis=0)),
                  reads=["idx16"], writes=[("Kc", ch % 2)], slot="s_kc%d" % (ch % 2))
            kres = ("Kc", ch % 2)
        else:
            kc_ = Kx
            kres = "Kx"
        ktb = KTb[ch % 2]
        ba = 2 * (ch % 2)
        for t in range(2 * npos):
            bk = ba + t // 8
            P.add("pe", lambda e, kc_=kc_, t=t, bk=bk: e.transpose(out=psb[bk][:, (t % 8) * 128:(t % 8 + 1) * 128],
                                                                 in_=kc_[:, t * 128:(t + 1) * 128], identity=ident),
                  reads=[kres, "ident"], writes=[("ps", bk)])
        nb_ = (2 * npos + 7) // 8
        for hb in range(nb_):
            ncol = min(8, 2 * npos - 8 * hb) * 128
            P.add("act" if hb == 0 else "dve",
                  (lambda e, ktb=ktb, hb=hb, ba=ba, ncol=ncol: e.copy(out=ktb[:, hb * 1024:hb * 1024 + ncol], in_=psb[ba + hb][:, 0:ncol])) if hb == 0 else
                  (lambda e, ktb=ktb, hb=hb, ba=ba, ncol=ncol: e.tensor_copy(out=ktb[:, hb * 1024:hb * 1024 + ncol], in_=psb[ba + hb][:, 0:ncol])),
                  reads=[("ps", ba + hb)], writes=[("KTb", ch % 2, hb)])
        for p in range(npos):
            pos = ch * 8 + p
            bk, col = s_col(pos)
            for g in range(2):
                t = 2 * p + g
                P.add("pe", lambda e, ktb=ktb, t=t, g=g, bk=bk, col=col: e.matmul(ps[bk][:, col + 4 * g:col + 4 * g + 4], lhsT=ktb[:, t * 128:(t + 1) * 128],
                                                                              rhs=qTs_bf[:, 4 * g:4 * g + 4], start=True, stop=True),
                      reads=[("KTb", ch % 2, t // 8), "qs_bf"], writes=[("ps", bk)])
    P.add("act", lambda e: e.activation(out=Psb[:, 0:512], in_=ps[4][:, 0:512], func=AF.Exp, scale=SCALE), reads=[("ps", 4)], writes=["Psb0"])
    P.add("act", lambda e: e.activation(out=Psb[:, 512:1024], in_=ps[5][:, 0:512], func=AF.Exp, scale=SCALE), reads=[("ps", 5)], writes=["Psb1"])
    P.add("act", lambda e: e.activation(out=Psb[:, 1024:1032], in_=ps[6][:, 0:8], func=AF.Exp, scale=SCALE), reads=[("ps", 6)], writes=["Psb2"])
    P3 = Psb[:, 0:1032].rearrange("p (s h) -> p s h", h=8)
    Pb3 = Pbf[:, 0:1032].rearrange("p (s h) -> p s h", h=8)
    P.add("dve", lambda e: e.tensor_tensor(out=Pb3, in0=P3, in1=vld.unsqueeze(2).to_broadcast([128, 129, 8]), op=ALU.mult),
          reads=["Psb0", "Psb1", "Psb2", "vld"], writes=["Pbf"])
    P.add("dve", lambda e: e.tensor_reduce(out=Pred[:, 0:8], in_=Pbf[:, 0:1032].rearrange("p (s h) -> p h s", h=8), axis=AX.X, op=ALU.add),
          reads=["Pbf"], writes=["Pred"])
    for ch in range(17):
        npos = 8 if ch < 16 else 1
        if ch < 16:
            vc_ = Kc[ch % 2]
            P.add("pool", lambda e, ch=ch, vc_=vc_: e.indirect_dma_start(out=vc_, out_offset=None, in_=pool_v16,
                                                                        in_offset=IOA(ap=idx16[:, ch:ch + 1], axis=0)),
                  reads=["idx16"], writes=[("Kc", ch % 2)], slot="s_kc%d" % (ch % 2))
            vres = ("Kc", ch % 2)
        else:
            vc_ = Vx
            vres = "Vx"
        for p in range(npos):
            pos = ch * 8 + p
            for g in range(2):
                P.add("pe", lambda e, vc_=vc_, p=p, g=g, pos=pos: e.matmul(ps[g][0:4, 0:128], lhsT=Pbf[:, pos * 8 + 4 * g:pos * 8 + 4 * g + 4],
                                                                       rhs=vc_[:, p * 256 + g * 128:p * 256 + (g + 1) * 128],
                                                                       start=(pos == 0), stop=(pos == 128)),
                      reads=[vres, "Pbf"], writes=[("ps", g)])
    for g in range(2):
        P.add("pe", lambda e, g=g: e.matmul(ps[2][0:4, g:g + 1], lhsT=Pred[:, 4 * g:4 * g + 4], rhs=ones_f[:, 0:1], start=True, stop=True),
              reads=["Pred", "ones_f"], writes=[("ps", 2)])
    P.add("dve", lambda e: e.reciprocal(out=smx[0:4, 16:18], in_=ps[2][0:4, 0:2]), reads=[("ps", 2)], writes=["rden_s"])
    for g in range(2):
        P.add("dve", lambda e, g=g: e.tensor_scalar(out=osb[0:4, g * 128:(g + 1) * 128], in0=ps[g][0:4, 0:128], scalar1=smx[0:4, 16 + g:17 + g], scalar2=None,
                                                   op0=ALU.mult), reads=[("ps", g), "rden_s"], writes=[("osb", g)])
        P.add("pe", lambda e, g=g: e.transpose(out=ps[3][:, 4 * g:4 * g + 4], in_=osb[0:4, g * 128:(g + 1) * 128], identity=identf[0:4, 0:4]),
              reads=[("osb", g), "identf"], writes=[("ps", 3)])
    P.add("dve", lambda e: e.tensor_copy(out=oaTs, in_=ps[3][:, 0:8]), reads=[("ps", 3)], writes=["oaTs"])

    P.barrier()
    S4 = 20 * KB
    xbuf = [buf(S4 + i * 8 * KB, 8 * KB, F32) for i in range(2)]
    hbuf = [buf(S4 + 16 * KB + i * 4 * KB, 4 * KB) for i in range(2)]
    junk = buf(S4 + 24 * KB, 4 * KB)
    own_pass(xbuf, hbuf, junk)

    P.barrier()
    uT = buf(20 * KB, 16 * KB)
    obT = buf(36 * KB, 16 * KB)
    wvbuf = buf(52 * KB, 32 * KB)
    wbuf = [buf(84 * KB + i * 4 * KB, 4 * KB) for i in range(3)]
    wpbuf = [buf(96 * KB + i * 2 * KB, 2 * KB) for i in range(2)]
    szb = [buf(100 * KB + i * KB, KB) for i in range(2)]
    ftmp = buf(102 * KB, 4 * KB, F32)
    ftmp2 = buf(188 * KB, 4 * KB, F32)
    gvb = buf(192 * KB, 4 * KB, F32)
    vnb = [buf(196 * KB + i * 2 * KB, 2 * KB) for i in range(2)]
    bspbc = buf(200 * KB, 4 * KB, F32)
    uT3 = v3(uT, 8)
    obT3 = v3(obT, 8)
    wv3 = v3(wvbuf, 16)
    P.add("sp", lambda e: e.dma_start(out=gbc[:, 0:1024], in_=ln_g_d.partition_broadcast(128)), writes=["gbc"], slot="c_gbc")
    P.add("sp", lambda e: e.dma_start(out=gbc[:, 1024:2048], in_=ln_b_d.partition_broadcast(128)), writes=["gbc"], slot="c_gbc")
    P.add("sp", lambda e: e.dma_start(out=bspbc, in_=bsp_d.partition_broadcast(128)), writes=["bspbc"], slot="c_bsp")
    P.add("pool", lambda e: e.dma_start(out=WsT, in_=wspT_d), writes=["WsT"], slot="c_wsp")
    P.add("pool", lambda e: e.affine_select(out=v3(WsT, 8), in_=v3(WsT, 8), pattern=[[0, 8], [1, 128]], compare_op=ALU.is_ge,
                                            fill=0.0, base=0, channel_multiplier=-1), reads=["WsT"], writes=["WsT"])
    for q4 in range(8):
        P.add("pool", lambda e, q4=q4: e.dma_start(out=wv3[:, q4 * 2:(q4 + 1) * 2, :], in_=v3(wvb_d, 16)[:, q4 * 2:(q4 + 1) * 2, :]),
              writes=["wvbuf"], slot="wvb%d" % (q4 % 4))
    fm_state["n"] = 0

    oa_all = [("oaT", i, g) for i in range(OWN) for g in range(2)]

    def evac_mul_act(dst3, func, dres):
        def mk(cb):
            def f(pa):
                for hf in range(2):
                    sz = szb[hf]
                    P.add("act", lambda e, hf=hf, sz=sz: e.activation(out=sz, in_=ps[pa + hf][:, 0:512], func=func),
                          reads=[("ps", pa + hf)], writes=[("szb", hf)])
                    P.add("dve", lambda e, hf=hf, sz=sz: e.tensor_tensor(out=dst3[:, cb, hf * 512:(hf + 1) * 512],
                                                                       in0=dst3[:, cb, hf * 512:(hf + 1) * 512], in1=sz, op=ALU.mult),
                          reads=[("szb", hf)] + dres, writes=[(id(dst3), "z", cb, hf)])
            return f
        return mk

    mk = evac_mul_act(oaT3, AF.Silu, oa_all)
    for cb in range(8):
        proj_fm(FM_ZA + cb, 24 + cb, mk(cb))
    oaz_all = [(id(oaT3), "z", cb, hf) for cb in range(8) for hf in range(2)]
    P.add("act", lambda e: e.activation(out=sm[:, 24:32], in_=ps[6][:, 24:32], func=AF.Silu), reads=[("ps", 6)], writes=["zaTs"])
    P.add("dve", lambda e: e.tensor_tensor(out=oazTs, in0=oaTs, in1=sm[:, 24:32], op=ALU.mult), reads=["zaTs", "oaTs"], writes=["srhs"])

    def evac_act(dst3, func, tag):
        def mk(cb):
            def f(pa):
                for hf in range(2):
                    P.add("act", lambda e, hf=hf: e.activation(out=dst3[:, cb, hf * 512:(hf + 1) * 512], in_=ps[pa + hf][:, 0:512], func=func),
                          reads=[("ps", pa + hf)], writes=[(tag, cb, hf)])
            return f
        return mk

    mk = evac_act(uT3, AF.Gelu, "uT")
    for cb in range(8):
        proj_fm(FM_U + cb, 32 + cb, mk(cb))
    uT_all = [("uT", cb, hf) for cb in range(8) for hf in range(2)]
    P.add("act", lambda e: e.activation(out=sm[:, 32:40], in_=ps[6][:, 32:40], func=AF.Gelu), reads=[("ps", 6)], writes=["uTs"])

    def layernorm(src, dst, sidx, nparts, src_res, dst_res, tmp, jk):
        s1 = stat[0:nparts, sidx:sidx + 1]
        s2 = stat[0:nparts, sidx + 1:sidx + 2]
        mu = stat[0:nparts, sidx + 2:sidx + 3]
        var = stat[0:nparts, sidx + 3:sidx + 4]
        sd = stat[0:nparts, sidx + 4:sidx + 5]
        rs = stat[0:nparts, sidx + 5:sidx + 6]
        P.add("dve", lambda e: e.reduce_sum(out=s1, in_=src, axis=AX.X), reads=[src_res], writes=[("st", sidx)])
        P.add("dve", lambda e: e.tensor_scalar(out=mu, in0=s1, scalar1=1.0 / 1024, scalar2=None, op0=ALU.mult), reads=[("st", sidx)], writes=[("st", sidx + 2)])
        P.add("dve", lambda e: e.tensor_scalar(out=tmp, in0=src, scalar1=mu, scalar2=None, op0=ALU.subtract), reads=[src_res, ("st", sidx + 2)], writes=[(dst_res, "t")])
        P.add("act", lambda e: e.activation(out=jk[0:nparts, 0:1024], in_=tmp, func=AF.Square, accum_out=s2), reads=[(dst_res, "t")], writes=["junk", ("st", sidx + 1)])
        P.add("act", lambda e: e.activation(out=sd, in_=s2, func=AF.Sqrt, scale=1.0 / 1024, bias=epsc[0:nparts, 1:2]), reads=[("st", sidx + 1), "epsc"], writes=[("st", sidx + 4)])
        P.add("dve", lambda e: e.reciprocal(out=rs, in_=sd), reads=[("st", sidx + 4)], writes=[("st", sidx + 5)])
        P.add("dve", lambda e: e.scalar_tensor_tensor(out=tmp, in0=tmp, scalar=rs, in1=gbc[0:nparts, 0:1024], op0=ALU.mult, op1=ALU.mult),
              reads=[(dst_res, "t"), ("st", sidx + 5), "gbc"], writes=[(dst_res, "t")])
        P.add("dve", lambda e: e.tensor_tensor(out=dst, in0=tmp, in1=gbc[0:nparts, 1024:2048], op=ALU.add), reads=[(dst_res, "t"), "gbc"], writes=[dst_res])

    junk = buf(106 * KB, 2 * KB)
    for i in range(OWN):
        tok = slice(i * 128, (i + 1) * 128)
        for hf in range(2):
            bk = hf
            for c in range(16):
                P.add("pe", lambda e, c=c, hf=hf, bk=bk, tok=tok: e.matmul(ps[bk][:, 0:512], lhsT=hT3[:, c, tok], rhs=wv3[:, c, hf * 512:(hf + 1) * 512],
                                                                        start=(c == 0), stop=(c == 15)),
                      reads=["wvbuf"] + hT_reads, writes=[("ps", bk)])
            P.add("act", lambda e, hf=hf, bk=bk: e.activation(out=gvb[:, hf * 512:(hf + 1) * 512], in_=ps[bk][:, 0:512], func=AF.Gelu),
                  reads=[("ps", bk)], writes=["gvb"])
        vn = vnb[i % 2]
        layernorm(gvb, vn, 16, 128, "gvb", ("vn", i % 2), ftmp, junk)
        for g in range(8):
            bk = 2 + g // 4
            P.add("pe", lambda e, g=g, bk=bk, vn=vn: e.matmul(ps[bk][:, (g % 4) * 128:(g % 4 + 1) * 128], lhsT=vn[:, g * 128:(g + 1) * 128],
                                                             rhs=v3(WsT, 8)[:, g, :], start=True, stop=True),
                  reads=[("vn", i % 2), "WsT"], writes=[("ps", bk)])
        for hh in range(2):
            bk = 2 + hh
            gs = slice(4 * hh, 4 * hh + 4)
            P.add("dve", lambda e, bk=bk, hh=hh: e.tensor_tensor(out=ftmp2[:, hh * 512:(hh + 1) * 512], in0=ps[bk][:, 0:512],
                                                                in1=bspbc[:, hh * 512:(hh + 1) * 512], op=ALU.add),
                  reads=[("ps", bk), "bspbc"], writes=[("ftmp2", hh)])
            P.add("dve", lambda e, hh=hh, gs=gs, tok=tok: e.tensor_tensor(out=obT3[:, gs, tok], in0=v3(ftmp2[:, hh * 512:(hh + 1) * 512], 4),
                                                                       in1=uT3[:, gs, tok], op=ALU.mult),
                  reads=[("ftmp2", hh)] + uT_all, writes=[("obT", i, hh)])
    ob_all = [("obT", i, hh) for i in range(OWN) for hh in range(2)]
    for hf in range(2):
        for c in range(16):
            P.add("pe", lambda e, c=c, hf=hf: e.matmul(ps[4 + hf][0:1, 0:512], lhsT=hsT[:, c:c + 1], rhs=wv3[:, c, hf * 512:(hf + 1) * 512],
                                                      start=(c == 0), stop=(c == 15)),
                  reads=["wvbuf", "hsT"], writes=[("ps", 4 + hf)])
        P.add("act", lambda e, hf=hf: e.activation(out=gvb[0:1, hf * 512:(hf + 1) * 512], in_=ps[4 + hf][0:1, 0:512], func=AF.Gelu),
              reads=[("ps", 4 + hf)], writes=["gvb"])
    vns = ftmp2[0:1, :]
    layernorm(gvb[0:1, :], vns, 24, 1, "gvb", "vns", ftmp[0:1, :], junk)
    P.add("sp", lambda e: e.dma_start(out=gvs_o, in_=vns), reads=["vns", ("gt", 1, 0), ("gt", 1, 1)], writes=["o_gvs"], slot="o_gvs")

    P.add("sp", lambda e: e.dma_start(out=gvb[0:1, :], in_=ws00_d), reads=["vns"], writes=["gvb"], slot="c_ws00")
    P.add("sp", lambda e: e.dma_start(out=ftmp[0:1, :], in_=bs0_d), reads=["vns"], writes=[("vns", "t")], slot="c_bs0")
    P.add("dve", lambda e: e.tensor_tensor(out=gvb[0:1, :], in0=vns, in1=gvb[0:1, :], op=ALU.mult), reads=["vns", "gvb"], writes=["gvb"])
    P.add("dve", lambda e: e.tensor_tensor(out=gvb[0:1, :], in0=gvb[0:1, :], in1=ftmp[0:1, :], op=ALU.add), reads=["gvb", ("vns", "t")], writes=["gvb"])
    for g in range(8):
        P.add("pe", lambda e, g=g: e.transpose(out=ps[4][:, g:g + 1], in_=gvb[0:1, g * 128:(g + 1) * 128], identity=identf[0:1, 0:1]),
              reads=["gvb", "identf"], writes=[("ps", 4)])
    P.add("dve", lambda e: e.tensor_tensor(out=sm[:, 32:40], in0=ps[4][:, 0:8], in1=sm[:, 32:40], op=ALU.mult), reads=[("ps", 4), "uTs"], writes=["obTs"])
    mk = evac_mul_act(obT3, AF.Silu, ob_all)
    for cb in range(8):
        proj_fm(FM_ZB + cb, 40 + cb, mk(cb))
    obz_all = [(id(obT3), "z", cb, hf) for cb in range(8) for hf in range(2)]
    P.add("act", lambda e: e.activation(out=sm[:, 40:48], in_=ps[6][:, 40:48], func=AF.Silu), reads=[("ps", 6)], writes=["zbTs"])
    P.add("dve", lambda e: e.tensor_tensor(out=obzTs, in0=sm[:, 32:40], in1=sm[:, 40:48], op=ALU.mult), reads=["zbTs", "obTs"], writes=["srhs"])
    if debug:
        P.add("sp", lambda e: e.dma_start(out=dbg["obT"], in_=obT), reads=obz_all, writes=["dbg_dob"], slot="dbg8")

    wp_n = {"n": 0}

    def proj_branch(wd, cb, src3, sres, bank, scol, srhs):
        n = wp_n["n"]
        wp_n["n"] += 1
        wsl = wpbuf[n % 2]
        w3 = v3(wsl, 8)
        P.add("pool", lambda e: e.dma_start(out=wsl, in_=wd[cb]), writes=[("wpbuf", n % 2)], slot="wpbuf%d" % (n % 2))
        for c in range(8):
            for hf in range(2):
                P.add("pe", lambda e, c=c, hf=hf: e.matmul(ps[bank + hf][:, 0:512], lhsT=w3[:, c, :], rhs=src3[:, c, hf * 512:(hf + 1) * 512],
                                                          start=(c == 0), stop=(c == 7)),
                      reads=[("wpbuf", n % 2)] + sres, writes=[("ps", bank + hf)])
            P.add("pe", lambda e, c=c: e.matmul(ps[7][:, scol:scol + 1], lhsT=w3[:, c, :], rhs=srhs[:, c:c + 1], start=(c == 0), stop=(c == 7)),
                  reads=[("wpbuf", n % 2), "srhs"], writes=[("ps", 7)])


    for cb in range(16):
        sg = [None, None]
        for br in range(2):
            gt = ftmp if br == 0 else ftmp2

            def ev_gate(pa, gt=gt, br=br):
                for hf in range(2):
                    P.add("act", lambda e, hf=hf: e.activation(out=gt[:, hf * 512:(hf + 1) * 512], in_=ps[pa + hf][:, 0:512], func=AF.Sigmoid),
                          reads=[("ps", pa + hf)], writes=[("gt", br, hf)])
                sg[br] = pa
            proj_fm(FM_G + 2 * cb + br, (48 + cb if br == 0 else 64 + cb), ev_gate, pa=2 * br)
            if br == 0:
                proj_branch(wpa_d, cb, oaT3, oaz_all, 4, cb, oazTs)
            else:
                proj_branch(wpb_d, cb, obT3, obz_all, 4, 16 + cb, obzTs)
            for hf in range(2):
                P.add("dve", lambda e, hf=hf, gt=gt: e.tensor_tensor(out=gt[:, hf * 512:(hf + 1) * 512], in0=ps[4 + hf][:, 0:512],
                                                                   in1=gt[:, hf * 512:(hf + 1) * 512], op=ALU.mult),
                      reads=[("ps", 4 + hf), ("gt", br, hf)], writes=[("gt", br, hf)])
        for hf in range(2):
            P.add("dve", lambda e, hf=hf, cb=cb: e.tensor_tensor(out=mixT3[:, cb, hf * 512:(hf + 1) * 512], in0=ftmp[:, hf * 512:(hf + 1) * 512],
                                                               in1=ftmp2[:, hf * 512:(hf + 1) * 512], op=ALU.add),
                  reads=[("gt", 0, hf), ("gt", 1, hf)], writes=[("mixT", cb, hf)])
    mix_all = [("mixT", cb, hf) for cb in range(16) for hf in range(2)]
    if debug:
        P.add("sp", lambda e: e.dma_start(out=dbg["mixT"], in_=mixT), reads=mix_all, writes=["dbg_dmx"], slot="dbg9")
    P.add("act", lambda e: e.activation(out=sm[:, 48:80], in_=ps[6][:, 48:80], func=AF.Sigmoid),
          reads=[("ps", 6)], writes=["sm_g"])
    P.add("dve", lambda e: e.tensor_tensor(out=sm[:, 80:112], in0=ps[7][:, 0:32], in1=sm[:, 48:80], op=ALU.mult),
          reads=[("ps", 7), "sm_g"], writes=["sm_y"])
    P.add("dve", lambda e: e.tensor_tensor(out=mixTs, in0=sm[:, 80:96], in1=sm[:, 96:112], op=ALU.add), reads=["sm_y"], writes=["mixTs"])

    if upto <= 4:
        return finish()
    P.barrier()
    wo = [buf(20 * KB + gidx * 16 * KB, 16 * KB) for gidx in range(4)]
    xbuf = [buf(84 * KB + i * 8 * KB, 8 * KB, F32) for i in range(2)]
    xo = [buf(100 * KB + i * 8 * KB, 8 * KB, F32) for i in range(2)]
    junk = buf(116 * KB, 4 * KB)
    xs2 = buf(120 * KB, 8 * KB, F32, parts=1)
    xos = buf(128 * KB, 8 * KB, F32, parts=1)
    P.add("sp", lambda e: e.dma_start(out=gbc, in_=g_f_d.partition_broadcast(128)), writes=["gbc"], slot="c_gbc")
    for gi in range(4):
        w3 = v3(wo[gi], 16)
        for q4 in range(4):
            P.add("pool", lambda e, gi=gi, q4=q4, w3=w3: e.dma_start(out=w3[:, q4 * 4:(q4 + 1) * 4, :],
                                                                   in_=v3(wout_d[gi], 16)[:, q4 * 4:(q4 + 1) * 4, :]),
                  writes=[("wo", gi)], slot="wo%d_%d" % (gi, q4))

    def final_block(src_ap, xb, xres, xslot, xo_t, lhs_fn, lhs_reads, banks, out_ap, sidx, nparts, oslot):
        P.add("sp", lambda e: e.dma_start(out=xb, in_=src_ap), writes=[xres], slot=xslot)
        for gi in range(4):
            w3 = v3(wo[gi], 16)
            for c in range(16):
                P.add("pe", lambda e, gi=gi, c=c, w3=w3: e.matmul(ps[banks[gi]][0:nparts, 0:512], lhsT=lhs_fn(c), rhs=w3[:, c, :],
                                                                start=(c == 0), stop=(c == 15)),
                      reads=[("wo", gi)] + lhs_reads, writes=[("ps", banks[gi])])
            P.add("dve", lambda e, gi=gi: e.tensor_tensor(out=xo_t[:, gi * 512:(gi + 1) * 512], in0=ps[banks[gi]][0:nparts, 0:512],
                                                         in1=xb[:, gi * 512:(gi + 1) * 512], op=ALU.add),
                  reads=[("ps", banks[gi]), xres], writes=[(xres, "xo", gi)])
        ss = stat[0:nparts, sidx:sidx + 1]
        sd = stat[0:nparts, sidx + 1:sidx + 2]
        rs = stat[0:nparts, sidx + 2:sidx + 3]
        xor = [(xres, "xo", gi) for gi in range(4)]
        P.add("act", lambda e: e.activation(out=junk[0:nparts, :], in_=xo_t, func=AF.Square, accum_out=ss), reads=xor, writes=["junk", ("st", sidx)])
        P.add("act", lambda e: e.activation(out=sd, in_=ss, func=AF.Sqrt, scale=1.0 / D, bias=epsc[0:nparts, 0:1]),
              reads=[("st", sidx), "epsc"], writes=[("st", sidx + 1)])
        P.add("dve", lambda e: e.reciprocal(out=rs, in_=sd), reads=[("st", sidx + 1)], writes=[("st", sidx + 2)])
        P.add("dve", lambda e: e.scalar_tensor_tensor(out=xb, in0=xo_t, scalar=rs, in1=gbc[0:nparts, :], op0=ALU.mult, op1=ALU.mult),
              reads=xor + [("st", sidx + 2), "gbc"], writes=[xres])
        P.add("sp", lambda e: e.dma_start(out=out_ap, in_=xb), reads=[xres], writes=[("o_y", oslot)], slot="o_y%s" % oslot)

    for i in range(OWN):
        pb = i % 2
        tok = slice(i * 128, (i + 1) * 128)
        banks = [0, 1, 2, 3] if pb == 0 else [4, 5, 6, 7]
        final_block(x_own[i * 128:(i + 1) * 128, :], xbuf[pb], ("xbuf", pb), "xbuf%d" % pb, xo[pb],
                    lambda c, tok=tok: mixT3[:, c, tok], mix_all, banks, y_own[i * 128:(i + 1) * 128, :], 4 * pb, 128, pb)
    final_block(x_s, xs2, "xs2", "xs2", xos, lambda c: mixTs[:, c:c + 1], ["mixTs"], [0, 1, 2, 3], ys_o, 8, 1, "s")

    return finish()


def own_blocks(core):
    j = core % 4
    return sorted([j, 7 - j, 8 + j, 15 - j, 16 + j, 23 - j, 24 + j, 31 - j])


def prep_shared(inputs):
    w_in = np.asarray(inputs["w_in"])[0]

    def chunked(cols):
        n = cols.shape[1]
        return np.ascontiguousarray(cols.reshape(16, 128, n).transpose(1, 0, 2))

    sh = {}
    kvki = np.concatenate([w_in[:, C_K:C_K + 256], w_in[:, C_V:C_V + 256], w_in[:, C_KI:C_KI + 128]], axis=1)
    sh["wkvki"] = chunked(kvki).reshape(128, -1)
    cbs = []
    for base, n in ((C_Q, 8), (C_QI, 16), (C_ZA, 8), (C_U, 8), (C_ZB, 8)):
        for j in range(n):
            cbs.append(w_in[:, base + j * 128: base + (j + 1) * 128])
    for j in range(16):
        cbs.append(w_in[:, C_GA + j * 128:C_GA + (j + 1) * 128])
        cbs.append(w_in[:, C_GB + j * 128:C_GB + (j + 1) * 128])
    sh["wfm"] = np.stack([chunked(c).reshape(128, -1) for c in cbs])
    sh["wwi"] = chunked(w_in[:, C_WI:C_WI + 16]).reshape(128, -1)
    sh["wvb"] = chunked(w_in[:, C_VB:C_VB + 1024]).reshape(128, -1)
    wpa = np.asarray(inputs["w_proj_a"])[0]
    wpb = np.asarray(inputs["w_proj_b"])[0]

    def chunk8(cols):
        return np.ascontiguousarray(cols.reshape(8, 128, 128).transpose(1, 0, 2)).reshape(128, -1)

    sh["wpa"] = np.stack([chunk8(wpa[:, j * 128:(j + 1) * 128]) for j in range(16)])
    sh["wpb"] = np.stack([chunk8(wpb[:, j * 128:(j + 1) * 128]) for j in range(16)])
    wout = np.asarray(inputs["w_out"])[0]
    sh["wout"] = np.stack([chunked(wout[:, j * 512:(j + 1) * 512]).reshape(128, -1) for j in range(4)])
    ws = np.asarray(inputs["w_spatial"])[0]
    sh["wspT"] = np.ascontiguousarray(ws.transpose(2, 0, 1)).reshape(128, -1)
    bsp = np.asarray(inputs["b_spatial"])[0]
    sh["bsp"] = np.ascontiguousarray(bsp.reshape(1, 1024))
    sh["ws00"] = np.ascontiguousarray(np.repeat(ws[:, 0, 0], 128).reshape(1, 1024))
    sh["bs0"] = np.ascontiguousarray(np.repeat(bsp[:, 0], 128).reshape(1, 1024))
    sh["pool_ki8"] = np.ascontiguousarray(np.asarray(inputs["cache_k_idx"])[0].reshape(1280 * 8, 2048))
    sh["pool_k16"] = np.ascontiguousarray(np.asarray(inputs["cache_k"])[0].reshape(1280 * 16, 2048))
    sh["pool_v16"] = np.ascontiguousarray(np.asarray(inputs["cache_v"])[0].reshape(1280 * 16, 2048))
    sh["g_in"] = np.ascontiguousarray(np.asarray(inputs["norm_in_g"]).reshape(1, D))
    sh["g_f"] = np.ascontiguousarray(np.asarray(inputs["norm_f_g"]).reshape(1, D))
    sh["ln_g"] = np.ascontiguousarray(np.asarray(inputs["ln_g"]).reshape(1, 1024))
    sh["ln_b"] = np.ascontiguousarray(np.asarray(inputs["ln_b"]).reshape(1, 1024))
    return {k: np.ascontiguousarray(v, dtype=v.dtype) for k, v in sh.items()}


def make_in_maps(inputs):
    sh = prep_shared(inputs)
    xp = np.asarray(inputs["x_prompt"])
    xs = np.asarray(inputs["x_sample"])
    pt = np.asarray(inputs["page_table"]).astype(np.int32)
    maps = []
    for c in range(NCORES):
        b = c // 4
        ob = own_blocks(c)
        m = dict(sh)
        m["x_all"] = np.ascontiguousarray(xp[b])
        m["x_own"] = np.ascontiguousarray(np.concatenate([xp[b, blk * 128:(blk + 1) * 128] for blk in ob], axis=0))
        t = np.arange(128, dtype=np.float32)[:, None]
        m["qrel"] = np.ascontiguousarray(
            np.concatenate([(ob[i] * 128 - 512 * i) + t for i in range(OWN)], axis=1).astype(np.float32))
        m["x_s"] = np.ascontiguousarray(xs[c].reshape(1, D))
        m["ptab"] = np.ascontiguousarray(pt[c].reshape(1, 128))
        maps.append(m)
    return maps


_CACHE = {}


def kernel(**inputs):
    if "nc" not in _CACHE:
        _CACHE["nc"] = build_program(debug=False)[0]
    nc = _CACHE["nc"]
    maps = make_in_maps(inputs)
    res = run_bass_kernel_spmd(nc, maps, core_ids=list(range(NCORES)))
    r = res.results
    y_prompt = np.zeros((2, SEQ, D), np.float32)
    for c in range(NCORES):
        b = c // 4
        for i, blk in enumerate(own_blocks(c)):
            y_prompt[b, blk * 128:(blk + 1) * 128] = r[c]["y_own"][i * 128:(i + 1) * 128]
    y_sample = np.stack([r[c]["ys"].reshape(1, D) for c in range(NCORES)]).astype(np.float32)
    nk = np.stack([r[4 * b]["knew"].reshape(SEQ, 2, 128) for b in range(2)])[None].astype(np.float32)
    nv = np.stack([r[4 * b]["vnew"].reshape(SEQ, 2, 128) for b in range(2)])[None].astype(np.float32)
    nki = np.stack([r[4 * b]["kinew"].reshape(SEQ, 128) for b in range(2)])[None].astype(np.float32)
    ks = np.stack([r[c]["ks"].reshape(1, 2, 128) for c in range(NCORES)])[None].astype(np.float32)
    vs = np.stack([r[c]["vs"].reshape(1, 2, 128) for c in range(NCORES)])[None].astype(np.float32)
    kis = np.stack([r[c]["kis"].reshape(1, 128) for c in range(NCORES)])[None].astype(np.float32)
    gvs = np.stack([r[c]["gvs"].reshape(1, 1024) for c in range(NCORES)])[None].astype(np.float32)
    return (y_prompt, y_sample, nk, nv, nki, ks, vs, kis, gvs)
```

```python
import numpy as np
from contextlib import ExitStack
import concourse.bass as bass
import concourse.mybir as mybir
from concourse.bass_utils import run_bass_kernel_spmd

F32 = mybir.dt.float32
BF16 = mybir.dt.bfloat16
I32 = mybir.dt.int32
AF = mybir.ActivationFunctionType
ALU = mybir.AluOpType
AX = mybir.AxisListType

NCORES = 8
D = 2048
NCH = 16
SEQ = 4096
NB = 32
OWN = 8
TOK = 1024
BIG = 30000.0
CBIG = 1.0e6
NIT = 16
TOPK = 256
COMPUTE = ("pe", "act", "dve", "pool")

C_Q, C_K, C_V, C_QI, C_WI, C_KI, C_ZA, C_U, C_VB, C_ZB, C_GA, C_GB = (
    0, 1024, 1280, 1536, 3584, 3600, 3728, 4752, 5776, 6800, 7824, 9872)
FM_Q, FM_QI, FM_ZA, FM_U, FM_ZB, FM_G = 0, 8, 24, 32, 40, 48
N_FM = 80


class _Op:
    __slots__ = ("eng", "fn", "deps", "is_dma", "slot", "marked", "mark_idx", "cum", "idx")


class Prog:
    def __init__(self, nc, stack):
        self.nc = nc
        self.stack = stack
        self.ops = []
        self.last_w = {}
        self.readers = {}
        self.slot_cum = {}
        self.bar = []
        self.psx = {}
        self.engs = {"pe": nc.tensor, "act": nc.scalar, "dve": nc.vector, "pool": nc.gpsimd, "sp": nc.sync}

    def add(self, eng, fn, reads=(), writes=(), slot=None):
        op = _Op()
        op.eng = eng
        op.fn = fn
        op.is_dma = slot is not None
        op.slot = slot
        op.marked = False
        op.mark_idx = None
        op.idx = len(self.ops)
        op.deps = set()
        if op.is_dma:
            self.slot_cum[slot] = self.slot_cum.get(slot, 0) + 16
            op.cum = self.slot_cum[slot]
        for y in self.bar:
            self._dep(op, y, "raw")
        for r in reads:
            lw = self.last_w.get(r)
            if lw is not None:
                self._dep(op, lw, "raw")
        for w in writes:
            lw = self.last_w.get(w)
            if lw is not None:
                self._dep(op, lw, "waw")
            for rd in self.readers.get(w, ()):
                self._dep(op, rd, "war")
        for r in reads:
            self.readers.setdefault(r, []).append(op)
        for w in writes:
            self.last_w[w] = op
            self.readers[w] = []
        for res in list(reads) + list(writes):
            if isinstance(res, tuple) and len(res) >= 2 and res[0] == "ps":
                lastx = self.psx.get(res[1])
                if lastx is not None and lastx is not op and lastx.eng != op.eng:
                    op.deps.add(lastx.idx)
                    if not lastx.is_dma:
                        lastx.marked = True
                self.psx[res[1]] = op
        self.ops.append(op)
        return op

    def _dep(self, x, y, kind):
        if y is x:
            return
        if not y.is_dma and not x.is_dma and y.eng == x.eng:
            if kind != "raw" or x.eng == "pe":
                return
        x.deps.add(y.idx)
        if not y.is_dma:
            y.marked = True

    def barrier(self):
        last = {}
        for op in self.ops:
            if op.fn is None:
                continue
            key = ("slot", op.slot) if op.is_dma else ("eng", op.eng)
            last[key] = op
        self.bar = list(last.values())

    def emit(self):
        nc = self.nc
        sems = {}
        for e in COMPUTE:
            sems[("eng", e)] = self.stack.enter_context(nc.semaphore("c_" + e))
        for i, s in enumerate(self.slot_cum):
            sems[("slot", s)] = self.stack.enter_context(nc.semaphore("d%d" % i))
        cnt = {e: 0 for e in COMPUTE}
        for op in self.ops:
            if op.marked:
                cnt[op.eng] += 1
                op.mark_idx = cnt[op.eng]
        waited = {}
        nw = 0
        for op in self.ops:
            eng = self.engs[op.eng]
            need = {}
            for di in op.deps:
                y = self.ops[di]
                if y.is_dma:
                    k, v = ("slot", y.slot), y.cum
                else:
                    k, v = ("eng", y.eng), y.mark_idx
                if need.get(k, 0) < v:
                    need[k] = v
            wd = waited.setdefault(op.eng, {})
            for k, v in need.items():
                if wd.get(k, 0) >= v:
                    continue
                eng.wait_ge(sems[k], v)
                wd[k] = v
                nw += 1
            if op.fn is None:
                continue
            ins = op.fn(eng)
            if op.is_dma:
                ins.then_inc(sems[("slot", op.slot)], 16)
            elif op.marked:
                ins.then_inc(sems[("eng", op.eng)], 1)
        return dict(nops=len(self.ops), nwaits=nw, marks=cnt, nsems=len(sems))


def build_program(debug=False, upto=99):
    nc = bass.Bass("TRN2", target_bir_lowering=False)

    def din(name, shape, dtype=F32):
        return nc.dram_tensor(name, list(shape), dtype, kind="ExternalInput").ap()

    def dout(name, shape, dtype=F32):
        return nc.dram_tensor(name, list(shape), dtype, kind="ExternalOutput").ap()

    x_all = din("x_all", [SEQ, D])
    x_own = din("x_own", [TOK, D])
    qrel_d = din("qrel", [128, OWN])
    x_s = din("x_s", [1, D])
    ptab = din("ptab", [1, 128], I32)
    pool_ki8 = din("pool_ki8", [1280 * 8, 2048])
    pool_k16 = din("pool_k16", [1280 * 16, 2048])
    pool_v16 = din("pool_v16", [1280 * 16, 2048])
    g_in_d = din("g_in", [1, D])
    g_f_d = din("g_f", [1, D])
    ln_g_d = din("ln_g", [1, 1024])
    ln_b_d = din("ln_b", [1, 1024])
    bsp_d = din("bsp", [1, 1024])
    ws00_d = din("ws00", [1, 1024])
    bs0_d = din("bs0", [1, 1024])
    wspT_d = din("wspT", [128, 8 * 128])
    wkvki_d = din("wkvki", [128, NCH * 640])
    wfm_d = din("wfm", [N_FM, 128, NCH * 128])
    wwi_d = din("wwi", [128, NCH * 16])
    wvb_d = din("wvb", [128, NCH * 1024])
    wpa_d = din("wpa", [16, 128, 8 * 128])
    wpb_d = din("wpb", [16, 128, 8 * 128])
    wout_d = din("wout", [4, 128, NCH * 512])

    y_own = dout("y_own", [TOK, D])
    knew = dout("knew", [SEQ, 256])
    vnew = dout("vnew", [SEQ, 256])
    kinew = dout("kinew", [SEQ, 128])
    ys_o = dout("ys", [1, D])
    ks_o = dout("ks", [1, 256])
    vs_o = dout("vs", [1, 256])
    kis_o = dout("kis", [1, 128])
    gvs_o = dout("gvs", [1, 1024])
    dbg = {}
    if debug:
        dbg["qT"] = dout("dbg_qT", [128, 8 * TOK], BF16)
        dbg["kT"] = dout("dbg_kT", [128, 2 * SEQ], BF16)
        dbg["kiT"] = dout("dbg_kiT", [128, SEQ], BF16)
        dbg["qiT"] = dout("dbg_qiT", [128, 16 * TOK], BF16)
        dbg["wabs"] = dout("dbg_wabs", [128, 128])
        dbg["acc"] = dout("dbg_acc", [128, OWN * 4096])
        dbg["thr"] = dout("dbg_thr", [128, OWN])
        dbg["oaT"] = dout("dbg_oaT", [128, 8 * TOK], BF16)
        dbg["obT"] = dout("dbg_obT", [128, 8 * TOK], BF16)
        dbg["mixT"] = dout("dbg_mixT", [128, 16 * TOK], BF16)

    st = ExitStack()
    P = Prog(nc, st)
    ARENA_B = 207 * 1024
    arena = st.enter_context(nc.sbuf_tensor("arena", [128, ARENA_B // 2], BF16))
    ps = [st.enter_context(nc.psum_tensor("ps%d" % i, [128, 512], F32)) for i in range(8)]
    psb = [p[:, :].bitcast(BF16) for p in ps]


    def finish():
        fin = P.add("sp", None)
        lastd = {}
        for op in P.ops:
            if op.is_dma:
                lastd[op.slot] = op
        for op in lastd.values():
            fin.deps.add(op.idx)
        info = P.emit()
        st.close()
        return nc, info

    KB = 1024

    def buf(off_b, nbytes, dtype=BF16, parts=128):
        if off_b >= 20 * KB:
            off_b += KB
        assert off_b + nbytes <= ARENA_B, (off_b, nbytes)
        a = arena[0:parts, off_b // 2:(off_b + nbytes) // 2]
        if dtype != BF16:
            a = a.bitcast(dtype)
        return a

    def v3(ap, a):
        return ap.rearrange("p (a b) -> p a b", a=a)

    o = 0
    ident = buf(o, 256); o += 256
    identB4 = buf(o, 1024); o += 1024
    ones_bf = buf(o, 256); o += 256
    identf = buf(o, 512, F32); o += 512
    iota_f = buf(o, 2048, F32); o += 2048
    pow2 = buf(o, 128, F32); o += 128
    qrel = buf(o, 32, F32); o += 32
    epsc = buf(o, 16, F32); o += 16
    stat = buf(o, 256, F32); o += 256
    wabs = buf(o, 512, F32); o += 512
    wsgn = buf(o, 512, F32); o += 512
    sm = buf(o, 2048, F32); o += 2048
    iota_i = sm.bitcast(I32)
    smb = buf(o, 1024, BF16); o += 1024
    WsT = buf(o, 2048); o += 2048
    gbc = buf(o, 8192, F32); o += 8192
    ksv = buf(o, 2560, F32, parts=1); o += 2560
    assert o <= 21 * KB, o
    hsT = smb[:, 0:16]
    oazTs = smb[:, 16:24]
    obzTs = smb[:, 24:32]
    mixTs = smb[:, 32:48]

    kT_all = buf(20 * KB, 16 * KB)
    kiT_all = buf(36 * KB, 8 * KB)
    v_all = buf(44 * KB, 16 * KB)
    qT = buf(60 * KB, 16 * KB)
    qiT = buf(76 * KB, 32 * KB)
    hT_own = buf(108 * KB, 32 * KB)
    oaT = buf(140 * KB, 16 * KB)
    mixT = buf(156 * KB, 32 * KB)
    kT3 = v3(kT_all, 2)
    v3a = v3(v_all, 32)
    qT3 = v3(qT, 8)
    qiT3 = v3(qiT, 16)
    hT3 = v3(hT_own, 16)
    oaT3 = v3(oaT, 8)
    mixT3 = v3(mixT, 16)

    P.add("pool", lambda e: e.memset(identf, 1.0), writes=["identf"])
    P.add("pool", lambda e: e.affine_select(out=identf, in_=identf, pattern=[[-1, 128]], compare_op=ALU.is_equal,
                                            fill=0.0, base=0, channel_multiplier=1), reads=["identf"], writes=["identf"])
    P.add("dve", lambda e: e.tensor_copy(out=ident, in_=identf), reads=["identf"], writes=["ident"])
    for j in range(4):
        P.add("dve", lambda e, j=j: e.tensor_scalar(out=identB4[:, j * 128:(j + 1) * 128], in0=identf, scalar1=BIG, scalar2=None,
                                                    op0=ALU.mult), reads=["identf"], writes=["identB4"])
    P.add("pool", lambda e: e.memset(ones_bf, 1.0), writes=["ones"])
    P.add("pool", lambda e: e.iota(iota_i, pattern=[[1, 512]], base=0, channel_multiplier=0), writes=["iota_i"])
    P.add("dve", lambda e: e.tensor_copy(out=iota_f, in_=iota_i), reads=["iota_i"], writes=["iota"])
    for k in range(NIT + 1):
        P.add("pool", lambda e, k=k: e.memset(pow2[:, k:k + 1], float(2.0 ** -(k + 1))), writes=["pow2"])
    P.add("pool", lambda e: e.memset(epsc[:, 0:1], 1e-6), writes=["epsc"])
    P.add("pool", lambda e: e.memset(epsc[:, 1:2], 1e-5), writes=["epsc"])
    P.add("sp", lambda e: e.dma_start(out=qrel, in_=qrel_d), writes=["qrel"], slot="c_qrel")
    P.add("sp", lambda e: e.dma_start(out=gbc, in_=g_in_d.partition_broadcast(128)), writes=["gbc"], slot="c_gbc")

    def rms_block(src_ap, xb, xres, xslot, hb, hres, junk, sidx, gb_ap, nparts=128):
        ss = stat[0:nparts, sidx:sidx + 1]
        sd = stat[0:nparts, sidx + 1:sidx + 2]
        rs = stat[0:nparts, sidx + 2:sidx + 3]
        P.add("sp", lambda e: e.dma_start(out=xb, in_=src_ap), writes=[xres], slot=xslot)
        P.add("act", lambda e: e.activation(out=junk, in_=xb, func=AF.Square, accum_out=ss),
              reads=[xres], writes=["junk", ("st", sidx)])
        P.add("act", lambda e: e.activation(out=sd, in_=ss, func=AF.Sqrt, scale=1.0 / D, bias=epsc[0:nparts, 0:1]),
              reads=[("st", sidx), "epsc"], writes=[("st", sidx + 1)])
        P.add("dve", lambda e: e.reciprocal(out=rs, in_=sd), reads=[("st", sidx + 1)], writes=[("st", sidx + 2)])
        P.add("dve", lambda e: e.scalar_tensor_tensor(out=hb, in0=xb, scalar=rs, in1=gb_ap, op0=ALU.mult, op1=ALU.mult),
              reads=[xres, ("st", sidx + 2), "gbc"], writes=[hres])

    def transposes16(hb, hres, bankA, bankB):
        for c in range(16):
            bk = bankA if c < 8 else bankB
            cc = c % 8
            P.add("pe", lambda e, c=c, bk=bk, cc=cc: e.transpose(out=psb[bk][:, cc * 128:(cc + 1) * 128],
                                                                 in_=hb[:, c * 128:(c + 1) * 128], identity=ident),
                  reads=[hres, "ident"], writes=[("ps", bk)])

    if upto <= 0:
        return finish()
    S1 = 60 * KB
    xbuf = [buf(S1 + i * 8 * KB, 8 * KB, F32) for i in range(2)]
    hbuf = [buf(S1 + 16 * KB + i * 4 * KB, 4 * KB) for i in range(2)]
    junk = buf(S1 + 24 * KB, 4 * KB)
    hTblk = [buf(S1 + 28 * KB + i * 4 * KB, 4 * KB) for i in range(2)]
    Wkvki = buf(S1 + 36 * KB, 20 * KB)
    kvf = [buf(S1 + 56 * KB + i * 2560, 2560, F32) for i in range(2)]
    kb16 = [buf(S1 + 62 * KB + i * 768, 768) for i in range(2)]
    Wk3 = v3(Wkvki, 16)
    for q4 in range(8):
        P.add("pool", lambda e, q4=q4: e.dma_start(out=Wk3[:, q4 * 2:(q4 + 1) * 2, :],
                                                  in_=v3(wkvki_d, 16)[:, q4 * 2:(q4 + 1) * 2, :]),
              writes=["Wkvki"], slot="wkvki%d" % (q4 % 4))
    xs_f = buf(S1 + 64 * KB, 8 * KB, F32, parts=1)
    hs_f = buf(S1 + 72 * KB, 8 * KB, F32, parts=1)
    kvs = buf(S1 + 80 * KB, 2560, F32, parts=1)
    import os as _os
    if _os.environ.get('SKIP_S1'):
        P.add('pool', lambda e: e.memset(hsT, 0.0), writes=['hsT'])
    else:
        rms_block(x_s, xs_f, "xs_f", "xs_f", hs_f, "hs_f", junk[0:1, :], 8, gbc[0:1, :], nparts=1)
        for c in range(16):
            P.add("pe", lambda e, c=c: e.transpose(out=ps[4][:, c:c + 1], in_=hs_f[0:1, c * 128:(c + 1) * 128], identity=identf[0:1, 0:1]),
                  reads=["hs_f", "identf"], writes=[("ps", 4)])
        P.add("act", lambda e: e.copy(out=hsT, in_=ps[4][:, 0:16]), reads=[("ps", 4)], writes=["hsT"])
        for c in range(16):
            P.add("pe", lambda e, c=c: e.matmul(ps[6][0:1, 0:512], lhsT=hsT[:, c:c + 1], rhs=Wk3[:, c, 0:512], start=(c == 0), stop=(c == 15)),
                  reads=["hsT", "Wkvki"], writes=[("ps", 6)])
            P.add("pe", lambda e, c=c: e.matmul(ps[7][0:1, 0:128], lhsT=hsT[:, c:c + 1], rhs=Wk3[:, c, 512:640], start=(c == 0), stop=(c == 15)),
                  reads=["hsT", "Wkvki"], writes=[("ps", 7)])
        P.add("dve", lambda e: e.tensor_copy(out=kvs[:, 0:512], in_=ps[6][0:1, 0:512]), reads=[("ps", 6)], writes=["kvs0"])
        P.add("dve", lambda e: e.tensor_copy(out=kvs[:, 512:640], in_=ps[7][0:1, 0:128]), reads=[("ps", 7)], writes=["kvs1"])
        P.add("dve", lambda e: e.tensor_copy(out=ksv, in_=kvs), reads=["kvs0", "kvs1"], writes=["ksv"])
        P.add("sp", lambda e: e.dma_start(out=ks_o, in_=kvs[:, 0:256]), reads=["kvs0"], writes=["o_ks"], slot="o_ks")
        P.add("sp", lambda e: e.dma_start(out=vs_o, in_=kvs[:, 256:512]), reads=["kvs0"], writes=["o_vs"], slot="o_vs")
        P.add("sp", lambda e: e.dma_start(out=kis_o, in_=kvs[:, 512:640]), reads=["kvs1"], writes=["o_kis"], slot="o_kis")
    def s1_front(b):
        pb = b % 2
        rms_block(x_all[b * 128:(b + 1) * 128, :], xbuf[pb], ("xbuf", pb), "xbuf%d" % pb, hbuf[pb], ("hbuf", pb), junk, 4 * pb, gbc)

    def s1_t16(b, xbuf=xbuf, hbuf=hbuf, hTblk=hTblk):
        pb = b % 2
        hb, htb = hbuf[pb], hTblk[pb]
        bA, bB = (0, 1) if pb == 0 else (2, 3)
        transposes16(hb, ("hbuf", pb), bA, bB)
        P.add("act", lambda e: e.copy(out=htb[:, 0:1024], in_=psb[bA][:, 0:1024]), reads=[("ps", bA)], writes=[("htb", pb, 0)])
        P.add("dve", lambda e: e.tensor_copy(out=htb[:, 1024:2048], in_=psb[bB][:, 0:1024]), reads=[("ps", bB)], writes=[("htb", pb, 1)])

    def s1_mm(b, hTblk=hTblk, kvf=kvf, kb16=kb16):
        pb = b % 2
        htb3 = v3(hTblk[pb], 16)
        bKV, bKI = (4, 5) if pb == 0 else (6, 7)
        for c in range(16):
            P.add("pe", lambda e, c=c: e.matmul(ps[bKV][:, 0:512], lhsT=htb3[:, c, :], rhs=Wk3[:, c, 0:512], start=(c == 0), stop=(c == 15)),
                  reads=[("htb", pb, c // 8), "Wkvki"], writes=[("ps", bKV)])
            P.add("pe", lambda e, c=c: e.matmul(ps[bKI][:, 0:128], lhsT=htb3[:, c, :], rhs=Wk3[:, c, 512:640], start=(c == 0), stop=(c == 15)),
                  reads=[("htb", pb, c // 8), "Wkvki"], writes=[("ps", bKI)])
        kf = kvf[pb]
        k16 = kb16[pb]
        P.add("dve", lambda e: e.tensor_copy(out=kf[:, 0:512], in_=ps[bKV][:, 0:512]), reads=[("ps", bKV)], writes=[("kvf", pb, 0)])
        P.add("act", lambda e: e.copy(out=kf[:, 512:640], in_=ps[bKI][:, 0:128]), reads=[("ps", bKI)], writes=[("kvf", pb, 1)])
        P.add("dve", lambda e: e.tensor_copy(out=k16[:, 0:256], in_=ps[bKV][:, 0:256]), reads=[("ps", bKV)], writes=[("kb16", pb)])
        P.add("act", lambda e: e.copy(out=k16[:, 256:384], in_=ps[bKI][:, 0:128]), reads=[("ps", bKI)], writes=[("kb16", pb)])
        P.add("dve", lambda e: e.tensor_copy(out=v3a[:, b, :], in_=ps[bKV][:, 256:512]), reads=[("ps", bKV)], writes=[("v_all", b)])
        rows = slice(b * 128, (b + 1) * 128)
        P.add("pool", lambda e: e.dma_start(out=knew[rows, :], in_=kf[:, 0:256]), reads=[("kvf", pb, 0)], writes=[("o_k", pb)], slot="o_k%d" % pb)
        P.add("pool", lambda e: e.dma_start(out=vnew[rows, :], in_=kf[:, 256:512]), reads=[("kvf", pb, 0)], writes=[("o_v", pb)], slot="o_v%d" % pb)
        P.add("pool", lambda e: e.dma_start(out=kinew[rows, :], in_=kf[:, 512:640]), reads=[("kvf", pb, 1)], writes=[("o_ki", pb)], slot="o_ki%d" % pb)

    def s1_tk(b, kb16=kb16):
        pb = b % 2
        k16 = kb16[pb]
        bKI = 5 if pb == 0 else 7
        for j in range(3):
            P.add("pe", lambda e, j=j: e.transpose(out=psb[bKI][:, 256 + j * 128:256 + (j + 1) * 128], in_=k16[:, j * 128:(j + 1) * 128], identity=ident),
                  reads=[("kb16", pb), "ident"], writes=[("ps", bKI)])
        P.add("act", lambda e: e.copy(out=kT3[:, :, b * 128:(b + 1) * 128], in_=v3(psb[bKI][:, 256:512], 2)), reads=[("ps", bKI)], writes=[("kT", b)])
        P.add("act", lambda e: e.copy(out=kiT_all[:, b * 128:(b + 1) * 128], in_=psb[bKI][:, 512:640]), reads=[("ps", bKI)], writes=[("kiT", b)])

    s1_front(0)
    s1_front(1)
    s1_t16(0)
    s1_t16(1)
    for b in range(NB):
        if b + 2 < NB:
            s1_front(b + 2)
        s1_mm(b)
        if b + 2 < NB:
            s1_t16(b + 2)
        s1_tk(b)

    if upto <= 1:
        return finish()
    P.barrier()
    S2 = 140 * KB
    xbuf = [buf(S2 + i * 8 * KB, 8 * KB, F32) for i in range(2)]
    hbuf = [buf(S2 + 16 * KB + i * 4 * KB, 4 * KB) for i in range(2)]
    junk = buf(S2 + 24 * KB, 4 * KB)
    wbuf = [buf(S2 + 28 * KB + i * 4 * KB, 4 * KB) for i in range(3)]
    wwi = buf(S2 + 40 * KB, 512)

    def own_pass(xbuf, hbuf, junk):
        def front(i):
            pb = i % 2
            rms_block(x_own[i * 128:(i + 1) * 128, :], xbuf[pb], ("xbuf", pb), "xbuf%d" % pb, hbuf[pb], ("hbuf", pb), junk, 4 * pb, gbc)

        def back(i):
            pb = i % 2
            hb = hbuf[pb]
            bA, bB = (0, 1) if pb == 0 else (2, 3)
            transposes16(hb, ("hbuf", pb), bA, bB)
            P.add("act", lambda e, bA=bA, i=i: e.copy(out=hT3[:, 0:8, i * 128:(i + 1) * 128], in_=v3(psb[bA][:, 0:1024], 8)),
                  reads=[("ps", bA)], writes=[("hT", i, 0)])
            P.add("dve", lambda e, bB=bB, i=i: e.tensor_copy(out=hT3[:, 8:16, i * 128:(i + 1) * 128], in_=v3(psb[bB][:, 0:1024], 8)),
                  reads=[("ps", bB)], writes=[("hT", i, 1)])

        front(0)
        for i in range(OWN):
            if i + 1 < OWN:
                front(i + 1)
            back(i)

    own_pass(xbuf, hbuf, junk)

    hT_reads = [("hT", i, h) for i in range(OWN) for h in range(2)]
    fm_state = {"n": 0}

    def proj_fm(cb, scol, evac, pa=None):
        n = fm_state["n"]
        fm_state["n"] += 1
        ws = n % 3
        wsl = wbuf[ws]
        w3 = v3(wsl, 16)
        P.add("pool", lambda e: e.dma_start(out=wsl, in_=wfm_d[cb]), writes=[("wbuf", ws)], slot="wbuf%d" % ws)
        if pa is None:
            pa = 2 * (n % 3)
        for c in range(16):
            for hf in range(2):
                P.add("pe", lambda e, c=c, hf=hf: e.matmul(ps[pa + hf][:, 0:512], lhsT=w3[:, c, :], rhs=hT3[:, c, hf * 512:(hf + 1) * 512],
                                                          start=(c == 0), stop=(c == 15)),
                      reads=[("wbuf", ws)] + hT_reads, writes=[("ps", pa + hf)])
            P.add("pe", lambda e, c=c: e.matmul(ps[6][:, scol:scol + 1], lhsT=w3[:, c, :], rhs=hsT[:, c:c + 1],
                                                start=(c == 0), stop=(c == 15)),
                  reads=[("wbuf", ws), "hsT"], writes=[("ps", 6)])
        evac(pa)

    def evac_copy(dst3, h):
        def f(pa):
            P.add("act", lambda e: e.copy(out=dst3[:, h, 0:512], in_=ps[pa][:, 0:512]), reads=[("ps", pa)], writes=[(id(dst3), h, 0)])
            P.add("dve", lambda e: e.tensor_copy(out=dst3[:, h, 512:1024], in_=ps[pa + 1][:, 0:512]), reads=[("ps", pa + 1)],
                  writes=[(id(dst3), h, 1)])
        return f

    for h in range(8):
        proj_fm(FM_Q + h, h, evac_copy(qT3, h))
    for h in range(16):
        proj_fm(FM_QI + h, 8 + h, evac_copy(qiT3, h))
    P.add("dve", lambda e: e.tensor_copy(out=sm[:, 0:24], in_=ps[6][:, 0:24]), reads=[("ps", 6)], writes=["sm_q"])
    P.add("pool", lambda e: e.dma_start(out=wwi, in_=wwi_d), writes=["wwi"], slot="wwi")
    wwi3 = v3(wwi, 16)
    for i in range(OWN):
        for c in range(16):
            P.add("pe", lambda e, i=i, c=c: e.matmul(ps[7][:, i * 16:(i + 1) * 16], lhsT=hT3[:, c, i * 128:(i + 1) * 128], rhs=wwi3[:, c, :],
                                                    start=(c == 0), stop=(c == 15)),
                  reads=["wwi"] + hT_reads, writes=[("ps", 7)])
    P.add("act", lambda e: e.activation(out=wabs, in_=ps[7][:, 0:128], func=AF.Abs), reads=[("ps", 7)], writes=["wabs"])
    P.add("act", lambda e: e.activation(out=wsgn, in_=ps[7][:, 0:128], func=AF.Sign), reads=[("ps", 7)], writes=["wsgn"])
    for c in range(16):
        P.add("pe", lambda e, c=c: e.matmul(ps[7][0:16, 128:129], lhsT=wwi3[:, c, :], rhs=hsT[:, c:c + 1], start=(c == 0), stop=(c == 15)),
              reads=["wwi", "hsT"], writes=[("ps", 7)])
    P.add("dve", lambda e: e.tensor_copy(out=sm[0:16, 120:121], in_=ps[7][0:16, 128:129]), reads=[("ps", 7)], writes=["sm_w"])
    if debug:
        P.add("sp", lambda e: e.dma_start(out=dbg["qT"], in_=qT), reads=[(id(qT3), h, j) for h in range(8) for j in range(2)], writes=["dbg_dq"], slot="dbg0")
        P.add("sp", lambda e: e.dma_start(out=dbg["qiT"], in_=qiT), reads=[(id(qiT3), h, j) for h in range(16) for j in range(2)], writes=["dbg_dqi"], slot="dbg1")
        P.add("sp", lambda e: e.dma_start(out=dbg["kT"], in_=kT_all), reads=[("kT", b) for b in range(NB)], writes=["dbg_dk"], slot="dbg2")
        P.add("sp", lambda e: e.dma_start(out=dbg["kiT"], in_=kiT_all), reads=[("kiT", b) for b in range(NB)], writes=["dbg_dki"], slot="dbg3")
        P.add("sp", lambda e: e.dma_start(out=dbg["wabs"], in_=wabs), reads=["wabs"], writes=["dbg_dwa"], slot="dbg4")

    if upto <= 2:
        return finish()
    P.barrier()
    acc = buf(108 * KB, 16 * KB, F32)
    negm = buf(124 * KB, 8 * KB)
    jnk3 = buf(132 * KB, 8 * KB)
    S3 = 156 * KB
    Rb = [buf(S3 + i * KB, KB) for i in range(4)]
    dsg = [buf(S3 + 16 * KB + i * 4 * KB, 4 * KB) for i in range(2)]
    PT = [buf(S3 + 6 * KB + i * KB, KB) for i in range(3)]
    pen = buf(S3 + 9 * KB, 2 * KB, F32)
    tmn = buf(S3 + 11 * KB, 2 * KB, F32)
    rden = buf(S3 + 13 * KB, 2 * KB, F32)
    bs = buf(S3 + 15 * KB, 512, F32)
    HK = bs[:, 0:NIT + 1]
    TC = bs[:, 32:32 + NIT + 2]
    cnt = bs[:, 64:65]
    tmpc = bs[:, 65:66]
    rmax = bs[:, 66:67]
    rmin = bs[:, 67:68]
    rmin2 = bs[:, 68:69]
    lo0 = bs[:, 69:70]
    w0 = bs[:, 70:71]
    thr = bs[:, 72:80]
    NTC = bs[:, 80:80 + NIT + 2]
    sA = bs[:, 100:101]
    cnt2 = bs[:, 101:102]
    SCALE = float(128 ** -0.5)
    rcount = {"s": 0, "st": 0, "pt": 0}
    for i in range(OWN):
        span = 512 * (i + 1)
        tok = slice(i * 128, (i + 1) * 128)
        dg = dsg[i % 2]
        dg3 = v3(dg, 16)
        for h in range(16):
            P.add("pool", lambda e, h=h, dg3=dg3, i=i: e.tensor_scalar(out=dg3[:, h, :], in0=identf, scalar1=wsgn[:, i * 16 + h:i * 16 + h + 1], scalar2=0.0,
                                                                     op0=ALU.mult, op1=ALU.add),
                  reads=["identf", "wsgn"], writes=[("dsg", i % 2)])
        for kg in range(i + 1):
            cols = slice(kg * 512, (kg + 1) * 512)

            def emit_s(h, cols=cols, i=i, tok=tok):
                n = rcount["s"]
                rcount["s"] += 1
                bk = n % 3
                rb = Rb[n % 4]
                P.add("pe", lambda e: e.matmul(ps[bk][:, 0:512], lhsT=qiT3[:, h, tok], rhs=kiT_all[:, cols], start=True, stop=True),
                      reads=["qiT", "kiT"], writes=[("ps", bk)])
                P.add("act", lambda e: e.activation(out=rb, in_=ps[bk][:, 0:512], func=AF.Relu, scale=wabs[:, i * 16 + h:i * 16 + h + 1]),
                      reads=[("ps", bk), "wabs"], writes=[("Rb", n % 4)])
                return rb, n % 4

            def emit_acc(h, rbn, dg3=dg3, i=i):
                rb, rn = rbn
                P.add("pe", lambda e: e.matmul(ps[7][:, 0:512], lhsT=dg3[:, h, :], rhs=rb, start=(h == 0), stop=(h == 15)),
                      reads=[("Rb", rn), ("dsg", i % 2)], writes=[("ps", 7)])

            prev = emit_s(0)
            for h in range(16):
                nxt = emit_s(h + 1) if h < 15 else None
                emit_acc(h, prev)
                prev = nxt
            P.add("dve", lambda e, cols=cols: e.tensor_copy(out=acc[:, cols], in_=ps[7][:, 0:512]), reads=[("ps", 7)], writes=[("acc", kg)])
        accr = [("acc", kg) for kg in range(i + 1)]
        last = slice(i * 512, (i + 1) * 512)
        P.add("dve", lambda e, i=i: e.tensor_scalar(out=pen, in0=iota_f, scalar1=qrel[:, i:i + 1], scalar2=CBIG, op0=ALU.is_gt, op1=ALU.mult),
              reads=["iota", "qrel"], writes=["pen"])
        P.add("dve", lambda e, last=last: e.tensor_tensor(out=tmn, in0=acc[:, last], in1=pen, op=ALU.add), reads=["pen", ("acc", i)], writes=["tmn"])
        P.add("dve", lambda e: e.tensor_reduce(out=rmin, in_=tmn, axis=AX.X, op=ALU.min), reads=["tmn"], writes=["rmin"])
        P.add("dve", lambda e, last=last: e.tensor_tensor(out=acc[:, last], in0=acc[:, last], in1=pen, op=ALU.subtract),
              reads=["pen", ("acc", i)], writes=[("acc", i)])
        P.add("dve", lambda e, span=span: e.tensor_reduce(out=rmax, in_=acc[:, 0:span], axis=AX.X, op=ALU.max), reads=accr, writes=["rmax"])
        if i > 0:
            P.add("dve", lambda e, i=i: e.tensor_reduce(out=rmin2, in_=acc[:, 0:i * 512], axis=AX.X, op=ALU.min), reads=accr, writes=["rmin2"])
            P.add("dve", lambda e: e.tensor_tensor(out=rmin, in0=rmin, in1=rmin2, op=ALU.min), reads=["rmin", "rmin2"], writes=["rmin"])
        P.add("dve", lambda e: e.tensor_tensor(out=w0, in0=rmax, in1=rmin, op=ALU.subtract), reads=["rmax", "rmin"], writes=["w0"])
        P.add("dve", lambda e: e.tensor_scalar(out=w0, in0=w0, scalar1=float(2.0 ** -10), scalar2=1e-3, op0=ALU.mult, op1=ALU.add),
              reads=["w0"], writes=["w0b"])
        P.add("dve", lambda e: e.tensor_tensor(out=lo0, in0=rmin, in1=w0, op=ALU.subtract), reads=["rmin", "w0b"], writes=["lo0"])
        P.add("dve", lambda e: e.tensor_tensor(out=w0, in0=rmax, in1=lo0, op=ALU.subtract), reads=["rmax", "lo0"], writes=["w0c"])
        P.add("dve", lambda e: e.tensor_scalar(out=HK, in0=pow2[:, 0:NIT + 1], scalar1=w0, scalar2=None, op0=ALU.mult),
              reads=["pow2", "w0c"], writes=["HK"])
        P.add("dve", lambda e: e.tensor_tensor(out=TC[:, 0:1], in0=lo0, in1=HK[:, 0:1], op=ALU.add), reads=["lo0", "HK"], writes=[("TC", 0)])
        nA = 0 if span < 1024 else 128 * int(round(span * 0.56 / 128))
        nD = span - nA
        if nA:
            P.add("dve", lambda e: e.tensor_scalar(out=NTC[:, 0:1], in0=TC[:, 0:1], scalar1=-1.0, scalar2=None, op0=ALU.mult), reads=[("TC", 0)], writes=[("NTC", 0)])
        for k in range(NIT):
            P.add("dve", lambda e, k=k, nD=nD: e.tensor_scalar(out=jnk3[:, 0:nD], in0=acc[:, 0:nD], scalar1=TC[:, k:k + 1], scalar2=None,
                                                              op0=ALU.is_ge, op1=ALU.add, accum_out=cnt),
                  reads=accr + [("TC", k)], writes=["jnk3", "cnt"])
            cres = "cnt"
            cap = cnt
            if nA:
                P.add("act", lambda e, k=k, nD=nD, span=span: e.activation(out=jnk3[:, nD:span], in_=acc[:, nD:span], func=AF.Sign, bias=NTC[:, k:k + 1],
                                                                        accum_out=sA),
                      reads=accr + [("NTC", k)], writes=["jnkA", "sA"])
                P.add("dve", lambda e: e.scalar_tensor_tensor(out=cnt2, in0=sA, scalar=0.5, in1=cnt, op0=ALU.mult, op1=ALU.add),
                      reads=["sA", "cnt"], writes=["cnt2"])
                cres = "cnt2"
                cap = cnt2
            P.add("dve", lambda e, k=k, cap=cap, nA=nA: e.scalar_tensor_tensor(out=tmpc, in0=cap, scalar=TOPK - 0.5 - nA / 2.0, in1=HK[:, k:k + 1],
                                                                          op0=ALU.is_ge, op1=ALU.mult),
                  reads=[cres, "HK"], writes=["tmpc"])
            P.add("dve", lambda e, k=k: e.scalar_tensor_tensor(out=TC[:, k + 1:k + 2], in0=tmpc, scalar=HK[:, k + 1:k + 2], in1=TC[:, k:k + 1],
                                                               op0=ALU.subtract, op1=ALU.add),
                  reads=["tmpc", "HK", ("TC", k)], writes=[("TC", k + 1)])
            if nA and k < NIT - 1:
                P.add("dve", lambda e, k=k: e.tensor_scalar(out=NTC[:, k + 1:k + 2], in0=TC[:, k + 1:k + 2], scalar1=-1.0, scalar2=None, op0=ALU.mult),
                      reads=[("TC", k + 1)], writes=[("NTC", k + 1)])
        P.add("dve", lambda e, i=i: e.tensor_tensor(out=thr[:, i:i + 1], in0=TC[:, NIT:NIT + 1], in1=HK[:, NIT:NIT + 1], op=ALU.subtract),
              reads=[("TC", NIT), "HK"], writes=[("thr", i)])
        P.add("dve", lambda e, i=i, span=span: e.tensor_scalar(out=negm[:, 0:span], in0=acc[:, 0:span], scalar1=thr[:, i:i + 1], scalar2=1.0,
                                                              op0=ALU.is_ge, op1=ALU.subtract),
              reads=accr + [("thr", i)], writes=["negm"])
        if debug:
            P.add("sp", lambda e, i=i, span=span: e.dma_start(out=dbg["acc"][:, i * 4096:i * 4096 + span], in_=acc[:, 0:span]),
                  reads=accr, writes=["dbg_dacc"], slot="dbg5")
        nkb = 4 * (i + 1)
        for g in range(2):
            qrhs = qT3[:, 4 * g:4 * g + 4, tok]

            def emit_st(kb, g=g, qrhs=qrhs):
                n = rcount["st"]
                rcount["st"] += 1
                bk = 3 + n % 2
                P.add("pe", lambda e: e.matmul(ps[bk][:, 0:512], lhsT=kT3[:, g, kb * 128:(kb + 1) * 128], rhs=qrhs, start=True, stop=False),
                      reads=["qT", "kT"], writes=[("ps", bk)])
                P.add("pe", lambda e: e.matmul(ps[bk][:, 0:512], lhsT=negm[:, kb * 128:(kb + 1) * 128], rhs=identB4, start=False, stop=True),
                      reads=["negm", "identB4"], writes=[("ps", bk)])
                return bk

            def emit_pv(kb, bk, g=g, nkb=nkb):
                n = rcount["pt"]
                rcount["pt"] += 1
                pt = PT[n % 3]
                P.add("act", lambda e: e.activation(out=pt, in_=ps[bk][:, 0:512], func=AF.Exp, scale=SCALE), reads=[("ps", bk)], writes=[("PT", n % 3)])
                P.add("pe", lambda e: e.matmul(ps[5][:, 0:512], lhsT=v3a[:, kb, g * 128:(g + 1) * 128], rhs=pt, start=(kb == 0), stop=(kb == nkb - 1)),
                      reads=[("PT", n % 3), "v_all"], writes=[("ps", 5)])
                P.add("pe", lambda e: e.matmul(ps[6][:, 0:512], lhsT=ones_bf, rhs=pt, start=(kb == 0), stop=(kb == nkb - 1)),
                      reads=[("PT", n % 3), "ones"], writes=[("ps", 6)])

            prev = None
            for kb in range(nkb):
                bk = emit_st(kb)
                if prev is not None:
                    emit_pv(*prev)
                prev = (kb, bk)
            emit_pv(*prev)
            P.add("dve", lambda e: e.reciprocal(out=rden, in_=ps[6][:, 0:512]), reads=[("ps", 6)], writes=["rden"])
            P.add("dve", lambda e, g=g, tok=tok: e.tensor_tensor(out=oaT3[:, 4 * g:4 * g + 4, tok], in0=v3(ps[5][:, 0:512], 4), in1=v3(rden, 4), op=ALU.mult),
                  reads=[("ps", 5), "rden"], writes=[("oaT", i, g)])
    if debug:
        P.add("sp", lambda e: e.dma_start(out=dbg["thr"], in_=thr), reads=[("thr", i) for i in range(OWN)], writes=["dbg_dthr"], slot="dbg6")
        P.add("sp", lambda e: e.dma_start(out=dbg["oaT"], in_=oaT), reads=[("oaT", i, g) for i in range(OWN) for g in range(2)], writes=["dbg_doa"], slot="dbg7")

    if upto <= 3:
        return finish()

    P.barrier()
    oaTs = sm[:, 112:120]
    B3 = 20 * KB
    idxp_i = buf(B3, 64, I32)
    idx8 = buf(B3 + 64, 64, I32)
    idx16 = buf(B3 + 128, 64, I32)
    pgf = buf(B3 + 192, 64, F32)
    ones_f = buf(B3 + 512, 512, F32)
    scb = buf(B3 + 1024, 1024, F32)
    sc = scb[:, 0:129]
    vld = buf(B3 + 2048, 1024, F32)[:, 0:129]
    kib = [buf(B3 + 4 * KB + i * 4 * KB, 4 * KB) for i in range(2)]
    kidT = [buf(B3 + 12 * KB + i * KB, KB) for i in range(2)]
    Rs = [buf(B3 + 14 * KB + i * 2 * KB, 2 * KB, F32) for i in range(2)]
    Kc = [buf(B3 + 18 * KB + i * 4 * KB, 4 * KB) for i in range(2)]
    KTb = [buf(B3 + 26 * KB + i * 4 * KB, 4 * KB) for i in range(2)]
    Kx = buf(B3 + 34 * KB, 512)
    Vx = buf(B3 + 35 * KB, 512)
    Psb = buf(B3 + 36 * KB, 4224, F32)
    Pbf = buf(B3 + 41 * KB, 2112)
    Pred = buf(B3 + 44 * KB, 64, F32)
    smx = buf(B3 + 45 * KB, 256, F32)
    osb = buf(B3 + 46 * KB, 1024, F32)
    qiTs_bf = smb[:, 56:72]
    qTs_bf = smb[:, 48:56]
    w_s = sm[0:16, 120:121]
    IOA = bass.IndirectOffsetOnAxis
    P.add("sp", lambda e: e.dma_start(out=idxp_i[:, 0:1], in_=ptab.rearrange("o p -> p o")), writes=["idxp"], slot="s_idx")
    P.add("dve", lambda e: e.tensor_copy(out=pgf[:, 0:1], in_=idxp_i[:, 0:1]), reads=["idxp"], writes=["pgf0"])
    P.add("dve", lambda e: e.tensor_scalar(out=pgf[:, 1:2], in0=pgf[:, 0:1], scalar1=8.0, scalar2=None, op0=ALU.mult), reads=["pgf0"], writes=["pgf1"])
    P.add("dve", lambda e: e.tensor_scalar(out=pgf[:, 2:3], in0=pgf[:, 0:1], scalar1=16.0, scalar2=None, op0=ALU.mult), reads=["pgf0"], writes=["pgf2"])
    P.add("dve", lambda e: e.tensor_scalar(out=idx8, in0=iota_f[:, 0:16], scalar1=pgf[:, 1:2], scalar2=None, op0=ALU.add), reads=["pgf1", "iota"], writes=["idx8"])
    P.add("dve", lambda e: e.tensor_scalar(out=idx16, in0=iota_f[:, 0:16], scalar1=pgf[:, 2:3], scalar2=None, op0=ALU.add), reads=["pgf2", "iota"], writes=["idx16"])
    P.add("pool", lambda e: e.memset(ones_f, 1.0), writes=["ones_f"])
    P.add("dve", lambda e: e.tensor_copy(out=qiTs_bf, in_=sm[:, 8:24]), reads=["sm_q"], writes=["qis_bf"])
    P.add("dve", lambda e: e.tensor_copy(out=qTs_bf, in_=sm[:, 0:8]), reads=["sm_q"], writes=["qs_bf"])
    ng = 0
    for ch in range(8):
        kb_ = kib[ch % 2]
        P.add("pool", lambda e, ch=ch, kb_=kb_: e.indirect_dma_start(out=kb_, out_offset=None, in_=pool_ki8,
                                                                    in_offset=IOA(ap=idx8[:, ch:ch + 1], axis=0)),
              reads=["idx8"], writes=[("kib", ch % 2)], slot="s_kib%d" % (ch % 2))
        for grp in range(4):
            tb_ = ng % 2
            for j in range(4):
                pos_l = 4 * grp + j
                P.add("pe", lambda e, kb_=kb_, pos_l=pos_l, j=j, tb_=tb_: e.transpose(out=psb[tb_][:, j * 128:(j + 1) * 128],
                                                                                 in_=kb_[:, pos_l * 128:(pos_l + 1) * 128], identity=ident),
                      reads=[("kib", ch % 2), "ident"], writes=[("ps", tb_)])
            kt_ = kidT[ng % 2]
            P.add("act", lambda e, kt_=kt_, tb_=tb_: e.copy(out=kt_, in_=psb[tb_][:, 0:512]), reads=[("ps", tb_)], writes=[("kidT", ng % 2)])
            P.add("pe", lambda e, kt_=kt_, tb_=tb_: e.matmul(ps[2 + tb_][0:16, 0:512], lhsT=qiTs_bf, rhs=kt_, start=True, stop=True),
                  reads=[("kidT", ng % 2), "qis_bf"], writes=[("ps", 2 + tb_)])
            rs_ = Rs[ng % 2]
            P.add("act", lambda e, rs_=rs_, tb_=tb_: e.activation(out=rs_[0:16, :], in_=ps[2 + tb_][0:16, 0:512], func=AF.Relu),
                  reads=[("ps", 2 + tb_)], writes=[("Rs", ng % 2)])
            for j in range(4):
                pos = ch * 16 + grp * 4 + j
                P.add("pe", lambda e, rs_=rs_, j=j, pos=pos: e.matmul(ps[4][:, pos:pos + 1], lhsT=rs_[0:16, j * 128:(j + 1) * 128], rhs=w_s,
                                                                   start=True, stop=True),
                      reads=[("Rs", ng % 2), "sm_w"], writes=[("ps", 4)])
            ng += 1
    P.add("pe", lambda e: e.transpose(out=ps[5][:, 0:1], in_=ksv[0:1, 512:640], identity=identf[0:1, 0:1]), reads=["ksv", "identf"], writes=[("ps", 5)])
    P.add("act", lambda e: e.copy(out=smb[:, 72:73], in_=ps[5][:, 0:1]), reads=[("ps", 5)], writes=["kisT"])
    P.add("pe", lambda e: e.matmul(ps[5][0:16, 8:9], lhsT=qiTs_bf, rhs=smb[:, 72:73], start=True, stop=True), reads=["kisT", "qis_bf"], writes=[("ps", 5)])
    P.add("act", lambda e: e.activation(out=smx[0:16, 0:1], in_=ps[5][0:16, 8:9], func=AF.Relu), reads=[("ps", 5)], writes=["Rn"])
    P.add("pe", lambda e: e.matmul(ps[5][0:1, 16:17], lhsT=smx[0:16, 0:1], rhs=w_s, start=True, stop=True), reads=["Rn", "sm_w"], writes=[("ps", 5)])
    P.add("pool", lambda e: e.memset(scb[:, 128:129], -CBIG), writes=["sc_x"])
    P.add("dve", lambda e: e.tensor_copy(out=scb[0:1, 128:129], in_=ps[5][0:1, 16:17]), reads=[("ps", 5), "sc_x"], writes=["sc_x"])
    P.add("dve", lambda e: e.tensor_copy(out=scb[:, 0:128], in_=ps[4][:, 0:128]), reads=[("ps", 4)], writes=["sc_m"])
    pm = smx[:, 2:4]
    P.add("dve", lambda e: e.tensor_reduce(out=smx[:, 2:3], in_=sc, axis=AX.X, op=ALU.max), reads=["sc_x", "sc_m"], writes=["pm0"])
    P.add("dve", lambda e: e.tensor_reduce(out=smx[:, 3:4], in_=scb[:, 0:128], axis=AX.X, op=ALU.min, negate=True), reads=["sc_m"], writes=["pm1"])
    P.add("pe", lambda e: e.transpose(out=ps[5][0:2, 32:160], in_=pm, identity=identf), reads=["pm0", "pm1", "identf"], writes=[("ps", 5)])
    P.add("dve", lambda e: e.tensor_reduce(out=smx[0:2, 4:5], in_=ps[5][0:2, 32:160], axis=AX.X, op=ALU.max), reads=[("ps", 5)], writes=["g2"])
    P.add("dve", lambda e: e.tensor_scalar(out=smx[0:2, 6:8], in0=identf[0:2, 0:2], scalar1=smx[0:2, 4:5], scalar2=None, op0=ALU.mult),
          reads=["g2", "identf"], writes=["g2d"])
    P.add("pe", lambda e: e.matmul(ps[5][:, 200:202], lhsT=ones_f[0:2, :], rhs=smx[0:2, 6:8], start=True, stop=True), reads=["g2d", "ones_f"], writes=[("ps", 5)])
    P.add("dve", lambda e: e.tensor_copy(out=smx[:, 8:10], in_=ps[5][:, 200:202]), reads=[("ps", 5)], writes=["gmm"])
    gmax = smx[:, 8:9]
    ngmin = smx[:, 9:10]
    P.add("dve", lambda e: e.tensor_tensor(out=w0, in0=gmax, in1=ngmin, op=ALU.add), reads=["gmm"], writes=["w0"])
    P.add("dve", lambda e: e.tensor_scalar(out=w0, in0=w0, scalar1=float(2.0 ** -10), scalar2=1e-3, op0=ALU.mult, op1=ALU.add), reads=["w0"], writes=["w0b"])
    P.add("dve", lambda e: e.tensor_tensor(out=lo0, in0=ngmin, in1=w0, op=ALU.add), reads=["gmm", "w0b"], writes=["lo0n"])
    P.add("dve", lambda e: e.tensor_scalar(out=lo0, in0=lo0, scalar1=-1.0, scalar2=None, op0=ALU.mult), reads=["lo0n"], writes=["lo0"])
    P.add("dve", lambda e: e.tensor_tensor(out=w0, in0=gmax, in1=lo0, op=ALU.subtract), reads=["gmm", "lo0"], writes=["w0c"])
    P.add("dve", lambda e: e.tensor_scalar(out=HK, in0=pow2[:, 0:NIT + 1], scalar1=w0, scalar2=None, op0=ALU.mult), reads=["pow2", "w0c"], writes=["HK"])
    P.add("dve", lambda e: e.tensor_tensor(out=TC[:, 0:1], in0=lo0, in1=HK[:, 0:1], op=ALU.add), reads=["lo0", "HK"], writes=[("TC", 0)])
    for k in range(NIT):
        P.add("dve", lambda e, k=k: e.tensor_scalar(out=vld, in0=sc, scalar1=TC[:, k:k + 1], scalar2=None, op0=ALU.is_ge, op1=ALU.add, accum_out=cnt),
              reads=["sc_x", "sc_m", ("TC", k)], writes=["vld", "cnt"])
        P.add("pe", lambda e: e.matmul(ps[6][:, 0:1], lhsT=ones_f, rhs=cnt, start=True, stop=True), reads=["cnt", "ones_f"], writes=[("ps", 6)])
        P.add("dve", lambda e, k=k: e.scalar_tensor_tensor(out=tmpc, in0=ps[6][:, 0:1], scalar=TOPK - 0.5, in1=HK[:, k:k + 1], op0=ALU.is_ge, op1=ALU.mult),
              reads=[("ps", 6), "HK"], writes=["tmpc"])
        P.add("dve", lambda e, k=k: e.scalar_tensor_tensor(out=TC[:, k + 1:k + 2], in0=tmpc, scalar=HK[:, k + 1:k + 2], in1=TC[:, k:k + 1],
                                                           op0=ALU.subtract, op1=ALU.add),
              reads=["tmpc", "HK", ("TC", k)], writes=[("TC", k + 1)])
    thr_s = smx[:, 12:13]
    P.add("dve", lambda e: e.tensor_tensor(out=thr_s, in0=TC[:, NIT:NIT + 1], in1=HK[:, NIT:NIT + 1], op=ALU.subtract), reads=[("TC", NIT), "HK"], writes=["thr_s"])
    P.add("dve", lambda e: e.tensor_scalar(out=vld, in0=sc, scalar1=thr_s, scalar2=None, op0=ALU.is_ge), reads=["sc_x", "sc_m", "thr_s"], writes=["vld"])
    P.add("pool", lambda e: e.memset(Kx, 0.0), writes=["Kx"])
    P.add("pool", lambda e: e.memset(Vx, 0.0), writes=["Vx"])
    P.add("dve", lambda e: e.tensor_copy(out=Kx[0:1, :], in_=ksv[0:1, 0:256]), reads=["ksv", "Kx"], writes=["Kx"])
    P.add("dve", lambda e: e.tensor_copy(out=Vx[0:1, :], in_=ksv[0:1, 256:512]), reads=["ksv", "Vx"], writes=["Vx"])

    def s_col(pos):
        return (4 + pos // 64, (pos % 64) * 8) if pos < 128 else (6, 0)

    for ch in range(17):
        npos = 8 if ch < 16 else 1
        if ch < 16:
            kc_ = Kc[ch % 2]
            P.add("pool", lambda e, ch=ch, kc_=kc_: e.indirect_dma_start(out=kc_, out_offset=None, in_=pool_k16,
                                                                        in_offset=IOA(ap=idx16[:, ch:ch + 1], axis=0)),
                  reads=["idx16"], writes=[("Kc", ch % 2)], slot="s_kc%d" % (ch % 2))
            kres = ("Kc", ch % 2)
        else:
            kc_ = Kx
            kres = "Kx"
        ktb = KTb[ch % 2]
        ba = 2 * (ch % 2)
        for t in range(2 * npos):
            bk = ba + t // 8
            P.add("pe", lambda e, kc_=kc_, t=t, bk=bk: e.transpose(out=psb[bk][:, (t % 8) * 128:(t % 8 + 1) * 128],
                                                                 in_=kc_[:, t * 128:(t + 1) * 128], identity=ident),
                  reads=[kres, "ident"], writes=[("ps", bk)])
        nb_ = (2 * npos + 7) // 8
        for hb in range(nb_):
            ncol = min(8, 2 * npos - 8 * hb) * 128
            P.add("act" if hb == 0 else "dve",
                  (lambda e, ktb=ktb, hb=hb, ba=ba, ncol=ncol: e.copy(out=ktb[:, hb * 1024:hb * 1024 + ncol], in_=psb[ba + hb][:, 0:ncol])) if hb == 0 else
                  (lambda e, ktb=ktb, hb=hb, ba=ba, ncol=ncol: e.tensor_copy(out=ktb[:, hb * 1024:hb * 1024 + ncol], in_=psb[ba + hb][:, 0:ncol])),
                  reads=[("ps", ba + hb)], writes=[("KTb", ch % 2, hb)])
        for p in range(npos):
            pos = ch * 8 + p
            bk, col = s_col(pos)
            for g in range(2):
                t = 2 * p + g
                P.add("pe", lambda e, ktb=ktb, t=t, g=g, bk=bk, col=col: e.matmul(ps[bk][:, col + 4 * g:col + 4 * g + 4], lhsT=ktb[:, t * 128:(t + 1) * 128],
                                                                              rhs=qTs_bf[:, 4 * g:4 * g + 4], start=True, stop=True),
                      reads=[("KTb", ch % 2, t // 8), "qs_bf"], writes=[("ps", bk)])
    P.add("act", lambda e: e.activation(out=Psb[:, 0:512], in_=ps[4][:, 0:512], func=AF.Exp, scale=SCALE), reads=[("ps", 4)], writes=["Psb0"])
    P.add("act", lambda e: e.activation(out=Psb[:, 512:1024], in_=ps[5][:, 0:512], func=AF.Exp, scale=SCALE), reads=[("ps", 5)], writes=["Psb1"])
    P.add("act", lambda e: e.activation(out=Psb[:, 1024:1032], in_=ps[6][:, 0:8], func=AF.Exp, scale=SCALE), reads=[("ps", 6)], writes=["Psb2"])
    P3 = Psb[:, 0:1032].rearrange("p (s h) -> p s h", h=8)
    Pb3 = Pbf[:, 0:1032].rearrange("p (s h) -> p s h", h=8)
    P.add("dve", lambda e: e.tensor_tensor(out=Pb3, in0=P3, in1=vld.unsqueeze(2).to_broadcast([128, 129, 8]), op=ALU.mult),
          reads=["Psb0", "Psb1", "Psb2", "vld"], writes=["Pbf"])
    P.add("dve", lambda e: e.tensor_reduce(out=Pred[:, 0:8], in_=Pbf[:, 0:1032].rearrange("p (s h) -> p h s", h=8), axis=AX.X, op=ALU.add),
          reads=["Pbf"], writes=["Pred"])
    for ch in range(17):
        npos = 8 if ch < 16 else 1
        if ch < 16:
            vc_ = Kc[ch % 2]
            P.add("pool", lambda e, ch=ch, vc_=vc_: e.indirect_dma_start(out=vc_, out_offset=None, in_=pool_v16,
                                                                        in_offset=IOA(ap=idx16[:, ch:ch + 1], axis=0)),
                  reads=["idx16"], writes=[("Kc", ch % 2)], slot="s_kc%d" % (ch % 2))
            vres = ("Kc", ch % 2)
        else:
            vc_ = Vx
            vres = "Vx"
        for p in range(npos):
            pos = ch * 8 + p
            for g in range(2):
                P.add("pe", lambda e, vc_=vc_, p=p, g=g, pos=pos: e.matmul(ps[g][0:4, 0:128], lhsT=Pbf[:, pos * 8 + 4 * g:pos * 8 + 4 * g + 4],
                                                                       rhs=vc_[:, p * 256 + g * 128:p * 256 + (g + 1) * 128],
                                                                       start=(pos == 0), stop=(pos == 128)),
                      reads=[vres, "Pbf"], writes=[("ps", g)])
    for g in range(2):
        P.add("pe", lambda e, g=g: e.matmul(ps[2][0:4, g:g + 1], lhsT=Pred[:, 4 * g:4 * g + 4], rhs=ones_f[:, 0:1], start=True, stop=True),
              reads=["Pred", "ones_f"], writes=[("ps", 2)])
    P.add("dve", lambda e: e.reciprocal(out=smx[0:4, 16:18], in_=ps[2][0:4, 0:2]), reads=[("ps", 2)], writes=["rden_s"])
    for g in range(2):
        P.add("dve", lambda e, g=g: e.tensor_scalar(out=osb[0:4, g * 128:(g + 1) * 128], in0=ps[g][0:4, 0:128], scalar1=smx[0:4, 16 + g:17 + g], scalar2=None,
                                                   op0=ALU.mult), reads=[("ps", g), "rden_s"], writes=[("osb", g)])
        P.add("pe", lambda e, g=g: e.transpose(out=ps[3][:, 4 * g:4 * g + 4], in_=osb[0:4, g * 128:(g + 1) * 128], identity=identf[0:4, 0:4]),
              reads=[("osb", g), "identf"], writes=[("ps", 3)])
    P.add("dve", lambda e: e.tensor_copy(out=oaTs, in_=ps[3][:, 0:8]), reads=[("ps", 3)], writes=["oaTs"])

    P.barrier()
    S4 = 20 * KB
    xbuf = [buf(S4 + i * 8 * KB, 8 * KB, F32) for i in range(2)]
    hbuf = [buf(S4 + 16 * KB + i * 4 * KB, 4 * KB) for i in range(2)]
    junk = buf(S4 + 24 * KB, 4 * KB)
    own_pass(xbuf, hbuf, junk)

    P.barrier()
    uT = buf(20 * KB, 16 * KB)
    obT = buf(36 * KB, 16 * KB)
    wvbuf = buf(52 * KB, 32 * KB)
    wbuf = [buf(84 * KB + i * 4 * KB, 4 * KB) for i in range(3)]
    wpbuf = [buf(96 * KB + i * 2 * KB, 2 * KB) for i in range(2)]
    szb = [buf(100 * KB + i * KB, KB) for i in range(2)]
    ftmp = buf(102 * KB, 4 * KB, F32)
    ftmp2 = buf(188 * KB, 4 * KB, F32)
    gvb = buf(192 * KB, 4 * KB, F32)
    vnb = [buf(196 * KB + i * 2 * KB, 2 * KB) for i in range(2)]
    bspbc = buf(200 * KB, 4 * KB, F32)
    uT3 = v3(uT, 8)
    obT3 = v3(obT, 8)
    wv3 = v3(wvbuf, 16)
    P.add("sp", lambda e: e.dma_start(out=gbc[:, 0:1024], in_=ln_g_d.partition_broadcast(128)), writes=["gbc"], slot="c_gbc")
    P.add("sp", lambda e: e.dma_start(out=gbc[:, 1024:2048], in_=ln_b_d.partition_broadcast(128)), writes=["gbc"], slot="c_gbc")
    P.add("sp", lambda e: e.dma_start(out=bspbc, in_=bsp_d.partition_broadcast(128)), writes=["bspbc"], slot="c_bsp")
    P.add("pool", lambda e: e.dma_start(out=WsT, in_=wspT_d), writes=["WsT"], slot="c_wsp")
    P.add("pool", lambda e: e.affine_select(out=v3(WsT, 8), in_=v3(WsT, 8), pattern=[[0, 8], [1, 128]], compare_op=ALU.is_ge,
                                            fill=0.0, base=0, channel_multiplier=-1), reads=["WsT"], writes=["WsT"])
    for q4 in range(8):
        P.add("pool", lambda e, q4=q4: e.dma_start(out=wv3[:, q4 * 2:(q4 + 1) * 2, :], in_=v3(wvb_d, 16)[:, q4 * 2:(q4 + 1) * 2, :]),
              writes=["wvbuf"], slot="wvb%d" % (q4 % 4))
    fm_state["n"] = 0

    oa_all = [("oaT", i, g) for i in range(OWN) for g in range(2)]

    def evac_mul_act(dst3, func, dres):
        def mk(cb):
            def f(pa):
                for hf in range(2):
                    sz = szb[hf]
                    P.add("act", lambda e, hf=hf, sz=sz: e.activation(out=sz, in_=ps[pa + hf][:, 0:512], func=func),
                          reads=[("ps", pa + hf)], writes=[("szb", hf)])
                    P.add("dve", lambda e, hf=hf, sz=sz: e.tensor_tensor(out=dst3[:, cb, hf * 512:(hf + 1) * 512],
                                                                       in0=dst3[:, cb, hf * 512:(hf + 1) * 512], in1=sz, op=ALU.mult),
                          reads=[("szb", hf)] + dres, writes=[(id(dst3), "z", cb, hf)])
            return f
        return mk

    mk = evac_mul_act(oaT3, AF.Silu, oa_all)
    for cb in range(8):
        proj_fm(FM_ZA + cb, 24 + cb, mk(cb))
    oaz_all = [(id(oaT3), "z", cb, hf) for cb in range(8) for hf in range(2)]
    P.add("act", lambda e: e.activation(out=sm[:, 24:32], in_=ps[6][:, 24:32], func=AF.Silu), reads=[("ps", 6)], writes=["zaTs"])
    P.add("dve", lambda e: e.tensor_tensor(out=oazTs, in0=oaTs, in1=sm[:, 24:32], op=ALU.mult), reads=["zaTs", "oaTs"], writes=["srhs"])

    def evac_act(dst3, func, tag):
        def mk(cb):
            def f(pa):
                for hf in range(2):
                    P.add("act", lambda e, hf=hf: e.activation(out=dst3[:, cb, hf * 512:(hf + 1) * 512], in_=ps[pa + hf][:, 0:512], func=func),
                          reads=[("ps", pa + hf)], writes=[(tag, cb, hf)])
            return f
        return mk

    mk = evac_act(uT3, AF.Gelu, "uT")
    for cb in range(8):
        proj_fm(FM_U + cb, 32 + cb, mk(cb))
    uT_all = [("uT", cb, hf) for cb in range(8) for hf in range(2)]
    P.add("act", lambda e: e.activation(out=sm[:, 32:40], in_=ps[6][:, 32:40], func=AF.Gelu), reads=[("ps", 6)], writes=["uTs"])

    def layernorm(src, dst, sidx, nparts, src_res, dst_res, tmp, jk):
        s1 = stat[0:nparts, sidx:sidx + 1]
        s2 = stat[0:nparts, sidx + 1:sidx + 2]
        mu = stat[0:nparts, sidx + 2:sidx + 3]
        var = stat[0:nparts, sidx + 3:sidx + 4]
        sd = stat[0:nparts, sidx + 4:sidx + 5]
        rs = stat[0:nparts, sidx + 5:sidx + 6]
        P.add("dve", lambda e: e.reduce_sum(out=s1, in_=src, axis=AX.X), reads=[src_res], writes=[("st", sidx)])
        P.add("dve", lambda e: e.tensor_scalar(out=mu, in0=s1, scalar1=1.0 / 1024, scalar2=None, op0=ALU.mult), reads=[("st", sidx)], writes=[("st", sidx + 2)])
        P.add("dve", lambda e: e.tensor_scalar(out=tmp, in0=src, scalar1=mu, scalar2=None, op0=ALU.subtract), reads=[src_res, ("st", sidx + 2)], writes=[(dst_res, "t")])
        P.add("act", lambda e: e.activation(out=jk[0:nparts, 0:1024], in_=tmp, func=AF.Square, accum_out=s2), reads=[(dst_res, "t")], writes=["junk", ("st", sidx + 1)])
        P.add("act", lambda e: e.activation(out=sd, in_=s2, func=AF.Sqrt, scale=1.0 / 1024, bias=epsc[0:nparts, 1:2]), reads=[("st", sidx + 1), "epsc"], writes=[("st", sidx + 4)])
        P.add("dve", lambda e: e.reciprocal(out=rs, in_=sd), reads=[("st", sidx + 4)], writes=[("st", sidx + 5)])
        P.add("dve", lambda e: e.scalar_tensor_tensor(out=tmp, in0=tmp, scalar=rs, in1=gbc[0:nparts, 0:1024], op0=ALU.mult, op1=ALU.mult),
              reads=[(dst_res, "t"), ("st", sidx + 5), "gbc"], writes=[(dst_res, "t")])
        P.add("dve", lambda e: e.tensor_tensor(out=dst, in0=tmp, in1=gbc[0:nparts, 1024:2048], op=ALU.add), reads=[(dst_res, "t"), "gbc"], writes=[dst_res])

    junk = buf(106 * KB, 2 * KB)
    for i in range(OWN):
        tok = slice(i * 128, (i + 1) * 128)
        for hf in range(2):
            bk = hf
            for c in range(16):
                P.add("pe", lambda e, c=c, hf=hf, bk=bk, tok=tok: e.matmul(ps[bk][:, 0:512], lhsT=hT3[:, c, tok], rhs=wv3[:, c, hf * 512:(hf + 1) * 512],
                                                                        start=(c == 0), stop=(c == 15)),
                      reads=["wvbuf"] + hT_reads, writes=[("ps", bk)])
            P.add("act", lambda e, hf=hf, bk=bk: e.activation(out=gvb[:, hf * 512:(hf + 1) * 512], in_=ps[bk][:, 0:512], func=AF.Gelu),
                  reads=[("ps", bk)], writes=["gvb"])
        vn = vnb[i % 2]
        layernorm(gvb, vn, 16, 128, "gvb", ("vn", i % 2), ftmp, junk)
        for g in range(8):
            bk = 2 + g // 4
            P.add("pe", lambda e, g=g, bk=bk, vn=vn: e.matmul(ps[bk][:, (g % 4) * 128:(g % 4 + 1) * 128], lhsT=vn[:, g * 128:(g + 1) * 128],
                                                             rhs=v3(WsT, 8)[:, g, :], start=True, stop=True),
                  reads=[("vn", i % 2), "WsT"], writes=[("ps", bk)])
        for hh in range(2):
            bk = 2 + hh
            gs = slice(4 * hh, 4 * hh + 4)
            P.add("dve", lambda e, bk=bk, hh=hh: e.tensor_tensor(out=ftmp2[:, hh * 512:(hh + 1) * 512], in0=ps[bk][:, 0:512],
                                                                in1=bspbc[:, hh * 512:(hh + 1) * 512], op=ALU.add),
                  reads=[("ps", bk), "bspbc"], writes=[("ftmp2", hh)])
            P.add("dve", lambda e, hh=hh, gs=gs, tok=tok: e.tensor_tensor(out=obT3[:, gs, tok], in0=v3(ftmp2[:, hh * 512:(hh + 1) * 512], 4),
                                                                       in1=uT3[:, gs, tok], op=ALU.mult),
                  reads=[("ftmp2", hh)] + uT_all, writes=[("obT", i, hh)])
    ob_all = [("obT", i, hh) for i in range(OWN) for hh in range(2)]
    for hf in range(2):
        for c in range(16):
            P.add("pe", lambda e, c=c, hf=hf: e.matmul(ps[4 + hf][0:1, 0:512], lhsT=hsT[:, c:c + 1], rhs=wv3[:, c, hf * 512:(hf + 1) * 512],
                                                      start=(c == 0), stop=(c == 15)),
                  reads=["wvbuf", "hsT"], writes=[("ps", 4 + hf)])
        P.add("act", lambda e, hf=hf: e.activation(out=gvb[0:1, hf * 512:(hf + 1) * 512], in_=ps[4 + hf][0:1, 0:512], func=AF.Gelu),
              reads=[("ps", 4 + hf)], writes=["gvb"])
    vns = ftmp2[0:1, :]
    layernorm(gvb[0:1, :], vns, 24, 1, "gvb", "vns", ftmp[0:1, :], junk)
    P.add("sp", lambda e: e.dma_start(out=gvs_o, in_=vns), reads=["vns", ("gt", 1, 0), ("gt", 1, 1)], writes=["o_gvs"], slot="o_gvs")

    P.add("sp", lambda e: e.dma_start(out=gvb[0:1, :], in_=ws00_d), reads=["vns"], writes=["gvb"], slot="c_ws00")
    P.add("sp", lambda e: e.dma_start(out=ftmp[0:1, :], in_=bs0_d), reads=["vns"], writes=[("vns", "t")], slot="c_bs0")
    P.add("dve", lambda e: e.tensor_tensor(out=gvb[0:1, :], in0=vns, in1=gvb[0:1, :], op=ALU.mult), reads=["vns", "gvb"], writes=["gvb"])
    P.add("dve", lambda e: e.tensor_tensor(out=gvb[0:1, :], in0=gvb[0:1, :], in1=ftmp[0:1, :], op=ALU.add), reads=["gvb", ("vns", "t")], writes=["gvb"])
    for g in range(8):
        P.add("pe", lambda e, g=g: e.transpose(out=ps[4][:, g:g + 1], in_=gvb[0:1, g * 128:(g + 1) * 128], identity=identf[0:1, 0:1]),
              reads=["gvb", "identf"], writes=[("ps", 4)])
    P.add("dve", lambda e: e.tensor_tensor(out=sm[:, 32:40], in0=ps[4][:, 0:8], in1=sm[:, 32:40], op=ALU.mult), reads=[("ps", 4), "uTs"], writes=["obTs"])
    mk = evac_mul_act(obT3, AF.Silu, ob_all)
    for cb in range(8):
        proj_fm(FM_ZB + cb, 40 + cb, mk(cb))
    obz_all = [(id(obT3), "z", cb, hf) for cb in range(8) for hf in range(2)]
    P.add("act", lambda e: e.activation(out=sm[:, 40:48], in_=ps[6][:, 40:48], func=AF.Silu), reads=[("ps", 6)], writes=["zbTs"])
    P.add("dve", lambda e: e.tensor_tensor(out=obzTs, in0=sm[:, 32:40], in1=sm[:, 40:48], op=ALU.mult), reads=["zbTs", "obTs"], writes=["srhs"])
    if debug:
        P.add("sp", lambda e: e.dma_start(out=dbg["obT"], in_=obT), reads=obz_all, writes=["dbg_dob"], slot="dbg8")

    wp_n = {"n": 0}

    def proj_branch(wd, cb, src3, sres, bank, scol, srhs):
        n = wp_n["n"]
        wp_n["n"] += 1
        wsl = wpbuf[n % 2]
        w3 = v3(wsl, 8)
        P.add("pool", lambda e: e.dma_start(out=wsl, in_=wd[cb]), writes=[("wpbuf", n % 2)], slot="wpbuf%d" % (n % 2))
        for c in range(8):
            for hf in range(2):
                P.add("pe", lambda e, c=c, hf=hf: e.matmul(ps[bank + hf][:, 0:512], lhsT=w3[:, c, :], rhs=src3[:, c, hf * 512:(hf + 1) * 512],
                                                          start=(c == 0), stop=(c == 7)),
                      reads=[("wpbuf", n % 2)] + sres, writes=[("ps", bank + hf)])
            P.add("pe", lambda e, c=c: e.matmul(ps[7][:, scol:scol + 1], lhsT=w3[:, c, :], rhs=srhs[:, c:c + 1], start=(c == 0), stop=(c == 7)),
                  reads=[("wpbuf", n % 2), "srhs"], writes=[("ps", 7)])


    for cb in range(16):
        sg = [None, None]
        for br in range(2):
            gt = ftmp if br == 0 else ftmp2

            def ev_gate(pa, gt=gt, br=br):
                for hf in range(2):
                    P.add("act", lambda e, hf=hf: e.activation(out=gt[:, hf * 512:(hf + 1) * 512], in_=ps[pa + hf][:, 0:512], func=AF.Sigmoid),
                          reads=[("ps", pa + hf)], writes=[("gt", br, hf)])
                sg[br] = pa
            proj_fm(FM_G + 2 * cb + br, (48 + cb if br == 0 else 64 + cb), ev_gate, pa=2 * br)
            if br == 0:
                proj_branch(wpa_d, cb, oaT3, oaz_all, 4, cb, oazTs)
            else:
                proj_branch(wpb_d, cb, obT3, obz_all, 4, 16 + cb, obzTs)
            for hf in range(2):
                P.add("dve", lambda e, hf=hf, gt=gt: e.tensor_tensor(out=gt[:, hf * 512:(hf + 1) * 512], in0=ps[4 + hf][:, 0:512],
                                                                   in1=gt[:, hf * 512:(hf + 1) * 512], op=ALU.mult),
                      reads=[("ps", 4 + hf), ("gt", br, hf)], writes=[("gt", br, hf)])
        for hf in range(2):
            P.add("dve", lambda e, hf=hf, cb=cb: e.tensor_tensor(out=mixT3[:, cb, hf * 512:(hf + 1) * 512], in0=ftmp[:, hf * 512:(hf + 1) * 512],
                                                               in1=ftmp2[:, hf * 512:(hf + 1) * 512], op=ALU.add),
                  reads=[("gt", 0, hf), ("gt", 1, hf)], writes=[("mixT", cb, hf)])
    mix_all = [("mixT", cb, hf) for cb in range(16) for hf in range(2)]
    if debug:
        P.add("sp", lambda e: e.dma_start(out=dbg["mixT"], in_=mixT), reads=mix_all, writes=["dbg_dmx"], slot="dbg9")
    P.add("act", lambda e: e.activation(out=sm[:, 48:80], in_=ps[6][:, 48:80], func=AF.Sigmoid),
          reads=[("ps", 6)], writes=["sm_g"])
    P.add("dve", lambda e: e.tensor_tensor(out=sm[:, 80:112], in0=ps[7][:, 0:32], in1=sm[:, 48:80], op=ALU.mult),
          reads=[("ps", 7), "sm_g"], writes=["sm_y"])
    P.add("dve", lambda e: e.tensor_tensor(out=mixTs, in0=sm[:, 80:96], in1=sm[:, 96:112], op=ALU.add), reads=["sm_y"], writes=["mixTs"])

    if upto <= 4:
        return finish()
    P.barrier()
    wo = [buf(20 * KB + gidx * 16 * KB, 16 * KB) for gidx in range(4)]
    xbuf = [buf(84 * KB + i * 8 * KB, 8 * KB, F32) for i in range(2)]
    xo = [buf(100 * KB + i * 8 * KB, 8 * KB, F32) for i in range(2)]
    junk = buf(116 * KB, 4 * KB)
    xs2 = buf(120 * KB, 8 * KB, F32, parts=1)
    xos = buf(128 * KB, 8 * KB, F32, parts=1)
    P.add("sp", lambda e: e.dma_start(out=gbc, in_=g_f_d.partition_broadcast(128)), writes=["gbc"], slot="c_gbc")
    for gi in range(4):
        w3 = v3(wo[gi], 16)
        for q4 in range(4):
            P.add("pool", lambda e, gi=gi, q4=q4, w3=w3: e.dma_start(out=w3[:, q4 * 4:(q4 + 1) * 4, :],
                                                                   in_=v3(wout_d[gi], 16)[:, q4 * 4:(q4 + 1) * 4, :]),
                  writes=[("wo", gi)], slot="wo%d_%d" % (gi, q4))

    def final_block(src_ap, xb, xres, xslot, xo_t, lhs_fn, lhs_reads, banks, out_ap, sidx, nparts, oslot):
        P.add("sp", lambda e: e.dma_start(out=xb, in_=src_ap), writes=[xres], slot=xslot)
        for gi in range(4):
            w3 = v3(wo[gi], 16)
            for c in range(16):
                P.add("pe", lambda e, gi=gi, c=c, w3=w3: e.matmul(ps[banks[gi]][0:nparts, 0:512], lhsT=lhs_fn(c), rhs=w3[:, c, :],
                                                                start=(c == 0), stop=(c == 15)),
                      reads=[("wo", gi)] + lhs_reads, writes=[("ps", banks[gi])])
            P.add("dve", lambda e, gi=gi: e.tensor_tensor(out=xo_t[:, gi * 512:(gi + 1) * 512], in0=ps[banks[gi]][0:nparts, 0:512],
                                                         in1=xb[:, gi * 512:(gi + 1) * 512], op=ALU.add),
                  reads=[("ps", banks[gi]), xres], writes=[(xres, "xo", gi)])
        ss = stat[0:nparts, sidx:sidx + 1]
        sd = stat[0:nparts, sidx + 1:sidx + 2]
        rs = stat[0:nparts, sidx + 2:sidx + 3]
        xor = [(xres, "xo", gi) for gi in range(4)]
        P.add("act", lambda e: e.activation(out=junk[0:nparts, :], in_=xo_t, func=AF.Square, accum_out=ss), reads=xor, writes=["junk", ("st", sidx)])
        P.add("act", lambda e: e.activation(out=sd, in_=ss, func=AF.Sqrt, scale=1.0 / D, bias=epsc[0:nparts, 0:1]),
              reads=[("st", sidx), "epsc"], writes=[("st", sidx + 1)])
        P.add("dve", lambda e: e.reciprocal(out=rs, in_=sd), reads=[("st", sidx + 1)], writes=[("st", sidx + 2)])
        P.add("dve", lambda e: e.scalar_tensor_tensor(out=xb, in0=xo_t, scalar=rs, in1=gbc[0:nparts, :], op0=ALU.mult, op1=ALU.mult),
              reads=xor + [("st", sidx + 2), "gbc"], writes=[xres])
        P.add("pool", lambda e: e.dma_start(out=out_ap, in_=xb), reads=[xres], writes=[("o_y", oslot)], slot="o_y%s" % oslot)

    for i in range(OWN):
        pb = i % 2
        tok = slice(i * 128, (i + 1) * 128)
        banks = [0, 1, 2, 3] if pb == 0 else [4, 5, 6, 7]
        final_block(x_own[i * 128:(i + 1) * 128, :], xbuf[pb], ("xbuf", pb), "xbuf%d" % pb, xo[pb],
                    lambda c, tok=tok: mixT3[:, c, tok], mix_all, banks, y_own[i * 128:(i + 1) * 128, :], 4 * pb, 128, pb)
    final_block(x_s, xs2, "xs2", "xs2", xos, lambda c: mixTs[:, c:c + 1], ["mixTs"], [0, 1, 2, 3], ys_o, 8, 1, "s")

    return finish()


def own_blocks(core):
    j = core % 4
    return sorted([j, 7 - j, 8 + j, 15 - j, 16 + j, 23 - j, 24 + j, 31 - j])


def prep_shared(inputs):
    w_in = np.asarray(inputs["w_in"])[0]

    def chunked(cols):
        n = cols.shape[1]
        return np.ascontiguousarray(cols.reshape(16, 128, n).transpose(1, 0, 2))

    sh = {}
    kvki = np.concatenate([w_in[:, C_K:C_K + 256], w_in[:, C_V:C_V + 256], w_in[:, C_KI:C_KI + 128]], axis=1)
    sh["wkvki"] = chunked(kvki).reshape(128, -1)
    cbs = []
    for base, n in ((C_Q, 8), (C_QI, 16), (C_ZA, 8), (C_U, 8), (C_ZB, 8)):
        for j in range(n):
            cbs.append(w_in[:, base + j * 128: base + (j + 1) * 128])
    for j in range(16):
        cbs.append(w_in[:, C_GA + j * 128:C_GA + (j + 1) * 128])
        cbs.append(w_in[:, C_GB + j * 128:C_GB + (j + 1) * 128])
    sh["wfm"] = np.stack([chunked(c).reshape(128, -1) for c in cbs])
    sh["wwi"] = chunked(w_in[:, C_WI:C_WI + 16]).reshape(128, -1)
    sh["wvb"] = chunked(w_in[:, C_VB:C_VB + 1024]).reshape(128, -1)
    wpa = np.asarray(inputs["w_proj_a"])[0]
    wpb = np.asarray(inputs["w_proj_b"])[0]

    def chunk8(cols):
        return np.ascontiguousarray(cols.reshape(8, 128, 128).transpose(1, 0, 2)).reshape(128, -1)

    sh["wpa"] = np.stack([chunk8(wpa[:, j * 128:(j + 1) * 128]) for j in range(16)])
    sh["wpb"] = np.stack([chunk8(wpb[:, j * 128:(j + 1) * 128]) for j in range(16)])
    wout = np.asarray(inputs["w_out"])[0]
    sh["wout"] = np.stack([chunked(wout[:, j * 512:(j + 1) * 512]).reshape(128, -1) for j in range(4)])
    ws = np.asarray(inputs["w_spatial"])[0]
    sh["wspT"] = np.ascontiguousarray(ws.transpose(2, 0, 1)).reshape(128, -1)
    bsp = np.asarray(inputs["b_spatial"])[0]
    sh["bsp"] = np.ascontiguousarray(bsp.reshape(1, 1024))
    sh["ws00"] = np.ascontiguousarray(np.repeat(ws[:, 0, 0], 128).reshape(1, 1024))
    sh["bs0"] = np.ascontiguousarray(np.repeat(bsp[:, 0], 128).reshape(1, 1024))
    sh["pool_ki8"] = np.ascontiguousarray(np.asarray(inputs["cache_k_idx"])[0].reshape(1280 * 8, 2048))
    sh["pool_k16"] = np.ascontiguousarray(np.asarray(inputs["cache_k"])[0].reshape(1280 * 16, 2048))
    sh["pool_v16"] = np.ascontiguousarray(np.asarray(inputs["cache_v"])[0].reshape(1280 * 16, 2048))
    sh["g_in"] = np.ascontiguousarray(np.asarray(inputs["norm_in_g"]).reshape(1, D))
    sh["g_f"] = np.ascontiguousarray(np.asarray(inputs["norm_f_g"]).reshape(1, D))
    sh["ln_g"] = np.ascontiguousarray(np.asarray(inputs["ln_g"]).reshape(1, 1024))
    sh["ln_b"] = np.ascontiguousarray(np.asarray(inputs["ln_b"]).reshape(1, 1024))
    return {k: np.ascontiguousarray(v, dtype=v.dtype) for k, v in sh.items()}


def make_in_maps(inputs):
    sh = prep_shared(inputs)
    xp = np.asarray(inputs["x_prompt"])
    xs = np.asarray(inputs["x_sample"])
    pt = np.asarray(inputs["page_table"]).astype(np.int32)
    maps = []
    for c in range(NCORES):
        b = c // 4
        ob = own_blocks(c)
        m = dict(sh)
        m["x_all"] = np.ascontiguousarray(xp[b])
        m["x_own"] = np.ascontiguousarray(np.concatenate([xp[b, blk * 128:(blk + 1) * 128] for blk in ob], axis=0))
        t = np.arange(128, dtype=np.float32)[:, None]
        m["qrel"] = np.ascontiguousarray(
            np.concatenate([(ob[i] * 128 - 512 * i) + t for i in range(OWN)], axis=1).astype(np.float32))
        m["x_s"] = np.ascontiguousarray(xs[c].reshape(1, D))
        m["ptab"] = np.ascontiguousarray(pt[c].reshape(1, 128))
        maps.append(m)
    return maps


_CACHE = {}


def kernel(**inputs):
    if "nc" not in _CACHE:
        _CACHE["nc"] = build_program(debug=False)[0]
    nc = _CACHE["nc"]
    maps = make_in_maps(inputs)
    res = run_bass_kernel_spmd(nc, maps, core_ids=list(range(NCORES)))
    r = res.results
    y_prompt = np.zeros((2, SEQ, D), np.float32)
    for c in range(NCORES):
        b = c // 4
        for i, blk in enumerate(own_blocks(c)):
            y_prompt[b, blk * 128:(blk + 1) * 128] = r[c]["y_own"][i * 128:(i + 1) * 128]
    y_sample = np.stack([r[c]["ys"].reshape(1, D) for c in range(NCORES)]).astype(np.float32)
    nk = np.stack([r[4 * b]["knew"].reshape(SEQ, 2, 128) for b in range(2)])[None].astype(np.float32)
    nv = np.stack([r[4 * b]["vnew"].reshape(SEQ, 2, 128) for b in range(2)])[None].astype(np.float32)
    nki = np.stack([r[4 * b]["kinew"].reshape(SEQ, 128) for b in range(2)])[None].astype(np.float32)
    ks = np.stack([r[c]["ks"].reshape(1, 2, 128) for c in range(NCORES)])[None].astype(np.float32)
    vs = np.stack([r[c]["vs"].reshape(1, 2, 128) for c in range(NCORES)])[None].astype(np.float32)
    kis = np.stack([r[c]["kis"].reshape(1, 128) for c in range(NCORES)])[None].astype(np.float32)
    gvs = np.stack([r[c]["gvs"].reshape(1, 1024) for c in range(NCORES)])[None].astype(np.float32)
    return (y_prompt, y_sample, nk, nv, nki, ks, vs, kis, gvs)
```

```python
import numpy as np
from contextlib import ExitStack
import concourse.bass as bass
import concourse.mybir as mybir
from concourse.bass_utils import run_bass_kernel_spmd

F32 = mybir.dt.float32
BF16 = mybir.dt.bfloat16
I32 = mybir.dt.int32
AF = mybir.ActivationFunctionType
ALU = mybir.AluOpType
AX = mybir.AxisListType

NCORES = 8
D = 2048
NCH = 16
SEQ = 4096
NB = 32
OWN = 8
TOK = 1024
BIG = 30000.0
CBIG = 1.0e6
NIT = 16
TOPK = 256
COMPUTE = ("pe", "act", "dve", "pool")

C_Q, C_K, C_V, C_QI, C_WI, C_KI, C_ZA, C_U, C_VB, C_ZB, C_GA, C_GB = (
    0, 1024, 1280, 1536, 3584, 3600, 3728, 4752, 5776, 6800, 7824, 9872)
FM_Q, FM_QI, FM_ZA, FM_U, FM_ZB, FM_G = 0, 8, 24, 32, 40, 48
N_FM = 80


class _Op:
    __slots__ = ("eng", "fn", "deps", "is_dma", "slot", "marked", "mark_idx", "cum", "idx")


class Prog:
    def __init__(self, nc, stack):
        self.nc = nc
        self.stack = stack
        self.ops = []
        self.last_w = {}
        self.readers = {}
        self.slot_cum = {}
        self.bar = []
        self.psx = {}
        self.engs = {"pe": nc.tensor, "act": nc.scalar, "dve": nc.vector, "pool": nc.gpsimd, "sp": nc.sync}

    def add(self, eng, fn, reads=(), writes=(), slot=None):
        op = _Op()
        op.eng = eng
        op.fn = fn
        op.is_dma = slot is not None
        op.slot = slot
        op.marked = False
        op.mark_idx = None
        op.idx = len(self.ops)
        op.deps = set()
        if op.is_dma:
            self.slot_cum[slot] = self.slot_cum.get(slot, 0) + 16
            op.cum = self.slot_cum[slot]
        for y in self.bar:
            self._dep(op, y, "raw")
        for r in reads:
            lw = self.last_w.get(r)
            if lw is not None:
                self._dep(op, lw, "raw")
        for w in writes:
            lw = self.last_w.get(w)
            if lw is not None:
                self._dep(op, lw, "waw")
            for rd in self.readers.get(w, ()):
                self._dep(op, rd, "war")
        for r in reads:
            self.readers.setdefault(r, []).append(op)
        for w in writes:
            self.last_w[w] = op
            self.readers[w] = []
        for res in list(reads) + list(writes):
            if isinstance(res, tuple) and len(res) >= 2 and res[0] == "ps":
                lastx = self.psx.get(res[1])
                if lastx is not None and lastx is not op and lastx.eng != op.eng:
                    op.deps.add(lastx.idx)
                    if not lastx.is_dma:
                        lastx.marked = True
                self.psx[res[1]] = op
        self.ops.append(op)
        return op

    def _dep(self, x, y, kind):
        if y is x:
            return
        if not y.is_dma and not x.is_dma and y.eng == x.eng:
            if kind != "raw" or x.eng == "pe":
                return
        x.deps.add(y.idx)
        if not y.is_dma:
            y.marked = True

    def barrier(self):
        last = {}
        for op in self.ops:
            if op.fn is None:
                continue
            key = ("slot", op.slot) if op.is_dma else ("eng", op.eng)
            last[key] = op
        self.bar = list(last.values())

    def emit(self):
        nc = self.nc
        sems = {}
        for e in COMPUTE:
            sems[("eng", e)] = self.stack.enter_context(nc.semaphore("c_" + e))
        for i, s in enumerate(self.slot_cum):
            sems[("slot", s)] = self.stack.enter_context(nc.semaphore("d%d" % i))
        cnt = {e: 0 for e in COMPUTE}
        for op in self.ops:
            if op.marked:
                cnt[op.eng] += 1
                op.mark_idx = cnt[op.eng]
        waited = {}
        nw = 0
        for op in self.ops:
            eng = self.engs[op.eng]
            need = {}
            for di in op.deps:
                y = self.ops[di]
                if y.is_dma:
                    k, v = ("slot", y.slot), y.cum
                else:
                    k, v = ("eng", y.eng), y.mark_idx
                if need.get(k, 0) < v:
                    need[k] = v
            wd = waited.setdefault(op.eng, {})
            for k, v in need.items():
                if wd.get(k, 0) >= v:
                    continue
                eng.wait_ge(sems[k], v)
                wd[k] = v
                nw += 1
            if op.fn is None:
                continue
            ins = op.fn(eng)
            if op.is_dma:
                ins.then_inc(sems[("slot", op.slot)], 16)
            elif op.marked:
                ins.then_inc(sems[("eng", op.eng)], 1)
        return dict(nops=len(self.ops), nwaits=nw, marks=cnt, nsems=len(sems))


def build_program(debug=False, upto=99):
    nc = bass.Bass("TRN2", target_bir_lowering=False)

    def din(name, shape, dtype=F32):
        return nc.dram_tensor(name, list(shape), dtype, kind="ExternalInput").ap()

    def dout(name, shape, dtype=F32):
        return nc.dram_tensor(name, list(shape), dtype, kind="ExternalOutput").ap()

    x_all = din("x_all", [SEQ, D])
    x_own = din("x_own", [TOK, D])
    qrel_d = din("qrel", [128, OWN])
    x_s = din("x_s", [1, D])
    ptab = din("ptab", [1, 128], I32)
    pool_ki8 = din("pool_ki8", [1280 * 8, 2048])
    pool_k16 = din("pool_k16", [1280 * 16, 2048])
    pool_v16 = din("pool_v16", [1280 * 16, 2048])
    g_in_d = din("g_in", [1, D])
    g_f_d = din("g_f", [1, D])
    ln_g_d = din("ln_g", [1, 1024])
    ln_b_d = din("ln_b", [1, 1024])
    bsp_d = din("bsp", [1, 1024])
    ws00_d = din("ws00", [1, 1024])
    bs0_d = din("bs0", [1, 1024])
    wspT_d = din("wspT", [128, 8 * 128])
    wkvki_d = din("wkvki", [128, NCH * 640])
    wfm_d = din("wfm", [N_FM, 128, NCH * 128])
    wwi_d = din("wwi", [128, NCH * 16])
    wvb_d = din("wvb", [128, NCH * 1024])
    wpa_d = din("wpa", [16, 128, 8 * 128])
    wpb_d = din("wpb", [16, 128, 8 * 128])
    wout_d = din("wout", [4, 128, NCH * 512])

    y_own = dout("y_own", [TOK, D])
    knew = dout("knew", [SEQ, 256])
    vnew = dout("vnew", [SEQ, 256])
    kinew = dout("kinew", [SEQ, 128])
    ys_o = dout("ys", [1, D])
    ks_o = dout("ks", [1, 256])
    vs_o = dout("vs", [1, 256])
    kis_o = dout("kis", [1, 128])
    gvs_o = dout("gvs", [1, 1024])
    dbg = {}
    if debug:
        dbg["qT"] = dout("dbg_qT", [128, 8 * TOK], BF16)
        dbg["kT"] = dout("dbg_kT", [128, 2 * SEQ], BF16)
        dbg["kiT"] = dout("dbg_kiT", [128, SEQ], BF16)
        dbg["qiT"] = dout("dbg_qiT", [128, 16 * TOK], BF16)
        dbg["wabs"] = dout("dbg_wabs", [128, 128])
        dbg["acc"] = dout("dbg_acc", [128, OWN * 4096])
        dbg["thr"] = dout("dbg_thr", [128, OWN])
        dbg["oaT"] = dout("dbg_oaT", [128, 8 * TOK], BF16)
        dbg["obT"] = dout("dbg_obT", [128, 8 * TOK], BF16)
        dbg["mixT"] = dout("dbg_mixT", [128, 16 * TOK], BF16)

    st = ExitStack()
    P = Prog(nc, st)
    ARENA_B = 207 * 1024
    arena = st.enter_context(nc.sbuf_tensor("arena", [128, ARENA_B // 2], BF16))
    psall = st.enter_context(nc.psum_tensor("psall", [128, 4096], F32))
    ps = [psall[:, i * 512:(i + 1) * 512] for i in range(8)]
    psb = [p.bitcast(BF16) for p in ps]


    def finish():
        fin = P.add("sp", None)
        lastd = {}
        for op in P.ops:
            if op.is_dma:
                lastd[op.slot] = op
        for op in lastd.values():
            fin.deps.add(op.idx)
        info = P.emit()
        st.close()
        return nc, info

    KB = 1024

    def buf(off_b, nbytes, dtype=BF16, parts=128):
        if off_b >= 20 * KB:
            off_b += KB
        assert off_b + nbytes <= ARENA_B, (off_b, nbytes)
        a = arena[0:parts, off_b // 2:(off_b + nbytes) // 2]
        if dtype != BF16:
            a = a.bitcast(dtype)
        return a

    def v3(ap, a):
        return ap.rearrange("p (a b) -> p a b", a=a)

    o = 0
    ident = buf(o, 256); o += 256
    identB4 = buf(o, 1024); o += 1024
    ones_bf = buf(o, 256); o += 256
    identf = buf(o, 512, F32); o += 512
    iota_f = buf(o, 2048, F32); o += 2048
    pow2 = buf(o, 128, F32); o += 128
    qrel = buf(o, 32, F32); o += 32
    epsc = buf(o, 16, F32); o += 16
    stat = buf(o, 256, F32); o += 256
    wabs = buf(o, 512, F32); o += 512
    wsgn = buf(o, 512, F32); o += 512
    sm = buf(o, 2048, F32); o += 2048
    iota_i = sm.bitcast(I32)
    smb = buf(o, 1024, BF16); o += 1024
    WsT = buf(o, 2048); o += 2048
    gbc = buf(o, 8192, F32); o += 8192
    ksv = buf(o, 2560, F32, parts=1); o += 2560
    assert o <= 21 * KB, o
    hsT = smb[:, 0:16]
    oazTs = smb[:, 16:24]
    obzTs = smb[:, 24:32]
    mixTs = smb[:, 32:48]

    kT_all = buf(20 * KB, 16 * KB)
    kiT_all = buf(36 * KB, 8 * KB)
    v_all = buf(44 * KB, 16 * KB)
    qT = buf(60 * KB, 16 * KB)
    qiT = buf(76 * KB, 32 * KB)
    hT_own = buf(108 * KB, 32 * KB)
    oaT = buf(140 * KB, 16 * KB)
    mixT = buf(156 * KB, 32 * KB)
    kT3 = v3(kT_all, 2)
    v3a = v3(v_all, 32)
    qT3 = v3(qT, 8)
    qiT3 = v3(qiT, 16)
    hT3 = v3(hT_own, 16)
    oaT3 = v3(oaT, 8)
    mixT3 = v3(mixT, 16)

    P.add("pool", lambda e: e.memset(identf, 1.0), writes=["identf"])
    P.add("pool", lambda e: e.affine_select(out=identf, in_=identf, pattern=[[-1, 128]], compare_op=ALU.is_equal,
                                            fill=0.0, base=0, channel_multiplier=1), reads=["identf"], writes=["identf"])
    P.add("dve", lambda e: e.tensor_copy(out=ident, in_=identf), reads=["identf"], writes=["ident"])
    for j in range(4):
        P.add("dve", lambda e, j=j: e.tensor_scalar(out=identB4[:, j * 128:(j + 1) * 128], in0=identf, scalar1=BIG, scalar2=None,
                                                    op0=ALU.mult), reads=["identf"], writes=["identB4"])
    P.add("pool", lambda e: e.memset(ones_bf, 1.0), writes=["ones"])
    P.add("pool", lambda e: e.iota(iota_i, pattern=[[1, 512]], base=0, channel_multiplier=0), writes=["iota_i"])
    P.add("dve", lambda e: e.tensor_copy(out=iota_f, in_=iota_i), reads=["iota_i"], writes=["iota"])
    for k in range(NIT + 1):
        P.add("pool", lambda e, k=k: e.memset(pow2[:, k:k + 1], float(2.0 ** -(k + 1))), writes=["pow2"])
    P.add("pool", lambda e: e.memset(epsc[:, 0:1], 1e-6), writes=["epsc"])
    P.add("pool", lambda e: e.memset(epsc[:, 1:2], 1e-5), writes=["epsc"])
    P.add("sp", lambda e: e.dma_start(out=qrel, in_=qrel_d), writes=["qrel"], slot="c_qrel")
    P.add("sp", lambda e: e.dma_start(out=gbc, in_=g_in_d.partition_broadcast(128)), writes=["gbc"], slot="c_gbc")

    def rms_block(src_ap, xb, xres, xslot, hb, hres, junk, sidx, gb_ap, nparts=128):
        ss = stat[0:nparts, sidx:sidx + 1]
        sd = stat[0:nparts, sidx + 1:sidx + 2]
        rs = stat[0:nparts, sidx + 2:sidx + 3]
        P.add("sp", lambda e: e.dma_start(out=xb, in_=src_ap), writes=[xres], slot=xslot)
        P.add("act", lambda e: e.activation(out=junk, in_=xb, func=AF.Square, accum_out=ss),
              reads=[xres], writes=["junk", ("st", sidx)])
        P.add("act", lambda e: e.activation(out=sd, in_=ss, func=AF.Sqrt, scale=1.0 / D, bias=epsc[0:nparts, 0:1]),
              reads=[("st", sidx), "epsc"], writes=[("st", sidx + 1)])
        P.add("dve", lambda e: e.reciprocal(out=rs, in_=sd), reads=[("st", sidx + 1)], writes=[("st", sidx + 2)])
        P.add("dve", lambda e: e.scalar_tensor_tensor(out=hb, in0=xb, scalar=rs, in1=gb_ap, op0=ALU.mult, op1=ALU.mult),
              reads=[xres, ("st", sidx + 2), "gbc"], writes=[hres])

    def transposes16(hb, hres, bankA, bankB):
        for c in range(16):
            bk = bankA if c < 8 else bankB
            cc = c % 8
            P.add("pe", lambda e, c=c, bk=bk, cc=cc: e.transpose(out=psb[bk][:, cc * 128:(cc + 1) * 128],
                                                                 in_=hb[:, c * 128:(c + 1) * 128], identity=ident),
                  reads=[hres, "ident"], writes=[("ps", bk)])

    if upto <= 0:
        return finish()
    S1 = 60 * KB
    xbuf = [buf(S1 + i * 8 * KB, 8 * KB, F32) for i in range(2)]
    hbuf = [buf(S1 + 16 * KB + i * 4 * KB, 4 * KB) for i in range(2)]
    junk = buf(S1 + 24 * KB, 4 * KB)
    hTblk = [buf(S1 + 28 * KB + i * 4 * KB, 4 * KB) for i in range(2)]
    Wkvki = buf(S1 + 36 * KB, 20 * KB)
    kvf = [buf(S1 + 56 * KB + i * 2560, 2560, F32) for i in range(2)]
    kb16 = [buf(S1 + 62 * KB + i * 768, 768) for i in range(2)]
    Wk3 = v3(Wkvki, 16)
    for q4 in range(8):
        P.add("pool", lambda e, q4=q4: e.dma_start(out=Wk3[:, q4 * 2:(q4 + 1) * 2, :],
                                                  in_=v3(wkvki_d, 16)[:, q4 * 2:(q4 + 1) * 2, :]),
              writes=["Wkvki"], slot="wkvki%d" % (q4 % 4))
    xs_f = buf(S1 + 64 * KB, 8 * KB, F32, parts=1)
    hs_f = buf(S1 + 72 * KB, 8 * KB, F32, parts=1)
    kvs = buf(S1 + 80 * KB, 2560, F32, parts=1)
    import os as _os
    if _os.environ.get('SKIP_S1'):
        P.add('pool', lambda e: e.memset(hsT, 0.0), writes=['hsT'])
    else:
        rms_block(x_s, xs_f, "xs_f", "xs_f", hs_f, "hs_f", junk[0:1, :], 8, gbc[0:1, :], nparts=1)
        for c in range(16):
            P.add("pe", lambda e, c=c: e.transpose(out=ps[4][:, c:c + 1], in_=hs_f[0:1, c * 128:(c + 1) * 128], identity=identf[0:1, 0:1]),
                  reads=["hs_f", "identf"], writes=[("ps", 4)])
        P.add("act", lambda e: e.copy(out=hsT, in_=ps[4][:, 0:16]), reads=[("ps", 4)], writes=["hsT"])
        for c in range(16):
            P.add("pe", lambda e, c=c: e.matmul(ps[6][0:1, 0:512], lhsT=hsT[:, c:c + 1], rhs=Wk3[:, c, 0:512], start=(c == 0), stop=(c == 15)),
                  reads=["hsT", "Wkvki"], writes=[("ps", 6)])
            P.add("pe", lambda e, c=c: e.matmul(ps[7][0:1, 0:128], lhsT=hsT[:, c:c + 1], rhs=Wk3[:, c, 512:640], start=(c == 0), stop=(c == 15)),
                  reads=["hsT", "Wkvki"], writes=[("ps", 7)])
        P.add("dve", lambda e: e.tensor_copy(out=kvs[:, 0:512], in_=ps[6][0:1, 0:512]), reads=[("ps", 6)], writes=["kvs0"])
        P.add("dve", lambda e: e.tensor_copy(out=kvs[:, 512:640], in_=ps[7][0:1, 0:128]), reads=[("ps", 7)], writes=["kvs1"])
        P.add("dve", lambda e: e.tensor_copy(out=ksv, in_=kvs), reads=["kvs0", "kvs1"], writes=["ksv"])
        P.add("sp", lambda e: e.dma_start(out=ks_o, in_=kvs[:, 0:256]), reads=["kvs0"], writes=["o_ks"], slot="o_ks")
        P.add("sp", lambda e: e.dma_start(out=vs_o, in_=kvs[:, 256:512]), reads=["kvs0"], writes=["o_vs"], slot="o_vs")
        P.add("sp", lambda e: e.dma_start(out=kis_o, in_=kvs[:, 512:640]), reads=["kvs1"], writes=["o_kis"], slot="o_kis")
    def s1_front(b):
        pb = b % 2
        rms_block(x_all[b * 128:(b + 1) * 128, :], xbuf[pb], ("xbuf", pb), "xbuf%d" % pb, hbuf[pb], ("hbuf", pb), junk, 4 * pb, gbc)

    def s1_t16(b, xbuf=xbuf, hbuf=hbuf, hTblk=hTblk):
        pb = b % 2
        hb, htb = hbuf[pb], hTblk[pb]
        bA, bB = (0, 1) if pb == 0 else (2, 3)
        transposes16(hb, ("hbuf", pb), bA, bB)
        P.add("act", lambda e: e.copy(out=htb[:, 0:1024], in_=psb[bA][:, 0:1024]), reads=[("ps", bA)], writes=[("htb", pb, 0)])
        P.add("dve", lambda e: e.tensor_copy(out=htb[:, 1024:2048], in_=psb[bB][:, 0:1024]), reads=[("ps", bB)], writes=[("htb", pb, 1)])

    def s1_mm(b, hTblk=hTblk, kvf=kvf, kb16=kb16):
        pb = b % 2
        htb3 = v3(hTblk[pb], 16)
        bKV, bKI = (4, 5) if pb == 0 else (6, 7)
        for c in range(16):
            P.add("pe", lambda e, c=c: e.matmul(ps[bKV][:, 0:512], lhsT=htb3[:, c, :], rhs=Wk3[:, c, 0:512], start=(c == 0), stop=(c == 15)),
                  reads=[("htb", pb, c // 8), "Wkvki"], writes=[("ps", bKV)])
            P.add("pe", lambda e, c=c: e.matmul(ps[bKI][:, 0:128], lhsT=htb3[:, c, :], rhs=Wk3[:, c, 512:640], start=(c == 0), stop=(c == 15)),
                  reads=[("htb", pb, c // 8), "Wkvki"], writes=[("ps", bKI)])
        kf = kvf[pb]
        k16 = kb16[pb]
        P.add("dve", lambda e: e.tensor_copy(out=kf[:, 0:512], in_=ps[bKV][:, 0:512]), reads=[("ps", bKV)], writes=[("kvf", pb, 0)])
        P.add("act", lambda e: e.copy(out=kf[:, 512:640], in_=ps[bKI][:, 0:128]), reads=[("ps", bKI)], writes=[("kvf", pb, 1)])
        P.add("dve", lambda e: e.tensor_copy(out=k16[:, 0:256], in_=ps[bKV][:, 0:256]), reads=[("ps", bKV)], writes=[("kb16", pb)])
        P.add("act", lambda e: e.copy(out=k16[:, 256:384], in_=ps[bKI][:, 0:128]), reads=[("ps", bKI)], writes=[("kb16", pb)])
        P.add("dve", lambda e: e.tensor_copy(out=v3a[:, b, :], in_=ps[bKV][:, 256:512]), reads=[("ps", bKV)], writes=[("v_all", b)])
        rows = slice(b * 128, (b + 1) * 128)
        P.add("pool", lambda e: e.dma_start(out=knew[rows, :], in_=kf[:, 0:256]), reads=[("kvf", pb, 0)], writes=[("o_k", pb)], slot="o_k%d" % pb)
        P.add("pool", lambda e: e.dma_start(out=vnew[rows, :], in_=kf[:, 256:512]), reads=[("kvf", pb, 0)], writes=[("o_v", pb)], slot="o_v%d" % pb)
        P.add("pool", lambda e: e.dma_start(out=kinew[rows, :], in_=kf[:, 512:640]), reads=[("kvf", pb, 1)], writes=[("o_ki", pb)], slot="o_ki%d" % pb)

    def s1_tk(b, kb16=kb16):
        pb = b % 2
        k16 = kb16[pb]
        bKI = 5 if pb == 0 else 7
        for j in range(3):
            P.add("pe", lambda e, j=j: e.transpose(out=psb[bKI][:, 256 + j * 128:256 + (j + 1) * 128], in_=k16[:, j * 128:(j + 1) * 128], identity=ident),
                  reads=[("kb16", pb), "ident"], writes=[("ps", bKI)])
        P.add("act", lambda e: e.copy(out=kT3[:, :, b * 128:(b + 1) * 128], in_=v3(psb[bKI][:, 256:512], 2)), reads=[("ps", bKI)], writes=[("kT", b)])
        P.add("act", lambda e: e.copy(out=kiT_all[:, b * 128:(b + 1) * 128], in_=psb[bKI][:, 512:640]), reads=[("ps", bKI)], writes=[("kiT", b)])

    s1_front(0)
    s1_front(1)
    s1_t16(0)
    s1_t16(1)
    for b in range(NB):
        if b + 2 < NB:
            s1_front(b + 2)
        s1_mm(b)
        if b + 2 < NB:
            s1_t16(b + 2)
        s1_tk(b)

    if upto <= 1:
        return finish()
    P.barrier()
    S2 = 140 * KB
    xbuf = [buf(S2 + i * 8 * KB, 8 * KB, F32) for i in range(2)]
    hbuf = [buf(S2 + 16 * KB + i * 4 * KB, 4 * KB) for i in range(2)]
    junk = buf(S2 + 24 * KB, 4 * KB)
    wbuf = [buf(S2 + 28 * KB + i * 4 * KB, 4 * KB) for i in range(3)]
    wwi = buf(S2 + 40 * KB, 512)

    def own_pass(xbuf, hbuf, junk):
        def front(i):
            pb = i % 2
            rms_block(x_own[i * 128:(i + 1) * 128, :], xbuf[pb], ("xbuf", pb), "xbuf%d" % pb, hbuf[pb], ("hbuf", pb), junk, 4 * pb, gbc)

        def back(i):
            pb = i % 2
            hb = hbuf[pb]
            bA, bB = (0, 1) if pb == 0 else (2, 3)
            transposes16(hb, ("hbuf", pb), bA, bB)
            P.add("act", lambda e, bA=bA, i=i: e.copy(out=hT3[:, 0:8, i * 128:(i + 1) * 128], in_=v3(psb[bA][:, 0:1024], 8)),
                  reads=[("ps", bA)], writes=[("hT", i, 0)])
            P.add("dve", lambda e, bB=bB, i=i: e.tensor_copy(out=hT3[:, 8:16, i * 128:(i + 1) * 128], in_=v3(psb[bB][:, 0:1024], 8)),
                  reads=[("ps", bB)], writes=[("hT", i, 1)])

        front(0)
        for i in range(OWN):
            if i + 1 < OWN:
                front(i + 1)
            back(i)

    own_pass(xbuf, hbuf, junk)

    hT_reads = [("hT", i, h) for i in range(OWN) for h in range(2)]
    fm_state = {"n": 0}

    def proj_fm(cb, scol, evac, pa=None):
        n = fm_state["n"]
        fm_state["n"] += 1
        ws = n % 3
        wsl = wbuf[ws]
        w3 = v3(wsl, 16)
        P.add("pool", lambda e: e.dma_start(out=wsl, in_=wfm_d[cb]), writes=[("wbuf", ws)], slot="wbuf%d" % ws)
        if pa is None:
            pa = 2 * (n % 3)
        for c in range(16):
            for hf in range(2):
                P.add("pe", lambda e, c=c, hf=hf: e.matmul(ps[pa + hf][:, 0:512], lhsT=w3[:, c, :], rhs=hT3[:, c, hf * 512:(hf + 1) * 512],
                                                          start=(c == 0), stop=(c == 15)),
                      reads=[("wbuf", ws)] + hT_reads, writes=[("ps", pa + hf)])
            P.add("pe", lambda e, c=c: e.matmul(ps[6][:, scol:scol + 1], lhsT=w3[:, c, :], rhs=hsT[:, c:c + 1],
                                                start=(c == 0), stop=(c == 15)),
                  reads=[("wbuf", ws), "hsT"], writes=[("ps", 6)])
        evac(pa)

    def evac_copy(dst3, h):
        def f(pa):
            P.add("act", lambda e: e.copy(out=dst3[:, h, 0:512], in_=ps[pa][:, 0:512]), reads=[("ps", pa)], writes=[(id(dst3), h, 0)])
            P.add("dve", lambda e: e.tensor_copy(out=dst3[:, h, 512:1024], in_=ps[pa + 1][:, 0:512]), reads=[("ps", pa + 1)],
                  writes=[(id(dst3), h, 1)])
        return f

    for h in range(8):
        proj_fm(FM_Q + h, h, evac_copy(qT3, h))
    for h in range(16):
        proj_fm(FM_QI + h, 8 + h, evac_copy(qiT3, h))
    P.add("dve", lambda e: e.tensor_copy(out=sm[:, 0:24], in_=ps[6][:, 0:24]), reads=[("ps", 6)], writes=["sm_q"])
    P.add("pool", lambda e: e.dma_start(out=wwi, in_=wwi_d), writes=["wwi"], slot="wwi")
    wwi3 = v3(wwi, 16)
    for i in range(OWN):
        for c in range(16):
            P.add("pe", lambda e, i=i, c=c: e.matmul(ps[7][:, i * 16:(i + 1) * 16], lhsT=hT3[:, c, i * 128:(i + 1) * 128], rhs=wwi3[:, c, :],
                                                    start=(c == 0), stop=(c == 15)),
                  reads=["wwi"] + hT_reads, writes=[("ps", 7)])
    P.add("act", lambda e: e.copy(out=wabs, in_=ps[7][:, 0:128]), reads=[("ps", 7)], writes=["wabs"])
    P.add("act", lambda e: e.activation(out=wsgn, in_=ps[7][:, 0:128], func=AF.Sign), reads=[("ps", 7)], writes=["wsgn"])
    for c in range(16):
        P.add("pe", lambda e, c=c: e.matmul(ps[7][0:16, 128:129], lhsT=wwi3[:, c, :], rhs=hsT[:, c:c + 1], start=(c == 0), stop=(c == 15)),
              reads=["wwi", "hsT"], writes=[("ps", 7)])
    P.add("dve", lambda e: e.tensor_copy(out=sm[0:16, 120:121], in_=ps[7][0:16, 128:129]), reads=[("ps", 7)], writes=["sm_w"])
    if debug:
        P.add("sp", lambda e: e.dma_start(out=dbg["qT"], in_=qT), reads=[(id(qT3), h, j) for h in range(8) for j in range(2)], writes=["dbg_dq"], slot="dbg0")
        P.add("sp", lambda e: e.dma_start(out=dbg["qiT"], in_=qiT), reads=[(id(qiT3), h, j) for h in range(16) for j in range(2)], writes=["dbg_dqi"], slot="dbg1")
        P.add("sp", lambda e: e.dma_start(out=dbg["kT"], in_=kT_all), reads=[("kT", b) for b in range(NB)], writes=["dbg_dk"], slot="dbg2")
        P.add("sp", lambda e: e.dma_start(out=dbg["kiT"], in_=kiT_all), reads=[("kiT", b) for b in range(NB)], writes=["dbg_dki"], slot="dbg3")
        P.add("sp", lambda e: e.dma_start(out=dbg["wabs"], in_=wabs), reads=["wabs"], writes=["dbg_dwa"], slot="dbg4")

    if upto <= 2:
        return finish()
    P.barrier()
    acc = buf(108 * KB, 16 * KB, F32)
    negm = buf(124 * KB, 8 * KB)
    jnk3 = buf(132 * KB, 8 * KB)
    S3 = 156 * KB
    Rb = [buf(S3 + i * 2 * KB, 2 * KB) for i in range(3)]
    dsg = [buf(S3 + 16 * KB + i * 4 * KB, 4 * KB) for i in range(2)]
    PT = [buf(S3 + 6 * KB + i * KB, KB) for i in range(3)]
    pen = buf(S3 + 9 * KB, 2 * KB, F32)
    tmn = buf(S3 + 11 * KB, 2 * KB, F32)
    rdens = [buf(S3 + 13 * KB, 2 * KB, F32), buf(S3 + 41 * KB, 2 * KB, F32)]
    bs = buf(S3 + 15 * KB, 512, F32)
    HK = bs[:, 0:NIT + 1]
    TC = bs[:, 32:32 + NIT + 2]
    cnt = bs[:, 64:65]
    tmpc = bs[:, 65:66]
    rmax = bs[:, 66:67]
    rmin = bs[:, 67:68]
    rmin2 = bs[:, 68:69]
    lo0 = bs[:, 69:70]
    w0 = bs[:, 70:71]
    thr = bs[:, 72:80]
    NTC = bs[:, 80:80 + NIT + 2]
    sA = bs[:, 100:101]
    cnt2 = bs[:, 101:102]
    SCALE = float(128 ** -0.5)
    rcount = {"s": 0, "st": 0, "pt": 0}
    accs = [acc, buf(S3 + 25 * KB, 16 * KB, F32)]

    def idx_prep(i):
        dg3 = v3(dsg[i % 2], 16)
        for h in range(16):
            P.add("pool", lambda e, h=h: e.tensor_scalar(out=dg3[:, h, :], in0=identf, scalar1=wabs[:, i * 16 + h:i * 16 + h + 1], scalar2=0.0,
                                                        op0=ALU.mult, op1=ALU.add),
                  reads=["identf", "wabs"], writes=[("dsg", i % 2)])

    def idx_group(i, kg):
        ac = accs[i % 2]
        dg3 = v3(dsg[i % 2], 16)
        tok = slice(i * 128, (i + 1) * 128)
        cols = slice(kg * 512, (kg + 1) * 512)

        def emit_s(j):
            n = rcount["s"]
            rcount["s"] += 1
            pa = 2 * (n % 2)
            rb = Rb[n % 3]
            for u in range(2):
                h = 2 * j + u
                P.add("pe", lambda e, h=h, u=u: e.matmul(ps[pa + u][:, 0:512], lhsT=qiT3[:, h, tok], rhs=kiT_all[:, cols], start=True, stop=True),
                      reads=["qiT", "kiT"], writes=[("ps", pa + u)])
            P.add("act", lambda e: e.activation(out=rb, in_=psall[:, pa * 512:(pa + 2) * 512], func=AF.Relu),
                  reads=[("ps", pa), ("ps", pa + 1)], writes=[("Rb", n % 3)])
            return rb, n % 3

        def emit_acc(j, rbn):
            rb, rn = rbn
            for u in range(2):
                h = 2 * j + u
                P.add("pe", lambda e, h=h, u=u: e.matmul(ps[4][:, 0:512], lhsT=dg3[:, h, :], rhs=rb[:, u * 512:(u + 1) * 512], start=(h == 0), stop=(h == 15)),
                      reads=[("Rb", rn), ("dsg", i % 2)], writes=[("ps", 4)])

        prev = emit_s(0)
        for j in range(8):
            nxt = emit_s(j + 1) if j < 7 else None
            emit_acc(j, prev)
            prev = nxt
        P.add("dve", lambda e: e.tensor_copy(out=ac[:, cols], in_=ps[4][:, 0:512]), reads=[("ps", 4)], writes=[("acc", i % 2, kg)])

    def bis_setup(i):
        ac = accs[i % 2]
        span = 512 * (i + 1)
        accr = [("acc", i % 2, kg) for kg in range(i + 1)]
        last = slice(i * 512, (i + 1) * 512)
        lres = ("acc", i % 2, i)
        P.add("dve", lambda e: e.tensor_scalar(out=pen, in0=iota_f, scalar1=qrel[:, i:i + 1], scalar2=CBIG, op0=ALU.is_gt, op1=ALU.mult),
              reads=["iota", "qrel"], writes=["pen"])
        P.add("dve", lambda e: e.tensor_tensor(out=tmn, in0=ac[:, last], in1=pen, op=ALU.add), reads=["pen", lres], writes=["tmn"])
        P.add("dve", lambda e: e.tensor_reduce(out=rmin, in_=tmn, axis=AX.X, op=ALU.min), reads=["tmn"], writes=["rmin"])
        P.add("dve", lambda e: e.tensor_tensor(out=ac[:, last], in0=ac[:, last], in1=pen, op=ALU.subtract), reads=["pen", lres], writes=[lres])
        P.add("dve", lambda e: e.tensor_reduce(out=rmax, in_=ac[:, 0:span], axis=AX.X, op=ALU.max), reads=accr, writes=["rmax"])
        if i > 0:
            P.add("dve", lambda e: e.tensor_reduce(out=rmin2, in_=ac[:, 0:i * 512], axis=AX.X, op=ALU.min), reads=accr, writes=["rmin2"])
            P.add("dve", lambda e: e.tensor_tensor(out=rmin, in0=rmin, in1=rmin2, op=ALU.min), reads=["rmin", "rmin2"], writes=["rmin"])
        P.add("dve", lambda e: e.tensor_tensor(out=w0, in0=rmax, in1=rmin, op=ALU.subtract), reads=["rmax", "rmin"], writes=["w0"])
        P.add("dve", lambda e: e.tensor_scalar(out=w0, in0=w0, scalar1=float(2.0 ** -10), scalar2=1e-3, op0=ALU.mult, op1=ALU.add), reads=["w0"], writes=["w0b"])
        P.add("dve", lambda e: e.tensor_tensor(out=lo0, in0=rmin, in1=w0, op=ALU.subtract), reads=["rmin", "w0b"], writes=["lo0"])
        P.add("dve", lambda e: e.tensor_tensor(out=w0, in0=rmax, in1=lo0, op=ALU.subtract), reads=["rmax", "lo0"], writes=["w0c"])
        P.add("dve", lambda e: e.tensor_scalar(out=HK, in0=pow2[:, 0:NIT + 1], scalar1=w0, scalar2=None, op0=ALU.mult), reads=["pow2", "w0c"], writes=["HK"])
        P.add("dve", lambda e: e.tensor_tensor(out=TC[:, 0:1], in0=lo0, in1=HK[:, 0:1], op=ALU.add), reads=["lo0", "HK"], writes=[("TC", 0)])

    def bis_iter(i, k):
        ac = accs[i % 2]
        span = 512 * (i + 1)
        accr = [("acc", i % 2, kg) for kg in range(i + 1)]
        P.add("dve", lambda e: e.tensor_scalar(out=jnk3[:, 0:span], in0=ac[:, 0:span], scalar1=TC[:, k:k + 1], scalar2=None,
                                              op0=ALU.is_ge, op1=ALU.add, accum_out=cnt),
              reads=accr + [("TC", k)], writes=["jnk3", "cnt"])
        P.add("dve", lambda e: e.scalar_tensor_tensor(out=tmpc, in0=cnt, scalar=TOPK - 0.5, in1=HK[:, k:k + 1], op0=ALU.is_ge, op1=ALU.mult),
              reads=["cnt", "HK"], writes=["tmpc"])
        P.add("dve", lambda e: e.scalar_tensor_tensor(out=TC[:, k + 1:k + 2], in0=tmpc, scalar=HK[:, k + 1:k + 2], in1=TC[:, k:k + 1],
                                                      op0=ALU.subtract, op1=ALU.add),
              reads=["tmpc", "HK", ("TC", k)], writes=[("TC", k + 1)])

    def bis_final(i):
        ac = accs[i % 2]
        span = 512 * (i + 1)
        accr = [("acc", i % 2, kg) for kg in range(i + 1)]
        P.add("dve", lambda e: e.tensor_tensor(out=thr[:, i:i + 1], in0=TC[:, NIT:NIT + 1], in1=HK[:, NIT:NIT + 1], op=ALU.subtract),
              reads=[("TC", NIT), "HK"], writes=[("thr", i)])
        P.add("dve", lambda e: e.tensor_scalar(out=negm[:, 0:span], in0=ac[:, 0:span], scalar1=thr[:, i:i + 1], scalar2=1.0,
                                              op0=ALU.is_ge, op1=ALU.subtract),
              reads=accr + [("thr", i)], writes=["negm"])
        if debug:
            P.add("sp", lambda e: e.dma_start(out=dbg["acc"][:, i * 4096:i * 4096 + span], in_=ac[:, 0:span]), reads=accr, writes=["dbg_dacc"], slot="dbg5")

    def attention(i, mid_cb=None):
        tok = slice(i * 128, (i + 1) * 128)
        nkb = 4 * (i + 1)
        for g in range(2):
            qrhs = qT3[:, 4 * g:4 * g + 4, tok]

            def emit_st(kb, g=g, qrhs=qrhs):
                n = rcount["st"]
                rcount["st"] += 1
                bk = 1 + 2 * (n % 2)
                P.add("pe", lambda e: e.matmul(ps[bk][:, 0:512], lhsT=kT3[:, g, kb * 128:(kb + 1) * 128], rhs=qrhs, start=True, stop=False),
                      reads=["qT", "kT"], writes=[("ps", bk)])
                P.add("pe", lambda e: e.matmul(ps[bk][:, 0:512], lhsT=negm[:, kb * 128:(kb + 1) * 128], rhs=identB4, start=False, stop=True),
                      reads=["negm", "identB4"], writes=[("ps", bk)])
                return bk

            bo, bd = (5, 6) if g == 0 else (7, 2)

            def emit_pv(kb, bk, g=g, nkb=nkb, bo=bo, bd=bd):
                n = rcount["pt"]
                rcount["pt"] += 1
                pt = PT[n % 3]
                P.add("act", lambda e: e.activation(out=pt, in_=ps[bk][:, 0:512], func=AF.Exp, scale=SCALE), reads=[("ps", bk)], writes=[("PT", n % 3)])
                P.add("pe", lambda e: e.matmul(ps[bo][:, 0:512], lhsT=v3a[:, kb, g * 128:(g + 1) * 128], rhs=pt, start=(kb == 0), stop=(kb == nkb - 1)),
                      reads=[("PT", n % 3), "v_all"], writes=[("ps", bo)])
                P.add("pe", lambda e: e.matmul(ps[bd][:, 0:512], lhsT=ones_bf, rhs=pt, start=(kb == 0), stop=(kb == nkb - 1)),
                      reads=[("PT", n % 3), "ones"], writes=[("ps", bd)])

            prev = None
            for kb in range(nkb):
                bk = emit_st(kb)
                if prev is not None:
                    emit_pv(*prev)
                prev = (kb, bk)
            emit_pv(*prev)
            if mid_cb is not None:
                mid_cb()
            rden = rdens[g]
            P.add("dve", lambda e, bd=bd, rden=rden: e.reciprocal(out=rden, in_=ps[bd][:, 0:512]), reads=[("ps", bd)], writes=[("rden", g)])
            P.add("dve", lambda e, g=g, bo=bo, rden=rden: e.tensor_tensor(out=oaT3[:, 4 * g:4 * g + 4, tok], in0=v3(ps[bo][:, 0:512], 4), in1=v3(rden, 4), op=ALU.mult),
                  reads=[("ps", bo), ("rden", g)], writes=[("oaT", i, g)])

    kd = {}
    setup_done = set()
    PRE_IT = 10

    def ensure_setup(i):
        if i not in setup_done:
            bis_setup(i)
            setup_done.add(i)
            kd[i] = 0

    def emit_iters(i, upto_k):
        while kd[i] < upto_k:
            bis_iter(i, kd[i])
            kd[i] += 1

    idx_prep(0)
    idx_group(0, 0)
    for i in range(OWN):
        ensure_setup(i)
        if i + 1 < OWN:
            idx_prep(i + 1)
            ngr = i + 2
            k0 = kd[i]
            for kg in range(ngr):
                idx_group(i + 1, kg)
                emit_iters(i, k0 + ((NIT - k0) * (kg + 1)) // ngr)
        emit_iters(i, NIT)
        bis_final(i)
        if i + 1 < OWN:
            ensure_setup(i + 1)
            attention(i, mid_cb=lambda i=i: emit_iters(i + 1, kd[i + 1] + PRE_IT // 2))
        else:
            attention(i)
    if debug:
        P.add("sp", lambda e: e.dma_start(out=dbg["thr"], in_=thr), reads=[("thr", i) for i in range(OWN)], writes=["dbg_dthr"], slot="dbg6")
        P.add("sp", lambda e: e.dma_start(out=dbg["oaT"], in_=oaT), reads=[("oaT", i, g) for i in range(OWN) for g in range(2)], writes=["dbg_doa"], slot="dbg7")

    if upto <= 3:
        return finish()

    P.barrier()
    oaTs = sm[:, 112:120]
    B3 = 20 * KB
    idxp_i = buf(B3, 64, I32)
    idx8 = buf(B3 + 64, 64, I32)
    idx16 = buf(B3 + 128, 64, I32)
    pgf = buf(B3 + 192, 64, F32)
    ones_f = buf(B3 + 512, 512, F32)
    scb = buf(B3 + 1024, 1024, F32)
    sc = scb[:, 0:129]
    vld = buf(B3 + 2048, 1024, F32)[:, 0:129]
    kib = [buf(B3 + 4 * KB + i * 4 * KB, 4 * KB) for i in range(2)]
    kidT = [buf(B3 + 12 * KB + i * KB, KB) for i in range(2)]
    Rs = [buf(B3 + 14 * KB + i * 2 * KB, KB) for i in range(2)]
    Kc = [buf(B3 + 18 * KB + i * 4 * KB, 4 * KB) for i in range(2)]
    KTb = [buf(B3 + 26 * KB + i * 4 * KB, 4 * KB) for i in range(2)]
    Kx = buf(B3 + 34 * KB, 512)
    Vx = buf(B3 + 35 * KB, 512)
    Psb = buf(B3 + 36 * KB, 4224, F32)
    Pbf = buf(B3 + 41 * KB, 2112)
    Pred = buf(B3 + 44 * KB, 64, F32)
    smx = buf(B3 + 45 * KB, 256, F32)
    osb = buf(B3 + 46 * KB, 1024, F32)
    qiTs_bf = smb[:, 56:72]
    qTs_bf = smb[:, 48:56]
    w_s = smb[0:16, 74:75]
    P.add("dve", lambda e: e.tensor_copy(out=w_s, in_=sm[0:16, 120:121]), reads=["sm_w"], writes=["w_s"])
    IOA = bass.IndirectOffsetOnAxis
    P.add("sp", lambda e: e.dma_start(out=idxp_i[:, 0:1], in_=ptab.rearrange("o p -> p o")), writes=["idxp"], slot="s_idx")
    P.add("dve", lambda e: e.tensor_copy(out=pgf[:, 0:1], in_=idxp_i[:, 0:1]), reads=["idxp"], writes=["pgf0"])
    P.add("dve", lambda e: e.tensor_scalar(out=pgf[:, 1:2], in0=pgf[:, 0:1], scalar1=8.0, scalar2=None, op0=ALU.mult), reads=["pgf0"], writes=["pgf1"])
    P.add("dve", lambda e: e.tensor_scalar(out=pgf[:, 2:3], in0=pgf[:, 0:1], scalar1=16.0, scalar2=None, op0=ALU.mult), reads=["pgf0"], writes=["pgf2"])
    P.add("dve", lambda e: e.tensor_scalar(out=idx8, in0=iota_f[:, 0:16], scalar1=pgf[:, 1:2], scalar2=None, op0=ALU.add), reads=["pgf1", "iota"], writes=["idx8"])
    P.add("dve", lambda e: e.tensor_scalar(out=idx16, in0=iota_f[:, 0:16], scalar1=pgf[:, 2:3], scalar2=None, op0=ALU.add), reads=["pgf2", "iota"], writes=["idx16"])
    P.add("pool", lambda e: e.memset(ones_f, 1.0), writes=["ones_f"])
    P.add("dve", lambda e: e.tensor_copy(out=qiTs_bf, in_=sm[:, 8:24]), reads=["sm_q"], writes=["qis_bf"])
    P.add("dve", lambda e: e.tensor_copy(out=qTs_bf, in_=sm[:, 0:8]), reads=["sm_q"], writes=["qs_bf"])
    ng = 0
    for ch in range(8):
        kb_ = kib[ch % 2]
        P.add("pool", lambda e, ch=ch, kb_=kb_: e.indirect_dma_start(out=kb_, out_offset=None, in_=pool_ki8,
                                                                    in_offset=IOA(ap=idx8[:, ch:ch + 1], axis=0)),
              reads=["idx8"], writes=[("kib", ch % 2)], slot="s_kib%d" % (ch % 2))
        for grp in range(4):
            tb_ = ng % 2
            for j in range(4):
                pos_l = 4 * grp + j
                P.add("pe", lambda e, kb_=kb_, pos_l=pos_l, j=j, tb_=tb_: e.transpose(out=psb[tb_][:, j * 128:(j + 1) * 128],
                                                                                 in_=kb_[:, pos_l * 128:(pos_l + 1) * 128], identity=ident),
                      reads=[("kib", ch % 2), "ident"], writes=[("ps", tb_)])
            kt_ = kidT[ng % 2]
            P.add("act", lambda e, kt_=kt_, tb_=tb_: e.copy(out=kt_, in_=psb[tb_][:, 0:512]), reads=[("ps", tb_)], writes=[("kidT", ng % 2)])
            P.add("pe", lambda e, kt_=kt_, tb_=tb_: e.matmul(ps[2 + tb_][0:16, 0:512], lhsT=qiTs_bf, rhs=kt_, start=True, stop=True),
                  reads=[("kidT", ng % 2), "qis_bf"], writes=[("ps", 2 + tb_)])
            rs_ = Rs[ng % 2]
            P.add("act", lambda e, rs_=rs_, tb_=tb_: e.activation(out=rs_[0:16, :], in_=ps[2 + tb_][0:16, 0:512], func=AF.Relu),
                  reads=[("ps", 2 + tb_)], writes=[("Rs", ng % 2)])
            for j in range(4):
                pos = ch * 16 + grp * 4 + j
                P.add("pe", lambda e, rs_=rs_, j=j, pos=pos: e.matmul(ps[4][:, pos:pos + 1], lhsT=rs_[0:16, j * 128:(j + 1) * 128], rhs=w_s,
                                                                   start=True, stop=True),
                      reads=[("Rs", ng % 2), "w_s"], writes=[("ps", 4)])
            ng += 1
    P.add("pe", lambda e: e.transpose(out=ps[5][:, 0:1], in_=ksv[0:1, 512:640], identity=identf[0:1, 0:1]), reads=["ksv", "identf"], writes=[("ps", 5)])
    P.add("act", lambda e: e.copy(out=smb[:, 72:73], in_=ps[5][:, 0:1]), reads=[("ps", 5)], writes=["kisT"])
    P.add("pe", lambda e: e.matmul(ps[5][0:16, 8:9], lhsT=qiTs_bf, rhs=smb[:, 72:73], start=True, stop=True), reads=["kisT", "qis_bf"], writes=[("ps", 5)])
    P.add("act", lambda e: e.activation(out=smb[0:16, 76:77], in_=ps[5][0:16, 8:9], func=AF.Relu), reads=[("ps", 5)], writes=["Rn"])
    P.add("pe", lambda e: e.matmul(ps[5][0:1, 16:17], lhsT=smb[0:16, 76:77], rhs=w_s, start=True, stop=True), reads=["Rn", "w_s"], writes=[("ps", 5)])
    P.add("pool", lambda e: e.memset(scb[:, 128:129], -CBIG), writes=["sc_x"])
    P.add("dve", lambda e: e.tensor_copy(out=scb[0:1, 128:129], in_=ps[5][0:1, 16:17]), reads=[("ps", 5), "sc_x"], writes=["sc_x"])
    P.add("dve", lambda e: e.tensor_copy(out=scb[:, 0:128], in_=ps[4][:, 0:128]), reads=[("ps", 4)], writes=["sc_m"])
    pm = smx[:, 2:4]
    P.add("dve", lambda e: e.tensor_reduce(out=smx[:, 2:3], in_=sc, axis=AX.X, op=ALU.max), reads=["sc_x", "sc_m"], writes=["pm0"])
    P.add("dve", lambda e: e.tensor_reduce(out=smx[:, 3:4], in_=scb[:, 0:128], axis=AX.X, op=ALU.min, negate=True), reads=["sc_m"], writes=["pm1"])
    P.add("pe", lambda e: e.transpose(out=ps[5][0:2, 32:160], in_=pm, identity=identf), reads=["pm0", "pm1", "identf"], writes=[("ps", 5)])
    P.add("dve", lambda e: e.tensor_reduce(out=smx[0:2, 4:5], in_=ps[5][0:2, 32:160], axis=AX.X, op=ALU.max), reads=[("ps", 5)], writes=["g2"])
    P.add("dve", lambda e: e.tensor_scalar(out=smx[0:2, 6:8], in0=identf[0:2, 0:2], scalar1=smx[0:2, 4:5], scalar2=None, op0=ALU.mult),
          reads=["g2", "identf"], writes=["g2d"])
    P.add("pe", lambda e: e.matmul(ps[5][:, 200:202], lhsT=ones_f[0:2, :], rhs=smx[0:2, 6:8], start=True, stop=True), reads=["g2d", "ones_f"], writes=[("ps", 5)])
    P.add("dve", lambda e: e.tensor_copy(out=smx[:, 8:10], in_=ps[5][:, 200:202]), reads=[("ps", 5)], writes=["gmm"])
    gmax = smx[:, 8:9]
    ngmin = smx[:, 9:10]
    P.add("dve", lambda e: e.tensor_tensor(out=w0, in0=gmax, in1=ngmin, op=ALU.add), reads=["gmm"], writes=["w0"])
    P.add("dve", lambda e: e.tensor_scalar(out=w0, in0=w0, scalar1=float(2.0 ** -10), scalar2=1e-3, op0=ALU.mult, op1=ALU.add), reads=["w0"], writes=["w0b"])
    P.add("dve", lambda e: e.tensor_tensor(out=lo0, in0=ngmin, in1=w0, op=ALU.add), reads=["gmm", "w0b"], writes=["lo0n"])
    P.add("dve", lambda e: e.tensor_scalar(out=lo0, in0=lo0, scalar1=-1.0, scalar2=None, op0=ALU.mult), reads=["lo0n"], writes=["lo0"])
    P.add("dve", lambda e: e.tensor_tensor(out=w0, in0=gmax, in1=lo0, op=ALU.subtract), reads=["gmm", "lo0"], writes=["w0c"])
    P.add("dve", lambda e: e.tensor_scalar(out=HK, in0=pow2[:, 0:NIT + 1], scalar1=w0, scalar2=None, op0=ALU.mult), reads=["pow2", "w0c"], writes=["HK"])
    P.add("dve", lambda e: e.tensor_tensor(out=TC[:, 0:1], in0=lo0, in1=HK[:, 0:1], op=ALU.add), reads=["lo0", "HK"], writes=[("TC", 0)])
    for k in range(NIT):
        P.add("dve", lambda e, k=k: e.tensor_scalar(out=vld, in0=sc, scalar1=TC[:, k:k + 1], scalar2=None, op0=ALU.is_ge, op1=ALU.add, accum_out=cnt),
              reads=["sc_x", "sc_m", ("TC", k)], writes=["vld", "cnt"])
        P.add("pe", lambda e: e.matmul(ps[6][:, 0:1], lhsT=ones_f, rhs=cnt, start=True, stop=True), reads=["cnt", "ones_f"], writes=[("ps", 6)])
        P.add("dve", lambda e, k=k: e.scalar_tensor_tensor(out=tmpc, in0=ps[6][:, 0:1], scalar=TOPK - 0.5, in1=HK[:, k:k + 1], op0=ALU.is_ge, op1=ALU.mult),
              reads=[("ps", 6), "HK"], writes=["tmpc"])
        P.add("dve", lambda e, k=k: e.scalar_tensor_tensor(out=TC[:, k + 1:k + 2], in0=tmpc, scalar=HK[:, k + 1:k + 2], in1=TC[:, k:k + 1],
                                                           op0=ALU.subtract, op1=ALU.add),
              reads=["tmpc", "HK", ("TC", k)], writes=[("TC", k + 1)])
    thr_s = smx[:, 12:13]
    P.add("dve", lambda e: e.tensor_tensor(out=thr_s, in0=TC[:, NIT:NIT + 1], in1=HK[:, NIT:NIT + 1], op=ALU.subtract), reads=[("TC", NIT), "HK"], writes=["thr_s"])
    P.add("dve", lambda e: e.tensor_scalar(out=vld, in0=sc, scalar1=thr_s, scalar2=None, op0=ALU.is_ge), reads=["sc_x", "sc_m", "thr_s"], writes=["vld"])
    P.add("pool", lambda e: e.memset(Kx, 0.0), writes=["Kx"])
    P.add("pool", lambda e: e.memset(Vx, 0.0), writes=["Vx"])
    P.add("dve", lambda e: e.tensor_copy(out=Kx[0:1, :], in_=ksv[0:1, 0:256]), reads=["ksv", "Kx"], writes=["Kx"])
    P.add("dve", lambda e: e.tensor_copy(out=Vx[0:1, :], in_=ksv[0:1, 256:512]), reads=["ksv", "Vx"], writes=["Vx"])

    def s_col(pos):
        return (4 + pos // 64, (pos % 64) * 8) if pos < 128 else (6, 0)

    for ch in range(17):
        npos = 8 if ch < 16 else 1
        if ch < 16:
            kc_ = Kc[ch % 2]
            P.add("pool", lambda e, ch=ch, kc_=kc_: e.indirect_dma_start(out=kc_, out_offset=None, in_=pool_k16,
                                                                        in_offset=IOA(ap=idx16[:, ch:ch + 1], axis=0)),
                  reads=["idx16"], writes=[("Kc", ch % 2)], slot="s_kc%d" % (ch % 2))
            kres = ("Kc", ch % 2)
        else:
            kc_ = Kx
            kres = "Kx"
        ktb = KTb[ch % 2]
        ba = 2 * (ch % 2)
        for t in range(2 * npos):
            bk = ba + t // 8
            P.add("pe", lambda e, kc_=kc_, t=t, bk=bk: e.transpose(out=psb[bk][:, (t % 8) * 128:(t % 8 + 1) * 128],
                                                                 in_=kc_[:, t * 128:(t + 1) * 128], identity=ident),
                  reads=[kres, "ident"], writes=[("ps", bk)])
        nb_ = (2 * npos + 7) // 8
        for hb in range(nb_):
            ncol = min(8, 2 * npos - 8 * hb) * 128
            P.add("act" if hb == 0 else "dve",
                  (lambda e, ktb=ktb, hb=hb, ba=ba, ncol=ncol: e.copy(out=ktb[:, hb * 1024:hb * 1024 + ncol], in_=psb[ba + hb][:, 0:ncol])) if hb == 0 else
                  (lambda e, ktb=ktb, hb=hb, ba=ba, ncol=ncol: e.tensor_copy(out=ktb[:, hb * 1024:hb * 1024 + ncol], in_=psb[ba + hb][:, 0:ncol])),
                  reads=[("ps", ba + hb)], writes=[("KTb", ch % 2, hb)])
        for p in range(npos):
            pos = ch * 8 + p
            bk, col = s_col(pos)
            for g in range(2):
                t = 2 * p + g
                P.add("pe", lambda e, ktb=ktb, t=t, g=g, bk=bk, col=col: e.matmul(ps[bk][:, col + 4 * g:col + 4 * g + 4], lhsT=ktb[:, t * 128:(t + 1) * 128],
                                                                              rhs=qTs_bf[:, 4 * g:4 * g + 4], start=True, stop=True),
                      reads=[("KTb", ch % 2, t // 8), "qs_bf"], writes=[("ps", bk)])
    P.add("act", lambda e: e.activation(out=Psb[:, 0:512], in_=ps[4][:, 0:512], func=AF.Exp, scale=SCALE), reads=[("ps", 4)], writes=["Psb0"])
    P.add("act", lambda e: e.activation(out=Psb[:, 512:1024], in_=ps[5][:, 0:512], func=AF.Exp, scale=SCALE), reads=[("ps", 5)], writes=["Psb1"])
    P.add("act", lambda e: e.activation(out=Psb[:, 1024:1032], in_=ps[6][:, 0:8], func=AF.Exp, scale=SCALE), reads=[("ps", 6)], writes=["Psb2"])
    P3 = Psb[:, 0:1032].rearrange("p (s h) -> p s h", h=8)
    Pb3 = Pbf[:, 0:1032].rearrange("p (s h) -> p s h", h=8)
    P.add("dve", lambda e: e.tensor_tensor(out=Pb3, in0=P3, in1=vld.unsqueeze(2).to_broadcast([128, 129, 8]), op=ALU.mult),
          reads=["Psb0", "Psb1", "Psb2", "vld"], writes=["Pbf"])
    P.add("dve", lambda e: e.tensor_reduce(out=Pred[:, 0:8], in_=Pbf[:, 0:1032].rearrange("p (s h) -> p h s", h=8), axis=AX.X, op=ALU.add),
          reads=["Pbf"], writes=["Pred"])
    for ch in range(17):
        npos = 8 if ch < 16 else 1
        if ch < 16:
            vc_ = Kc[ch % 2]
            P.add("pool", lambda e, ch=ch, vc_=vc_: e.indirect_dma_start(out=vc_, out_offset=None, in_=pool_v16,
                                                                        in_offset=IOA(ap=idx16[:, ch:ch + 1], axis=0)),
                  reads=["idx16"], writes=[("Kc", ch % 2)], slot="s_kc%d" % (ch % 2))
            vres = ("Kc", ch % 2)
        else:
            vc_ = Vx
            vres = "Vx"
        for p in range(npos):
            pos = ch * 8 + p
            for g in range(2):
                P.add("pe", lambda e, vc_=vc_, p=p, g=g, pos=pos: e.matmul(ps[g][0:4, 0:128], lhsT=Pbf[:, pos * 8 + 4 * g:pos * 8 + 4 * g + 4],
                                                                       rhs=vc_[:, p * 256 + g * 128:p * 256 + (g + 1) * 128],
                                                                       start=(pos == 0), stop=(pos == 128)),
                      reads=[vres, "Pbf"], writes=[("ps", g)])
    for g in range(2):
        P.add("pe", lambda e, g=g: e.matmul(ps[2][0:4, g:g + 1], lhsT=Pred[:, 4 * g:4 * g + 4], rhs=ones_f[:, 0:1], start=True, stop=True),
              reads=["Pred", "ones_f"], writes=[("ps", 2)])
    P.add("dve", lambda e: e.reciprocal(out=smx[0:4, 16:18], in_=ps[2][0:4, 0:2]), reads=[("ps", 2)], writes=["rden_s"])
    for g in range(2):
        P.add("dve", lambda e, g=g: e.tensor_scalar(out=osb[0:4, g * 128:(g + 1) * 128], in0=ps[g][0:4, 0:128], scalar1=smx[0:4, 16 + g:17 + g], scalar2=None,
                                                   op0=ALU.mult), reads=[("ps", g), "rden_s"], writes=[("osb", g)])
        P.add("pe", lambda e, g=g: e.transpose(out=ps[3][:, 4 * g:4 * g + 4], in_=osb[0:4, g * 128:(g + 1) * 128], identity=identf[0:4, 0:4]),
              reads=[("osb", g), "identf"], writes=[("ps", 3)])
    P.add("dve", lambda e: e.tensor_copy(out=oaTs, in_=ps[3][:, 0:8]), reads=[("ps", 3)], writes=["oaTs"])

    P.barrier()
    S4 = 20 * KB
    xbuf = [buf(S4 + i * 8 * KB, 8 * KB, F32) for i in range(2)]
    hbuf = [buf(S4 + 16 * KB + i * 4 * KB, 4 * KB) for i in range(2)]
    junk = buf(S4 + 24 * KB, 4 * KB)
    wvbuf = buf(52 * KB, 32 * KB)
    wv3 = v3(wvbuf, 16)
    bspbc = buf(200 * KB, 4 * KB, F32)
    for q4 in range(8):
        P.add("pool", lambda e, q4=q4: e.dma_start(out=wv3[:, q4 * 2:(q4 + 1) * 2, :], in_=v3(wvb_d, 16)[:, q4 * 2:(q4 + 1) * 2, :]),
              writes=["wvbuf"], slot="wvb%d" % (q4 % 4))
    P.add("sp", lambda e: e.dma_start(out=bspbc, in_=bsp_d.partition_broadcast(128)), writes=["bspbc"], slot="c_bsp")
    P.add("pool", lambda e: e.dma_start(out=WsT, in_=wspT_d), writes=["WsT"], slot="c_wsp")
    P.add("pool", lambda e: e.affine_select(out=v3(WsT, 8), in_=v3(WsT, 8), pattern=[[0, 8], [1, 128]], compare_op=ALU.is_ge,
                                            fill=0.0, base=0, channel_multiplier=-1), reads=["WsT"], writes=["WsT"])
    own_pass(xbuf, hbuf, junk)

    P.barrier()
    uT = buf(20 * KB, 16 * KB)
    obT = buf(36 * KB, 16 * KB)
    wbuf = [buf(84 * KB + i * 4 * KB, 4 * KB) for i in range(3)]
    wpbuf = [buf(96 * KB + i * 2 * KB, 2 * KB) for i in range(2)]
    szb = [buf(100 * KB + i * KB, KB) for i in range(2)]
    ftmp = buf(102 * KB, 4 * KB, F32)
    ftmp2 = buf(188 * KB, 4 * KB, F32)
    gvb = buf(192 * KB, 4 * KB, F32)
    vnb = [buf(196 * KB + i * 2 * KB, 2 * KB) for i in range(2)]
    uT3 = v3(uT, 8)
    obT3 = v3(obT, 8)
    P.add("sp", lambda e: e.dma_start(out=gbc[:, 0:1024], in_=ln_g_d.partition_broadcast(128)), writes=["gbc"], slot="c_gbc")
    P.add("sp", lambda e: e.dma_start(out=gbc[:, 1024:2048], in_=ln_b_d.partition_broadcast(128)), writes=["gbc"], slot="c_gbc")
    fm_state["n"] = 0

    oa_all = [("oaT", i, g) for i in range(OWN) for g in range(2)]

    def evac_mul_act(dst3, func, dres):
        def mk(cb):
            def f(pa):
                for hf in range(2):
                    sz = szb[hf]
                    P.add("act", lambda e, hf=hf, sz=sz: e.activation(out=sz, in_=ps[pa + hf][:, 0:512], func=func),
                          reads=[("ps", pa + hf)], writes=[("szb", hf)])
                    P.add("dve", lambda e, hf=hf, sz=sz: e.tensor_tensor(out=dst3[:, cb, hf * 512:(hf + 1) * 512],
                                                                       in0=dst3[:, cb, hf * 512:(hf + 1) * 512], in1=sz, op=ALU.mult),
                          reads=[("szb", hf)] + dres, writes=[(id(dst3), "z", cb, hf)])
            return f
        return mk

    mk = evac_mul_act(oaT3, AF.Silu, oa_all)
    for cb in range(8):
        proj_fm(FM_ZA + cb, 24 + cb, mk(cb))
    oaz_all = [(id(oaT3), "z", cb, hf) for cb in range(8) for hf in range(2)]
    P.add("act", lambda e: e.activation(out=sm[:, 24:32], in_=ps[6][:, 24:32], func=AF.Silu), reads=[("ps", 6)], writes=["zaTs"])
    P.add("dve", lambda e: e.tensor_tensor(out=oazTs, in0=oaTs, in1=sm[:, 24:32], op=ALU.mult), reads=["zaTs", "oaTs"], writes=["srhs"])

    def evac_act(dst3, func, tag):
        def mk(cb):
            def f(pa):
                for hf in range(2):
                    P.add("act", lambda e, hf=hf: e.activation(out=dst3[:, cb, hf * 512:(hf + 1) * 512], in_=ps[pa + hf][:, 0:512], func=func),
                          reads=[("ps", pa + hf)], writes=[(tag, cb, hf)])
            return f
        return mk

    mk = evac_act(uT3, AF.Gelu, "uT")
    for cb in range(8):
        proj_fm(FM_U + cb, 32 + cb, mk(cb))
    uT_all = [("uT", cb, hf) for cb in range(8) for hf in range(2)]
    P.add("act", lambda e: e.activation(out=sm[:, 32:40], in_=ps[6][:, 32:40], func=AF.Gelu), reads=[("ps", 6)], writes=["uTs"])

    def layernorm(src, dst, sidx, nparts, src_res, dst_res, tmp, jk):
        s1 = stat[0:nparts, sidx:sidx + 1]
        s2 = stat[0:nparts, sidx + 1:sidx + 2]
        mu = stat[0:nparts, sidx + 2:sidx + 3]
        var = stat[0:nparts, sidx + 3:sidx + 4]
        sd = stat[0:nparts, sidx + 4:sidx + 5]
        rs = stat[0:nparts, sidx + 5:sidx + 6]
        P.add("dve", lambda e: e.reduce_sum(out=s1, in_=src, axis=AX.X), reads=[src_res], writes=[("st", sidx)])
        P.add("dve", lambda e: e.tensor_scalar(out=mu, in0=s1, scalar1=1.0 / 1024, scalar2=None, op0=ALU.mult), reads=[("st", sidx)], writes=[("st", sidx + 2)])
        P.add("dve", lambda e: e.tensor_scalar(out=tmp, in0=src, scalar1=mu, scalar2=None, op0=ALU.subtract), reads=[src_res, ("st", sidx + 2)], writes=[(dst_res, "t")])
        P.add("act", lambda e: e.activation(out=jk[0:nparts, 0:1024], in_=tmp, func=AF.Square, accum_out=s2), reads=[(dst_res, "t")], writes=["junk", ("st", sidx + 1)])
        P.add("act", lambda e: e.activation(out=sd, in_=s2, func=AF.Sqrt, scale=1.0 / 1024, bias=epsc[0:nparts, 1:2]), reads=[("st", sidx + 1), "epsc"], writes=[("st", sidx + 4)])
        P.add("dve", lambda e: e.reciprocal(out=rs, in_=sd), reads=[("st", sidx + 4)], writes=[("st", sidx + 5)])
        P.add("dve", lambda e: e.scalar_tensor_tensor(out=tmp, in0=tmp, scalar=rs, in1=gbc[0:nparts, 0:1024], op0=ALU.mult, op1=ALU.mult),
              reads=[(dst_res, "t"), ("st", sidx + 5), "gbc"], writes=[(dst_res, "t")])
        P.add("dve", lambda e: e.tensor_tensor(out=dst, in0=tmp, in1=gbc[0:nparts, 1024:2048], op=ALU.add), reads=[(dst_res, "t"), "gbc"], writes=[dst_res])

    junk = buf(106 * KB, 2 * KB)
    for i in range(OWN):
        tok = slice(i * 128, (i + 1) * 128)
        for hf in range(2):
            bk = hf
            for c in range(16):
                P.add("pe", lambda e, c=c, hf=hf, bk=bk, tok=tok: e.matmul(ps[bk][:, 0:512], lhsT=hT3[:, c, tok], rhs=wv3[:, c, hf * 512:(hf + 1) * 512],
                                                                        start=(c == 0), stop=(c == 15)),
                      reads=["wvbuf"] + hT_reads, writes=[("ps", bk)])
            P.add("act", lambda e, hf=hf, bk=bk: e.activation(out=gvb[:, hf * 512:(hf + 1) * 512], in_=ps[bk][:, 0:512], func=AF.Gelu),
                  reads=[("ps", bk)], writes=["gvb"])
        vn = vnb[i % 2]
        layernorm(gvb, vn, 16, 128, "gvb", ("vn", i % 2), ftmp, junk)
        for g in range(8):
            bk = 2 + g // 4
            P.add("pe", lambda e, g=g, bk=bk, vn=vn: e.matmul(ps[bk][:, (g % 4) * 128:(g % 4 + 1) * 128], lhsT=vn[:, g * 128:(g + 1) * 128],
                                                             rhs=v3(WsT, 8)[:, g, :], start=True, stop=True),
                  reads=[("vn", i % 2), "WsT"], writes=[("ps", bk)])
        for hh in range(2):
            bk = 2 + hh
            gs = slice(4 * hh, 4 * hh + 4)
            P.add("dve", lambda e, bk=bk, hh=hh: e.tensor_tensor(out=ftmp2[:, hh * 512:(hh + 1) * 512], in0=ps[bk][:, 0:512],
                                                                in1=bspbc[:, hh * 512:(hh + 1) * 512], op=ALU.add),
                  reads=[("ps", bk), "bspbc"], writes=[("ftmp2", hh)])
            P.add("dve", lambda e, hh=hh, gs=gs, tok=tok: e.tensor_tensor(out=obT3[:, gs, tok], in0=v3(ftmp2[:, hh * 512:(hh + 1) * 512], 4),
                                                                       in1=uT3[:, gs, tok], op=ALU.mult),
                  reads=[("ftmp2", hh)] + uT_all, writes=[("obT", i, hh)])
    ob_all = [("obT", i, hh) for i in range(OWN) for hh in range(2)]
    for hf in range(2):
        for c in range(16):
            P.add("pe", lambda e, c=c, hf=hf: e.matmul(ps[4 + hf][0:1, 0:512], lhsT=hsT[:, c:c + 1], rhs=wv3[:, c, hf * 512:(hf + 1) * 512],
                                                      start=(c == 0), stop=(c == 15)),
                  reads=["wvbuf", "hsT"], writes=[("ps", 4 + hf)])
        P.add("act", lambda e, hf=hf: e.activation(out=gvb[0:1, hf * 512:(hf + 1) * 512], in_=ps[4 + hf][0:1, 0:512], func=AF.Gelu),
              reads=[("ps", 4 + hf)], writes=["gvb"])
    vns = ftmp2[0:1, :]
    layernorm(gvb[0:1, :], vns, 24, 1, "gvb", "vns", ftmp[0:1, :], junk)
    P.add("sp", lambda e: e.dma_start(out=gvs_o, in_=vns), reads=["vns", ("gt", 1, 0), ("gt", 1, 1)], writes=["o_gvs"], slot="o_gvs")

    P.add("sp", lambda e: e.dma_start(out=gvb[0:1, :], in_=ws00_d), reads=["vns"], writes=["gvb"], slot="c_ws00")
    P.add("sp", lambda e: e.dma_start(out=ftmp[0:1, :], in_=bs0_d), reads=["vns"], writes=[("vns", "t")], slot="c_bs0")
    P.add("dve", lambda e: e.tensor_tensor(out=gvb[0:1, :], in0=vns, in1=gvb[0:1, :], op=ALU.mult), reads=["vns", "gvb"], writes=["gvb"])
    P.add("dve", lambda e: e.tensor_tensor(out=gvb[0:1, :], in0=gvb[0:1, :], in1=ftmp[0:1, :], op=ALU.add), reads=["gvb", ("vns", "t")], writes=["gvb"])
    for g in range(8):
        P.add("pe", lambda e, g=g: e.transpose(out=ps[4][:, g:g + 1], in_=gvb[0:1, g * 128:(g + 1) * 128], identity=identf[0:1, 0:1]),
              reads=["gvb", "identf"], writes=[("ps", 4)])
    P.add("dve", lambda e: e.tensor_tensor(out=sm[:, 32:40], in0=ps[4][:, 0:8], in1=sm[:, 32:40], op=ALU.mult), reads=[("ps", 4), "uTs"], writes=["obTs"])
    mk = evac_mul_act(obT3, AF.Silu, ob_all)
    for cb in range(8):
        proj_fm(FM_ZB + cb, 40 + cb, mk(cb))
    obz_all = [(id(obT3), "z", cb, hf) for cb in range(8) for hf in range(2)]
    P.add("act", lambda e: e.activation(out=sm[:, 40:48], in_=ps[6][:, 40:48], func=AF.Silu), reads=[("ps", 6)], writes=["zbTs"])
    P.add("dve", lambda e: e.tensor_tensor(out=obzTs, in0=sm[:, 32:40], in1=sm[:, 40:48], op=ALU.mult), reads=["zbTs", "obTs"], writes=["srhs"])
    if debug:
        P.add("sp", lambda e: e.dma_start(out=dbg["obT"], in_=obT), reads=obz_all, writes=["dbg_dob"], slot="dbg8")

    wp_n = {"n": 0}

    def proj_branch(wd, cb, src3, sres, bank, scol, srhs):
        n = wp_n["n"]
        wp_n["n"] += 1
        wsl = wpbuf[n % 2]
        w3 = v3(wsl, 8)
        P.add("pool", lambda e: e.dma_start(out=wsl, in_=wd[cb]), writes=[("wpbuf", n % 2)], slot="wpbuf%d" % (n % 2))
        for c in range(8):
            for hf in range(2):
                P.add("pe", lambda e, c=c, hf=hf: e.matmul(ps[bank + hf][:, 0:512], lhsT=w3[:, c, :], rhs=src3[:, c, hf * 512:(hf + 1) * 512],
                                                          start=(c == 0), stop=(c == 7)),
                      reads=[("wpbuf", n % 2)] + sres, writes=[("ps", bank + hf)])
            P.add("pe", lambda e, c=c: e.matmul(ps[7][:, scol:scol + 1], lhsT=w3[:, c, :], rhs=srhs[:, c:c + 1], start=(c == 0), stop=(c == 7)),
                  reads=[("wpbuf", n % 2), "srhs"], writes=[("ps", 7)])


    for cb in range(16):
        sg = [None, None]
        for br in range(2):
            gt = ftmp if br == 0 else ftmp2

            def ev_gate(pa, gt=gt, br=br):
                for hf in range(2):
                    P.add("act", lambda e, hf=hf: e.activation(out=gt[:, hf * 512:(hf + 1) * 512], in_=ps[pa + hf][:, 0:512], func=AF.Sigmoid),
                          reads=[("ps", pa + hf)], writes=[("gt", br, hf)])
                sg[br] = pa
            proj_fm(FM_G + 2 * cb + br, (48 + cb if br == 0 else 64 + cb), ev_gate, pa=2 * br)
            if br == 0:
                proj_branch(wpa_d, cb, oaT3, oaz_all, 4, cb, oazTs)
            else:
                proj_branch(wpb_d, cb, obT3, obz_all, 4, 16 + cb, obzTs)
            for hf in range(2):
                P.add("dve", lambda e, hf=hf, gt=gt: e.tensor_tensor(out=gt[:, hf * 512:(hf + 1) * 512], in0=ps[4 + hf][:, 0:512],
                                                                   in1=gt[:, hf * 512:(hf + 1) * 512], op=ALU.mult),
                      reads=[("ps", 4 + hf), ("gt", br, hf)], writes=[("gt", br, hf)])
        for hf in range(2):
            P.add("dve", lambda e, hf=hf, cb=cb: e.tensor_tensor(out=mixT3[:, cb, hf * 512:(hf + 1) * 512], in0=ftmp[:, hf * 512:(hf + 1) * 512],
                                                               in1=ftmp2[:, hf * 512:(hf + 1) * 512], op=ALU.add),
                  reads=[("gt", 0, hf), ("gt", 1, hf)], writes=[("mixT", cb, hf)])
    mix_all = [("mixT", cb, hf) for cb in range(16) for hf in range(2)]
    if debug:
        P.add("sp", lambda e: e.dma_start(out=dbg["mixT"], in_=mixT), reads=mix_all, writes=["dbg_dmx"], slot="dbg9")
    P.add("act", lambda e: e.activation(out=sm[:, 48:80], in_=ps[6][:, 48:80], func=AF.Sigmoid),
          reads=[("ps", 6)], writes=["sm_g"])
    P.add("dve", lambda e: e.tensor_tensor(out=sm[:, 80:112], in0=ps[7][:, 0:32], in1=sm[:, 48:80], op=ALU.mult),
          reads=[("ps", 7), "sm_g"], writes=["sm_y"])
    P.add("dve", lambda e: e.tensor_tensor(out=mixTs, in0=sm[:, 80:96], in1=sm[:, 96:112], op=ALU.add), reads=["sm_y"], writes=["mixTs"])

    if upto <= 4:
        return finish()
    P.barrier()
    wo = [buf(20 * KB + gidx * 16 * KB, 16 * KB) for gidx in range(4)]
    xbuf = [buf(84 * KB + i * 8 * KB, 8 * KB, F32) for i in range(2)]
    xo = [buf(100 * KB + i * 8 * KB, 8 * KB, F32) for i in range(2)]
    junk = buf(116 * KB, 4 * KB)
    xs2 = buf(120 * KB, 8 * KB, F32, parts=1)
    xos = buf(128 * KB, 8 * KB, F32, parts=1)
    P.add("sp", lambda e: e.dma_start(out=gbc, in_=g_f_d.partition_broadcast(128)), writes=["gbc"], slot="c_gbc")
    for gi in range(4):
        w3 = v3(wo[gi], 16)
        for q4 in range(4):
            P.add("pool", lambda e, gi=gi, q4=q4, w3=w3: e.dma_start(out=w3[:, q4 * 4:(q4 + 1) * 4, :],
                                                                   in_=v3(wout_d[gi], 16)[:, q4 * 4:(q4 + 1) * 4, :]),
                  writes=[("wo", gi)], slot="wo%d_%d" % (gi, q4))

    def final_block(src_ap, xb, xres, xslot, xo_t, lhs_fn, lhs_reads, banks, out_ap, sidx, nparts, oslot):
        P.add("sp", lambda e: e.dma_start(out=xb, in_=src_ap), writes=[xres], slot=xslot)
        for gi in range(4):
            w3 = v3(wo[gi], 16)
            for c in range(16):
                P.add("pe", lambda e, gi=gi, c=c, w3=w3: e.matmul(ps[banks[gi]][0:nparts, 0:512], lhsT=lhs_fn(c), rhs=w3[:, c, :],
                                                                start=(c == 0), stop=(c == 15)),
                      reads=[("wo", gi)] + lhs_reads, writes=[("ps", banks[gi])])
            P.add("dve", lambda e, gi=gi: e.tensor_tensor(out=xo_t[:, gi * 512:(gi + 1) * 512], in0=ps[banks[gi]][0:nparts, 0:512],
                                                         in1=xb[:, gi * 512:(gi + 1) * 512], op=ALU.add),
                  reads=[("ps", banks[gi]), xres], writes=[(xres, "xo", gi)])
        ss = stat[0:nparts, sidx:sidx + 1]
        sd = stat[0:nparts, sidx + 1:sidx + 2]
        rs = stat[0:nparts, sidx + 2:sidx + 3]
        xor = [(xres, "xo", gi) for gi in range(4)]
        P.add("act", lambda e: e.activation(out=junk[0:nparts, :], in_=xo_t, func=AF.Square, accum_out=ss), reads=xor, writes=["junk", ("st", sidx)])
        P.add("act", lambda e: e.activation(out=sd, in_=ss, func=AF.Sqrt, scale=1.0 / D, bias=epsc[0:nparts, 0:1]),
              reads=[("st", sidx), "epsc"], writes=[("st", sidx + 1)])
        P.add("dve", lambda e: e.reciprocal(out=rs, in_=sd), reads=[("st", sidx + 1)], writes=[("st", sidx + 2)])
        P.add("dve", lambda e: e.scalar_tensor_tensor(out=xb, in0=xo_t, scalar=rs, in1=gbc[0:nparts, :], op0=ALU.mult, op1=ALU.mult),
              reads=xor + [("st", sidx + 2), "gbc"], writes=[xres])
        P.add("pool", lambda e: e.dma_start(out=out_ap, in_=xb), reads=[xres], writes=[("o_y", oslot)], slot="o_y%s" % oslot)

    for i in range(OWN):
        pb = i % 2
        tok = slice(i * 128, (i + 1) * 128)
        banks = [0, 1, 2, 3] if pb == 0 else [4, 5, 6, 7]
        final_block(x_own[i * 128:(i + 1) * 128, :], xbuf[pb], ("xbuf", pb), "xbuf%d" % pb, xo[pb],
                    lambda c, tok=tok: mixT3[:, c, tok], mix_all, banks, y_own[i * 128:(i + 1) * 128, :], 4 * pb, 128, pb)
    final_block(x_s, xs2, "xs2", "xs2", xos, lambda c: mixTs[:, c:c + 1], ["mixTs"], [0, 1, 2, 3], ys_o, 8, 1, "s")

    return finish()


def own_blocks(core):
    j = core % 4
    return sorted([j, 7 - j, 8 + j, 15 - j, 16 + j, 23 - j, 24 + j, 31 - j])


def prep_shared(inputs):
    w_in = np.asarray(inputs["w_in"])[0]

    def chunked(cols):
        n = cols.shape[1]
        return np.ascontiguousarray(cols.reshape(16, 128, n).transpose(1, 0, 2))

    sh = {}
    kvki = np.concatenate([w_in[:, C_K:C_K + 256], w_in[:, C_V:C_V + 256], w_in[:, C_KI:C_KI + 128]], axis=1)
    sh["wkvki"] = chunked(kvki).reshape(128, -1)
    cbs = []
    for base, n in ((C_Q, 8), (C_QI, 16), (C_ZA, 8), (C_U, 8), (C_ZB, 8)):
        for j in range(n):
            cbs.append(w_in[:, base + j * 128: base + (j + 1) * 128])
    for j in range(16):
        cbs.append(w_in[:, C_GA + j * 128:C_GA + (j + 1) * 128])
        cbs.append(w_in[:, C_GB + j * 128:C_GB + (j + 1) * 128])
    sh["wfm"] = np.stack([chunked(c).reshape(128, -1) for c in cbs])
    sh["wwi"] = chunked(w_in[:, C_WI:C_WI + 16]).reshape(128, -1)
    sh["wvb"] = chunked(w_in[:, C_VB:C_VB + 1024]).reshape(128, -1)
    wpa = np.asarray(inputs["w_proj_a"])[0]
    wpb = np.asarray(inputs["w_proj_b"])[0]

    def chunk8(cols):
        return np.ascontiguousarray(cols.reshape(8, 128, 128).transpose(1, 0, 2)).reshape(128, -1)

    sh["wpa"] = np.stack([chunk8(wpa[:, j * 128:(j + 1) * 128]) for j in range(16)])
    sh["wpb"] = np.stack([chunk8(wpb[:, j * 128:(j + 1) * 128]) for j in range(16)])
    wout = np.asarray(inputs["w_out"])[0]
    sh["wout"] = np.stack([chunked(wout[:, j * 512:(j + 1) * 512]).reshape(128, -1) for j in range(4)])
    ws = np.asarray(inputs["w_spatial"])[0]
    sh["wspT"] = np.ascontiguousarray(ws.transpose(2, 0, 1)).reshape(128, -1)
    bsp = np.asarray(inputs["b_spatial"])[0]
    sh["bsp"] = np.ascontiguousarray(bsp.reshape(1, 1024))
    sh["ws00"] = np.ascontiguousarray(np.repeat(ws[:, 0, 0], 128).reshape(1, 1024))
    sh["bs0"] = np.ascontiguousarray(np.repeat(bsp[:, 0], 128).reshape(1, 1024))
    sh["pool_ki8"] = np.ascontiguousarray(np.asarray(inputs["cache_k_idx"])[0].reshape(1280 * 8, 2048))
    sh["pool_k16"] = np.ascontiguousarray(np.asarray(inputs["cache_k"])[0].reshape(1280 * 16, 2048))
    sh["pool_v16"] = np.ascontiguousarray(np.asarray(inputs["cache_v"])[0].reshape(1280 * 16, 2048))
    sh["g_in"] = np.ascontiguousarray(np.asarray(inputs["norm_in_g"]).reshape(1, D))
    sh["g_f"] = np.ascontiguousarray(np.asarray(inputs["norm_f_g"]).reshape(1, D))
    sh["ln_g"] = np.ascontiguousarray(np.asarray(inputs["ln_g"]).reshape(1, 1024))
    sh["ln_b"] = np.ascontiguousarray(np.asarray(inputs["ln_b"]).reshape(1, 1024))
    return {k: np.ascontiguousarray(v, dtype=v.dtype) for k, v in sh.items()}


def make_in_maps(inputs):
    sh = prep_shared(inputs)
    xp = np.asarray(inputs["x_prompt"])
    xs = np.asarray(inputs["x_sample"])
    pt = np.asarray(inputs["page_table"]).astype(np.int32)
    maps = []
    for c in range(NCORES):
        b = c // 4
        ob = own_blocks(c)
        m = dict(sh)
        m["x_all"] = np.ascontiguousarray(xp[b])
        m["x_own"] = np.ascontiguousarray(np.concatenate([xp[b, blk * 128:(blk + 1) * 128] for blk in ob], axis=0))
        t = np.arange(128, dtype=np.float32)[:, None]
        m["qrel"] = np.ascontiguousarray(
            np.concatenate([(ob[i] * 128 - 512 * i) + t for i in range(OWN)], axis=1).astype(np.float32))
        m["x_s"] = np.ascontiguousarray(xs[c].reshape(1, D))
        m["ptab"] = np.ascontiguousarray(pt[c].reshape(1, 128))
        maps.append(m)
    return maps


_CACHE = {}


def kernel(**inputs):
    if "nc" not in _CACHE:
        _CACHE["nc"] = build_program(debug=False)[0]
    nc = _CACHE["nc"]
    maps = make_in_maps(inputs)
    res = run_bass_kernel_spmd(nc, maps, core_ids=list(range(NCORES)))
    r = res.results
    y_prompt = np.zeros((2, SEQ, D), np.float32)
    for c in range(NCORES):
        b = c // 4
        for i, blk in enumerate(own_blocks(c)):
            y_prompt[b, blk * 128:(blk + 1) * 128] = r[c]["y_own"][i * 128:(i + 1) * 128]
    y_sample = np.stack([r[c]["ys"].reshape(1, D) for c in range(NCORES)]).astype(np.float32)
    nk = np.stack([r[4 * b]["knew"].reshape(SEQ, 2, 128) for b in range(2)])[None].astype(np.float32)
    nv = np.stack([r[4 * b]["vnew"].reshape(SEQ, 2, 128) for b in range(2)])[None].astype(np.float32)
    nki = np.stack([r[4 * b]["kinew"].reshape(SEQ, 128) for b in range(2)])[None].astype(np.float32)
    ks = np.stack([r[c]["ks"].reshape(1, 2, 128) for c in range(NCORES)])[None].astype(np.float32)
    vs = np.stack([r[c]["vs"].reshape(1, 2, 128) for c in range(NCORES)])[None].astype(np.float32)
    kis = np.stack([r[c]["kis"].reshape(1, 128) for c in range(NCORES)])[None].astype(np.float32)
    gvs = np.stack([r[c]["gvs"].reshape(1, 1024) for c in range(NCORES)])[None].astype(np.float32)
    return (y_prompt, y_sample, nk, nv, nki, ks, vs, kis, gvs)
```

```python
import numpy as np
from contextlib import ExitStack
import concourse.bass as bass
import concourse.mybir as mybir
from concourse.bass_utils import run_bass_kernel_spmd

F32 = mybir.dt.float32
BF16 = mybir.dt.bfloat16
I32 = mybir.dt.int32
AF = mybir.ActivationFunctionType
ALU = mybir.AluOpType
AX = mybir.AxisListType

NCORES = 8
D = 2048
NCH = 16
SEQ = 4096
NB = 32
OWN = 8
TOK = 1024
BIG = 30000.0
CBIG = 1.0e6
NIT = 16
TOPK = 256
COMPUTE = ("pe", "act", "dve", "pool")

C_Q, C_K, C_V, C_QI, C_WI, C_KI, C_ZA, C_U, C_VB, C_ZB, C_GA, C_GB = (
    0, 1024, 1280, 1536, 3584, 3600, 3728, 4752, 5776, 6800, 7824, 9872)
FM_Q, FM_QI, FM_ZA, FM_U, FM_ZB, FM_G = 0, 8, 24, 32, 40, 48
N_FM = 80


class _Op:
    __slots__ = ("eng", "fn", "deps", "is_dma", "slot", "marked", "mark_idx", "cum", "idx")


class Prog:
    def __init__(self, nc, stack):
        self.nc = nc
        self.stack = stack
        self.ops = []
        self.last_w = {}
        self.readers = {}
        self.slot_cum = {}
        self.bar = []
        self.psx = {}
        self.engs = {"pe": nc.tensor, "act": nc.scalar, "dve": nc.vector, "pool": nc.gpsimd, "sp": nc.sync}

    def add(self, eng, fn, reads=(), writes=(), slot=None):
        op = _Op()
        op.eng = eng
        op.fn = fn
        op.is_dma = slot is not None
        op.slot = slot
        op.marked = False
        op.mark_idx = None
        op.idx = len(self.ops)
        op.deps = set()
        if op.is_dma:
            self.slot_cum[slot] = self.slot_cum.get(slot, 0) + 16
            op.cum = self.slot_cum[slot]
        for y in self.bar:
            self._dep(op, y, "raw")
        for r in reads:
            lw = self.last_w.get(r)
            if lw is not None:
                self._dep(op, lw, "raw")
        for w in writes:
            lw = self.last_w.get(w)
            if lw is not None:
                self._dep(op, lw, "waw")
            for rd in self.readers.get(w, ()):
                self._dep(op, rd, "war")
        for r in reads:
            self.readers.setdefault(r, []).append(op)
        for w in writes:
            self.last_w[w] = op
            self.readers[w] = []
        for res in list(reads) + list(writes):
            if isinstance(res, tuple) and len(res) >= 2 and res[0] == "ps":
                lastx = self.psx.get(res[1])
                if lastx is not None and lastx is not op and lastx.eng != op.eng:
                    op.deps.add(lastx.idx)
                    if not lastx.is_dma:
                        lastx.marked = True
                self.psx[res[1]] = op
        self.ops.append(op)
        return op

    def _dep(self, x, y, kind):
        if y is x:
            return
        if not y.is_dma and not x.is_dma and y.eng == x.eng:
            if kind != "raw" or x.eng == "pe":
                return
        x.deps.add(y.idx)
        if not y.is_dma:
            y.marked = True

    def barrier(self):
        last = {}
        for op in self.ops:
            if op.fn is None:
                continue
            key = ("slot", op.slot) if op.is_dma else ("eng", op.eng)
            last[key] = op
        self.bar = list(last.values())

    def emit(self):
        nc = self.nc
        sems = {}
        for e in COMPUTE:
            sems[("eng", e)] = self.stack.enter_context(nc.semaphore("c_" + e))
        for i, s in enumerate(self.slot_cum):
            sems[("slot", s)] = self.stack.enter_context(nc.semaphore("d%d" % i))
        cnt = {e: 0 for e in COMPUTE}
        for op in self.ops:
            if op.marked:
                cnt[op.eng] += 1
                op.mark_idx = cnt[op.eng]
        waited = {}
        nw = 0
        for op in self.ops:
            eng = self.engs[op.eng]
            need = {}
            for di in op.deps:
                y = self.ops[di]
                if y.is_dma:
                    k, v = ("slot", y.slot), y.cum
                else:
                    k, v = ("eng", y.eng), y.mark_idx
                if need.get(k, 0) < v:
                    need[k] = v
            wd = waited.setdefault(op.eng, {})
            for k, v in need.items():
                if wd.get(k, 0) >= v:
                    continue
                eng.wait_ge(sems[k], v)
                wd[k] = v
                nw += 1
            if op.fn is None:
                continue
            ins = op.fn(eng)
            if op.is_dma:
                ins.then_inc(sems[("slot", op.slot)], 16)
            elif op.marked:
                ins.then_inc(sems[("eng", op.eng)], 1)
        return dict(nops=len(self.ops), nwaits=nw, marks=cnt, nsems=len(sems))


def build_program(debug=False, upto=99):
    nc = bass.Bass("TRN2", target_bir_lowering=False)

    def din(name, shape, dtype=F32):
        return nc.dram_tensor(name, list(shape), dtype, kind="ExternalInput").ap()

    def dout(name, shape, dtype=F32):
        return nc.dram_tensor(name, list(shape), dtype, kind="ExternalOutput").ap()

    x_all = din("x_all", [SEQ, D])
    x_own = din("x_own", [TOK, D])
    qrel_d = din("qrel", [128, OWN])
    x_s = din("x_s", [1, D])
    ptab = din("ptab", [1, 128], I32)
    pool_ki8 = din("pool_ki8", [1280 * 8, 2048])
    pool_k16 = din("pool_k16", [1280 * 16, 2048])
    pool_v16 = din("pool_v16", [1280 * 16, 2048])
    g_in_d = din("g_in", [1, D])
    g_f_d = din("g_f", [1, D])
    ln_g_d = din("ln_g", [1, 1024])
    ln_b_d = din("ln_b", [1, 1024])
    bsp_d = din("bsp", [1, 1024])
    ws00_d = din("ws00", [1, 1024])
    bs0_d = din("bs0", [1, 1024])
    wspT_d = din("wspT", [128, 8 * 128])
    wkvki_d = din("wkvki", [128, NCH * 640])
    wfm_d = din("wfm", [N_FM, 128, NCH * 128])
    wwi_d = din("wwi", [128, NCH * 16])
    wvb_d = din("wvb", [128, NCH * 1024])
    wpa_d = din("wpa", [16, 128, 8 * 128])
    wpb_d = din("wpb", [16, 128, 8 * 128])
    wout_d = din("wout", [4, 128, NCH * 512])

    y_own = dout("y_own", [TOK, D])
    knew = dout("knew", [SEQ, 256])
    vnew = dout("vnew", [SEQ, 256])
    kinew = dout("kinew", [SEQ, 128])
    ys_o = dout("ys", [1, D])
    ks_o = dout("ks", [1, 256])
    vs_o = dout("vs", [1, 256])
    kis_o = dout("kis", [1, 128])
    gvs_o = dout("gvs", [1, 1024])
    dbg = {}
    if debug:
        dbg["qT"] = dout("dbg_qT", [128, 8 * TOK], BF16)
        dbg["kT"] = dout("dbg_kT", [128, 2 * SEQ], BF16)
        dbg["kiT"] = dout("dbg_kiT", [128, SEQ], BF16)
        dbg["qiT"] = dout("dbg_qiT", [128, 16 * TOK], BF16)
        dbg["wabs"] = dout("dbg_wabs", [128, 128])
        dbg["acc"] = dout("dbg_acc", [128, OWN * 4096])
        dbg["thr"] = dout("dbg_thr", [128, OWN])
        dbg["oaT"] = dout("dbg_oaT", [128, 8 * TOK], BF16)
        dbg["obT"] = dout("dbg_obT", [128, 8 * TOK], BF16)
        dbg["mixT"] = dout("dbg_mixT", [128, 16 * TOK], BF16)

    st = ExitStack()
    P = Prog(nc, st)
    ARENA_B = 207 * 1024
    arena = st.enter_context(nc.sbuf_tensor("arena", [128, ARENA_B // 2], BF16))
    psall = st.enter_context(nc.psum_tensor("psall", [128, 4096], F32))
    ps = [psall[:, i * 512:(i + 1) * 512] for i in range(8)]
    psb = [p.bitcast(BF16) for p in ps]


    def finish():
        fin = P.add("sp", None)
        lastd = {}
        for op in P.ops:
            if op.is_dma:
                lastd[op.slot] = op
        for op in lastd.values():
            fin.deps.add(op.idx)
        info = P.emit()
        st.close()
        return nc, info

    KB = 1024

    def buf(off_b, nbytes, dtype=BF16, parts=128):
        if off_b >= 20 * KB:
            off_b += KB
        assert off_b + nbytes <= ARENA_B, (off_b, nbytes)
        a = arena[0:parts, off_b // 2:(off_b + nbytes) // 2]
        if dtype != BF16:
            a = a.bitcast(dtype)
        return a

    def v3(ap, a):
        return ap.rearrange("p (a b) -> p a b", a=a)

    o = 0
    ident = buf(o, 256); o += 256
    identB4 = buf(o, 1024); o += 1024
    ones_bf = buf(o, 256); o += 256
    identf = buf(o, 512, F32); o += 512
    iota_f = buf(o, 2048, F32); o += 2048
    pow2 = buf(o, 128, F32); o += 128
    qrel = buf(o, 32, F32); o += 32
    epsc = buf(o, 16, F32); o += 16
    stat = buf(o, 256, F32); o += 256
    wabs = buf(o, 512, F32); o += 512
    wsgn = buf(o, 512, F32); o += 512
    sm = buf(o, 2048, F32); o += 2048
    iota_i = sm.bitcast(I32)
    smb = buf(o, 1024, BF16); o += 1024
    WsT = buf(o, 2048); o += 2048
    gbc = buf(o, 8192, F32); o += 8192
    ksv = buf(o, 2560, F32, parts=1); o += 2560
    assert o <= 21 * KB, o
    hsT = smb[:, 0:16]
    oazTs = smb[:, 16:24]
    obzTs = smb[:, 24:32]
    mixTs = smb[:, 32:48]

    kT_all = buf(20 * KB, 16 * KB)
    kiT_all = buf(36 * KB, 8 * KB)
    v_all = buf(44 * KB, 16 * KB)
    qT = buf(60 * KB, 16 * KB)
    qiT = buf(76 * KB, 32 * KB)
    hT_own = buf(108 * KB, 32 * KB)
    oaT = buf(140 * KB, 16 * KB)
    mixT = buf(156 * KB, 32 * KB)
    kT3 = v3(kT_all, 2)
    v3a = v3(v_all, 32)
    qT3 = v3(qT, 8)
    qiT3 = v3(qiT, 16)
    hT3 = v3(hT_own, 16)
    oaT3 = v3(oaT, 8)
    mixT3 = v3(mixT, 16)

    P.add("pool", lambda e: e.memset(identf, 1.0), writes=["identf"])
    P.add("pool", lambda e: e.affine_select(out=identf, in_=identf, pattern=[[-1, 128]], compare_op=ALU.is_equal,
                                            fill=0.0, base=0, channel_multiplier=1), reads=["identf"], writes=["identf"])
    P.add("dve", lambda e: e.tensor_copy(out=ident, in_=identf), reads=["identf"], writes=["ident"])
    for j in range(4):
        P.add("dve", lambda e, j=j: e.tensor_scalar(out=identB4[:, j * 128:(j + 1) * 128], in0=identf, scalar1=BIG, scalar2=None,
                                                    op0=ALU.mult), reads=["identf"], writes=["identB4"])
    P.add("pool", lambda e: e.memset(ones_bf, 1.0), writes=["ones"])
    P.add("pool", lambda e: e.iota(iota_i, pattern=[[1, 512]], base=0, channel_multiplier=0), writes=["iota_i"])
    P.add("dve", lambda e: e.tensor_copy(out=iota_f, in_=iota_i), reads=["iota_i"], writes=["iota"])
    for k in range(NIT + 1):
        P.add("pool", lambda e, k=k: e.memset(pow2[:, k:k + 1], float(2.0 ** -(k + 1))), writes=["pow2"])
    P.add("pool", lambda e: e.memset(epsc[:, 0:1], 1e-6), writes=["epsc"])
    P.add("pool", lambda e: e.memset(epsc[:, 1:2], 1e-5), writes=["epsc"])
    P.add("pool", lambda e: e.memset(epsc[:, 2:3], -0.5), writes=["epsc"])
    P.add("sp", lambda e: e.dma_start(out=qrel, in_=qrel_d), writes=["qrel"], slot="c_qrel")
    P.add("sp", lambda e: e.dma_start(out=gbc, in_=g_in_d.partition_broadcast(128)), writes=["gbc"], slot="c_gbc")

    def rms_block(src_ap, xb, xres, xslot, hb, hres, junk, sidx, gb_ap, nparts=128):
        ss = stat[0:nparts, sidx:sidx + 1]
        sd = stat[0:nparts, sidx + 1:sidx + 2]
        rs = stat[0:nparts, sidx + 2:sidx + 3]
        P.add("sp", lambda e: e.dma_start(out=xb, in_=src_ap), writes=[xres], slot=xslot)
        P.add("act", lambda e: e.activation(out=junk, in_=xb, func=AF.Square, accum_out=ss),
              reads=[xres], writes=["junk", ("st", sidx)])
        P.add("act", lambda e: e.activation(out=sd, in_=ss, func=AF.Sqrt, scale=1.0 / D, bias=epsc[0:nparts, 0:1]),
              reads=[("st", sidx), "epsc"], writes=[("st", sidx + 1)])
        P.add("dve", lambda e: e.reciprocal(out=rs, in_=sd), reads=[("st", sidx + 1)], writes=[("st", sidx + 2)])
        P.add("dve", lambda e: e.scalar_tensor_tensor(out=hb, in0=xb, scalar=rs, in1=gb_ap, op0=ALU.mult, op1=ALU.mult),
              reads=[xres, ("st", sidx + 2), "gbc"], writes=[hres])

    def transposes16(hb, hres, bankA, bankB):
        for c in range(16):
            bk = bankA if c < 8 else bankB
            cc = c % 8
            P.add("pe", lambda e, c=c, bk=bk, cc=cc: e.transpose(out=psb[bk][:, cc * 128:(cc + 1) * 128],
                                                                 in_=hb[:, c * 128:(c + 1) * 128], identity=ident),
                  reads=[hres, "ident"], writes=[("ps", bk)])

    if upto <= 0:
        return finish()
    S1 = 60 * KB
    xbuf = [buf(S1 + i * 8 * KB, 8 * KB, F32) for i in range(2)]
    hbuf = [buf(S1 + 16 * KB + i * 4 * KB, 4 * KB) for i in range(2)]
    junk = buf(S1 + 24 * KB, 4 * KB)
    hTblk = [buf(S1 + 28 * KB + i * 4 * KB, 4 * KB) for i in range(2)]
    Wkvki = buf(S1 + 36 * KB, 20 * KB)
    kvf = [buf(S1 + 56 * KB + i * 2560, 2560, F32) for i in range(2)]
    kb16 = [buf(S1 + 62 * KB + i * 768, 768) for i in range(2)]
    Wk3 = v3(Wkvki, 16)
    for q4 in range(8):
        P.add("pool", lambda e, q4=q4: e.dma_start(out=Wk3[:, q4 * 2:(q4 + 1) * 2, :],
                                                  in_=v3(wkvki_d, 16)[:, q4 * 2:(q4 + 1) * 2, :]),
              writes=["Wkvki"], slot="wkvki%d" % (q4 % 4))
    xs_f = buf(S1 + 64 * KB, 8 * KB, F32, parts=1)
    hs_f = buf(S1 + 72 * KB, 8 * KB, F32, parts=1)
    kvs = buf(S1 + 80 * KB, 2560, F32, parts=1)
    import os as _os
    if _os.environ.get('SKIP_S1'):
        P.add('pool', lambda e: e.memset(hsT, 0.0), writes=['hsT'])
    else:
        rms_block(x_s, xs_f, "xs_f", "xs_f", hs_f, "hs_f", junk[0:1, :], 8, gbc[0:1, :], nparts=1)
        for c in range(16):
            P.add("pe", lambda e, c=c: e.transpose(out=ps[4][:, c:c + 1], in_=hs_f[0:1, c * 128:(c + 1) * 128], identity=identf[0:1, 0:1]),
                  reads=["hs_f", "identf"], writes=[("ps", 4)])
        P.add("act", lambda e: e.copy(out=hsT, in_=ps[4][:, 0:16]), reads=[("ps", 4)], writes=["hsT"])
        for c in range(16):
            P.add("pe", lambda e, c=c: e.matmul(ps[6][0:1, 0:512], lhsT=hsT[:, c:c + 1], rhs=Wk3[:, c, 0:512], start=(c == 0), stop=(c == 15)),
                  reads=["hsT", "Wkvki"], writes=[("ps", 6)])
            P.add("pe", lambda e, c=c: e.matmul(ps[7][0:1, 0:128], lhsT=hsT[:, c:c + 1], rhs=Wk3[:, c, 512:640], start=(c == 0), stop=(c == 15)),
                  reads=["hsT", "Wkvki"], writes=[("ps", 7)])
        P.add("dve", lambda e: e.tensor_copy(out=kvs[:, 0:512], in_=ps[6][0:1, 0:512]), reads=[("ps", 6)], writes=["kvs0"])
        P.add("dve", lambda e: e.tensor_copy(out=kvs[:, 512:640], in_=ps[7][0:1, 0:128]), reads=[("ps", 7)], writes=["kvs1"])
        P.add("dve", lambda e: e.tensor_copy(out=ksv, in_=kvs), reads=["kvs0", "kvs1"], writes=["ksv"])
        P.add("sp", lambda e: e.dma_start(out=ks_o, in_=kvs[:, 0:256]), reads=["kvs0"], writes=["o_ks"], slot="o_ks")
        P.add("sp", lambda e: e.dma_start(out=vs_o, in_=kvs[:, 256:512]), reads=["kvs0"], writes=["o_vs"], slot="o_vs")
        P.add("sp", lambda e: e.dma_start(out=kis_o, in_=kvs[:, 512:640]), reads=["kvs1"], writes=["o_kis"], slot="o_kis")
    def s1_front(b):
        pb = b % 2
        rms_block(x_all[b * 128:(b + 1) * 128, :], xbuf[pb], ("xbuf", pb), "xbuf%d" % pb, hbuf[pb], ("hbuf", pb), junk, 4 * pb, gbc)

    def s1_t16(b, xbuf=xbuf, hbuf=hbuf, hTblk=hTblk):
        pb = b % 2
        hb, htb = hbuf[pb], hTblk[pb]
        bA, bB = (0, 1) if pb == 0 else (2, 3)
        transposes16(hb, ("hbuf", pb), bA, bB)
        P.add("act", lambda e: e.copy(out=htb[:, 0:1024], in_=psb[bA][:, 0:1024]), reads=[("ps", bA)], writes=[("htb", pb, 0)])
        P.add("dve", lambda e: e.tensor_copy(out=htb[:, 1024:2048], in_=psb[bB][:, 0:1024]), reads=[("ps", bB)], writes=[("htb", pb, 1)])

    def s1_mm(b, hTblk=hTblk, kvf=kvf, kb16=kb16):
        pb = b % 2
        htb3 = v3(hTblk[pb], 16)
        bKV, bKI = (4, 5) if pb == 0 else (6, 7)
        for c in range(16):
            P.add("pe", lambda e, c=c: e.matmul(ps[bKV][:, 0:512], lhsT=htb3[:, c, :], rhs=Wk3[:, c, 0:512], start=(c == 0), stop=(c == 15)),
                  reads=[("htb", pb, c // 8), "Wkvki"], writes=[("ps", bKV)])
            P.add("pe", lambda e, c=c: e.matmul(ps[bKI][:, 0:128], lhsT=htb3[:, c, :], rhs=Wk3[:, c, 512:640], start=(c == 0), stop=(c == 15)),
                  reads=[("htb", pb, c // 8), "Wkvki"], writes=[("ps", bKI)])
        kf = kvf[pb]
        k16 = kb16[pb]
        P.add("dve", lambda e: e.tensor_copy(out=kf[:, 0:512], in_=ps[bKV][:, 0:512]), reads=[("ps", bKV)], writes=[("kvf", pb, 0)])
        P.add("act", lambda e: e.copy(out=kf[:, 512:640], in_=ps[bKI][:, 0:128]), reads=[("ps", bKI)], writes=[("kvf", pb, 1)])
        P.add("dve", lambda e: e.tensor_copy(out=k16[:, 0:256], in_=ps[bKV][:, 0:256]), reads=[("ps", bKV)], writes=[("kb16", pb)])
        P.add("act", lambda e: e.copy(out=k16[:, 256:384], in_=ps[bKI][:, 0:128]), reads=[("ps", bKI)], writes=[("kb16", pb)])
        P.add("dve", lambda e: e.tensor_copy(out=v3a[:, b, :], in_=ps[bKV][:, 256:512]), reads=[("ps", bKV)], writes=[("v_all", b)])
        rows = slice(b * 128, (b + 1) * 128)
        P.add("pool", lambda e: e.dma_start(out=knew[rows, :], in_=kf[:, 0:256]), reads=[("kvf", pb, 0)], writes=[("o_k", pb)], slot="o_k%d" % pb)
        P.add("pool", lambda e: e.dma_start(out=vnew[rows, :], in_=kf[:, 256:512]), reads=[("kvf", pb, 0)], writes=[("o_v", pb)], slot="o_v%d" % pb)
        P.add("pool", lambda e: e.dma_start(out=kinew[rows, :], in_=kf[:, 512:640]), reads=[("kvf", pb, 1)], writes=[("o_ki", pb)], slot="o_ki%d" % pb)

    def s1_tk(b, kb16=kb16):
        pb = b % 2
        k16 = kb16[pb]
        bKI = 5 if pb == 0 else 7
        for j in range(3):
            P.add("pe", lambda e, j=j: e.transpose(out=psb[bKI][:, 256 + j * 128:256 + (j + 1) * 128], in_=k16[:, j * 128:(j + 1) * 128], identity=ident),
                  reads=[("kb16", pb), "ident"], writes=[("ps", bKI)])
        P.add("act", lambda e: e.copy(out=kT3[:, :, b * 128:(b + 1) * 128], in_=v3(psb[bKI][:, 256:512], 2)), reads=[("ps", bKI)], writes=[("kT", b)])
        P.add("act", lambda e: e.copy(out=kiT_all[:, b * 128:(b + 1) * 128], in_=psb[bKI][:, 512:640]), reads=[("ps", bKI)], writes=[("kiT", b)])

    s1_front(0)
    s1_front(1)
    s1_t16(0)
    s1_t16(1)
    for b in range(NB):
        if b + 2 < NB:
            s1_front(b + 2)
        s1_mm(b)
        if b + 2 < NB:
            s1_t16(b + 2)
        s1_tk(b)

    if upto <= 1:
        return finish()
    P.barrier()
    S2 = 140 * KB
    xbuf = [buf(S2 + i * 8 * KB, 8 * KB, F32) for i in range(2)]
    hbuf = [buf(S2 + 16 * KB + i * 4 * KB, 4 * KB) for i in range(2)]
    junk = buf(S2 + 24 * KB, 4 * KB)
    wbuf = [buf(S2 + 28 * KB + i * 4 * KB, 4 * KB) for i in range(3)]
    wwi = buf(S2 + 40 * KB, 512)

    def own_pass(xbuf, hbuf, junk):
        def front(i):
            pb = i % 2
            rms_block(x_own[i * 128:(i + 1) * 128, :], xbuf[pb], ("xbuf", pb), "xbuf%d" % pb, hbuf[pb], ("hbuf", pb), junk, 4 * pb, gbc)

        def back(i):
            pb = i % 2
            hb = hbuf[pb]
            bA, bB = (0, 1) if pb == 0 else (2, 3)
            transposes16(hb, ("hbuf", pb), bA, bB)
            P.add("act", lambda e, bA=bA, i=i: e.copy(out=hT3[:, 0:8, i * 128:(i + 1) * 128], in_=v3(psb[bA][:, 0:1024], 8)),
                  reads=[("ps", bA)], writes=[("hT", i, 0)])
            P.add("dve", lambda e, bB=bB, i=i: e.tensor_copy(out=hT3[:, 8:16, i * 128:(i + 1) * 128], in_=v3(psb[bB][:, 0:1024], 8)),
                  reads=[("ps", bB)], writes=[("hT", i, 1)])

        front(0)
        for i in range(OWN):
            if i + 1 < OWN:
                front(i + 1)
            back(i)

    own_pass(xbuf, hbuf, junk)

    hT_reads = [("hT", i, h) for i in range(OWN) for h in range(2)]
    fm_state = {"n": 0}

    fm_pref = {}

    def fm_load(cb):
        if cb in fm_pref:
            return fm_pref.pop(cb)
        ws = fm_state.get("nl", 0) % 3
        fm_state["nl"] = fm_state.get("nl", 0) + 1
        wsl = wbuf[ws]
        P.add("pool", lambda e: e.dma_start(out=wsl, in_=wfm_d[cb]), writes=[("wbuf", ws)], slot="wbuf%d" % ws)
        return ws

    def fm_prefetch(cb):
        if cb < N_FM and cb not in fm_pref:
            fm_pref[cb] = fm_load(cb)

    def proj_fm(cb, scol, evac, pa=None):
        n = fm_state["n"]
        fm_state["n"] += 1
        ws = fm_load(cb)
        wsl = wbuf[ws]
        w3 = v3(wsl, 16)
        if pa is None:
            pa = 2 * (n % 3)
        for c in range(16):
            for hf in range(2):
                P.add("pe", lambda e, c=c, hf=hf: e.matmul(ps[pa + hf][:, 0:512], lhsT=w3[:, c, :], rhs=hT3[:, c, hf * 512:(hf + 1) * 512],
                                                          start=(c == 0), stop=(c == 15)),
                      reads=[("wbuf", ws)] + hT_reads, writes=[("ps", pa + hf)])
            P.add("pe", lambda e, c=c: e.matmul(ps[6][:, scol:scol + 1], lhsT=w3[:, c, :], rhs=hsT[:, c:c + 1],
                                                start=(c == 0), stop=(c == 15)),
                  reads=[("wbuf", ws), "hsT"], writes=[("ps", 6)])
        evac(pa)

    def evac_copy(dst3, h):
        def f(pa):
            P.add("act", lambda e: e.copy(out=dst3[:, h, 0:512], in_=ps[pa][:, 0:512]), reads=[("ps", pa)], writes=[(id(dst3), h, 0)])
            P.add("dve", lambda e: e.tensor_copy(out=dst3[:, h, 512:1024], in_=ps[pa + 1][:, 0:512]), reads=[("ps", pa + 1)],
                  writes=[(id(dst3), h, 1)])
        return f

    for h in range(8):
        proj_fm(FM_Q + h, h, evac_copy(qT3, h))
    for h in range(16):
        proj_fm(FM_QI + h, 8 + h, evac_copy(qiT3, h))
    P.add("dve", lambda e: e.tensor_copy(out=sm[:, 0:24], in_=ps[6][:, 0:24]), reads=[("ps", 6)], writes=["sm_q"])
    P.add("pool", lambda e: e.dma_start(out=wwi, in_=wwi_d), writes=["wwi"], slot="wwi")
    wwi3 = v3(wwi, 16)
    for i in range(OWN):
        for c in range(16):
            P.add("pe", lambda e, i=i, c=c: e.matmul(ps[7][:, i * 16:(i + 1) * 16], lhsT=hT3[:, c, i * 128:(i + 1) * 128], rhs=wwi3[:, c, :],
                                                    start=(c == 0), stop=(c == 15)),
                  reads=["wwi"] + hT_reads, writes=[("ps", 7)])
    P.add("act", lambda e: e.copy(out=wabs, in_=ps[7][:, 0:128]), reads=[("ps", 7)], writes=["wabs"])
    P.add("act", lambda e: e.activation(out=wsgn, in_=ps[7][:, 0:128], func=AF.Sign), reads=[("ps", 7)], writes=["wsgn"])
    for c in range(16):
        P.add("pe", lambda e, c=c: e.matmul(ps[7][0:16, 128:129], lhsT=wwi3[:, c, :], rhs=hsT[:, c:c + 1], start=(c == 0), stop=(c == 15)),
              reads=["wwi", "hsT"], writes=[("ps", 7)])
    P.add("dve", lambda e: e.tensor_copy(out=sm[0:16, 120:121], in_=ps[7][0:16, 128:129]), reads=[("ps", 7)], writes=["sm_w"])
    if debug:
        P.add("sp", lambda e: e.dma_start(out=dbg["qT"], in_=qT), reads=[(id(qT3), h, j) for h in range(8) for j in range(2)], writes=["dbg_dq"], slot="dbg0")
        P.add("sp", lambda e: e.dma_start(out=dbg["qiT"], in_=qiT), reads=[(id(qiT3), h, j) for h in range(16) for j in range(2)], writes=["dbg_dqi"], slot="dbg1")
        P.add("sp", lambda e: e.dma_start(out=dbg["kT"], in_=kT_all), reads=[("kT", b) for b in range(NB)], writes=["dbg_dk"], slot="dbg2")
        P.add("sp", lambda e: e.dma_start(out=dbg["kiT"], in_=kiT_all), reads=[("kiT", b) for b in range(NB)], writes=["dbg_dki"], slot="dbg3")
        P.add("sp", lambda e: e.dma_start(out=dbg["wabs"], in_=wabs), reads=["wabs"], writes=["dbg_dwa"], slot="dbg4")

    if upto <= 2:
        return finish()
    P.barrier()
    acc = buf(108 * KB, 16 * KB, F32)
    negm = buf(124 * KB, 8 * KB)
    jnk3 = buf(132 * KB, 8 * KB)
    S3 = 156 * KB
    Rb = [buf(S3 + i * 2 * KB, 2 * KB) for i in range(3)]
    dsg = [buf(S3 + 16 * KB + i * 4 * KB, 4 * KB) for i in range(2)]
    PT = [buf(S3 + 6 * KB + i * KB, KB) for i in range(3)]
    pen = buf(S3 + 9 * KB, 2 * KB, F32)
    tmn = buf(S3 + 11 * KB, 2 * KB, F32)
    rdens = [buf(S3 + 13 * KB, 2 * KB, F32), buf(S3 + 41 * KB, 2 * KB, F32)]
    bs = buf(S3 + 15 * KB, 512, F32)
    HK = bs[:, 0:NIT + 1]
    TC = bs[:, 32:32 + NIT + 2]
    cnt = bs[:, 64:65]
    tmpc = bs[:, 65:66]
    rmax = bs[:, 66:67]
    rmin = bs[:, 67:68]
    rmin2 = bs[:, 68:69]
    lo0 = bs[:, 69:70]
    w0 = bs[:, 70:71]
    thr = bs[:, 72:80]
    NTC = bs[:, 80:80 + NIT + 2]
    sA = bs[:, 100:101]
    cnt2 = bs[:, 101:102]
    SCALE = float(128 ** -0.5)
    rcount = {"s": 0, "st": 0, "pt": 0}
    accs = [acc, buf(S3 + 25 * KB, 16 * KB, F32)]

    def idx_prep(i):
        dg3 = v3(dsg[i % 2], 16)
        for h in range(16):
            P.add("pool", lambda e, h=h: e.tensor_scalar(out=dg3[:, h, :], in0=identf, scalar1=wabs[:, i * 16 + h:i * 16 + h + 1], scalar2=0.0,
                                                        op0=ALU.mult, op1=ALU.add),
                  reads=["identf", "wabs"], writes=[("dsg", i % 2)])

    def idx_group(i, kg):
        ac = accs[i % 2]
        dg3 = v3(dsg[i % 2], 16)
        tok = slice(i * 128, (i + 1) * 128)
        cols = slice(kg * 512, (kg + 1) * 512)

        def emit_s(j):
            n = rcount["s"]
            rcount["s"] += 1
            pa = 2 * (n % 2)
            rb = Rb[n % 3]
            for u in range(2):
                h = 2 * j + u
                P.add("pe", lambda e, h=h, u=u: e.matmul(ps[pa + u][:, 0:512], lhsT=qiT3[:, h, tok], rhs=kiT_all[:, cols], start=True, stop=True),
                      reads=["qiT", "kiT"], writes=[("ps", pa + u)])
            P.add("act", lambda e: e.activation(out=rb, in_=psall[:, pa * 512:(pa + 2) * 512], func=AF.Relu),
                  reads=[("ps", pa), ("ps", pa + 1)], writes=[("Rb", n % 3)])
            return rb, n % 3

        def emit_acc(j, rbn):
            rb, rn = rbn
            for u in range(2):
                h = 2 * j + u
                P.add("pe", lambda e, h=h, u=u: e.matmul(ps[4][:, 0:512], lhsT=dg3[:, h, :], rhs=rb[:, u * 512:(u + 1) * 512], start=(h == 0), stop=(h == 15)),
                      reads=[("Rb", rn), ("dsg", i % 2)], writes=[("ps", 4)])

        prev = emit_s(0)
        for j in range(8):
            nxt = emit_s(j + 1) if j < 7 else None
            emit_acc(j, prev)
            prev = nxt
        P.add("dve", lambda e: e.tensor_copy(out=ac[:, cols], in_=ps[4][:, 0:512]), reads=[("ps", 4)], writes=[("acc", i % 2, kg)])

    def bis_setup(i):
        ac = accs[i % 2]
        span = 512 * (i + 1)
        accr = [("acc", i % 2, kg) for kg in range(i + 1)]
        last = slice(i * 512, (i + 1) * 512)
        lres = ("acc", i % 2, i)
        P.add("dve", lambda e: e.tensor_scalar(out=pen, in0=iota_f, scalar1=qrel[:, i:i + 1], scalar2=CBIG, op0=ALU.is_gt, op1=ALU.mult),
              reads=["iota", "qrel"], writes=["pen"])
        P.add("dve", lambda e: e.tensor_tensor(out=tmn, in0=ac[:, last], in1=pen, op=ALU.add), reads=["pen", lres], writes=["tmn"])
        P.add("dve", lambda e: e.tensor_reduce(out=rmin, in_=tmn, axis=AX.X, op=ALU.min), reads=["tmn"], writes=["rmin"])
        P.add("dve", lambda e: e.tensor_tensor(out=ac[:, last], in0=ac[:, last], in1=pen, op=ALU.subtract), reads=["pen", lres], writes=[lres])
        P.add("dve", lambda e: e.tensor_reduce(out=rmax, in_=ac[:, 0:span], axis=AX.X, op=ALU.max), reads=accr, writes=["rmax"])
        if i > 0:
            P.add("dve", lambda e: e.tensor_reduce(out=rmin2, in_=ac[:, 0:i * 512], axis=AX.X, op=ALU.min), reads=accr, writes=["rmin2"])
            P.add("dve", lambda e: e.tensor_tensor(out=rmin, in0=rmin, in1=rmin2, op=ALU.min), reads=["rmin", "rmin2"], writes=["rmin"])
        P.add("dve", lambda e: e.tensor_tensor(out=w0, in0=rmax, in1=rmin, op=ALU.subtract), reads=["rmax", "rmin"], writes=["w0"])
        P.add("dve", lambda e: e.tensor_scalar(out=w0, in0=w0, scalar1=float(2.0 ** -10), scalar2=1e-3, op0=ALU.mult, op1=ALU.add), reads=["w0"], writes=["w0b"])
        P.add("dve", lambda e: e.tensor_tensor(out=lo0, in0=rmin, in1=w0, op=ALU.subtract), reads=["rmin", "w0b"], writes=["lo0"])
        P.add("dve", lambda e: e.tensor_tensor(out=w0, in0=rmax, in1=lo0, op=ALU.subtract), reads=["rmax", "lo0"], writes=["w0c"])
        P.add("dve", lambda e: e.tensor_scalar(out=HK, in0=pow2[:, 0:NIT + 1], scalar1=w0, scalar2=None, op0=ALU.mult), reads=["pow2", "w0c"], writes=["HK"])
        P.add("dve", lambda e: e.tensor_tensor(out=TC[:, 0:1], in0=lo0, in1=HK[:, 0:1], op=ALU.add), reads=["lo0", "HK"], writes=[("TC", 0)])

    def bis_iter(i, k):
        ac = accs[i % 2]
        span = 512 * (i + 1)
        accr = [("acc", i % 2, kg) for kg in range(i + 1)]
        P.add("dve", lambda e: e.tensor_scalar(out=jnk3[:, 0:span], in0=ac[:, 0:span], scalar1=TC[:, k:k + 1], scalar2=None,
                                              op0=ALU.is_ge, op1=ALU.add, accum_out=cnt),
              reads=accr + [("TC", k)], writes=["jnk3", "cnt"])
        P.add("dve", lambda e: e.scalar_tensor_tensor(out=tmpc, in0=cnt, scalar=TOPK - 0.5, in1=HK[:, k:k + 1], op0=ALU.is_ge, op1=ALU.mult),
              reads=["cnt", "HK"], writes=["tmpc"])
        P.add("dve", lambda e: e.scalar_tensor_tensor(out=TC[:, k + 1:k + 2], in0=tmpc, scalar=HK[:, k + 1:k + 2], in1=TC[:, k:k + 1],
                                                      op0=ALU.subtract, op1=ALU.add),
              reads=["tmpc", "HK", ("TC", k)], writes=[("TC", k + 1)])

    def bis_final(i):
        ac = accs[i % 2]
        span = 512 * (i + 1)
        accr = [("acc", i % 2, kg) for kg in range(i + 1)]
        P.add("dve", lambda e: e.tensor_tensor(out=thr[:, i:i + 1], in0=TC[:, NIT:NIT + 1], in1=HK[:, NIT:NIT + 1], op=ALU.subtract),
              reads=[("TC", NIT), "HK"], writes=[("thr", i)])
        P.add("dve", lambda e: e.tensor_scalar(out=negm[:, 0:span], in0=ac[:, 0:span], scalar1=thr[:, i:i + 1], scalar2=1.0,
                                              op0=ALU.is_ge, op1=ALU.subtract),
              reads=accr + [("thr", i)], writes=["negm"])
        if debug:
            P.add("sp", lambda e: e.dma_start(out=dbg["acc"][:, i * 4096:i * 4096 + span], in_=ac[:, 0:span]), reads=accr, writes=["dbg_dacc"], slot="dbg5")

    def attention(i, mid_cb=None):
        tok = slice(i * 128, (i + 1) * 128)
        nkb = 4 * (i + 1)
        for g in range(2):
            qrhs = qT3[:, 4 * g:4 * g + 4, tok]

            def emit_st(kb, g=g, qrhs=qrhs):
                n = rcount["st"]
                rcount["st"] += 1
                bk = 1 + 2 * (n % 2)
                P.add("pe", lambda e: e.matmul(ps[bk][:, 0:512], lhsT=kT3[:, g, kb * 128:(kb + 1) * 128], rhs=qrhs, start=True, stop=False),
                      reads=["qT", "kT"], writes=[("ps", bk)])
                P.add("pe", lambda e: e.matmul(ps[bk][:, 0:512], lhsT=negm[:, kb * 128:(kb + 1) * 128], rhs=identB4, start=False, stop=True),
                      reads=["negm", "identB4"], writes=[("ps", bk)])
                return bk

            bo, bd = (5, 6) if g == 0 else (7, 2)

            def emit_pv(kb, bk, g=g, nkb=nkb, bo=bo, bd=bd):
                n = rcount["pt"]
                rcount["pt"] += 1
                pt = PT[n % 3]
                P.add("act", lambda e: e.activation(out=pt, in_=ps[bk][:, 0:512], func=AF.Exp, scale=SCALE), reads=[("ps", bk)], writes=[("PT", n % 3)])
                P.add("pe", lambda e: e.matmul(ps[bo][:, 0:512], lhsT=v3a[:, kb, g * 128:(g + 1) * 128], rhs=pt, start=(kb == 0), stop=(kb == nkb - 1)),
                      reads=[("PT", n % 3), "v_all"], writes=[("ps", bo)])
                P.add("pe", lambda e: e.matmul(ps[bd][:, 0:512], lhsT=ones_bf, rhs=pt, start=(kb == 0), stop=(kb == nkb - 1)),
                      reads=[("PT", n % 3), "ones"], writes=[("ps", bd)])

            prev = None
            for kb in range(nkb):
                bk = emit_st(kb)
                if prev is not None:
                    emit_pv(*prev)
                prev = (kb, bk)
            emit_pv(*prev)
            if mid_cb is not None:
                mid_cb()
            rden = rdens[g]
            P.add("dve", lambda e, bd=bd, rden=rden: e.reciprocal(out=rden, in_=ps[bd][:, 0:512]), reads=[("ps", bd)], writes=[("rden", g)])
            P.add("dve", lambda e, g=g, bo=bo, rden=rden: e.tensor_tensor(out=oaT3[:, 4 * g:4 * g + 4, tok], in0=v3(ps[bo][:, 0:512], 4), in1=v3(rden, 4), op=ALU.mult),
                  reads=[("ps", bo), ("rden", g)], writes=[("oaT", i, g)])

    kd = {}
    setup_done = set()
    PRE_IT = 10

    def ensure_setup(i):
        if i not in setup_done:
            bis_setup(i)
            setup_done.add(i)
            kd[i] = 0

    def emit_iters(i, upto_k):
        while kd[i] < upto_k:
            bis_iter(i, kd[i])
            kd[i] += 1

    idx_prep(0)
    idx_group(0, 0)
    for i in range(OWN):
        ensure_setup(i)
        if i + 1 < OWN:
            idx_prep(i + 1)
            ngr = i + 2
            k0 = kd[i]
            for kg in range(ngr):
                idx_group(i + 1, kg)
                emit_iters(i, k0 + ((NIT - k0) * (kg + 1)) // ngr)
        emit_iters(i, NIT)
        bis_final(i)
        if i + 1 < OWN:
            ensure_setup(i + 1)
            attention(i, mid_cb=lambda i=i: emit_iters(i + 1, kd[i + 1] + PRE_IT // 2))
        else:
            attention(i)
    if debug:
        P.add("sp", lambda e: e.dma_start(out=dbg["thr"], in_=thr), reads=[("thr", i) for i in range(OWN)], writes=["dbg_dthr"], slot="dbg6")
        P.add("sp", lambda e: e.dma_start(out=dbg["oaT"], in_=oaT), reads=[("oaT", i, g) for i in range(OWN) for g in range(2)], writes=["dbg_doa"], slot="dbg7")

    if upto <= 3:
        return finish()

    P.barrier()
    oaTs = sm[:, 112:120]
    B3 = 20 * KB
    idxp_i = buf(B3, 64, I32)
    idx8 = buf(B3 + 64, 64, I32)
    idx16 = buf(B3 + 128, 64, I32)
    pgf = buf(B3 + 192, 64, F32)
    ones_f = buf(B3 + 512, 512, F32)
    scb = buf(B3 + 1024, 1024, F32)
    sc = scb[:, 0:129]
    vld = buf(B3 + 2048, 1024, F32)[:, 0:129]
    kib = [buf(B3 + 4 * KB + i * 4 * KB, 4 * KB) for i in range(2)]
    kidT = [buf(B3 + 12 * KB + i * KB, KB) for i in range(2)]
    Rs = [buf(B3 + 14 * KB + i * 2 * KB, KB) for i in range(2)]
    Kc = [buf(B3 + 18 * KB + i * 4 * KB, 4 * KB) for i in range(2)]
    KTb = [buf(B3 + 26 * KB + i * 4 * KB, 4 * KB) for i in range(2)]
    Kx = buf(B3 + 34 * KB, 512)
    Vx = buf(B3 + 35 * KB, 512)
    Psb = buf(B3 + 36 * KB, 4224, F32)
    Pbf = buf(B3 + 41 * KB, 2112)
    Pred = buf(B3 + 44 * KB, 64, F32)
    smx = buf(B3 + 45 * KB, 256, F32)
    osb = buf(B3 + 46 * KB, 1024, F32)
    qiTs_bf = smb[:, 56:72]
    qTs_bf = smb[:, 48:56]
    w_s = smb[0:16, 74:75]
    P.add("dve", lambda e: e.tensor_copy(out=w_s, in_=sm[0:16, 120:121]), reads=["sm_w"], writes=["w_s"])
    IOA = bass.IndirectOffsetOnAxis
    P.add("sp", lambda e: e.dma_start(out=idxp_i[:, 0:1], in_=ptab.rearrange("o p -> p o")), writes=["idxp"], slot="s_idx")
    P.add("dve", lambda e: e.tensor_copy(out=pgf[:, 0:1], in_=idxp_i[:, 0:1]), reads=["idxp"], writes=["pgf0"])
    P.add("dve", lambda e: e.tensor_scalar(out=pgf[:, 1:2], in0=pgf[:, 0:1], scalar1=8.0, scalar2=None, op0=ALU.mult), reads=["pgf0"], writes=["pgf1"])
    P.add("dve", lambda e: e.tensor_scalar(out=pgf[:, 2:3], in0=pgf[:, 0:1], scalar1=16.0, scalar2=None, op0=ALU.mult), reads=["pgf0"], writes=["pgf2"])
    P.add("dve", lambda e: e.tensor_scalar(out=idx8, in0=iota_f[:, 0:16], scalar1=pgf[:, 1:2], scalar2=None, op0=ALU.add), reads=["pgf1", "iota"], writes=["idx8"])
    P.add("dve", lambda e: e.tensor_scalar(out=idx16, in0=iota_f[:, 0:16], scalar1=pgf[:, 2:3], scalar2=None, op0=ALU.add), reads=["pgf2", "iota"], writes=["idx16"])
    P.add("pool", lambda e: e.memset(ones_f, 1.0), writes=["ones_f"])
    P.add("dve", lambda e: e.tensor_copy(out=qiTs_bf, in_=sm[:, 8:24]), reads=["sm_q"], writes=["qis_bf"])
    P.add("dve", lambda e: e.tensor_copy(out=qTs_bf, in_=sm[:, 0:8]), reads=["sm_q"], writes=["qs_bf"])
    ng = 0
    for ch in range(8):
        kb_ = kib[ch % 2]
        P.add("pool", lambda e, ch=ch, kb_=kb_: e.indirect_dma_start(out=kb_, out_offset=None, in_=pool_ki8,
                                                                    in_offset=IOA(ap=idx8[:, ch:ch + 1], axis=0)),
              reads=["idx8"], writes=[("kib", ch % 2)], slot="s_kib%d" % (ch % 2))
        for grp in range(4):
            tb_ = ng % 2
            for j in range(4):
                pos_l = 4 * grp + j
                P.add("pe", lambda e, kb_=kb_, pos_l=pos_l, j=j, tb_=tb_: e.transpose(out=psb[tb_][:, j * 128:(j + 1) * 128],
                                                                                 in_=kb_[:, pos_l * 128:(pos_l + 1) * 128], identity=ident),
                      reads=[("kib", ch % 2), "ident"], writes=[("ps", tb_)])
            kt_ = kidT[ng % 2]
            P.add("act", lambda e, kt_=kt_, tb_=tb_: e.copy(out=kt_, in_=psb[tb_][:, 0:512]), reads=[("ps", tb_)], writes=[("kidT", ng % 2)])
            P.add("pe", lambda e, kt_=kt_, tb_=tb_: e.matmul(ps[2 + tb_][0:16, 0:512], lhsT=qiTs_bf, rhs=kt_, start=True, stop=True),
                  reads=[("kidT", ng % 2), "qis_bf"], writes=[("ps", 2 + tb_)])
            rs_ = Rs[ng % 2]
            P.add("act", lambda e, rs_=rs_, tb_=tb_: e.activation(out=rs_[0:16, :], in_=ps[2 + tb_][0:16, 0:512], func=AF.Relu),
                  reads=[("ps", 2 + tb_)], writes=[("Rs", ng % 2)])
            for j in range(4):
                pos = ch * 16 + grp * 4 + j
                P.add("pe", lambda e, rs_=rs_, j=j, pos=pos: e.matmul(ps[4][:, pos:pos + 1], lhsT=rs_[0:16, j * 128:(j + 1) * 128], rhs=w_s,
                                                                   start=True, stop=True),
                      reads=[("Rs", ng % 2), "w_s"], writes=[("ps", 4)])
            ng += 1
    P.add("pe", lambda e: e.transpose(out=ps[5][:, 0:1], in_=ksv[0:1, 512:640], identity=identf[0:1, 0:1]), reads=["ksv", "identf"], writes=[("ps", 5)])
    P.add("act", lambda e: e.copy(out=smb[:, 72:73], in_=ps[5][:, 0:1]), reads=[("ps", 5)], writes=["kisT"])
    P.add("pe", lambda e: e.matmul(ps[5][0:16, 8:9], lhsT=qiTs_bf, rhs=smb[:, 72:73], start=True, stop=True), reads=["kisT", "qis_bf"], writes=[("ps", 5)])
    P.add("act", lambda e: e.activation(out=smb[0:16, 76:77], in_=ps[5][0:16, 8:9], func=AF.Relu), reads=[("ps", 5)], writes=["Rn"])
    P.add("pe", lambda e: e.matmul(ps[5][0:1, 16:17], lhsT=smb[0:16, 76:77], rhs=w_s, start=True, stop=True), reads=["Rn", "w_s"], writes=[("ps", 5)])
    P.add("pool", lambda e: e.memset(scb[:, 128:129], -CBIG), writes=["sc_x"])
    P.add("dve", lambda e: e.tensor_copy(out=scb[0:1, 128:129], in_=ps[5][0:1, 16:17]), reads=[("ps", 5), "sc_x"], writes=["sc_x"])
    P.add("dve", lambda e: e.tensor_copy(out=scb[:, 0:128], in_=ps[4][:, 0:128]), reads=[("ps", 4)], writes=["sc_m"])
    pm = smx[:, 2:4]
    P.add("dve", lambda e: e.tensor_reduce(out=smx[:, 2:3], in_=sc, axis=AX.X, op=ALU.max), reads=["sc_x", "sc_m"], writes=["pm0"])
    P.add("dve", lambda e: e.tensor_reduce(out=smx[:, 3:4], in_=scb[:, 0:128], axis=AX.X, op=ALU.min, negate=True), reads=["sc_m"], writes=["pm1"])
    P.add("pe", lambda e: e.transpose(out=ps[5][0:2, 32:160], in_=pm, identity=identf), reads=["pm0", "pm1", "identf"], writes=[("ps", 5)])
    P.add("dve", lambda e: e.tensor_reduce(out=smx[0:2, 4:5], in_=ps[5][0:2, 32:160], axis=AX.X, op=ALU.max), reads=[("ps", 5)], writes=["g2"])
    P.add("dve", lambda e: e.tensor_scalar(out=smx[0:2, 6:8], in0=identf[0:2, 0:2], scalar1=smx[0:2, 4:5], scalar2=None, op0=ALU.mult),
          reads=["g2", "identf"], writes=["g2d"])
    P.add("pe", lambda e: e.matmul(ps[5][:, 200:202], lhsT=ones_f[0:2, :], rhs=smx[0:2, 6:8], start=True, stop=True), reads=["g2d", "ones_f"], writes=[("ps", 5)])
    P.add("dve", lambda e: e.tensor_copy(out=smx[:, 8:10], in_=ps[5][:, 200:202]), reads=[("ps", 5)], writes=["gmm"])
    gmax = smx[:, 8:9]
    ngmin = smx[:, 9:10]
    P.add("dve", lambda e: e.tensor_tensor(out=w0, in0=gmax, in1=ngmin, op=ALU.add), reads=["gmm"], writes=["w0"])
    P.add("dve", lambda e: e.tensor_scalar(out=w0, in0=w0, scalar1=float(2.0 ** -10), scalar2=1e-3, op0=ALU.mult, op1=ALU.add), reads=["w0"], writes=["w0b"])
    P.add("dve", lambda e: e.tensor_tensor(out=lo0, in0=ngmin, in1=w0, op=ALU.add), reads=["gmm", "w0b"], writes=["lo0n"])
    P.add("dve", lambda e: e.tensor_scalar(out=lo0, in0=lo0, scalar1=-1.0, scalar2=None, op0=ALU.mult), reads=["lo0n"], writes=["lo0"])
    P.add("dve", lambda e: e.tensor_tensor(out=w0, in0=gmax, in1=lo0, op=ALU.subtract), reads=["gmm", "lo0"], writes=["w0c"])
    P.add("dve", lambda e: e.tensor_scalar(out=HK, in0=pow2[:, 0:NIT + 1], scalar1=w0, scalar2=None, op0=ALU.mult), reads=["pow2", "w0c"], writes=["HK"])
    P.add("dve", lambda e: e.tensor_tensor(out=TC[:, 0:1], in0=lo0, in1=HK[:, 0:1], op=ALU.add), reads=["lo0", "HK"], writes=[("TC", 0)])
    for k in range(NIT):
        P.add("dve", lambda e, k=k: e.tensor_scalar(out=vld, in0=sc, scalar1=TC[:, k:k + 1], scalar2=None, op0=ALU.is_ge, op1=ALU.add, accum_out=cnt),
              reads=["sc_x", "sc_m", ("TC", k)], writes=["vld", "cnt"])
        P.add("pe", lambda e: e.matmul(ps[6][:, 0:1], lhsT=ones_f, rhs=cnt, start=True, stop=True), reads=["cnt", "ones_f"], writes=[("ps", 6)])
        P.add("dve", lambda e, k=k: e.scalar_tensor_tensor(out=tmpc, in0=ps[6][:, 0:1], scalar=TOPK - 0.5, in1=HK[:, k:k + 1], op0=ALU.is_ge, op1=ALU.mult),
              reads=[("ps", 6), "HK"], writes=["tmpc"])
        P.add("dve", lambda e, k=k: e.scalar_tensor_tensor(out=TC[:, k + 1:k + 2], in0=tmpc, scalar=HK[:, k + 1:k + 2], in1=TC[:, k:k + 1],
                                                           op0=ALU.subtract, op1=ALU.add),
              reads=["tmpc", "HK", ("TC", k)], writes=[("TC", k + 1)])
    thr_s = smx[:, 12:13]
    P.add("dve", lambda e: e.tensor_tensor(out=thr_s, in0=TC[:, NIT:NIT + 1], in1=HK[:, NIT:NIT + 1], op=ALU.subtract), reads=[("TC", NIT), "HK"], writes=["thr_s"])
    P.add("dve", lambda e: e.tensor_scalar(out=vld, in0=sc, scalar1=thr_s, scalar2=None, op0=ALU.is_ge), reads=["sc_x", "sc_m", "thr_s"], writes=["vld"])
    P.add("pool", lambda e: e.memset(Kx, 0.0), writes=["Kx"])
    P.add("pool", lambda e: e.memset(Vx, 0.0), writes=["Vx"])
    P.add("dve", lambda e: e.tensor_copy(out=Kx[0:1, :], in_=ksv[0:1, 0:256]), reads=["ksv", "Kx"], writes=["Kx"])
    P.add("dve", lambda e: e.tensor_copy(out=Vx[0:1, :], in_=ksv[0:1, 256:512]), reads=["ksv", "Vx"], writes=["Vx"])

    def s_col(pos):
        return (4 + pos // 64, (pos % 64) * 8) if pos < 128 else (6, 0)

    for ch in range(17):
        npos = 8 if ch < 16 else 1
        if ch < 16:
            kc_ = Kc[ch % 2]
            P.add("pool", lambda e, ch=ch, kc_=kc_: e.indirect_dma_start(out=kc_, out_offset=None, in_=pool_k16,
                                                                        in_offset=IOA(ap=idx16[:, ch:ch + 1], axis=0)),
                  reads=["idx16"], writes=[("Kc", ch % 2)], slot="s_kc%d" % (ch % 2))
            kres = ("Kc", ch % 2)
        else:
            kc_ = Kx
            kres = "Kx"
        ktb = KTb[ch % 2]
        ba = 2 * (ch % 2)
        for t in range(2 * npos):
            bk = ba + t // 8
            P.add("pe", lambda e, kc_=kc_, t=t, bk=bk: e.transpose(out=psb[bk][:, (t % 8) * 128:(t % 8 + 1) * 128],
                                                                 in_=kc_[:, t * 128:(t + 1) * 128], identity=ident),
                  reads=[kres, "ident"], writes=[("ps", bk)])
        nb_ = (2 * npos + 7) // 8
        for hb in range(nb_):
            ncol = min(8, 2 * npos - 8 * hb) * 128
            P.add("act" if hb == 0 else "dve",
                  (lambda e, ktb=ktb, hb=hb, ba=ba, ncol=ncol: e.copy(out=ktb[:, hb * 1024:hb * 1024 + ncol], in_=psb[ba + hb][:, 0:ncol])) if hb == 0 else
                  (lambda e, ktb=ktb, hb=hb, ba=ba, ncol=ncol: e.tensor_copy(out=ktb[:, hb * 1024:hb * 1024 + ncol], in_=psb[ba + hb][:, 0:ncol])),
                  reads=[("ps", ba + hb)], writes=[("KTb", ch % 2, hb)])
        for p in range(npos):
            pos = ch * 8 + p
            bk, col = s_col(pos)
            for g in range(2):
                t = 2 * p + g
                P.add("pe", lambda e, ktb=ktb, t=t, g=g, bk=bk, col=col: e.matmul(ps[bk][:, col + 4 * g:col + 4 * g + 4], lhsT=ktb[:, t * 128:(t + 1) * 128],
                                                                              rhs=qTs_bf[:, 4 * g:4 * g + 4], start=True, stop=True),
                      reads=[("KTb", ch % 2, t // 8), "qs_bf"], writes=[("ps", bk)])
    P.add("act", lambda e: e.activation(out=Psb[:, 0:512], in_=ps[4][:, 0:512], func=AF.Exp, scale=SCALE), reads=[("ps", 4)], writes=["Psb0"])
    P.add("act", lambda e: e.activation(out=Psb[:, 512:1024], in_=ps[5][:, 0:512], func=AF.Exp, scale=SCALE), reads=[("ps", 5)], writes=["Psb1"])
    P.add("act", lambda e: e.activation(out=Psb[:, 1024:1032], in_=ps[6][:, 0:8], func=AF.Exp, scale=SCALE), reads=[("ps", 6)], writes=["Psb2"])
    P3 = Psb[:, 0:1032].rearrange("p (s h) -> p s h", h=8)
    Pb3 = Pbf[:, 0:1032].rearrange("p (s h) -> p s h", h=8)
    P.add("dve", lambda e: e.tensor_tensor(out=Pb3, in0=P3, in1=vld.unsqueeze(2).to_broadcast([128, 129, 8]), op=ALU.mult),
          reads=["Psb0", "Psb1", "Psb2", "vld"], writes=["Pbf"])
    P.add("dve", lambda e: e.tensor_reduce(out=Pred[:, 0:8], in_=Pbf[:, 0:1032].rearrange("p (s h) -> p h s", h=8), axis=AX.X, op=ALU.add),
          reads=["Pbf"], writes=["Pred"])
    for ch in range(17):
        npos = 8 if ch < 16 else 1
        if ch < 16:
            vc_ = Kc[ch % 2]
            P.add("pool", lambda e, ch=ch, vc_=vc_: e.indirect_dma_start(out=vc_, out_offset=None, in_=pool_v16,
                                                                        in_offset=IOA(ap=idx16[:, ch:ch + 1], axis=0)),
                  reads=["idx16"], writes=[("Kc", ch % 2)], slot="s_kc%d" % (ch % 2))
            vres = ("Kc", ch % 2)
        else:
            vc_ = Vx
            vres = "Vx"
        for p in range(npos):
            pos = ch * 8 + p
            for g in range(2):
                P.add("pe", lambda e, vc_=vc_, p=p, g=g, pos=pos: e.matmul(ps[g][0:4, 0:128], lhsT=Pbf[:, pos * 8 + 4 * g:pos * 8 + 4 * g + 4],
                                                                       rhs=vc_[:, p * 256 + g * 128:p * 256 + (g + 1) * 128],
                                                                       start=(pos == 0), stop=(pos == 128)),
                      reads=[vres, "Pbf"], writes=[("ps", g)])
    for g in range(2):
        P.add("pe", lambda e, g=g: e.matmul(ps[2][0:4, g:g + 1], lhsT=Pred[:, 4 * g:4 * g + 4], rhs=ones_f[:, 0:1], start=True, stop=True),
              reads=["Pred", "ones_f"], writes=[("ps", 2)])
    P.add("dve", lambda e: e.reciprocal(out=smx[0:4, 16:18], in_=ps[2][0:4, 0:2]), reads=[("ps", 2)], writes=["rden_s"])
    for g in range(2):
        P.add("dve", lambda e, g=g: e.tensor_scalar(out=osb[0:4, g * 128:(g + 1) * 128], in0=ps[g][0:4, 0:128], scalar1=smx[0:4, 16 + g:17 + g], scalar2=None,
                                                   op0=ALU.mult), reads=[("ps", g), "rden_s"], writes=[("osb", g)])
        P.add("pe", lambda e, g=g: e.transpose(out=ps[3][:, 4 * g:4 * g + 4], in_=osb[0:4, g * 128:(g + 1) * 128], identity=identf[0:4, 0:4]),
              reads=[("osb", g), "identf"], writes=[("ps", 3)])
    P.add("dve", lambda e: e.tensor_copy(out=oaTs, in_=ps[3][:, 0:8]), reads=[("ps", 3)], writes=["oaTs"])

    P.barrier()
    S4 = 20 * KB
    xbuf = [buf(S4 + i * 8 * KB, 8 * KB, F32) for i in range(2)]
    hbuf = [buf(S4 + 16 * KB + i * 4 * KB, 4 * KB) for i in range(2)]
    junk = buf(S4 + 24 * KB, 4 * KB)
    wvbuf = buf(52 * KB, 32 * KB)
    wv3 = v3(wvbuf, 16)
    bspbc = buf(200 * KB, 4 * KB, F32)
    for q4 in range(8):
        P.add("pool", lambda e, q4=q4: e.dma_start(out=wv3[:, q4 * 2:(q4 + 1) * 2, :], in_=v3(wvb_d, 16)[:, q4 * 2:(q4 + 1) * 2, :]),
              writes=["wvbuf"], slot="wvb%d" % (q4 % 4))
    P.add("sp", lambda e: e.dma_start(out=bspbc, in_=bsp_d.partition_broadcast(128)), writes=["bspbc"], slot="c_bsp")
    P.add("pool", lambda e: e.dma_start(out=WsT, in_=wspT_d), writes=["WsT"], slot="c_wsp")
    P.add("pool", lambda e: e.affine_select(out=v3(WsT, 8), in_=v3(WsT, 8), pattern=[[0, 8], [1, 128]], compare_op=ALU.is_ge,
                                            fill=0.0, base=0, channel_multiplier=-1), reads=["WsT"], writes=["WsT"])
    own_pass(xbuf, hbuf, junk)

    P.barrier()
    uT = buf(20 * KB, 16 * KB)
    obT = buf(36 * KB, 16 * KB)
    wbuf = [buf(84 * KB + i * 4 * KB, 4 * KB) for i in range(3)]
    wpbuf = [buf(96 * KB + i * 2 * KB, 2 * KB) for i in range(2)]
    szb = [buf(100 * KB + i * KB, KB) for i in range(2)]
    ftmp = buf(102 * KB, 4 * KB, F32)
    ftmp2 = buf(188 * KB, 4 * KB, F32)
    gvb = buf(192 * KB, 4 * KB, F32)
    vnb = [buf(196 * KB + i * 2 * KB, 2 * KB) for i in range(2)]
    uT3 = v3(uT, 8)
    obT3 = v3(obT, 8)
    P.add("sp", lambda e: e.dma_start(out=gbc[:, 0:1024], in_=ln_g_d.partition_broadcast(128)), writes=["gbc"], slot="c_gbc")
    P.add("sp", lambda e: e.dma_start(out=gbc[:, 1024:2048], in_=ln_b_d.partition_broadcast(128)), writes=["gbc"], slot="c_gbc")
    fm_state["n"] = 0

    def evac_act(dst3, func, tag):
        def mk(cb):
            def f(pa):
                for hf in range(2):
                    P.add("act", lambda e, hf=hf: e.activation(out=dst3[:, cb, hf * 512:(hf + 1) * 512], in_=ps[pa + hf][:, 0:512], func=func),
                          reads=[("ps", pa + hf)], writes=[(tag, cb, hf)])
            return f
        return mk

    oa_all = [("oaT", i, g) for i in range(OWN) for g in range(2)]

    def evac_mul_act(dst3, func, dres):
        def mk(cb):
            def f(pa):
                for hf in range(2):
                    sz = szb[hf]
                    P.add("act", lambda e, hf=hf, sz=sz: e.activation(out=sz, in_=ps[pa + hf][:, 0:512], func=func),
                          reads=[("ps", pa + hf)], writes=[("szb", hf)])
                    P.add("dve", lambda e, hf=hf, sz=sz: e.tensor_tensor(out=dst3[:, cb, hf * 512:(hf + 1) * 512],
                                                                       in0=dst3[:, cb, hf * 512:(hf + 1) * 512], in1=sz, op=ALU.mult),
                          reads=[("szb", hf)] + dres, writes=[(id(dst3), "z", cb, hf)])
            return f
        return mk

    def layernorm(src, dst, sidx, nparts, src_res, dst_res, tmp, jk):
        s1 = stat[0:nparts, sidx:sidx + 1]
        s2 = stat[0:nparts, sidx + 1:sidx + 2]
        mu = stat[0:nparts, sidx + 2:sidx + 3]
        var = stat[0:nparts, sidx + 3:sidx + 4]
        sd = stat[0:nparts, sidx + 4:sidx + 5]
        rs = stat[0:nparts, sidx + 5:sidx + 6]
        P.add("dve", lambda e: e.reduce_sum(out=s1, in_=src, axis=AX.X), reads=[src_res], writes=[("st", sidx)])
        P.add("dve", lambda e: e.tensor_scalar(out=mu, in0=s1, scalar1=1.0 / 1024, scalar2=None, op0=ALU.mult), reads=[("st", sidx)], writes=[("st", sidx + 2)])
        P.add("dve", lambda e: e.tensor_scalar(out=tmp, in0=src, scalar1=mu, scalar2=None, op0=ALU.subtract), reads=[src_res, ("st", sidx + 2)], writes=[(dst_res, "t")])
        P.add("dve", lambda e: e.scalar_tensor_tensor(out=jk[0:nparts, 0:1024], in0=tmp, scalar=1.0, in1=tmp, op0=ALU.mult, op1=ALU.mult, accum_out=s2),
              reads=[(dst_res, "t")], writes=["junk", ("st", sidx + 1)])
        P.add("dve", lambda e: e.tensor_scalar(out=sd, in0=s2, scalar1=1.0 / 1024, scalar2=1e-5, op0=ALU.mult, op1=ALU.add),
              reads=[("st", sidx + 1)], writes=[("st", sidx + 4)])
        P.add("pool", lambda e: e.tensor_tensor(out=rs, in0=sd, in1=epsc[0:nparts, 2:3], op=ALU.pow), reads=[("st", sidx + 4), "epsc"], writes=[("st", sidx + 5)])
        P.add("dve", lambda e: e.scalar_tensor_tensor(out=tmp, in0=tmp, scalar=rs, in1=gbc[0:nparts, 0:1024], op0=ALU.mult, op1=ALU.mult),
              reads=[(dst_res, "t"), ("st", sidx + 5), "gbc"], writes=[(dst_res, "t")])
        P.add("dve", lambda e: e.tensor_tensor(out=dst, in0=tmp, in1=gbc[0:nparts, 1024:2048], op=ALU.add), reads=[(dst_res, "t"), "gbc"], writes=[dst_res])

    junk = buf(106 * KB, 2 * KB)
    def vb_block(i):
        tok = slice(i * 128, (i + 1) * 128)
        for hf in range(2):
            bk = 4 + hf
            for c in range(16):
                P.add("pe", lambda e, c=c, hf=hf, bk=bk: e.matmul(ps[bk][:, 0:512], lhsT=hT3[:, c, tok], rhs=wv3[:, c, hf * 512:(hf + 1) * 512],
                                                                start=(c == 0), stop=(c == 15)),
                      reads=["wvbuf"] + hT_reads, writes=[("ps", bk)])
            P.add("act", lambda e, hf=hf, bk=bk: e.activation(out=gvb[:, hf * 512:(hf + 1) * 512], in_=ps[bk][:, 0:512], func=AF.Gelu),
                  reads=[("ps", bk)], writes=["gvb"])

    def vb_ln(i):
        vn = vnb[i % 2]
        layernorm(gvb, vn, 16, 128, "gvb", ("vn", i % 2), ftmp, junk)

    def vb_back(i):
        tok = slice(i * 128, (i + 1) * 128)
        vn = vnb[i % 2]
        for hh in range(2):
            gs = slice(4 * hh, 4 * hh + 4)
            for g in range(4 * hh, 4 * hh + 4):
                P.add("pe", lambda e, g=g: e.matmul(ps[7][:, (g % 4) * 128:(g % 4 + 1) * 128], lhsT=vn[:, g * 128:(g + 1) * 128],
                                                   rhs=v3(WsT, 8)[:, g, :], start=True, stop=True),
                      reads=[("vn", i % 2), "WsT"], writes=[("ps", 7)])
            P.add("dve", lambda e, hh=hh, gs=gs: e.tensor_tensor(out=obT3[:, gs, tok], in0=v3(ps[7][:, 0:512], 4),
                                                                in1=v3(bspbc[:, hh * 512:(hh + 1) * 512], 4), op=ALU.add),
                  reads=[("ps", 7), "bspbc"], writes=[("obT", i, hh)])

    mk = evac_act(uT3, AF.Gelu, "uT")
    for cb in range(8):
        proj_fm(FM_U + cb, 32 + cb, mk(cb), pa=2 * (cb % 2))
        fm_prefetch(FM_U + cb + 1 if cb < 7 else FM_ZA)
        vb_block(cb)
        if cb >= 1:
            vb_back(cb - 1)
        vb_ln(cb)
    vb_back(7)
    uT_all = [("uT", cb, hf) for cb in range(8) for hf in range(2)]
    P.add("act", lambda e: e.activation(out=sm[:, 32:40], in_=ps[6][:, 32:40], func=AF.Gelu), reads=[("ps", 6)], writes=["uTs"])
    for cb in range(8):
        P.add("dve", lambda e, cb=cb: e.tensor_tensor(out=obT3[:, cb, :], in0=obT3[:, cb, :], in1=uT3[:, cb, :], op=ALU.mult),
              reads=[("obT", i, cb // 4) for i in range(OWN)] + [("uT", cb, 0), ("uT", cb, 1)], writes=[("obT", i, cb // 4) for i in range(OWN)])
    mk = evac_mul_act(oaT3, AF.Silu, oa_all)
    for cb in range(8):
        proj_fm(FM_ZA + cb, 24 + cb, mk(cb))
    oaz_all = [(id(oaT3), "z", cb, hf) for cb in range(8) for hf in range(2)]
    P.add("act", lambda e: e.activation(out=sm[:, 24:32], in_=ps[6][:, 24:32], func=AF.Silu), reads=[("ps", 6)], writes=["zaTs"])
    P.add("dve", lambda e: e.tensor_tensor(out=oazTs, in0=oaTs, in1=sm[:, 24:32], op=ALU.mult), reads=["zaTs", "oaTs"], writes=["srhs"])

    ob_all = [("obT", i, hh) for i in range(OWN) for hh in range(2)]
    for hf in range(2):
        for c in range(16):
            P.add("pe", lambda e, c=c, hf=hf: e.matmul(ps[4 + hf][0:1, 0:512], lhsT=hsT[:, c:c + 1], rhs=wv3[:, c, hf * 512:(hf + 1) * 512],
                                                      start=(c == 0), stop=(c == 15)),
                  reads=["wvbuf", "hsT"], writes=[("ps", 4 + hf)])
        P.add("act", lambda e, hf=hf: e.activation(out=gvb[0:1, hf * 512:(hf + 1) * 512], in_=ps[4 + hf][0:1, 0:512], func=AF.Gelu),
              reads=[("ps", 4 + hf)], writes=["gvb"])
    vns = ftmp2[0:1, :]
    layernorm(gvb[0:1, :], vns, 24, 1, "gvb", "vns", ftmp[0:1, :], junk)
    P.add("sp", lambda e: e.dma_start(out=gvs_o, in_=vns), reads=["vns", ("gt", 1, 0), ("gt", 1, 1)], writes=["o_gvs"], slot="o_gvs")

    P.add("sp", lambda e: e.dma_start(out=gvb[0:1, :], in_=ws00_d), reads=["vns"], writes=["gvb"], slot="c_ws00")
    P.add("sp", lambda e: e.dma_start(out=ftmp[0:1, :], in_=bs0_d), reads=["vns"], writes=[("vns", "t")], slot="c_bs0")
    P.add("dve", lambda e: e.tensor_tensor(out=gvb[0:1, :], in0=vns, in1=gvb[0:1, :], op=ALU.mult), reads=["vns", "gvb"], writes=["gvb"])
    P.add("dve", lambda e: e.tensor_tensor(out=gvb[0:1, :], in0=gvb[0:1, :], in1=ftmp[0:1, :], op=ALU.add), reads=["gvb", ("vns", "t")], writes=["gvb"])
    for g in range(8):
        P.add("pe", lambda e, g=g: e.transpose(out=ps[4][:, g:g + 1], in_=gvb[0:1, g * 128:(g + 1) * 128], identity=identf[0:1, 0:1]),
              reads=["gvb", "identf"], writes=[("ps", 4)])
    P.add("dve", lambda e: e.tensor_tensor(out=sm[:, 32:40], in0=ps[4][:, 0:8], in1=sm[:, 32:40], op=ALU.mult), reads=[("ps", 4), "uTs"], writes=["obTs"])
    wo = [buf(20 * KB + gidx * 16 * KB, 16 * KB) for gidx in range(4)]

    pending_wo = []

    def load_wo(gi, extra, defer=False):
        w3 = v3(wo[gi], 16)
        for q4 in range(4):
            def emit(q4=q4):
                P.add("pool", lambda e: e.dma_start(out=w3[:, q4 * 4:(q4 + 1) * 4, :], in_=v3(wout_d[gi], 16)[:, q4 * 4:(q4 + 1) * 4, :]),
                      writes=[("wo", gi)] + extra, slot="wo%d_%d" % (gi, q4))
            if defer:
                pending_wo.append(emit)
            else:
                emit()

    load_wo(0, uT_all, defer=True)
    load_wo(2, ["wvbuf"], defer=True)
    load_wo(3, ["wvbuf"], defer=True)
    mk = evac_mul_act(obT3, AF.Silu, ob_all)
    for cb in range(8):
        proj_fm(FM_ZB + cb, 40 + cb, mk(cb))
    obz_all = [(id(obT3), "z", cb, hf) for cb in range(8) for hf in range(2)]
    P.add("act", lambda e: e.activation(out=sm[:, 40:48], in_=ps[6][:, 40:48], func=AF.Silu), reads=[("ps", 6)], writes=["zbTs"])
    P.add("dve", lambda e: e.tensor_tensor(out=obzTs, in0=sm[:, 32:40], in1=sm[:, 40:48], op=ALU.mult), reads=["zbTs", "obTs"], writes=["srhs"])
    if debug:
        P.add("sp", lambda e: e.dma_start(out=dbg["obT"], in_=obT), reads=obz_all, writes=["dbg_dob"], slot="dbg8")

    wp_n = {"n": 0}

    def proj_branch(wd, cb, src3, sres, bank, scol, srhs):
        n = wp_n["n"]
        wp_n["n"] += 1
        wsl = wpbuf[n % 2]
        w3 = v3(wsl, 8)
        P.add("pool", lambda e: e.dma_start(out=wsl, in_=wd[cb]), writes=[("wpbuf", n % 2)], slot="wpbuf%d" % (n % 2))
        for c in range(8):
            for hf in range(2):
                P.add("pe", lambda e, c=c, hf=hf: e.matmul(ps[bank + hf][:, 0:512], lhsT=w3[:, c, :], rhs=src3[:, c, hf * 512:(hf + 1) * 512],
                                                          start=(c == 0), stop=(c == 7)),
                      reads=[("wpbuf", n % 2)] + sres, writes=[("ps", bank + hf)])
            P.add("pe", lambda e, c=c: e.matmul(ps[7][:, scol:scol + 1], lhsT=w3[:, c, :], rhs=srhs[:, c:c + 1], start=(c == 0), stop=(c == 7)),
                  reads=[("wpbuf", n % 2), "srhs"], writes=[("ps", 7)])


    for cb in range(16):
        sg = [None, None]
        for br in range(2):
            gt = ftmp if br == 0 else ftmp2

            def ev_gate(pa, gt=gt, br=br):
                for hf in range(2):
                    P.add("act", lambda e, hf=hf: e.activation(out=gt[:, hf * 512:(hf + 1) * 512], in_=ps[pa + hf][:, 0:512], func=AF.Sigmoid),
                          reads=[("ps", pa + hf)], writes=[("gt", br, hf)])
                sg[br] = pa
            proj_fm(FM_G + 2 * cb + br, (48 + cb if br == 0 else 64 + cb), ev_gate, pa=2 * br)
            if pending_wo:
                pending_wo.pop(0)()
            if br == 0:
                proj_branch(wpa_d, cb, oaT3, oaz_all, 4, cb, oazTs)
            else:
                proj_branch(wpb_d, cb, obT3, obz_all, 4, 16 + cb, obzTs)
            for hf in range(2):
                P.add("dve", lambda e, hf=hf, gt=gt: e.tensor_tensor(out=gt[:, hf * 512:(hf + 1) * 512], in0=ps[4 + hf][:, 0:512],
                                                                   in1=gt[:, hf * 512:(hf + 1) * 512], op=ALU.mult),
                      reads=[("ps", 4 + hf), ("gt", br, hf)], writes=[("gt", br, hf)])
        for hf in range(2):
            P.add("dve", lambda e, hf=hf, cb=cb: e.tensor_tensor(out=mixT3[:, cb, hf * 512:(hf + 1) * 512], in0=ftmp[:, hf * 512:(hf + 1) * 512],
                                                               in1=ftmp2[:, hf * 512:(hf + 1) * 512], op=ALU.add),
                  reads=[("gt", 0, hf), ("gt", 1, hf)], writes=[("mixT", cb, hf)])
    mix_all = [("mixT", cb, hf) for cb in range(16) for hf in range(2)]
    if debug:
        P.add("sp", lambda e: e.dma_start(out=dbg["mixT"], in_=mixT), reads=mix_all, writes=["dbg_dmx"], slot="dbg9")
    P.add("act", lambda e: e.activation(out=sm[:, 48:80], in_=ps[6][:, 48:80], func=AF.Sigmoid),
          reads=[("ps", 6)], writes=["sm_g"])
    P.add("dve", lambda e: e.tensor_tensor(out=sm[:, 80:112], in0=ps[7][:, 0:32], in1=sm[:, 48:80], op=ALU.mult),
          reads=[("ps", 7), "sm_g"], writes=["sm_y"])
    P.add("dve", lambda e: e.tensor_tensor(out=mixTs, in0=sm[:, 80:96], in1=sm[:, 96:112], op=ALU.add), reads=["sm_y"], writes=["mixTs"])

    if upto <= 4:
        return finish()
    P.barrier()
    xbuf = [buf(84 * KB + i * 8 * KB, 8 * KB, F32) for i in range(2)]
    xo = [buf(100 * KB + i * 8 * KB, 8 * KB, F32) for i in range(2)]
    junk = buf(116 * KB, 4 * KB)
    xs2 = buf(120 * KB, 8 * KB, F32, parts=1)
    xos = buf(128 * KB, 8 * KB, F32, parts=1)
    P.add("sp", lambda e: e.dma_start(out=gbc, in_=g_f_d.partition_broadcast(128)), writes=["gbc"], slot="c_gbc")
    load_wo(1, [])

    def final_block(src_ap, xb, xres, xslot, xo_t, lhs_fn, lhs_reads, banks, out_ap, sidx, nparts, oslot):
        P.add("sp", lambda e: e.dma_start(out=xb, in_=src_ap), writes=[xres], slot=xslot)
        for gi in range(4):
            w3 = v3(wo[gi], 16)
            for c in range(16):
                P.add("pe", lambda e, gi=gi, c=c, w3=w3: e.matmul(ps[banks[gi]][0:nparts, 0:512], lhsT=lhs_fn(c), rhs=w3[:, c, :],
                                                                start=(c == 0), stop=(c == 15)),
                      reads=[("wo", gi)] + lhs_reads, writes=[("ps", banks[gi])])
            P.add("dve", lambda e, gi=gi: e.tensor_tensor(out=xo_t[:, gi * 512:(gi + 1) * 512], in0=ps[banks[gi]][0:nparts, 0:512],
                                                         in1=xb[:, gi * 512:(gi + 1) * 512], op=ALU.add),
                  reads=[("ps", banks[gi]), xres], writes=[(xres, "xo", gi)])
        ss = stat[0:nparts, sidx:sidx + 1]
        sd = stat[0:nparts, sidx + 1:sidx + 2]
        rs = stat[0:nparts, sidx + 2:sidx + 3]
        xor = [(xres, "xo", gi) for gi in range(4)]
        P.add("act", lambda e: e.activation(out=junk[0:nparts, :], in_=xo_t, func=AF.Square, accum_out=ss), reads=xor, writes=["junk", ("st", sidx)])
        P.add("act", lambda e: e.activation(out=sd, in_=ss, func=AF.Sqrt, scale=1.0 / D, bias=epsc[0:nparts, 0:1]),
              reads=[("st", sidx), "epsc"], writes=[("st", sidx + 1)])
        P.add("dve", lambda e: e.reciprocal(out=rs, in_=sd), reads=[("st", sidx + 1)], writes=[("st", sidx + 2)])
        P.add("dve", lambda e: e.scalar_tensor_tensor(out=xb, in0=xo_t, scalar=rs, in1=gbc[0:nparts, :], op0=ALU.mult, op1=ALU.mult),
              reads=xor + [("st", sidx + 2), "gbc"], writes=[xres])
        P.add("pool", lambda e: e.dma_start(out=out_ap, in_=xb), reads=[xres], writes=[("o_y", oslot)], slot="o_y%s" % oslot)

    for i in range(OWN):
        pb = i % 2
        tok = slice(i * 128, (i + 1) * 128)
        banks = [0, 1, 2, 3] if pb == 0 else [4, 5, 6, 7]
        final_block(x_own[i * 128:(i + 1) * 128, :], xbuf[pb], ("xbuf", pb), "xbuf%d" % pb, xo[pb],
                    lambda c, tok=tok: mixT3[:, c, tok], mix_all, banks, y_own[i * 128:(i + 1) * 128, :], 4 * pb, 128, pb)
    final_block(x_s, xs2, "xs2", "xs2", xos, lambda c: mixTs[:, c:c + 1], ["mixTs"], [0, 1, 2, 3], ys_o, 8, 1, "s")

    return finish()


def own_blocks(core):
    j = core % 4
    return sorted([j, 7 - j, 8 + j, 15 - j, 16 + j, 23 - j, 24 + j, 31 - j])


def prep_shared(inputs):
    w_in = np.asarray(inputs["w_in"])[0]

    def chunked(cols):
        n = cols.shape[1]
        return np.ascontiguousarray(cols.reshape(16, 128, n).transpose(1, 0, 2))

    sh = {}
    kvki = np.concatenate([w_in[:, C_K:C_K + 256], w_in[:, C_V:C_V + 256], w_in[:, C_KI:C_KI + 128]], axis=1)
    sh["wkvki"] = chunked(kvki).reshape(128, -1)
    cbs = []
    for base, n in ((C_Q, 8), (C_QI, 16), (C_ZA, 8), (C_U, 8), (C_ZB, 8)):
        for j in range(n):
            cbs.append(w_in[:, base + j * 128: base + (j + 1) * 128])
    for j in range(16):
        cbs.append(w_in[:, C_GA + j * 128:C_GA + (j + 1) * 128])
        cbs.append(w_in[:, C_GB + j * 128:C_GB + (j + 1) * 128])
    sh["wfm"] = np.stack([chunked(c).reshape(128, -1) for c in cbs])
    sh["wwi"] = chunked(w_in[:, C_WI:C_WI + 16]).reshape(128, -1)
    sh["wvb"] = chunked(w_in[:, C_VB:C_VB + 1024]).reshape(128, -1)
    wpa = np.asarray(inputs["w_proj_a"])[0]
    wpb = np.asarray(inputs["w_proj_b"])[0]

    def chunk8(cols):
        return np.ascontiguousarray(cols.reshape(8, 128, 128).transpose(1, 0, 2)).reshape(128, -1)

    sh["wpa"] = np.stack([chunk8(wpa[:, j * 128:(j + 1) * 128]) for j in range(16)])
    sh["wpb"] = np.stack([chunk8(wpb[:, j * 128:(j + 1) * 128]) for j in range(16)])
    wout = np.asarray(inputs["w_out"])[0]
    sh["wout"] = np.stack([chunked(wout[:, j * 512:(j + 1) * 512]).reshape(128, -1) for j in range(4)])
    ws = np.asarray(inputs["w_spatial"])[0]
    sh["wspT"] = np.ascontiguousarray(ws.transpose(2, 0, 1)).reshape(128, -1)
    bsp = np.asarray(inputs["b_spatial"])[0]
    sh["bsp"] = np.ascontiguousarray(bsp.reshape(1, 1024))
    sh["ws00"] = np.ascontiguousarray(np.repeat(ws[:, 0, 0], 128).reshape(1, 1024))
    sh["bs0"] = np.ascontiguousarray(np.repeat(bsp[:, 0], 128).reshape(1, 1024))
    sh["pool_ki8"] = np.ascontiguousarray(np.asarray(inputs["cache_k_idx"])[0].reshape(1280 * 8, 2048))
    sh["pool_k16"] = np.ascontiguousarray(np.asarray(inputs["cache_k"])[0].reshape(1280 * 16, 2048))
    sh["pool_v16"] = np.ascontiguousarray(np.asarray(inputs["cache_v"])[0].reshape(1280 * 16, 2048))
    sh["g_in"] = np.ascontiguousarray(np.asarray(inputs["norm_in_g"]).reshape(1, D))
    sh["g_f"] = np.ascontiguousarray(np.asarray(inputs["norm_f_g"]).reshape(1, D))
    sh["ln_g"] = np.ascontiguousarray(np.asarray(inputs["ln_g"]).reshape(1, 1024))
    sh["ln_b"] = np.ascontiguousarray(np.asarray(inputs["ln_b"]).reshape(1, 1024))
    return {k: np.ascontiguousarray(v, dtype=v.dtype) for k, v in sh.items()}


def make_in_maps(inputs):
    sh = prep_shared(inputs)
    xp = np.asarray(inputs["x_prompt"])
    xs = np.asarray(inputs["x_sample"])
    pt = np.asarray(inputs["page_table"]).astype(np.int32)
    maps = []
    for c in range(NCORES):
        b = c // 4
        ob = own_blocks(c)
        m = dict(sh)
        m["x_all"] = np.ascontiguousarray(xp[b])
        m["x_own"] = np.ascontiguousarray(np.concatenate([xp[b, blk * 128:(blk + 1) * 128] for blk in ob], axis=0))
        t = np.arange(128, dtype=np.float32)[:, None]
        m["qrel"] = np.ascontiguousarray(
            np.concatenate([(ob[i] * 128 - 512 * i) + t for i in range(OWN)], axis=1).astype(np.float32))
        m["x_s"] = np.ascontiguousarray(xs[c].reshape(1, D))
        m["ptab"] = np.ascontiguousarray(pt[c].reshape(1, 128))
        maps.append(m)
    return maps


_CACHE = {}


def kernel(**inputs):
    if "nc" not in _CACHE:
        _CACHE["nc"] = build_program(debug=False)[0]
    nc = _CACHE["nc"]
    maps = make_in_maps(inputs)
    res = run_bass_kernel_spmd(nc, maps, core_ids=list(range(NCORES)))
    r = res.results
    y_prompt = np.zeros((2, SEQ, D), np.float32)
    for c in range(NCORES):
        b = c // 4
        for i, blk in enumerate(own_blocks(c)):
            y_prompt[b, blk * 128:(blk + 1) * 128] = r[c]["y_own"][i * 128:(i + 1) * 128]
    y_sample = np.stack([r[c]["ys"].reshape(1, D) for c in range(NCORES)]).astype(np.float32)
    nk = np.stack([r[4 * b]["knew"].reshape(SEQ, 2, 128) for b in range(2)])[None].astype(np.float32)
    nv = np.stack([r[4 * b]["vnew"].reshape(SEQ, 2, 128) for b in range(2)])[None].astype(np.float32)
    nki = np.stack([r[4 * b]["kinew"].reshape(SEQ, 128) for b in range(2)])[None].astype(np.float32)
    ks = np.stack([r[c]["ks"].reshape(1, 2, 128) for c in range(NCORES)])[None].astype(np.float32)
    vs = np.stack([r[c]["vs"].reshape(1, 2, 128) for c in range(NCORES)])[None].astype(np.float32)
    kis = np.stack([r[c]["kis"].reshape(1, 128) for c in range(NCORES)])[None].astype(np.float32)
    gvs = np.stack([r[c]["gvs"].reshape(1, 1024) for c in range(NCORES)])[None].astype(np.float32)
    return (y_prompt, y_sample, nk, nv, nki, ks, vs, kis, gvs)
```

```python
import numpy as np
from contextlib import ExitStack
import concourse.bass as bass
import concourse.mybir as mybir
from concourse.bass_utils import run_bass_kernel_spmd

F32 = mybir.dt.float32
BF16 = mybir.dt.bfloat16
I32 = mybir.dt.int32
AF = mybir.ActivationFunctionType
ALU = mybir.AluOpType
AX = mybir.AxisListType

NCORES = 8
D = 2048
NCH = 16
SEQ = 4096
NB = 32
OWN = 8
TOK = 1024
BIG = 30000.0
CBIG = 1.0e6
NIT = 16
TOPK = 256
COMPUTE = ("pe", "act", "dve", "pool")

C_Q, C_K, C_V, C_QI, C_WI, C_KI, C_ZA, C_U, C_VB, C_ZB, C_GA, C_GB = (
    0, 1024, 1280, 1536, 3584, 3600, 3728, 4752, 5776, 6800, 7824, 9872)
FM_Q, FM_QI, FM_ZA, FM_U, FM_ZB, FM_G = 0, 8, 24, 32, 40, 48
N_FM = 80


class _Op:
    __slots__ = ("eng", "fn", "deps", "is_dma", "slot", "marked", "mark_idx", "cum", "idx")


class Prog:
    def __init__(self, nc, stack):
        self.nc = nc
        self.stack = stack
        self.ops = []
        self.last_w = {}
        self.readers = {}
        self.slot_cum = {}
        self.bar = []
        self.psx = {}
        self.engs = {"pe": nc.tensor, "act": nc.scalar, "dve": nc.vector, "pool": nc.gpsimd, "sp": nc.sync}

    def add(self, eng, fn, reads=(), writes=(), slot=None):
        op = _Op()
        op.eng = eng
        op.fn = fn
        op.is_dma = slot is not None
        op.slot = slot
        op.marked = False
        op.mark_idx = None
        op.idx = len(self.ops)
        op.deps = set()
        if op.is_dma:
            self.slot_cum[slot] = self.slot_cum.get(slot, 0) + 16
            op.cum = self.slot_cum[slot]
        for y in self.bar:
            self._dep(op, y, "raw")
        for r in reads:
            lw = self.last_w.get(r)
            if lw is not None:
                self._dep(op, lw, "raw")
        for w in writes:
            lw = self.last_w.get(w)
            if lw is not None:
                self._dep(op, lw, "waw")
            for rd in self.readers.get(w, ()):
                self._dep(op, rd, "war")
        for r in reads:
            self.readers.setdefault(r, []).append(op)
        for w in writes:
            self.last_w[w] = op
            self.readers[w] = []
        for res in list(reads) + list(writes):
            if isinstance(res, tuple) and len(res) >= 2 and res[0] == "ps":
                lastx = self.psx.get(res[1])
                if lastx is not None and lastx is not op and lastx.eng != op.eng:
                    op.deps.add(lastx.idx)
                    if not lastx.is_dma:
                        lastx.marked = True
                self.psx[res[1]] = op
        self.ops.append(op)
        return op

    def _dep(self, x, y, kind):
        if y is x:
            return
        if not y.is_dma and not x.is_dma and y.eng == x.eng:
            if kind != "raw" or x.eng == "pe":
                return
        x.deps.add(y.idx)
        if not y.is_dma:
            y.marked = True

    def barrier(self):
        last = {}
        for op in self.ops:
            if op.fn is None:
                continue
            key = ("slot", op.slot) if op.is_dma else ("eng", op.eng)
            last[key] = op
        self.bar = list(last.values())

    def emit(self):
        nc = self.nc
        sems = {}
        for e in COMPUTE:
            sems[("eng", e)] = self.stack.enter_context(nc.semaphore("c_" + e))
        for i, s in enumerate(self.slot_cum):
            sems[("slot", s)] = self.stack.enter_context(nc.semaphore("d%d" % i))
        cnt = {e: 0 for e in COMPUTE}
        for op in self.ops:
            if op.marked:
                cnt[op.eng] += 1
                op.mark_idx = cnt[op.eng]
        waited = {}
        nw = 0
        for op in self.ops:
            eng = self.engs[op.eng]
            need = {}
            for di in op.deps:
                y = self.ops[di]
                if y.is_dma:
                    k, v = ("slot", y.slot), y.cum
                else:
                    k, v = ("eng", y.eng), y.mark_idx
                if need.get(k, 0) < v:
                    need[k] = v
            wd = waited.setdefault(op.eng, {})
            for k, v in need.items():
                if wd.get(k, 0) >= v:
                    continue
                eng.wait_ge(sems[k], v)
                wd[k] = v
                nw += 1
            if op.fn is None:
                continue
            ins = op.fn(eng)
            if op.is_dma:
                ins.then_inc(sems[("slot", op.slot)], 16)
            elif op.marked:
                ins.then_inc(sems[("eng", op.eng)], 1)
        return dict(nops=len(self.ops), nwaits=nw, marks=cnt, nsems=len(sems))


def build_program(debug=False, upto=99):
    nc = bass.Bass("TRN2", target_bir_lowering=False)

    def din(name, shape, dtype=F32):
        return nc.dram_tensor(name, list(shape), dtype, kind="ExternalInput").ap()

    def dout(name, shape, dtype=F32):
        return nc.dram_tensor(name, list(shape), dtype, kind="ExternalOutput").ap()

    x_all = din("x_all", [SEQ, D])
    x_own = din("x_own", [TOK, D])
    qrel_d = din("qrel", [128, OWN])
    x_s = din("x_s", [1, D])
    ptab = din("ptab", [1, 128], I32)
    pool_ki8 = din("pool_ki8", [1280 * 8, 2048])
    pool_k16 = din("pool_k16", [1280 * 16, 2048])
    pool_v16 = din("pool_v16", [1280 * 16, 2048])
    g_in_d = din("g_in", [1, D])
    g_f_d = din("g_f", [1, D])
    ln_g_d = din("ln_g", [1, 1024])
    ln_b_d = din("ln_b", [1, 1024])
    bsp_d = din("bsp", [1, 1024])
    ws00_d = din("ws00", [1, 1024])
    bs0_d = din("bs0", [1, 1024])
    wspT_d = din("wspT", [128, 8 * 128])
    wkvki_d = din("wkvki", [128, NCH * 640])
    wfm_d = din("wfm", [N_FM, 128, NCH * 128])
    wwi_d = din("wwi", [128, NCH * 16])
    wvb_d = din("wvb", [128, NCH * 1024])
    wpa_d = din("wpa", [16, 128, 8 * 128])
    wpb_d = din("wpb", [16, 128, 8 * 128])
    wout_d = din("wout", [4, 128, NCH * 512])

    y_own = dout("y_own", [TOK, D])
    knew = dout("knew", [SEQ, 256])
    vnew = dout("vnew", [SEQ, 256])
    kinew = dout("kinew", [SEQ, 128])
    ys_o = dout("ys", [1, D])
    ks_o = dout("ks", [1, 256])
    vs_o = dout("vs", [1, 256])
    kis_o = dout("kis", [1, 128])
    gvs_o = dout("gvs", [1, 1024])
    dbg = {}
    if debug:
        dbg["qT"] = dout("dbg_qT", [128, 8 * TOK], BF16)
        dbg["kT"] = dout("dbg_kT", [128, 2 * SEQ], BF16)
        dbg["kiT"] = dout("dbg_kiT", [128, SEQ], BF16)
        dbg["qiT"] = dout("dbg_qiT", [128, 16 * TOK], BF16)
        dbg["wabs"] = dout("dbg_wabs", [128, 128])
        dbg["acc"] = dout("dbg_acc", [128, OWN * 4096])
        dbg["thr"] = dout("dbg_thr", [128, OWN])
        dbg["oaT"] = dout("dbg_oaT", [128, 8 * TOK], BF16)
        dbg["obT"] = dout("dbg_obT", [128, 8 * TOK], BF16)
        dbg["mixT"] = dout("dbg_mixT", [128, 16 * TOK], BF16)

    st = ExitStack()
    P = Prog(nc, st)
    ARENA_B = 207 * 1024
    arena = st.enter_context(nc.sbuf_tensor("arena", [128, ARENA_B // 2], BF16))
    psall = st.enter_context(nc.psum_tensor("psall", [128, 4096], F32))
    ps = [psall[:, i * 512:(i + 1) * 512] for i in range(8)]
    psb = [p.bitcast(BF16) for p in ps]


    def finish():
        fin = P.add("sp", None)
        lastd = {}
        for op in P.ops:
            if op.is_dma:
                lastd[op.slot] = op
        for op in lastd.values():
            fin.deps.add(op.idx)
        info = P.emit()
        st.close()
        return nc, info

    KB = 1024

    def buf(off_b, nbytes, dtype=BF16, parts=128):
        if off_b >= 20 * KB:
            off_b += KB
        assert off_b + nbytes <= ARENA_B, (off_b, nbytes)
        a = arena[0:parts, off_b // 2:(off_b + nbytes) // 2]
        if dtype != BF16:
            a = a.bitcast(dtype)
        return a

    def v3(ap, a):
        return ap.rearrange("p (a b) -> p a b", a=a)

    o = 0
    ident = buf(o, 256); o += 256
    identB4 = buf(o, 1024); o += 1024
    ones_bf = buf(o, 256); o += 256
    identf = buf(o, 512, F32); o += 512
    iota_f = buf(o, 2048, F32); o += 2048
    pow2 = buf(o, 128, F32); o += 128
    qrel = buf(o, 32, F32); o += 32
    epsc = buf(o, 16, F32); o += 16
    stat = buf(o, 256, F32); o += 256
    wabs = buf(o, 512, F32); o += 512
    wsgn = buf(o, 512, F32); o += 512
    sm = buf(o, 2048, F32); o += 2048
    iota_i = sm.bitcast(I32)
    smb = buf(o, 1024, BF16); o += 1024
    WsT = buf(o, 2048); o += 2048
    gbc = buf(o, 8192, F32); o += 8192
    ksv = buf(o, 2560, F32, parts=1); o += 2560
    assert o <= 21 * KB, o
    hsT = smb[:, 0:16]
    oazTs = smb[:, 16:24]
    obzTs = smb[:, 24:32]
    mixTs = smb[:, 32:48]

    kT_all = buf(20 * KB, 16 * KB)
    kiT_all = buf(36 * KB, 8 * KB)
    v_all = buf(44 * KB, 16 * KB)
    qT = buf(60 * KB, 16 * KB)
    qiT = buf(76 * KB, 32 * KB)
    hT_own = buf(108 * KB, 32 * KB)
    oaT = buf(140 * KB, 16 * KB)
    mixT = buf(156 * KB, 32 * KB)
    kT3 = v3(kT_all, 2)
    v3a = v3(v_all, 32)
    qT3 = v3(qT, 8)
    qiT3 = v3(qiT, 16)
    hT3 = v3(hT_own, 16)
    oaT3 = v3(oaT, 8)
    mixT3 = v3(mixT, 16)

    P.add("pool", lambda e: e.memset(identf, 1.0), writes=["identf"])
    P.add("pool", lambda e: e.affine_select(out=identf, in_=identf, pattern=[[-1, 128]], compare_op=ALU.is_equal,
                                            fill=0.0, base=0, channel_multiplier=1), reads=["identf"], writes=["identf"])
    P.add("dve", lambda e: e.tensor_copy(out=ident, in_=identf), reads=["identf"], writes=["ident"])
    for j in range(4):
        P.add("dve", lambda e, j=j: e.tensor_scalar(out=identB4[:, j * 128:(j + 1) * 128], in0=identf, scalar1=BIG, scalar2=None,
                                                    op0=ALU.mult), reads=["identf"], writes=["identB4"])
    P.add("pool", lambda e: e.memset(ones_bf, 1.0), writes=["ones"])
    P.add("pool", lambda e: e.iota(iota_i, pattern=[[1, 512]], base=0, channel_multiplier=0), writes=["iota_i"])
    P.add("dve", lambda e: e.tensor_copy(out=iota_f, in_=iota_i), reads=["iota_i"], writes=["iota"])
    for k in range(NIT + 1):
        P.add("pool", lambda e, k=k: e.memset(pow2[:, k:k + 1], float(2.0 ** -(k + 1))), writes=["pow2"])
    P.add("pool", lambda e: e.memset(epsc[:, 0:1], 1e-6), writes=["epsc"])
    P.add("pool", lambda e: e.memset(epsc[:, 1:2], 1e-5), writes=["epsc"])
    P.add("pool", lambda e: e.memset(epsc[:, 2:3], -0.5), writes=["epsc"])
    P.add("sp", lambda e: e.dma_start(out=qrel, in_=qrel_d), writes=["qrel"], slot="c_qrel")
    P.add("sp", lambda e: e.dma_start(out=gbc, in_=g_in_d.partition_broadcast(128)), writes=["gbc"], slot="c_gbc")

    def rms_block(src_ap, xb, xres, xslot, hb, hres, junk, sidx, gb_ap, nparts=128):
        ss = stat[0:nparts, sidx:sidx + 1]
        sd = stat[0:nparts, sidx + 1:sidx + 2]
        rs = stat[0:nparts, sidx + 2:sidx + 3]
        P.add("sp", lambda e: e.dma_start(out=xb, in_=src_ap), writes=[xres], slot=xslot)
        P.add("act", lambda e: e.activation(out=junk, in_=xb, func=AF.Square, accum_out=ss),
              reads=[xres], writes=["junk", ("st", sidx)])
        P.add("act", lambda e: e.activation(out=sd, in_=ss, func=AF.Sqrt, scale=1.0 / D, bias=epsc[0:nparts, 0:1]),
              reads=[("st", sidx), "epsc"], writes=[("st", sidx + 1)])
        P.add("dve", lambda e: e.reciprocal(out=rs, in_=sd), reads=[("st", sidx + 1)], writes=[("st", sidx + 2)])
        P.add("dve", lambda e: e.scalar_tensor_tensor(out=hb, in0=xb, scalar=rs, in1=gb_ap, op0=ALU.mult, op1=ALU.mult),
              reads=[xres, ("st", sidx + 2), "gbc"], writes=[hres])

    def transposes16(hb, hres, bankA, bankB):
        for c in range(16):
            bk = bankA if c < 8 else bankB
            cc = c % 8
            P.add("pe", lambda e, c=c, bk=bk, cc=cc: e.transpose(out=psb[bk][:, cc * 128:(cc + 1) * 128],
                                                                 in_=hb[:, c * 128:(c + 1) * 128], identity=ident),
                  reads=[hres, "ident"], writes=[("ps", bk)])

    if upto <= 0:
        return finish()
    S1 = 60 * KB
    xbuf = [buf(S1 + i * 8 * KB, 8 * KB, F32) for i in range(2)]
    hbuf = [buf(S1 + 16 * KB + i * 4 * KB, 4 * KB) for i in range(2)]
    junk = buf(S1 + 24 * KB, 4 * KB)
    hTblk = [buf(S1 + 28 * KB + i * 4 * KB, 4 * KB) for i in range(2)]
    Wkvki = buf(S1 + 36 * KB, 20 * KB)
    kvf = [buf(S1 + 56 * KB + i * 2560, 2560, F32) for i in range(2)]
    kb16 = [buf(S1 + 62 * KB + i * 768, 768) for i in range(2)]
    Wk3 = v3(Wkvki, 16)
    for q4 in range(8):
        P.add("pool", lambda e, q4=q4: e.dma_start(out=Wk3[:, q4 * 2:(q4 + 1) * 2, :],
                                                  in_=v3(wkvki_d, 16)[:, q4 * 2:(q4 + 1) * 2, :]),
              writes=["Wkvki"], slot="wkvki%d" % (q4 % 4))
    xs_f = buf(S1 + 64 * KB, 8 * KB, F32, parts=1)
    hs_f = buf(S1 + 72 * KB, 8 * KB, F32, parts=1)
    kvs = buf(S1 + 80 * KB, 2560, F32, parts=1)
    import os as _os
    if _os.environ.get('SKIP_S1'):
        P.add('pool', lambda e: e.memset(hsT, 0.0), writes=['hsT'])
    else:
        rms_block(x_s, xs_f, "xs_f", "xs_f", hs_f, "hs_f", junk[0:1, :], 8, gbc[0:1, :], nparts=1)
        for c in range(16):
            P.add("pe", lambda e, c=c: e.transpose(out=ps[4][:, c:c + 1], in_=hs_f[0:1, c * 128:(c + 1) * 128], identity=identf[0:1, 0:1]),
                  reads=["hs_f", "identf"], writes=[("ps", 4)])
        P.add("act", lambda e: e.copy(out=hsT, in_=ps[4][:, 0:16]), reads=[("ps", 4)], writes=["hsT"])
        for c in range(16):
            P.add("pe", lambda e, c=c: e.matmul(ps[6][0:1, 0:512], lhsT=hsT[:, c:c + 1], rhs=Wk3[:, c, 0:512], start=(c == 0), stop=(c == 15)),
                  reads=["hsT", "Wkvki"], writes=[("ps", 6)])
            P.add("pe", lambda e, c=c: e.matmul(ps[7][0:1, 0:128], lhsT=hsT[:, c:c + 1], rhs=Wk3[:, c, 512:640], start=(c == 0), stop=(c == 15)),
                  reads=["hsT", "Wkvki"], writes=[("ps", 7)])
        P.add("dve", lambda e: e.tensor_copy(out=kvs[:, 0:512], in_=ps[6][0:1, 0:512]), reads=[("ps", 6)], writes=["kvs0"])
        P.add("dve", lambda e: e.tensor_copy(out=kvs[:, 512:640], in_=ps[7][0:1, 0:128]), reads=[("ps", 7)], writes=["kvs1"])
        P.add("dve", lambda e: e.tensor_copy(out=ksv, in_=kvs), reads=["kvs0", "kvs1"], writes=["ksv"])
        P.add("sp", lambda e: e.dma_start(out=ks_o, in_=kvs[:, 0:256]), reads=["kvs0"], writes=["o_ks"], slot="o_ks")
        P.add("sp", lambda e: e.dma_start(out=vs_o, in_=kvs[:, 256:512]), reads=["kvs0"], writes=["o_vs"], slot="o_vs")
        P.add("sp", lambda e: e.dma_start(out=kis_o, in_=kvs[:, 512:640]), reads=["kvs1"], writes=["o_kis"], slot="o_kis")
    def s1_front(b):
        pb = b % 2
        rms_block(x_all[b * 128:(b + 1) * 128, :], xbuf[pb], ("xbuf", pb), "xbuf%d" % pb, hbuf[pb], ("hbuf", pb), junk, 4 * pb, gbc)

    def s1_t16(b, xbuf=xbuf, hbuf=hbuf, hTblk=hTblk):
        pb = b % 2
        hb, htb = hbuf[pb], hTblk[pb]
        bA, bB = (0, 1) if pb == 0 else (2, 3)
        transposes16(hb, ("hbuf", pb), bA, bB)
        P.add("act", lambda e: e.copy(out=htb[:, 0:1024], in_=psb[bA][:, 0:1024]), reads=[("ps", bA)], writes=[("htb", pb, 0)])
        P.add("dve", lambda e: e.tensor_copy(out=htb[:, 1024:2048], in_=psb[bB][:, 0:1024]), reads=[("ps", bB)], writes=[("htb", pb, 1)])

    def s1_mm(b, hTblk=hTblk, kvf=kvf, kb16=kb16):
        pb = b % 2
        htb3 = v3(hTblk[pb], 16)
        bKV, bKI = (4, 5) if pb == 0 else (6, 7)
        for c in range(16):
            P.add("pe", lambda e, c=c: e.matmul(ps[bKV][:, 0:512], lhsT=htb3[:, c, :], rhs=Wk3[:, c, 0:512], start=(c == 0), stop=(c == 15)),
                  reads=[("htb", pb, c // 8), "Wkvki"], writes=[("ps", bKV)])
            P.add("pe", lambda e, c=c: e.matmul(ps[bKI][:, 0:128], lhsT=htb3[:, c, :], rhs=Wk3[:, c, 512:640], start=(c == 0), stop=(c == 15)),
                  reads=[("htb", pb, c // 8), "Wkvki"], writes=[("ps", bKI)])
        kf = kvf[pb]
        k16 = kb16[pb]
        P.add("dve", lambda e: e.tensor_copy(out=kf[:, 0:512], in_=ps[bKV][:, 0:512]), reads=[("ps", bKV)], writes=[("kvf", pb, 0)])
        P.add("act", lambda e: e.copy(out=kf[:, 512:640], in_=ps[bKI][:, 0:128]), reads=[("ps", bKI)], writes=[("kvf", pb, 1)])
        P.add("dve", lambda e: e.tensor_copy(out=k16[:, 0:256], in_=ps[bKV][:, 0:256]), reads=[("ps", bKV)], writes=[("kb16", pb)])
        P.add("act", lambda e: e.copy(out=k16[:, 256:384], in_=ps[bKI][:, 0:128]), reads=[("ps", bKI)], writes=[("kb16", pb)])
        P.add("dve", lambda e: e.tensor_copy(out=v3a[:, b, :], in_=ps[bKV][:, 256:512]), reads=[("ps", bKV)], writes=[("v_all", b)])
        rows = slice(b * 128, (b + 1) * 128)
        P.add("pool", lambda e: e.dma_start(out=knew[rows, :], in_=kf[:, 0:256]), reads=[("kvf", pb, 0)], writes=[("o_k", pb)], slot="o_k%d" % pb)
        P.add("pool", lambda e: e.dma_start(out=vnew[rows, :], in_=kf[:, 256:512]), reads=[("kvf", pb, 0)], writes=[("o_v", pb)], slot="o_v%d" % pb)
        P.add("pool", lambda e: e.dma_start(out=kinew[rows, :], in_=kf[:, 512:640]), reads=[("kvf", pb, 1)], writes=[("o_ki", pb)], slot="o_ki%d" % pb)

    def s1_tk(b, kb16=kb16):
        pb = b % 2
        k16 = kb16[pb]
        bKI = 5 if pb == 0 else 7
        for j in range(3):
            P.add("pe", lambda e, j=j: e.transpose(out=psb[bKI][:, 256 + j * 128:256 + (j + 1) * 128], in_=k16[:, j * 128:(j + 1) * 128], identity=ident),
                  reads=[("kb16", pb), "ident"], writes=[("ps", bKI)])
        P.add("act", lambda e: e.copy(out=kT3[:, :, b * 128:(b + 1) * 128], in_=v3(psb[bKI][:, 256:512], 2)), reads=[("ps", bKI)], writes=[("kT", b)])
        P.add("act", lambda e: e.copy(out=kiT_all[:, b * 128:(b + 1) * 128], in_=psb[bKI][:, 512:640]), reads=[("ps", bKI)], writes=[("kiT", b)])

    s1_front(0)
    s1_front(1)
    s1_t16(0)
    s1_t16(1)
    for b in range(NB):
        if b + 2 < NB:
            s1_front(b + 2)
        s1_mm(b)
        if b + 2 < NB:
            s1_t16(b + 2)
        s1_tk(b)

    if upto <= 1:
        return finish()
    P.barrier()
    S2 = 140 * KB
    xbuf = [buf(S2 + i * 8 * KB, 8 * KB, F32) for i in range(2)]
    hbuf = [buf(S2 + 16 * KB + i * 4 * KB, 4 * KB) for i in range(2)]
    junk = buf(S2 + 24 * KB, 4 * KB)
    wbuf = [buf(S2 + 28 * KB + i * 4 * KB, 4 * KB) for i in range(3)]
    wwi = buf(S2 + 40 * KB, 512)

    def own_pass(xbuf, hbuf, junk):
        def front(i):
            pb = i % 2
            rms_block(x_own[i * 128:(i + 1) * 128, :], xbuf[pb], ("xbuf", pb), "xbuf%d" % pb, hbuf[pb], ("hbuf", pb), junk, 4 * pb, gbc)

        def back(i):
            pb = i % 2
            hb = hbuf[pb]
            bA, bB = (0, 1) if pb == 0 else (2, 3)
            transposes16(hb, ("hbuf", pb), bA, bB)
            P.add("act", lambda e, bA=bA, i=i: e.copy(out=hT3[:, 0:8, i * 128:(i + 1) * 128], in_=v3(psb[bA][:, 0:1024], 8)),
                  reads=[("ps", bA)], writes=[("hT", i, 0)])
            P.add("dve", lambda e, bB=bB, i=i: e.tensor_copy(out=hT3[:, 8:16, i * 128:(i + 1) * 128], in_=v3(psb[bB][:, 0:1024], 8)),
                  reads=[("ps", bB)], writes=[("hT", i, 1)])

        front(0)
        for i in range(OWN):
            if i + 1 < OWN:
                front(i + 1)
            back(i)

    own_pass(xbuf, hbuf, junk)

    hT_reads = [("hT", i, h) for i in range(OWN) for h in range(2)]
    fm_state = {"n": 0}

    fm_pref = {}

    def fm_load(cb):
        if cb in fm_pref:
            return fm_pref.pop(cb)
        ws = fm_state.get("nl", 0) % 3
        fm_state["nl"] = fm_state.get("nl", 0) + 1
        wsl = wbuf[ws]
        P.add("pool", lambda e: e.dma_start(out=wsl, in_=wfm_d[cb]), writes=[("wbuf", ws)], slot="wbuf%d" % ws)
        return ws

    def fm_prefetch(cb):
        if cb < N_FM and cb not in fm_pref:
            fm_pref[cb] = fm_load(cb)

    def proj_fm(cb, scol, evac, pa=None):
        n = fm_state["n"]
        fm_state["n"] += 1
        ws = fm_load(cb)
        wsl = wbuf[ws]
        w3 = v3(wsl, 16)
        if pa is None:
            pa = 2 * (n % 3)
        for c in range(16):
            for hf in range(2):
                P.add("pe", lambda e, c=c, hf=hf: e.matmul(ps[pa + hf][:, 0:512], lhsT=w3[:, c, :], rhs=hT3[:, c, hf * 512:(hf + 1) * 512],
                                                          start=(c == 0), stop=(c == 15)),
                      reads=[("wbuf", ws)] + hT_reads, writes=[("ps", pa + hf)])
            P.add("pe", lambda e, c=c: e.matmul(ps[6][:, scol:scol + 1], lhsT=w3[:, c, :], rhs=hsT[:, c:c + 1],
                                                start=(c == 0), stop=(c == 15)),
                  reads=[("wbuf", ws), "hsT"], writes=[("ps", 6)])
        evac(pa)

    def evac_copy(dst3, h):
        def f(pa):
            P.add("act", lambda e: e.copy(out=dst3[:, h, 0:512], in_=ps[pa][:, 0:512]), reads=[("ps", pa)], writes=[(id(dst3), h, 0)])
            P.add("dve", lambda e: e.tensor_copy(out=dst3[:, h, 512:1024], in_=ps[pa + 1][:, 0:512]), reads=[("ps", pa + 1)],
                  writes=[(id(dst3), h, 1)])
        return f

    for h in range(8):
        proj_fm(FM_Q + h, h, evac_copy(qT3, h))
    for h in range(16):
        proj_fm(FM_QI + h, 8 + h, evac_copy(qiT3, h))
    P.add("dve", lambda e: e.tensor_copy(out=sm[:, 0:24], in_=ps[6][:, 0:24]), reads=[("ps", 6)], writes=["sm_q"])
    P.add("pool", lambda e: e.dma_start(out=wwi, in_=wwi_d), writes=["wwi"], slot="wwi")
    wwi3 = v3(wwi, 16)
    for i in range(OWN):
        for c in range(16):
            P.add("pe", lambda e, i=i, c=c: e.matmul(ps[7][:, i * 16:(i + 1) * 16], lhsT=hT3[:, c, i * 128:(i + 1) * 128], rhs=wwi3[:, c, :],
                                                    start=(c == 0), stop=(c == 15)),
                  reads=["wwi"] + hT_reads, writes=[("ps", 7)])
    P.add("act", lambda e: e.copy(out=wabs, in_=ps[7][:, 0:128]), reads=[("ps", 7)], writes=["wabs"])
    P.add("act", lambda e: e.activation(out=wsgn, in_=ps[7][:, 0:128], func=AF.Sign), reads=[("ps", 7)], writes=["wsgn"])
    for c in range(16):
        P.add("pe", lambda e, c=c: e.matmul(ps[7][0:16, 128:129], lhsT=wwi3[:, c, :], rhs=hsT[:, c:c + 1], start=(c == 0), stop=(c == 15)),
              reads=["wwi", "hsT"], writes=[("ps", 7)])
    P.add("dve", lambda e: e.tensor_copy(out=sm[0:16, 120:121], in_=ps[7][0:16, 128:129]), reads=[("ps", 7)], writes=["sm_w"])
    if debug:
        P.add("sp", lambda e: e.dma_start(out=dbg["qT"], in_=qT), reads=[(id(qT3), h, j) for h in range(8) for j in range(2)], writes=["dbg_dq"], slot="dbg0")
        P.add("sp", lambda e: e.dma_start(out=dbg["qiT"], in_=qiT), reads=[(id(qiT3), h, j) for h in range(16) for j in range(2)], writes=["dbg_dqi"], slot="dbg1")
        P.add("sp", lambda e: e.dma_start(out=dbg["kT"], in_=kT_all), reads=[("kT", b) for b in range(NB)], writes=["dbg_dk"], slot="dbg2")
        P.add("sp", lambda e: e.dma_start(out=dbg["kiT"], in_=kiT_all), reads=[("kiT", b) for b in range(NB)], writes=["dbg_dki"], slot="dbg3")
        P.add("sp", lambda e: e.dma_start(out=dbg["wabs"], in_=wabs), reads=["wabs"], writes=["dbg_dwa"], slot="dbg4")

    if upto <= 2:
        return finish()
    P.barrier()
    acc = buf(108 * KB, 16 * KB, F32)
    negm = buf(124 * KB, 8 * KB)
    jnk3 = buf(132 * KB, 8 * KB)
    S3 = 156 * KB
    Rb = [buf(S3 + i * 2 * KB, 2 * KB) for i in range(3)]
    dsg = [buf(S3 + 16 * KB + i * 4 * KB, 4 * KB) for i in range(2)]
    PT = [buf(S3 + 6 * KB + i * KB, KB) for i in range(3)]
    pen = buf(S3 + 9 * KB, 2 * KB, F32)
    tmn = buf(S3 + 11 * KB, 2 * KB, F32)
    rdens = [buf(S3 + 13 * KB, 2 * KB, F32), buf(S3 + 41 * KB, 2 * KB, F32)]
    bs = buf(S3 + 15 * KB, 512, F32)
    HK = bs[:, 0:NIT + 1]
    TC = bs[:, 32:32 + NIT + 2]
    cnt = bs[:, 64:65]
    tmpc = bs[:, 65:66]
    rmax = bs[:, 66:67]
    rmin = bs[:, 67:68]
    rmin2 = bs[:, 68:69]
    lo0 = bs[:, 69:70]
    w0 = bs[:, 70:71]
    thr = bs[:, 72:80]
    NTC = bs[:, 80:80 + NIT + 2]
    sA = bs[:, 100:101]
    cnt2 = bs[:, 101:102]
    SCALE = float(128 ** -0.5)
    rcount = {"s": 0, "st": 0, "pt": 0}
    accs = [acc, buf(S3 + 25 * KB, 16 * KB, F32)]

    def idx_prep(i):
        dg3 = v3(dsg[i % 2], 16)
        for h in range(16):
            P.add("pool", lambda e, h=h: e.tensor_scalar(out=dg3[:, h, :], in0=identf, scalar1=wabs[:, i * 16 + h:i * 16 + h + 1], scalar2=0.0,
                                                        op0=ALU.mult, op1=ALU.add),
                  reads=["identf", "wabs"], writes=[("dsg", i % 2)])

    def idx_group(i, kg):
        ac = accs[i % 2]
        dg3 = v3(dsg[i % 2], 16)
        tok = slice(i * 128, (i + 1) * 128)
        cols = slice(kg * 512, (kg + 1) * 512)

        def emit_s(j):
            n = rcount["s"]
            rcount["s"] += 1
            pa = 2 * (n % 2)
            rb = Rb[n % 3]
            for u in range(2):
                h = 2 * j + u
                P.add("pe", lambda e, h=h, u=u: e.matmul(ps[pa + u][:, 0:512], lhsT=qiT3[:, h, tok], rhs=kiT_all[:, cols], start=True, stop=True),
                      reads=["qiT", "kiT"], writes=[("ps", pa + u)])
            P.add("act", lambda e: e.activation(out=rb, in_=psall[:, pa * 512:(pa + 2) * 512], func=AF.Relu),
                  reads=[("ps", pa), ("ps", pa + 1)], writes=[("Rb", n % 3)])
            return rb, n % 3

        def emit_acc(j, rbn):
            rb, rn = rbn
            for u in range(2):
                h = 2 * j + u
                P.add("pe", lambda e, h=h, u=u: e.matmul(ps[4][:, 0:512], lhsT=dg3[:, h, :], rhs=rb[:, u * 512:(u + 1) * 512], start=(h == 0), stop=(h == 15)),
                      reads=[("Rb", rn), ("dsg", i % 2)], writes=[("ps", 4)])

        prev = emit_s(0)
        for j in range(8):
            nxt = emit_s(j + 1) if j < 7 else None
            emit_acc(j, prev)
            prev = nxt
        P.add("dve", lambda e: e.tensor_copy(out=ac[:, cols], in_=ps[4][:, 0:512]), reads=[("ps", 4)], writes=[("acc", i % 2, kg)])

    def bis_setup(i):
        ac = accs[i % 2]
        span = 512 * (i + 1)
        accr = [("acc", i % 2, kg) for kg in range(i + 1)]
        last = slice(i * 512, (i + 1) * 512)
        lres = ("acc", i % 2, i)
        P.add("dve", lambda e: e.tensor_scalar(out=pen, in0=iota_f, scalar1=qrel[:, i:i + 1], scalar2=CBIG, op0=ALU.is_gt, op1=ALU.mult),
              reads=["iota", "qrel"], writes=["pen"])
        P.add("dve", lambda e: e.tensor_tensor(out=tmn, in0=ac[:, last], in1=pen, op=ALU.add), reads=["pen", lres], writes=["tmn"])
        P.add("dve", lambda e: e.tensor_reduce(out=rmin, in_=tmn, axis=AX.X, op=ALU.min), reads=["tmn"], writes=["rmin"])
        P.add("dve", lambda e: e.tensor_tensor(out=ac[:, last], in0=ac[:, last], in1=pen, op=ALU.subtract), reads=["pen", lres], writes=[lres])
        P.add("dve", lambda e: e.tensor_reduce(out=rmax, in_=ac[:, 0:span], axis=AX.X, op=ALU.max), reads=accr, writes=["rmax"])
        if i > 0:
            P.add("dve", lambda e: e.tensor_reduce(out=rmin2, in_=ac[:, 0:i * 512], axis=AX.X, op=ALU.min), reads=accr, writes=["rmin2"])
            P.add("dve", lambda e: e.tensor_tensor(out=rmin, in0=rmin, in1=rmin2, op=ALU.min), reads=["rmin", "rmin2"], writes=["rmin"])
        P.add("dve", lambda e: e.tensor_tensor(out=w0, in0=rmax, in1=rmin, op=ALU.subtract), reads=["rmax", "rmin"], writes=["w0"])
        P.add("dve", lambda e: e.tensor_scalar(out=w0, in0=w0, scalar1=float(2.0 ** -10), scalar2=1e-3, op0=ALU.mult, op1=ALU.add), reads=["w0"], writes=["w0b"])
        P.add("dve", lambda e: e.tensor_tensor(out=lo0, in0=rmin, in1=w0, op=ALU.subtract), reads=["rmin", "w0b"], writes=["lo0"])
        P.add("dve", lambda e: e.tensor_tensor(out=w0, in0=rmax, in1=lo0, op=ALU.subtract), reads=["rmax", "lo0"], writes=["w0c"])
        P.add("dve", lambda e: e.tensor_scalar(out=HK, in0=pow2[:, 0:NIT + 1], scalar1=w0, scalar2=None, op0=ALU.mult), reads=["pow2", "w0c"], writes=["HK"])
        P.add("dve", lambda e: e.tensor_tensor(out=TC[:, 0:1], in0=lo0, in1=HK[:, 0:1], op=ALU.add), reads=["lo0", "HK"], writes=[("TC", 0)])

    def bis_iter(i, k):
        ac = accs[i % 2]
        span = 512 * (i + 1)
        accr = [("acc", i % 2, kg) for kg in range(i + 1)]
        P.add("dve", lambda e: e.tensor_scalar(out=jnk3[:, 0:span], in0=ac[:, 0:span], scalar1=TC[:, k:k + 1], scalar2=None,
                                              op0=ALU.is_ge, op1=ALU.add, accum_out=cnt),
              reads=accr + [("TC", k)], writes=["jnk3", "cnt"])
        P.add("dve", lambda e: e.scalar_tensor_tensor(out=tmpc, in0=cnt, scalar=TOPK - 0.5, in1=HK[:, k:k + 1], op0=ALU.is_ge, op1=ALU.mult),
              reads=["cnt", "HK"], writes=["tmpc"])
        P.add("dve", lambda e: e.scalar_tensor_tensor(out=TC[:, k + 1:k + 2], in0=tmpc, scalar=HK[:, k + 1:k + 2], in1=TC[:, k:k + 1],
                                                      op0=ALU.subtract, op1=ALU.add),
              reads=["tmpc", "HK", ("TC", k)], writes=[("TC", k + 1)])

    def bis_final(i):
        ac = accs[i % 2]
        span = 512 * (i + 1)
        accr = [("acc", i % 2, kg) for kg in range(i + 1)]
        P.add("dve", lambda e: e.tensor_tensor(out=thr[:, i:i + 1], in0=TC[:, NIT:NIT + 1], in1=HK[:, NIT:NIT + 1], op=ALU.subtract),
              reads=[("TC", NIT), "HK"], writes=[("thr", i)])
        P.add("dve", lambda e: e.tensor_scalar(out=negm[:, 0:span], in0=ac[:, 0:span], scalar1=thr[:, i:i + 1], scalar2=1.0,
                                              op0=ALU.is_ge, op1=ALU.subtract),
              reads=accr + [("thr", i)], writes=["negm"])
        if debug:
            P.add("sp", lambda e: e.dma_start(out=dbg["acc"][:, i * 4096:i * 4096 + span], in_=ac[:, 0:span]), reads=accr, writes=["dbg_dacc"], slot="dbg5")

    def attention(i, mid_cb=None):
        tok = slice(i * 128, (i + 1) * 128)
        nkb = 4 * (i + 1)
        for g in range(2):
            qrhs = qT3[:, 4 * g:4 * g + 4, tok]

            def emit_st(kb, g=g, qrhs=qrhs):
                n = rcount["st"]
                rcount["st"] += 1
                bk = 1 + 2 * (n % 2)
                P.add("pe", lambda e: e.matmul(ps[bk][:, 0:512], lhsT=kT3[:, g, kb * 128:(kb + 1) * 128], rhs=qrhs, start=True, stop=False),
                      reads=["qT", "kT"], writes=[("ps", bk)])
                P.add("pe", lambda e: e.matmul(ps[bk][:, 0:512], lhsT=negm[:, kb * 128:(kb + 1) * 128], rhs=identB4, start=False, stop=True),
                      reads=["negm", "identB4"], writes=[("ps", bk)])
                return bk

            bo, bd = (5, 6) if g == 0 else (7, 2)

            def emit_pv(kb, bk, g=g, nkb=nkb, bo=bo, bd=bd):
                n = rcount["pt"]
                rcount["pt"] += 1
                pt = PT[n % 3]
                P.add("act", lambda e: e.activation(out=pt, in_=ps[bk][:, 0:512], func=AF.Exp, scale=SCALE), reads=[("ps", bk)], writes=[("PT", n % 3)])
                P.add("pe", lambda e: e.matmul(ps[bo][:, 0:512], lhsT=v3a[:, kb, g * 128:(g + 1) * 128], rhs=pt, start=(kb == 0), stop=(kb == nkb - 1)),
                      reads=[("PT", n % 3), "v_all"], writes=[("ps", bo)])
                P.add("pe", lambda e: e.matmul(ps[bd][:, 0:512], lhsT=ones_bf, rhs=pt, start=(kb == 0), stop=(kb == nkb - 1)),
                      reads=[("PT", n % 3), "ones"], writes=[("ps", bd)])

            prev = None
            for kb in range(nkb):
                bk = emit_st(kb)
                if prev is not None:
                    emit_pv(*prev)
                prev = (kb, bk)
            emit_pv(*prev)
            if mid_cb is not None:
                mid_cb()
            rden = rdens[g]
            P.add("dve", lambda e, bd=bd, rden=rden: e.reciprocal(out=rden, in_=ps[bd][:, 0:512]), reads=[("ps", bd)], writes=[("rden", g)])
            P.add("dve", lambda e, g=g, bo=bo, rden=rden: e.tensor_tensor(out=oaT3[:, 4 * g:4 * g + 4, tok], in0=v3(ps[bo][:, 0:512], 4), in1=v3(rden, 4), op=ALU.mult),
                  reads=[("ps", bo), ("rden", g)], writes=[("oaT", i, g)])

    kd = {}
    setup_done = set()
    PRE_IT = 10

    def ensure_setup(i):
        if i not in setup_done:
            bis_setup(i)
            setup_done.add(i)
            kd[i] = 0

    def emit_iters(i, upto_k):
        while kd[i] < upto_k:
            bis_iter(i, kd[i])
            kd[i] += 1

    idx_prep(0)
    idx_group(0, 0)
    for i in range(OWN):
        ensure_setup(i)
        if i + 1 < OWN:
            idx_prep(i + 1)
            ngr = i + 2
            k0 = kd[i]
            for kg in range(ngr):
                idx_group(i + 1, kg)
                emit_iters(i, k0 + ((NIT - k0) * (kg + 1)) // ngr)
        emit_iters(i, NIT)
        bis_final(i)
        if i + 1 < OWN:
            ensure_setup(i + 1)
            attention(i, mid_cb=lambda i=i: emit_iters(i + 1, kd[i + 1] + PRE_IT // 2))
        else:
            attention(i)
    if debug:
        P.add("sp", lambda e: e.dma_start(out=dbg["thr"], in_=thr), reads=[("thr", i) for i in range(OWN)], writes=["dbg_dthr"], slot="dbg6")
        P.add("sp", lambda e: e.dma_start(out=dbg["oaT"], in_=oaT), reads=[("oaT", i, g) for i in range(OWN) for g in range(2)], writes=["dbg_doa"], slot="dbg7")

    if upto <= 3:
        return finish()

    P.barrier()
    oaTs = sm[:, 112:120]
    B3 = 20 * KB
    idxp_i = buf(B3, 64, I32)
    idx8 = buf(B3 + 64, 64, I32)
    idx16 = buf(B3 + 128, 64, I32)
    pgf = buf(B3 + 192, 64, F32)
    ones_f = buf(B3 + 512, 512, F32)
    scb = buf(B3 + 1024, 1024, F32)
    sc = scb[:, 0:129]
    vld = buf(B3 + 2048, 1024, F32)[:, 0:129]
    kib = [buf(B3 + 4 * KB + i * 4 * KB, 4 * KB) for i in range(2)] + [buf(172 * KB + i * 4 * KB, 4 * KB) for i in range(6)]
    Vall = [buf(68 * KB + i * 4 * KB, 4 * KB) for i in range(16)]
    kidT = [buf(B3 + 12 * KB + i * KB, KB) for i in range(2)]
    Rs = [buf(B3 + 14 * KB + i * 2 * KB, KB) for i in range(2)]
    Kc = [buf(B3 + 18 * KB + i * 4 * KB, 4 * KB) for i in range(2)] + [buf(132 * KB + i * 4 * KB, 4 * KB) for i in range(2)]
    KTb = [buf(B3 + 26 * KB + i * 4 * KB, 4 * KB) for i in range(2)]
    Kx = buf(B3 + 34 * KB, 512)
    Vx = buf(B3 + 35 * KB, 512)
    Psb = buf(B3 + 36 * KB, 4224, F32)
    Pbf = buf(B3 + 41 * KB, 2112)
    Pred = buf(B3 + 44 * KB, 64, F32)
    smx = buf(B3 + 45 * KB, 256, F32)
    osb = buf(B3 + 46 * KB, 1024, F32)
    qiTs_bf = smb[:, 56:72]
    qTs_bf = smb[:, 48:56]
    w_s = smb[0:16, 74:75]
    P.add("dve", lambda e: e.tensor_copy(out=w_s, in_=sm[0:16, 120:121]), reads=["sm_w"], writes=["w_s"])
    IOA = bass.IndirectOffsetOnAxis
    P.add("sp", lambda e: e.dma_start(out=idxp_i[:, 0:1], in_=ptab.rearrange("o p -> p o")), writes=["idxp"], slot="s_idx")
    P.add("dve", lambda e: e.tensor_copy(out=pgf[:, 0:1], in_=idxp_i[:, 0:1]), reads=["idxp"], writes=["pgf0"])
    P.add("dve", lambda e: e.tensor_scalar(out=pgf[:, 1:2], in0=pgf[:, 0:1], scalar1=8.0, scalar2=None, op0=ALU.mult), reads=["pgf0"], writes=["pgf1"])
    P.add("dve", lambda e: e.tensor_scalar(out=pgf[:, 2:3], in0=pgf[:, 0:1], scalar1=16.0, scalar2=None, op0=ALU.mult), reads=["pgf0"], writes=["pgf2"])
    P.add("dve", lambda e: e.tensor_scalar(out=idx8, in0=iota_f[:, 0:16], scalar1=pgf[:, 1:2], scalar2=None, op0=ALU.add), reads=["pgf1", "iota"], writes=["idx8"])
    P.add("dve", lambda e: e.tensor_scalar(out=idx16, in0=iota_f[:, 0:16], scalar1=pgf[:, 2:3], scalar2=None, op0=ALU.add), reads=["pgf2", "iota"], writes=["idx16"])
    P.add("pool", lambda e: e.memset(ones_f, 1.0), writes=["ones_f"])
    P.add("dve", lambda e: e.tensor_copy(out=qiTs_bf, in_=sm[:, 8:24]), reads=["sm_q"], writes=["qis_bf"])
    P.add("dve", lambda e: e.tensor_copy(out=qTs_bf, in_=sm[:, 0:8]), reads=["sm_q"], writes=["qs_bf"])
    for ch in range(8):
        P.add("pool", lambda e, ch=ch: e.indirect_dma_start(out=kib[ch], out_offset=None, in_=pool_ki8, in_offset=IOA(ap=idx8[:, ch:ch + 1], axis=0)),
              reads=["idx8"], writes=[("kib", ch)], slot="s_kib%d" % ch)
    for ch in range(16):
        P.add("pool", lambda e, ch=ch: e.indirect_dma_start(out=Vall[ch], out_offset=None, in_=pool_v16, in_offset=IOA(ap=idx16[:, ch:ch + 1], axis=0)),
              reads=["idx16"], writes=[("Vall", ch)], slot="s_v%d" % ch)

    def ix_T(g):
        ch, grp = g // 4, g % 4
        tb_ = g % 2
        for j in range(4):
            pos_l = 4 * grp + j
            P.add("pe", lambda e, j=j, pos_l=pos_l: e.transpose(out=psb[tb_][:, j * 128:(j + 1) * 128], in_=kib[ch][:, pos_l * 128:(pos_l + 1) * 128], identity=ident),
                  reads=[("kib", ch), "ident"], writes=[("ps", tb_)])
        P.add("act", lambda e: e.copy(out=kidT[g % 2], in_=psb[tb_][:, 0:512]), reads=[("ps", tb_)], writes=[("kidT", g % 2)])

    def ix_S(g):
        tb_ = g % 2
        P.add("pe", lambda e: e.matmul(ps[2 + tb_][0:16, 0:512], lhsT=qiTs_bf, rhs=kidT[g % 2], start=True, stop=True),
              reads=[("kidT", g % 2), "qis_bf"], writes=[("ps", 2 + tb_)])
        P.add("act", lambda e: e.activation(out=Rs[g % 2][0:16, :], in_=ps[2 + tb_][0:16, 0:512], func=AF.Relu), reads=[("ps", 2 + tb_)], writes=[("Rs", g % 2)])

    def ix_SC(g):
        for j in range(4):
            pos = g * 4 + j
            P.add("pe", lambda e, j=j, pos=pos: e.matmul(ps[4][:, pos:pos + 1], lhsT=Rs[g % 2][0:16, j * 128:(j + 1) * 128], rhs=w_s, start=True, stop=True),
                  reads=[("Rs", g % 2), "w_s"], writes=[("ps", 4)])

    for step in range(32 + 2):
        if step < 32:
            ix_T(step)
        if 1 <= step <= 32:
            ix_S(step - 1)
        if step >= 2:
            ix_SC(step - 2)
    P.add("pe", lambda e: e.transpose(out=ps[5][:, 0:1], in_=ksv[0:1, 512:640], identity=identf[0:1, 0:1]), reads=["ksv", "identf"], writes=[("ps", 5)])
    P.add("act", lambda e: e.copy(out=smb[:, 72:73], in_=ps[5][:, 0:1]), reads=[("ps", 5)], writes=["kisT"])
    P.add("pe", lambda e: e.matmul(ps[5][0:16, 8:9], lhsT=qiTs_bf, rhs=smb[:, 72:73], start=True, stop=True), reads=["kisT", "qis_bf"], writes=[("ps", 5)])
    P.add("act", lambda e: e.activation(out=smb[0:16, 76:77], in_=ps[5][0:16, 8:9], func=AF.Relu), reads=[("ps", 5)], writes=["Rn"])
    P.add("pe", lambda e: e.matmul(ps[5][0:1, 16:17], lhsT=smb[0:16, 76:77], rhs=w_s, start=True, stop=True), reads=["Rn", "w_s"], writes=[("ps", 5)])
    P.add("pool", lambda e: e.memset(scb[:, 128:129], -CBIG), writes=["sc_x"])
    P.add("dve", lambda e: e.tensor_copy(out=scb[0:1, 128:129], in_=ps[5][0:1, 16:17]), reads=[("ps", 5), "sc_x"], writes=["sc_x"])
    P.add("dve", lambda e: e.tensor_copy(out=scb[:, 0:128], in_=ps[4][:, 0:128]), reads=[("ps", 4)], writes=["sc_m"])
    pm = smx[:, 2:4]
    P.add("dve", lambda e: e.tensor_reduce(out=smx[:, 2:3], in_=sc, axis=AX.X, op=ALU.max), reads=["sc_x", "sc_m"], writes=["pm0"])
    P.add("dve", lambda e: e.tensor_reduce(out=smx[:, 3:4], in_=scb[:, 0:128], axis=AX.X, op=ALU.min, negate=True), reads=["sc_m"], writes=["pm1"])
    P.add("pe", lambda e: e.transpose(out=ps[5][0:2, 32:160], in_=pm, identity=identf), reads=["pm0", "pm1", "identf"], writes=[("ps", 5)])
    P.add("dve", lambda e: e.tensor_reduce(out=smx[0:2, 4:5], in_=ps[5][0:2, 32:160], axis=AX.X, op=ALU.max), reads=[("ps", 5)], writes=["g2"])
    P.add("dve", lambda e: e.tensor_scalar(out=smx[0:2, 6:8], in0=identf[0:2, 0:2], scalar1=smx[0:2, 4:5], scalar2=None, op0=ALU.mult),
          reads=["g2", "identf"], writes=["g2d"])
    P.add("pe", lambda e: e.matmul(ps[5][:, 200:202], lhsT=ones_f[0:2, :], rhs=smx[0:2, 6:8], start=True, stop=True), reads=["g2d", "ones_f"], writes=[("ps", 5)])
    P.add("dve", lambda e: e.tensor_copy(out=smx[:, 8:10], in_=ps[5][:, 200:202]), reads=[("ps", 5)], writes=["gmm"])
    gmax = smx[:, 8:9]
    ngmin = smx[:, 9:10]
    P.add("dve", lambda e: e.tensor_tensor(out=w0, in0=gmax, in1=ngmin, op=ALU.add), reads=["gmm"], writes=["w0"])
    P.add("dve", lambda e: e.tensor_scalar(out=w0, in0=w0, scalar1=float(2.0 ** -10), scalar2=1e-3, op0=ALU.mult, op1=ALU.add), reads=["w0"], writes=["w0b"])
    P.add("dve", lambda e: e.tensor_tensor(out=lo0, in0=ngmin, in1=w0, op=ALU.add), reads=["gmm", "w0b"], writes=["lo0n"])
    P.add("dve", lambda e: e.tensor_scalar(out=lo0, in0=lo0, scalar1=-1.0, scalar2=None, op0=ALU.mult), reads=["lo0n"], writes=["lo0"])
    P.add("dve", lambda e: e.tensor_tensor(out=w0, in0=gmax, in1=lo0, op=ALU.subtract), reads=["gmm", "lo0"], writes=["w0c"])
    P.add("dve", lambda e: e.tensor_scalar(out=HK, in0=pow2[:, 0:NIT + 1], scalar1=w0, scalar2=None, op0=ALU.mult), reads=["pow2", "w0c"], writes=["HK"])
    P.add("dve", lambda e: e.tensor_tensor(out=TC[:, 0:1], in0=lo0, in1=HK[:, 0:1], op=ALU.add), reads=["lo0", "HK"], writes=[("TC", 0)])
    for k in range(NIT):
        P.add("dve", lambda e, k=k: e.tensor_scalar(out=vld, in0=sc, scalar1=TC[:, k:k + 1], scalar2=None, op0=ALU.is_ge, op1=ALU.add, accum_out=cnt),
              reads=["sc_x", "sc_m", ("TC", k)], writes=["vld", "cnt"])
        P.add("pe", lambda e: e.matmul(ps[6][:, 0:1], lhsT=ones_f, rhs=cnt, start=True, stop=True), reads=["cnt", "ones_f"], writes=[("ps", 6)])
        P.add("dve", lambda e, k=k: e.scalar_tensor_tensor(out=tmpc, in0=ps[6][:, 0:1], scalar=TOPK - 0.5, in1=HK[:, k:k + 1], op0=ALU.is_ge, op1=ALU.mult),
              reads=[("ps", 6), "HK"], writes=["tmpc"])
        P.add("dve", lambda e, k=k: e.scalar_tensor_tensor(out=TC[:, k + 1:k + 2], in0=tmpc, scalar=HK[:, k + 1:k + 2], in1=TC[:, k:k + 1],
                                                           op0=ALU.subtract, op1=ALU.add),
              reads=["tmpc", "HK", ("TC", k)], writes=[("TC", k + 1)])
    thr_s = smx[:, 12:13]
    P.add("dve", lambda e: e.tensor_tensor(out=thr_s, in0=TC[:, NIT:NIT + 1], in1=HK[:, NIT:NIT + 1], op=ALU.subtract), reads=[("TC", NIT), "HK"], writes=["thr_s"])
    P.add("dve", lambda e: e.tensor_scalar(out=vld, in0=sc, scalar1=thr_s, scalar2=None, op0=ALU.is_ge), reads=["sc_x", "sc_m", "thr_s"], writes=["vld"])
    P.add("pool", lambda e: e.memset(Kx, 0.0), writes=["Kx"])
    P.add("pool", lambda e: e.memset(Vx, 0.0), writes=["Vx"])
    P.add("dve", lambda e: e.tensor_copy(out=Kx[0:1, :], in_=ksv[0:1, 0:256]), reads=["ksv", "Kx"], writes=["Kx"])
    P.add("dve", lambda e: e.tensor_copy(out=Vx[0:1, :], in_=ksv[0:1, 256:512]), reads=["ksv", "Vx"], writes=["Vx"])

    def s_col(pos):
        return (4 + pos // 64, (pos % 64) * 8) if pos < 128 else (6, 0)

    def k_T(ch):
        npos = 8 if ch < 16 else 1
        if ch < 16:
            kc_ = Kc[ch % 4]
            kres = ("Kc", ch % 4)
        else:
            kc_ = Kx
            kres = "Kx"
        ktb = KTb[ch % 2]
        ba = 2 * (ch % 2)
        for t in range(2 * npos):
            bk = ba + t // 8
            P.add("pe", lambda e, t=t, bk=bk: e.transpose(out=psb[bk][:, (t % 8) * 128:(t % 8 + 1) * 128], in_=kc_[:, t * 128:(t + 1) * 128], identity=ident),
                  reads=[kres, "ident"], writes=[("ps", bk)])
        nb_ = (2 * npos + 7) // 8
        for hb in range(nb_):
            ncol = min(8, 2 * npos - 8 * hb) * 128
            if hb == 0:
                P.add("act", lambda e, ncol=ncol: e.copy(out=ktb[:, 0:ncol], in_=psb[ba][:, 0:ncol]), reads=[("ps", ba)], writes=[("KTb", ch % 2, 0)])
            else:
                P.add("dve", lambda e, ncol=ncol: e.tensor_copy(out=ktb[:, 1024:1024 + ncol], in_=psb[ba + 1][:, 0:ncol]), reads=[("ps", ba + 1)], writes=[("KTb", ch % 2, 1)])

    def k_MM(ch):
        npos = 8 if ch < 16 else 1
        ktb = KTb[ch % 2]
        for p in range(npos):
            pos = ch * 8 + p
            bk, col = s_col(pos)
            for g in range(2):
                t = 2 * p + g
                P.add("pe", lambda e, t=t, g=g, bk=bk, col=col: e.matmul(ps[bk][:, col + 4 * g:col + 4 * g + 4], lhsT=ktb[:, t * 128:(t + 1) * 128],
                                                                     rhs=qTs_bf[:, 4 * g:4 * g + 4], start=True, stop=True),
                      reads=[("KTb", ch % 2, t // 8), "qs_bf"], writes=[("ps", bk)])

    def k_gather(ch):
        if ch < 16:
            P.add("pool", lambda e: e.indirect_dma_start(out=Kc[ch % 4], out_offset=None, in_=pool_k16, in_offset=IOA(ap=idx16[:, ch:ch + 1], axis=0)),
                  reads=["idx16"], writes=[("Kc", ch % 4)], slot="s_kc%d" % (ch % 4))

    for ch in range(3):
        k_gather(ch)
    k_T(0)
    for ch in range(17):
        k_gather(ch + 3)
        if ch + 1 < 17:
            k_T(ch + 1)
        k_MM(ch)
    P.add("act", lambda e: e.activation(out=Psb[:, 0:512], in_=ps[4][:, 0:512], func=AF.Exp, scale=SCALE), reads=[("ps", 4)], writes=["Psb0"])
    P.add("act", lambda e: e.activation(out=Psb[:, 512:1024], in_=ps[5][:, 0:512], func=AF.Exp, scale=SCALE), reads=[("ps", 5)], writes=["Psb1"])
    P.add("act", lambda e: e.activation(out=Psb[:, 1024:1032], in_=ps[6][:, 0:8], func=AF.Exp, scale=SCALE), reads=[("ps", 6)], writes=["Psb2"])
    P3 = Psb[:, 0:1032].rearrange("p (s h) -> p s h", h=8)
    Pb3 = Pbf[:, 0:1032].rearrange("p (s h) -> p s h", h=8)
    P.add("dve", lambda e: e.tensor_tensor(out=Pb3, in0=P3, in1=vld.unsqueeze(2).to_broadcast([128, 129, 8]), op=ALU.mult),
          reads=["Psb0", "Psb1", "Psb2", "vld"], writes=["Pbf"])
    P.add("dve", lambda e: e.tensor_reduce(out=Pred[:, 0:8], in_=Pbf[:, 0:1032].rearrange("p (s h) -> p h s", h=8), axis=AX.X, op=ALU.add),
          reads=["Pbf"], writes=["Pred"])
    for ch in range(17):
        npos = 8 if ch < 16 else 1
        vc_ = Vall[ch] if ch < 16 else Vx
        vres = ("Vall", ch) if ch < 16 else "Vx"
        for p in range(npos):
            pos = ch * 8 + p
            for g in range(2):
                P.add("pe", lambda e, vc_=vc_, p=p, g=g, pos=pos: e.matmul(ps[g][0:4, 0:128], lhsT=Pbf[:, pos * 8 + 4 * g:pos * 8 + 4 * g + 4],
                                                                       rhs=vc_[:, p * 256 + g * 128:p * 256 + (g + 1) * 128],
                                                                       start=(pos == 0), stop=(pos == 128)),
                      reads=[vres, "Pbf"], writes=[("ps", g)])
    for g in range(2):
        P.add("pe", lambda e, g=g: e.matmul(ps[2][0:4, g:g + 1], lhsT=Pred[:, 4 * g:4 * g + 4], rhs=ones_f[:, 0:1], start=True, stop=True),
              reads=["Pred", "ones_f"], writes=[("ps", 2)])
    P.add("dve", lambda e: e.reciprocal(out=smx[0:4, 16:18], in_=ps[2][0:4, 0:2]), reads=[("ps", 2)], writes=["rden_s"])
    for g in range(2):
        P.add("dve", lambda e, g=g: e.tensor_scalar(out=osb[0:4, g * 128:(g + 1) * 128], in0=ps[g][0:4, 0:128], scalar1=smx[0:4, 16 + g:17 + g], scalar2=None,
                                                   op0=ALU.mult), reads=[("ps", g), "rden_s"], writes=[("osb", g)])
        P.add("pe", lambda e, g=g: e.transpose(out=ps[3][:, 4 * g:4 * g + 4], in_=osb[0:4, g * 128:(g + 1) * 128], identity=identf[0:4, 0:4]),
              reads=[("osb", g), "identf"], writes=[("ps", 3)])
    P.add("dve", lambda e: e.tensor_copy(out=oaTs, in_=ps[3][:, 0:8]), reads=[("ps", 3)], writes=["oaTs"])

    P.barrier()
    S4 = 20 * KB
    xbuf = [buf(S4 + i * 8 * KB, 8 * KB, F32) for i in range(2)]
    hbuf = [buf(S4 + 16 * KB + i * 4 * KB, 4 * KB) for i in range(2)]
    junk = buf(S4 + 24 * KB, 4 * KB)
    wvbuf = buf(52 * KB, 32 * KB)
    wv3 = v3(wvbuf, 16)
    bspbc = buf(200 * KB, 4 * KB, F32)
    for q4 in range(8):
        P.add("pool", lambda e, q4=q4: e.dma_start(out=wv3[:, q4 * 2:(q4 + 1) * 2, :], in_=v3(wvb_d, 16)[:, q4 * 2:(q4 + 1) * 2, :]),
              writes=["wvbuf"], slot="wvb%d" % (q4 % 4))
    P.add("sp", lambda e: e.dma_start(out=bspbc, in_=bsp_d.partition_broadcast(128)), writes=["bspbc"], slot="c_bsp")
    P.add("pool", lambda e: e.dma_start(out=WsT, in_=wspT_d), writes=["WsT"], slot="c_wsp")
    P.add("pool", lambda e: e.affine_select(out=v3(WsT, 8), in_=v3(WsT, 8), pattern=[[0, 8], [1, 128]], compare_op=ALU.is_ge,
                                            fill=0.0, base=0, channel_multiplier=-1), reads=["WsT"], writes=["WsT"])
    own_pass(xbuf, hbuf, junk)

    P.barrier()
    uT = buf(20 * KB, 16 * KB)
    obT = buf(36 * KB, 16 * KB)
    wbuf = [buf(84 * KB + i * 4 * KB, 4 * KB) for i in range(3)]
    wpbuf = [buf(96 * KB + i * 2 * KB, 2 * KB) for i in range(2)]
    szb = [buf(100 * KB + i * KB, KB) for i in range(2)]
    ftmp = buf(102 * KB, 4 * KB, F32)
    ftmp2 = buf(188 * KB, 4 * KB, F32)
    gvb = buf(192 * KB, 4 * KB, F32)
    vnb = [buf(196 * KB + i * 2 * KB, 2 * KB) for i in range(2)]
    uT3 = v3(uT, 8)
    obT3 = v3(obT, 8)
    P.add("sp", lambda e: e.dma_start(out=gbc[:, 0:1024], in_=ln_g_d.partition_broadcast(128)), writes=["gbc"], slot="c_gbc")
    P.add("sp", lambda e: e.dma_start(out=gbc[:, 1024:2048], in_=ln_b_d.partition_broadcast(128)), writes=["gbc"], slot="c_gbc")
    fm_state["n"] = 0

    def evac_act(dst3, func, tag):
        def mk(cb):
            def f(pa):
                for hf in range(2):
                    P.add("act", lambda e, hf=hf: e.activation(out=dst3[:, cb, hf * 512:(hf + 1) * 512], in_=ps[pa + hf][:, 0:512], func=func),
                          reads=[("ps", pa + hf)], writes=[(tag, cb, hf)])
            return f
        return mk

    oa_all = [("oaT", i, g) for i in range(OWN) for g in range(2)]

    def evac_mul_act(dst3, func, dres):
        def mk(cb):
            def f(pa):
                for hf in range(2):
                    sz = szb[hf]
                    P.add("act", lambda e, hf=hf, sz=sz: e.activation(out=sz, in_=ps[pa + hf][:, 0:512], func=func),
                          reads=[("ps", pa + hf)], writes=[("szb", hf)])
                    P.add("dve", lambda e, hf=hf, sz=sz: e.tensor_tensor(out=dst3[:, cb, hf * 512:(hf + 1) * 512],
                                                                       in0=dst3[:, cb, hf * 512:(hf + 1) * 512], in1=sz, op=ALU.mult),
                          reads=[("szb", hf)] + dres, writes=[(id(dst3), "z", cb, hf)])
            return f
        return mk

    def layernorm(src, dst, sidx, nparts, src_res, dst_res, tmp, jk):
        s1 = stat[0:nparts, sidx:sidx + 1]
        s2 = stat[0:nparts, sidx + 1:sidx + 2]
        mu = stat[0:nparts, sidx + 2:sidx + 3]
        var = stat[0:nparts, sidx + 3:sidx + 4]
        sd = stat[0:nparts, sidx + 4:sidx + 5]
        rs = stat[0:nparts, sidx + 5:sidx + 6]
        P.add("dve", lambda e: e.reduce_sum(out=s1, in_=src, axis=AX.X), reads=[src_res], writes=[("st", sidx)])
        P.add("dve", lambda e: e.tensor_scalar(out=mu, in0=s1, scalar1=1.0 / 1024, scalar2=None, op0=ALU.mult), reads=[("st", sidx)], writes=[("st", sidx + 2)])
        P.add("dve", lambda e: e.tensor_scalar(out=tmp, in0=src, scalar1=mu, scalar2=None, op0=ALU.subtract), reads=[src_res, ("st", sidx + 2)], writes=[(dst_res, "t")])
        P.add("dve", lambda e: e.scalar_tensor_tensor(out=jk[0:nparts, 0:1024], in0=tmp, scalar=1.0, in1=tmp, op0=ALU.mult, op1=ALU.mult, accum_out=s2),
              reads=[(dst_res, "t")], writes=["junk", ("st", sidx + 1)])
        P.add("dve", lambda e: e.tensor_scalar(out=sd, in0=s2, scalar1=1.0 / 1024, scalar2=1e-5, op0=ALU.mult, op1=ALU.add),
              reads=[("st", sidx + 1)], writes=[("st", sidx + 4)])
        P.add("pool", lambda e: e.tensor_tensor(out=rs, in0=sd, in1=epsc[0:nparts, 2:3], op=ALU.pow), reads=[("st", sidx + 4), "epsc"], writes=[("st", sidx + 5)])
        P.add("dve", lambda e: e.scalar_tensor_tensor(out=tmp, in0=tmp, scalar=rs, in1=gbc[0:nparts, 0:1024], op0=ALU.mult, op1=ALU.mult),
              reads=[(dst_res, "t"), ("st", sidx + 5), "gbc"], writes=[(dst_res, "t")])
        P.add("dve", lambda e: e.tensor_tensor(out=dst, in0=tmp, in1=gbc[0:nparts, 1024:2048], op=ALU.add), reads=[(dst_res, "t"), "gbc"], writes=[dst_res])

    junk = buf(106 * KB, 2 * KB)
    def vb_block(i):
        tok = slice(i * 128, (i + 1) * 128)
        for hf in range(2):
            bk = 4 + hf
            for c in range(16):
                P.add("pe", lambda e, c=c, hf=hf, bk=bk: e.matmul(ps[bk][:, 0:512], lhsT=hT3[:, c, tok], rhs=wv3[:, c, hf * 512:(hf + 1) * 512],
                                                                start=(c == 0), stop=(c == 15)),
                      reads=["wvbuf"] + hT_reads, writes=[("ps", bk)])
            P.add("act", lambda e, hf=hf, bk=bk: e.activation(out=gvb[:, hf * 512:(hf + 1) * 512], in_=ps[bk][:, 0:512], func=AF.Gelu),
                  reads=[("ps", bk)], writes=["gvb"])

    def vb_ln(i):
        vn = vnb[i % 2]
        layernorm(gvb, vn, 16, 128, "gvb", ("vn", i % 2), ftmp, junk)

    def vb_back(i):
        tok = slice(i * 128, (i + 1) * 128)
        vn = vnb[i % 2]
        for hh in range(2):
            gs = slice(4 * hh, 4 * hh + 4)
            for g in range(4 * hh, 4 * hh + 4):
                P.add("pe", lambda e, g=g: e.matmul(ps[7][:, (g % 4) * 128:(g % 4 + 1) * 128], lhsT=vn[:, g * 128:(g + 1) * 128],
                                                   rhs=v3(WsT, 8)[:, g, :], start=True, stop=True),
                      reads=[("vn", i % 2), "WsT"], writes=[("ps", 7)])
            P.add("dve", lambda e, hh=hh, gs=gs: e.tensor_tensor(out=obT3[:, gs, tok], in0=v3(ps[7][:, 0:512], 4),
                                                                in1=v3(bspbc[:, hh * 512:(hh + 1) * 512], 4), op=ALU.add),
                  reads=[("ps", 7), "bspbc"], writes=[("obT", i, hh)])

    mk = evac_act(uT3, AF.Gelu, "uT")
    for cb in range(8):
        proj_fm(FM_U + cb, 32 + cb, mk(cb), pa=2 * (cb % 2))
        fm_prefetch(FM_U + cb + 1 if cb < 7 else FM_ZA)
        vb_block(cb)
        if cb >= 1:
            vb_back(cb - 1)
        vb_ln(cb)
    vb_back(7)
    uT_all = [("uT", cb, hf) for cb in range(8) for hf in range(2)]
    P.add("act", lambda e: e.activation(out=sm[:, 32:40], in_=ps[6][:, 32:40], func=AF.Gelu), reads=[("ps", 6)], writes=["uTs"])
    for cb in range(8):
        P.add("dve", lambda e, cb=cb: e.tensor_tensor(out=obT3[:, cb, :], in0=obT3[:, cb, :], in1=uT3[:, cb, :], op=ALU.mult),
              reads=[("obT", i, cb // 4) for i in range(OWN)] + [("uT", cb, 0), ("uT", cb, 1)], writes=[("obT", i, cb // 4) for i in range(OWN)])
    mk = evac_mul_act(oaT3, AF.Silu, oa_all)
    for cb in range(8):
        proj_fm(FM_ZA + cb, 24 + cb, mk(cb))
    oaz_all = [(id(oaT3), "z", cb, hf) for cb in range(8) for hf in range(2)]
    P.add("act", lambda e: e.activation(out=sm[:, 24:32], in_=ps[6][:, 24:32], func=AF.Silu), reads=[("ps", 6)], writes=["zaTs"])
    P.add("dve", lambda e: e.tensor_tensor(out=oazTs, in0=oaTs, in1=sm[:, 24:32], op=ALU.mult), reads=["zaTs", "oaTs"], writes=["srhs"])

    ob_all = [("obT", i, hh) for i in range(OWN) for hh in range(2)]
    for hf in range(2):
        for c in range(16):
            P.add("pe", lambda e, c=c, hf=hf: e.matmul(ps[4 + hf][0:1, 0:512], lhsT=hsT[:, c:c + 1], rhs=wv3[:, c, hf * 512:(hf + 1) * 512],
                                                      start=(c == 0), stop=(c == 15)),
                  reads=["wvbuf", "hsT"], writes=[("ps", 4 + hf)])
        P.add("act", lambda e, hf=hf: e.activation(out=gvb[0:1, hf * 512:(hf + 1) * 512], in_=ps[4 + hf][0:1, 0:512], func=AF.Gelu),
              reads=[("ps", 4 + hf)], writes=["gvb"])
    vns = ftmp2[0:1, :]
    layernorm(gvb[0:1, :], vns, 24, 1, "gvb", "vns", ftmp[0:1, :], junk)
    P.add("sp", lambda e: e.dma_start(out=gvs_o, in_=vns), reads=["vns", ("gt", 1, 0), ("gt", 1, 1)], writes=["o_gvs"], slot="o_gvs")

    P.add("sp", lambda e: e.dma_start(out=gvb[0:1, :], in_=ws00_d), reads=["vns"], writes=["gvb"], slot="c_ws00")
    P.add("sp", lambda e: e.dma_start(out=ftmp[0:1, :], in_=bs0_d), reads=["vns"], writes=[("vns", "t")], slot="c_bs0")
    P.add("dve", lambda e: e.tensor_tensor(out=gvb[0:1, :], in0=vns, in1=gvb[0:1, :], op=ALU.mult), reads=["vns", "gvb"], writes=["gvb"])
    P.add("dve", lambda e: e.tensor_tensor(out=gvb[0:1, :], in0=gvb[0:1, :], in1=ftmp[0:1, :], op=ALU.add), reads=["gvb", ("vns", "t")], writes=["gvb"])
    for g in range(8):
        P.add("pe", lambda e, g=g: e.transpose(out=ps[4][:, g:g + 1], in_=gvb[0:1, g * 128:(g + 1) * 128], identity=identf[0:1, 0:1]),
              reads=["gvb", "identf"], writes=[("ps", 4)])
    P.add("dve", lambda e: e.tensor_tensor(out=sm[:, 32:40], in0=ps[4][:, 0:8], in1=sm[:, 32:40], op=ALU.mult), reads=[("ps", 4), "uTs"], writes=["obTs"])
    wo = [buf(20 * KB + gidx * 16 * KB, 16 * KB) for gidx in range(4)]

    pending_wo = []

    def load_wo(gi, extra, defer=False):
        w3 = v3(wo[gi], 16)
        for q4 in range(4):
            def emit(q4=q4):
                P.add("pool", lambda e: e.dma_start(out=w3[:, q4 * 4:(q4 + 1) * 4, :], in_=v3(wout_d[gi], 16)[:, q4 * 4:(q4 + 1) * 4, :]),
                      writes=[("wo", gi)] + extra, slot="wo%d_%d" % (gi, q4))
            if defer:
                pending_wo.append(emit)
            else:
                emit()

    load_wo(0, uT_all, defer=True)
    load_wo(2, ["wvbuf"], defer=True)
    load_wo(3, ["wvbuf"], defer=True)
    mk = evac_mul_act(obT3, AF.Silu, ob_all)
    for cb in range(8):
        proj_fm(FM_ZB + cb, 40 + cb, mk(cb))
    obz_all = [(id(obT3), "z", cb, hf) for cb in range(8) for hf in range(2)]
    P.add("act", lambda e: e.activation(out=sm[:, 40:48], in_=ps[6][:, 40:48], func=AF.Silu), reads=[("ps", 6)], writes=["zbTs"])
    P.add("dve", lambda e: e.tensor_tensor(out=obzTs, in0=sm[:, 32:40], in1=sm[:, 40:48], op=ALU.mult), reads=["zbTs", "obTs"], writes=["srhs"])
    if debug:
        P.add("sp", lambda e: e.dma_start(out=dbg["obT"], in_=obT), reads=obz_all, writes=["dbg_dob"], slot="dbg8")

    wp_n = {"n": 0}

    def proj_branch(wd, cb, src3, sres, bank, scol, srhs):
        n = wp_n["n"]
        wp_n["n"] += 1
        wsl = wpbuf[n % 2]
        w3 = v3(wsl, 8)
        P.add("pool", lambda e: e.dma_start(out=wsl, in_=wd[cb]), writes=[("wpbuf", n % 2)], slot="wpbuf%d" % (n % 2))
        for c in range(8):
            for hf in range(2):
                P.add("pe", lambda e, c=c, hf=hf: e.matmul(ps[bank + hf][:, 0:512], lhsT=w3[:, c, :], rhs=src3[:, c, hf * 512:(hf + 1) * 512],
                                                          start=(c == 0), stop=(c == 7)),
                      reads=[("wpbuf", n % 2)] + sres, writes=[("ps", bank + hf)])
            P.add("pe", lambda e, c=c: e.matmul(ps[7][:, scol:scol + 1], lhsT=w3[:, c, :], rhs=srhs[:, c:c + 1], start=(c == 0), stop=(c == 7)),
                  reads=[("wpbuf", n % 2), "srhs"], writes=[("ps", 7)])


    for cb in range(16):
        sg = [None, None]
        for br in range(2):
            gt = ftmp if br == 0 else ftmp2

            def ev_gate(pa, gt=gt, br=br):
                for hf in range(2):
                    P.add("act", lambda e, hf=hf: e.activation(out=gt[:, hf * 512:(hf + 1) * 512], in_=ps[pa + hf][:, 0:512], func=AF.Sigmoid),
                          reads=[("ps", pa + hf)], writes=[("gt", br, hf)])
                sg[br] = pa
            proj_fm(FM_G + 2 * cb + br, (48 + cb if br == 0 else 64 + cb), ev_gate, pa=2 * br)
            if pending_wo:
                pending_wo.pop(0)()
            if br == 0:
                proj_branch(wpa_d, cb, oaT3, oaz_all, 4, cb, oazTs)
            else:
                proj_branch(wpb_d, cb, obT3, obz_all, 4, 16 + cb, obzTs)
            for hf in range(2):
                P.add("dve", lambda e, hf=hf, gt=gt: e.tensor_tensor(out=gt[:, hf * 512:(hf + 1) * 512], in0=ps[4 + hf][:, 0:512],
                                                                   in1=gt[:, hf * 512:(hf + 1) * 512], op=ALU.mult),
                      reads=[("ps", 4 + hf), ("gt", br, hf)], writes=[("gt", br, hf)])
        for hf in range(2):
            P.add("dve", lambda e, hf=hf, cb=cb: e.tensor_tensor(out=mixT3[:, cb, hf * 512:(hf + 1) * 512], in0=ftmp[:, hf * 512:(hf + 1) * 512],
                                                               in1=ftmp2[:, hf * 512:(hf + 1) * 512], op=ALU.add),
                  reads=[("gt", 0, hf), ("gt", 1, hf)], writes=[("mixT", cb, hf)])
    mix_all = [("mixT", cb, hf) for cb in range(16) for hf in range(2)]
    if debug:
        P.add("sp", lambda e: e.dma_start(out=dbg["mixT"], in_=mixT), reads=mix_all, writes=["dbg_dmx"], slot="dbg9")
    P.add("act", lambda e: e.activation(out=sm[:, 48:80], in_=ps[6][:, 48:80], func=AF.Sigmoid),
          reads=[("ps", 6)], writes=["sm_g"])
    P.add("dve", lambda e: e.tensor_tensor(out=sm[:, 80:112], in0=ps[7][:, 0:32], in1=sm[:, 48:80], op=ALU.mult),
          reads=[("ps", 7), "sm_g"], writes=["sm_y"])
    P.add("dve", lambda e: e.tensor_tensor(out=mixTs, in0=sm[:, 80:96], in1=sm[:, 96:112], op=ALU.add), reads=["sm_y"], writes=["mixTs"])

    if upto <= 4:
        return finish()
    P.barrier()
    xbuf = [buf(84 * KB + i * 8 * KB, 8 * KB, F32) for i in range(2)]
    xo = [buf(100 * KB + i * 8 * KB, 8 * KB, F32) for i in range(2)]
    junk = buf(116 * KB, 4 * KB)
    xs2 = buf(120 * KB, 8 * KB, F32, parts=1)
    xos = buf(128 * KB, 8 * KB, F32, parts=1)
    P.add("sp", lambda e: e.dma_start(out=gbc, in_=g_f_d.partition_broadcast(128)), writes=["gbc"], slot="c_gbc")
    load_wo(1, [])

    def final_block(src_ap, xb, xres, xslot, xo_t, lhs_fn, lhs_reads, banks, out_ap, sidx, nparts, oslot):
        P.add("sp", lambda e: e.dma_start(out=xb, in_=src_ap), writes=[xres], slot=xslot)
        for gi in range(4):
            w3 = v3(wo[gi], 16)
            for c in range(16):
                P.add("pe", lambda e, gi=gi, c=c, w3=w3: e.matmul(ps[banks[gi]][0:nparts, 0:512], lhsT=lhs_fn(c), rhs=w3[:, c, :],
                                                                start=(c == 0), stop=(c == 15)),
                      reads=[("wo", gi)] + lhs_reads, writes=[("ps", banks[gi])])
            P.add("dve", lambda e, gi=gi: e.tensor_tensor(out=xo_t[:, gi * 512:(gi + 1) * 512], in0=ps[banks[gi]][0:nparts, 0:512],
                                                         in1=xb[:, gi * 512:(gi + 1) * 512], op=ALU.add),
                  reads=[("ps", banks[gi]), xres], writes=[(xres, "xo", gi)])
        ss = stat[0:nparts, sidx:sidx + 1]
        sd = stat[0:nparts, sidx + 1:sidx + 2]
        rs = stat[0:nparts, sidx + 2:sidx + 3]
        xor = [(xres, "xo", gi) for gi in range(4)]
        P.add("act", lambda e: e.activation(out=junk[0:nparts, :], in_=xo_t, func=AF.Square, accum_out=ss), reads=xor, writes=["junk", ("st", sidx)])
        P.add("act", lambda e: e.activation(out=sd, in_=ss, func=AF.Sqrt, scale=1.0 / D, bias=epsc[0:nparts, 0:1]),
              reads=[("st", sidx), "epsc"], writes=[("st", sidx + 1)])
        P.add("dve", lambda e: e.reciprocal(out=rs, in_=sd), reads=[("st", sidx + 1)], writes=[("st", sidx + 2)])
        P.add("dve", lambda e: e.scalar_tensor_tensor(out=xb, in0=xo_t, scalar=rs, in1=gbc[0:nparts, :], op0=ALU.mult, op1=ALU.mult),
              reads=xor + [("st", sidx + 2), "gbc"], writes=[xres])
        P.add("pool", lambda e: e.dma_start(out=out_ap, in_=xb), reads=[xres], writes=[("o_y", oslot)], slot="o_y%s" % oslot)

    for i in range(OWN):
        pb = i % 2
        tok = slice(i * 128, (i + 1) * 128)
        banks = [0, 1, 2, 3] if pb == 0 else [4, 5, 6, 7]
        final_block(x_own[i * 128:(i + 1) * 128, :], xbuf[pb], ("xbuf", pb), "xbuf%d" % pb, xo[pb],
                    lambda c, tok=tok: mixT3[:, c, tok], mix_all, banks, y_own[i * 128:(i + 1) * 128, :], 4 * pb, 128, pb)
    final_block(x_s, xs2, "xs2", "xs2", xos, lambda c: mixTs[:, c:c + 1], ["mixTs"], [0, 1, 2, 3], ys_o, 8, 1, "s")

    return finish()


def own_blocks(core):
    j = core % 4
    return sorted([j, 7 - j, 8 + j, 15 - j, 16 + j, 23 - j, 24 + j, 31 - j])


def prep_shared(inputs):
    w_in = np.asarray(inputs["w_in"])[0]

    def chunked(cols):
        n = cols.shape[1]
        return np.ascontiguousarray(cols.reshape(16, 128, n).transpose(1, 0, 2))

    sh = {}
    kvki = np.concatenate([w_in[:, C_K:C_K + 256], w_in[:, C_V:C_V + 256], w_in[:, C_KI:C_KI + 128]], axis=1)
    sh["wkvki"] = chunked(kvki).reshape(128, -1)
    cbs = []
    for base, n in ((C_Q, 8), (C_QI, 16), (C_ZA, 8), (C_U, 8), (C_ZB, 8)):
        for j in range(n):
            cbs.append(w_in[:, base + j * 128: base + (j + 1) * 128])
    for j in range(16):
        cbs.append(w_in[:, C_GA + j * 128:C_GA + (j + 1) * 128])
        cbs.append(w_in[:, C_GB + j * 128:C_GB + (j + 1) * 128])
    sh["wfm"] = np.stack([chunked(c).reshape(128, -1) for c in cbs])
    sh["wwi"] = chunked(w_in[:, C_WI:C_WI + 16]).reshape(128, -1)
    sh["wvb"] = chunked(w_in[:, C_VB:C_VB + 1024]).reshape(128, -1)
    wpa = np.asarray(inputs["w_proj_a"])[0]
    wpb = np.asarray(inputs["w_proj_b"])[0]

    def chunk8(cols):
        return np.ascontiguousarray(cols.reshape(8, 128, 128).transpose(1, 0, 2)).reshape(128, -1)

    sh["wpa"] = np.stack([chunk8(wpa[:, j * 128:(j + 1) * 128]) for j in range(16)])
    sh["wpb"] = np.stack([chunk8(wpb[:, j * 128:(j + 1) * 128]) for j in range(16)])
    wout = np.asarray(inputs["w_out"])[0]
    sh["wout"] = np.stack([chunked(wout[:, j * 512:(j + 1) * 512]).reshape(128, -1) for j in range(4)])
    ws = np.asarray(inputs["w_spatial"])[0]
    sh["wspT"] = np.ascontiguousarray(ws.transpose(2, 0, 1)).reshape(128, -1)
    bsp = np.asarray(inputs["b_spatial"])[0]
    sh["bsp"] = np.ascontiguousarray(bsp.reshape(1, 1024))
    sh["ws00"] = np.ascontiguousarray(np.repeat(ws[:, 0, 0], 128).reshape(1, 1024))
    sh["bs0"] = np.ascontiguousarray(np.repeat(bsp[:, 0], 128).reshape(1, 1024))
    sh["pool_ki8"] = np.ascontiguousarray(np.asarray(inputs["cache_k_idx"])[0].reshape(1280 * 8, 2048))
    sh["pool_k16"] = np.ascontiguousarray(np.asarray(inputs["cache_k"])[0].reshape(1280 * 16, 2048))
    sh["pool_v16"] = np.ascontiguousarray(np.asarray(inputs["cache_v"])[0].reshape(1280 * 16, 2048))
    sh["g_in"] = np.ascontiguousarray(np.asarray(inputs["norm_in_g"]).reshape(1, D))
    sh["g_f"] = np.ascontiguousarray(np.asarray(inputs["norm_f_g"]).reshape(1, D))
    sh["ln_g"] = np.ascontiguousarray(np.asarray(inputs["ln_g"]).reshape(1, 1024))
    sh["ln_b"] = np.ascontiguousarray(np.asarray(inputs["ln_b"]).reshape(1, 1024))
    return {k: np.ascontiguousarray(v, dtype=v.dtype) for k, v in sh.items()}


def make_in_maps(inputs):
    sh = prep_shared(inputs)
    xp = np.asarray(inputs["x_prompt"])
    xs = np.asarray(inputs["x_sample"])
    pt = np.asarray(inputs["page_table"]).astype(np.int32)
    maps = []
    for c in range(NCORES):
        b = c // 4
        ob = own_blocks(c)
        m = dict(sh)
        m["x_all"] = np.ascontiguousarray(xp[b])
        m["x_own"] = np.ascontiguousarray(np.concatenate([xp[b, blk * 128:(blk + 1) * 128] for blk in ob], axis=0))
        t = np.arange(128, dtype=np.float32)[:, None]
        m["qrel"] = np.ascontiguousarray(
            np.concatenate([(ob[i] * 128 - 512 * i) + t for i in range(OWN)], axis=1).astype(np.float32))
        m["x_s"] = np.ascontiguousarray(xs[c].reshape(1, D))
        m["ptab"] = np.ascontiguousarray(pt[c].reshape(1, 128))
        maps.append(m)
    return maps


_CACHE = {}


def kernel(**inputs):
    if "nc" not in _CACHE:
        _CACHE["nc"] = build_program(debug=False)[0]
    nc = _CACHE["nc"]
    maps = make_in_maps(inputs)
    res = run_bass_kernel_spmd(nc, maps, core_ids=list(range(NCORES)))
    r = res.results
    y_prompt = np.zeros((2, SEQ, D), np.float32)
    for c in range(NCORES):
        b = c // 4
        for i, blk in enumerate(own_blocks(c)):
            y_prompt[b, blk * 128:(blk + 1) * 128] = r[c]["y_own"][i * 128:(i + 1) * 128]
    y_sample = np.stack([r[c]["ys"].reshape(1, D) for c in range(NCORES)]).astype(np.float32)
    nk = np.stack([r[4 * b]["knew"].reshape(SEQ, 2, 128) for b in range(2)])[None].astype(np.float32)
    nv = np.stack([r[4 * b]["vnew"].reshape(SEQ, 2, 128) for b in range(2)])[None].astype(np.float32)
    nki = np.stack([r[4 * b]["kinew"].reshape(SEQ, 128) for b in range(2)])[None].astype(np.float32)
    ks = np.stack([r[c]["ks"].reshape(1, 2, 128) for c in range(NCORES)])[None].astype(np.float32)
    vs = np.stack([r[c]["vs"].reshape(1, 2, 128) for c in range(NCORES)])[None].astype(np.float32)
    kis = np.stack([r[c]["kis"].reshape(1, 128) for c in range(NCORES)])[None].astype(np.float32)
    gvs = np.stack([r[c]["gvs"].reshape(1, 1024) for c in range(NCORES)])[None].astype(np.float32)
    return (y_prompt, y_sample, nk, nv, nki, ks, vs, kis, gvs)
```

```python
import numpy as np
from contextlib import ExitStack
import concourse.bass as bass
import concourse.mybir as mybir
from concourse.bass_utils import run_bass_kernel_spmd

F32 = mybir.dt.float32
BF16 = mybir.dt.bfloat16
I32 = mybir.dt.int32
AF = mybir.ActivationFunctionType
ALU = mybir.AluOpType
AX = mybir.AxisListType

NCORES = 8
D = 2048
NCH = 16
SEQ = 4096
NB = 32
OWN = 8
TOK = 1024
BIG = 30000.0
CBIG = 1.0e6
NIT = 16
TOPK = 256
COMPUTE = ("pe", "act", "dve", "pool")

C_Q, C_K, C_V, C_QI, C_WI, C_KI, C_ZA, C_U, C_VB, C_ZB, C_GA, C_GB = (
    0, 1024, 1280, 1536, 3584, 3600, 3728, 4752, 5776, 6800, 7824, 9872)
FM_Q, FM_QI, FM_ZA, FM_U, FM_ZB, FM_G = 0, 8, 24, 32, 40, 48
N_FM = 80


class _Op:
    __slots__ = ("eng", "fn", "deps", "is_dma", "slot", "marked", "mark_idx", "cum", "idx")


class Prog:
    def __init__(self, nc, stack):
        self.nc = nc
        self.stack = stack
        self.ops = []
        self.last_w = {}
        self.readers = {}
        self.slot_cum = {}
        self.bar = []
        self.psx = {}
        self.engs = {"pe": nc.tensor, "act": nc.scalar, "dve": nc.vector, "pool": nc.gpsimd, "sp": nc.sync}

    def add(self, eng, fn, reads=(), writes=(), slot=None):
        op = _Op()
        op.eng = eng
        op.fn = fn
        op.is_dma = slot is not None
        op.slot = slot
        op.marked = False
        op.mark_idx = None
        op.idx = len(self.ops)
        op.deps = set()
        if op.is_dma:
            self.slot_cum[slot] = self.slot_cum.get(slot, 0) + 16
            op.cum = self.slot_cum[slot]
        for y in self.bar:
            self._dep(op, y, "raw")
        for r in reads:
            lw = self.last_w.get(r)
            if lw is not None:
                self._dep(op, lw, "raw")
        for w in writes:
            lw = self.last_w.get(w)
            if lw is not None:
                self._dep(op, lw, "waw")
            for rd in self.readers.get(w, ()):
                self._dep(op, rd, "war")
        for r in reads:
            self.readers.setdefault(r, []).append(op)
        for w in writes:
            self.last_w[w] = op
            self.readers[w] = []
        for res in list(reads) + list(writes):
            if isinstance(res, tuple) and len(res) >= 2 and res[0] == "ps":
                lastx = self.psx.get(res[1])
                if lastx is not None and lastx is not op and lastx.eng != op.eng:
                    op.deps.add(lastx.idx)
                    if not lastx.is_dma:
                        lastx.marked = True
                self.psx[res[1]] = op
        self.ops.append(op)
        return op

    def _dep(self, x, y, kind):
        if y is x:
            return
        if not y.is_dma and not x.is_dma and y.eng == x.eng:
            if kind != "raw" or x.eng == "pe":
                return
        x.deps.add(y.idx)
        if not y.is_dma:
            y.marked = True

    def barrier(self):
        last = {}
        for op in self.ops:
            if op.fn is None:
                continue
            key = ("slot", op.slot) if op.is_dma else ("eng", op.eng)
            last[key] = op
        self.bar = list(last.values())

    def emit(self):
        nc = self.nc
        sems = {}
        for e in COMPUTE:
            sems[("eng", e)] = self.stack.enter_context(nc.semaphore("c_" + e))
        for i, s in enumerate(self.slot_cum):
            sems[("slot", s)] = self.stack.enter_context(nc.semaphore("d%d" % i))
        cnt = {e: 0 for e in COMPUTE}
        for op in self.ops:
            if op.marked:
                cnt[op.eng] += 1
                op.mark_idx = cnt[op.eng]
        waited = {}
        nw = 0
        for op in self.ops:
            eng = self.engs[op.eng]
            need = {}
            for di in op.deps:
                y = self.ops[di]
                if y.is_dma:
                    k, v = ("slot", y.slot), y.cum
                else:
                    k, v = ("eng", y.eng), y.mark_idx
                if need.get(k, 0) < v:
                    need[k] = v
            wd = waited.setdefault(op.eng, {})
            for k, v in need.items():
                if wd.get(k, 0) >= v:
                    continue
                eng.wait_ge(sems[k], v)
                wd[k] = v
                nw += 1
            if op.fn is None:
                continue
            ins = op.fn(eng)
            if op.is_dma:
                ins.then_inc(sems[("slot", op.slot)], 16)
            elif op.marked:
                ins.then_inc(sems[("eng", op.eng)], 1)
        return dict(nops=len(self.ops), nwaits=nw, marks=cnt, nsems=len(sems))


def build_program(debug=False, upto=99):
    nc = bass.Bass("TRN2", target_bir_lowering=False)

    def din(name, shape, dtype=F32):
        return nc.dram_tensor(name, list(shape), dtype, kind="ExternalInput").ap()

    def dout(name, shape, dtype=F32):
        return nc.dram_tensor(name, list(shape), dtype, kind="ExternalOutput").ap()

    x_all = din("x_all", [SEQ, D])
    x_own = din("x_own", [TOK, D])
    qrel_d = din("qrel", [128, OWN])
    x_s = din("x_s", [1, D])
    ptab = din("ptab", [1, 128], I32)
    pool_ki8 = din("pool_ki8", [1280 * 8, 2048])
    pool_k16 = din("pool_k16", [1280 * 16, 2048])
    pool_v16 = din("pool_v16", [1280 * 16, 2048])
    g_in_d = din("g_in", [1, D])
    g_f_d = din("g_f", [1, D])
    ln_g_d = din("ln_g", [1, 1024])
    ln_b_d = din("ln_b", [1, 1024])
    bsp_d = din("bsp", [1, 1024])
    ws00_d = din("ws00", [1, 1024])
    bs0_d = din("bs0", [1, 1024])
    wspT_d = din("wspT", [128, 8 * 128])
    wkvki_d = din("wkvki", [128, NCH * 640])
    wfm_d = din("wfm", [N_FM, 128, NCH * 128])
    wwi_d = din("wwi", [128, NCH * 16])
    wvb_d = din("wvb", [128, NCH * 1024])
    wpa_d = din("wpa", [16, 128, 8 * 128])
    wpb_d = din("wpb", [16, 128, 8 * 128])
    wout_d = din("wout", [4, 128, NCH * 512])

    y_own = dout("y_own", [TOK, D])
    knew = dout("knew", [SEQ, 256])
    vnew = dout("vnew", [SEQ, 256])
    kinew = dout("kinew", [SEQ, 128])
    ys_o = dout("ys", [1, D])
    ks_o = dout("ks", [1, 256])
    vs_o = dout("vs", [1, 256])
    kis_o = dout("kis", [1, 128])
    gvs_o = dout("gvs", [1, 1024])
    dbg = {}
    if debug:
        dbg["qT"] = dout("dbg_qT", [128, 8 * TOK], BF16)
        dbg["kT"] = dout("dbg_kT", [128, 2 * SEQ], BF16)
        dbg["kiT"] = dout("dbg_kiT", [128, SEQ], BF16)
        dbg["qiT"] = dout("dbg_qiT", [128, 16 * TOK], BF16)
        dbg["wabs"] = dout("dbg_wabs", [128, 128])
        dbg["acc"] = dout("dbg_acc", [128, OWN * 4096])
        dbg["thr"] = dout("dbg_thr", [128, OWN])
        dbg["oaT"] = dout("dbg_oaT", [128, 8 * TOK], BF16)
        dbg["obT"] = dout("dbg_obT", [128, 8 * TOK], BF16)
        dbg["mixT"] = dout("dbg_mixT", [128, 16 * TOK], BF16)

    st = ExitStack()
    P = Prog(nc, st)
    ARENA_B = 207 * 1024
    arena = st.enter_context(nc.sbuf_tensor("arena", [128, ARENA_B // 2], BF16))
    psall = st.enter_context(nc.psum_tensor("psall", [128, 4096], F32))
    ps = [psall[:, i * 512:(i + 1) * 512] for i in range(8)]
    psb = [p.bitcast(BF16) for p in ps]


    def finish():
        fin = P.add("sp", None)
        lastd = {}
        for op in P.ops:
            if op.is_dma:
                lastd[op.slot] = op
        for op in lastd.values():
            fin.deps.add(op.idx)
        info = P.emit()
        st.close()
        return nc, info

    KB = 1024

    def buf(off_b, nbytes, dtype=BF16, parts=128):
        if off_b >= 20 * KB:
            off_b += KB
        assert off_b + nbytes <= ARENA_B, (off_b, nbytes)
        a = arena[0:parts, off_b // 2:(off_b + nbytes) // 2]
        if dtype != BF16:
            a = a.bitcast(dtype)
        return a

    def v3(ap, a):
        return ap.rearrange("p (a b) -> p a b", a=a)

    o = 0
    ident = buf(o, 256); o += 256
    identB4 = buf(o, 1024); o += 1024
    ones_bf = buf(o, 256); o += 256
    identf = buf(o, 512, F32); o += 512
    iota_f = buf(o, 2048, F32); o += 2048
    pow2 = buf(o, 128, F32); o += 128
    qrel = buf(o, 32, F32); o += 32
    epsc = buf(o, 16, F32); o += 16
    stat = buf(o, 256, F32); o += 256
    wabs = buf(o, 512, F32); o += 512
    wsgn = buf(o, 512, F32); o += 512
    sm = buf(o, 2048, F32); o += 2048
    iota_i = sm.bitcast(I32)
    smb = buf(o, 1024, BF16); o += 1024
    WsT = buf(o, 2048); o += 2048
    gbc = buf(o, 8192, F32); o += 8192
    ksv = buf(o, 2560, F32, parts=1); o += 2560
    assert o <= 21 * KB, o
    hsT = smb[:, 0:16]
    oazTs = smb[:, 16:24]
    obzTs = smb[:, 24:32]
    mixTs = smb[:, 32:48]

    kT_all = buf(20 * KB, 16 * KB)
    kiT_all = buf(36 * KB, 8 * KB)
    v_all = buf(44 * KB, 16 * KB)
    qT = buf(60 * KB, 16 * KB)
    qiT = buf(76 * KB, 32 * KB)
    hT_own = buf(108 * KB, 32 * KB)
    oaT = buf(140 * KB, 16 * KB)
    mixT = buf(156 * KB, 32 * KB)
    kT3 = v3(kT_all, 2)
    v3a = v3(v_all, 32)
    qT3 = v3(qT, 8)
    qiT3 = v3(qiT, 16)
    hT3 = v3(hT_own, 16)
    oaT3 = v3(oaT, 8)
    mixT3 = v3(mixT, 16)

    P.add("pool", lambda e: e.memset(identf, 1.0), writes=["identf"])
    P.add("pool", lambda e: e.affine_select(out=identf, in_=identf, pattern=[[-1, 128]], compare_op=ALU.is_equal,
                                            fill=0.0, base=0, channel_multiplier=1), reads=["identf"], writes=["identf"])
    P.add("dve", lambda e: e.tensor_copy(out=ident, in_=identf), reads=["identf"], writes=["ident"])
    for j in range(4):
        P.add("dve", lambda e, j=j: e.tensor_scalar(out=identB4[:, j * 128:(j + 1) * 128], in0=identf, scalar1=BIG, scalar2=None,
                                                    op0=ALU.mult), reads=["identf"], writes=["identB4"])
    P.add("pool", lambda e: e.memset(ones_bf, 1.0), writes=["ones"])
    P.add("pool", lambda e: e.iota(iota_i, pattern=[[1, 512]], base=0, channel_multiplier=0), writes=["iota_i"])
    P.add("dve", lambda e: e.tensor_copy(out=iota_f, in_=iota_i), reads=["iota_i"], writes=["iota"])
    for k in range(NIT + 1):
        P.add("pool", lambda e, k=k: e.memset(pow2[:, k:k + 1], float(2.0 ** -(k + 1))), writes=["pow2"])
    P.add("pool", lambda e: e.memset(epsc[:, 0:1], 1e-6), writes=["epsc"])
    P.add("pool", lambda e: e.memset(epsc[:, 1:2], 1e-5), writes=["epsc"])
    P.add("pool", lambda e: e.memset(epsc[:, 2:3], -0.5), writes=["epsc"])
    P.add("sp", lambda e: e.dma_start(out=qrel, in_=qrel_d), writes=["qrel"], slot="c_qrel")
    P.add("sp", lambda e: e.dma_start(out=gbc, in_=g_in_d.partition_broadcast(128)), writes=["gbc"], slot="c_gbc")

    def rms_block(src_ap, xb, xres, xslot, hb, hres, junk, sidx, gb_ap, nparts=128):
        ss = stat[0:nparts, sidx:sidx + 1]
        sd = stat[0:nparts, sidx + 1:sidx + 2]
        rs = stat[0:nparts, sidx + 2:sidx + 3]
        P.add("sp", lambda e: e.dma_start(out=xb, in_=src_ap), writes=[xres], slot=xslot)
        P.add("act", lambda e: e.activation(out=junk, in_=xb, func=AF.Square, accum_out=ss),
              reads=[xres], writes=["junk", ("st", sidx)])
        P.add("act", lambda e: e.activation(out=sd, in_=ss, func=AF.Sqrt, scale=1.0 / D, bias=epsc[0:nparts, 0:1]),
              reads=[("st", sidx), "epsc"], writes=[("st", sidx + 1)])
        P.add("dve", lambda e: e.reciprocal(out=rs, in_=sd), reads=[("st", sidx + 1)], writes=[("st", sidx + 2)])
        P.add("dve", lambda e: e.scalar_tensor_tensor(out=hb, in0=xb, scalar=rs, in1=gb_ap, op0=ALU.mult, op1=ALU.mult),
              reads=[xres, ("st", sidx + 2), "gbc"], writes=[hres])

    def transposes16(hb, hres, bankA, bankB):
        for c in range(16):
            bk = bankA if c < 8 else bankB
            cc = c % 8
            P.add("pe", lambda e, c=c, bk=bk, cc=cc: e.transpose(out=psb[bk][:, cc * 128:(cc + 1) * 128],
                                                                 in_=hb[:, c * 128:(c + 1) * 128], identity=ident),
                  reads=[hres, "ident"], writes=[("ps", bk)])

    if upto <= 0:
        return finish()
    S1 = 60 * KB
    xbuf = [buf(S1 + i * 8 * KB, 8 * KB, F32) for i in range(2)]
    hbuf = [buf(S1 + 16 * KB + i * 4 * KB, 4 * KB) for i in range(2)]
    junk = buf(S1 + 24 * KB, 4 * KB)
    hTblk = [buf(S1 + 28 * KB + i * 4 * KB, 4 * KB) for i in range(2)]
    Wkvki = buf(S1 + 36 * KB, 20 * KB)
    kvf = [buf(S1 + 56 * KB + i * 2560, 2560, F32) for i in range(2)]
    kb16 = [buf(S1 + 62 * KB + i * 768, 768) for i in range(2)]
    Wk3 = v3(Wkvki, 16)
    for q4 in range(8):
        P.add("pool", lambda e, q4=q4: e.dma_start(out=Wk3[:, q4 * 2:(q4 + 1) * 2, :],
                                                  in_=v3(wkvki_d, 16)[:, q4 * 2:(q4 + 1) * 2, :]),
              writes=["Wkvki"], slot="wkvki%d" % (q4 % 4))
    xs_f = buf(S1 + 64 * KB, 8 * KB, F32, parts=1)
    hs_f = buf(S1 + 72 * KB, 8 * KB, F32, parts=1)
    kvs = buf(S1 + 80 * KB, 2560, F32, parts=1)
    import os as _os
    if _os.environ.get('SKIP_S1'):
        P.add('pool', lambda e: e.memset(hsT, 0.0), writes=['hsT'])
    else:
        rms_block(x_s, xs_f, "xs_f", "xs_f", hs_f, "hs_f", junk[0:1, :], 8, gbc[0:1, :], nparts=1)
        for c in range(16):
            P.add("pe", lambda e, c=c: e.transpose(out=ps[4][:, c:c + 1], in_=hs_f[0:1, c * 128:(c + 1) * 128], identity=identf[0:1, 0:1]),
                  reads=["hs_f", "identf"], writes=[("ps", 4)])
        P.add("act", lambda e: e.copy(out=hsT, in_=ps[4][:, 0:16]), reads=[("ps", 4)], writes=["hsT"])
        for c in range(16):
            P.add("pe", lambda e, c=c: e.matmul(ps[6][0:1, 0:512], lhsT=hsT[:, c:c + 1], rhs=Wk3[:, c, 0:512], start=(c == 0), stop=(c == 15)),
                  reads=["hsT", "Wkvki"], writes=[("ps", 6)])
            P.add("pe", lambda e, c=c: e.matmul(ps[7][0:1, 0:128], lhsT=hsT[:, c:c + 1], rhs=Wk3[:, c, 512:640], start=(c == 0), stop=(c == 15)),
                  reads=["hsT", "Wkvki"], writes=[("ps", 7)])
        P.add("dve", lambda e: e.tensor_copy(out=kvs[:, 0:512], in_=ps[6][0:1, 0:512]), reads=[("ps", 6)], writes=["kvs0"])
        P.add("dve", lambda e: e.tensor_copy(out=kvs[:, 512:640], in_=ps[7][0:1, 0:128]), reads=[("ps", 7)], writes=["kvs1"])
        P.add("dve", lambda e: e.tensor_copy(out=ksv, in_=kvs), reads=["kvs0", "kvs1"], writes=["ksv"])
        P.add("sp", lambda e: e.dma_start(out=ks_o, in_=kvs[:, 0:256]), reads=["kvs0"], writes=["o_ks"], slot="o_ks")
        P.add("sp", lambda e: e.dma_start(out=vs_o, in_=kvs[:, 256:512]), reads=["kvs0"], writes=["o_vs"], slot="o_vs")
        P.add("sp", lambda e: e.dma_start(out=kis_o, in_=kvs[:, 512:640]), reads=["kvs1"], writes=["o_kis"], slot="o_kis")
    def s1_front(b):
        pb = b % 2
        rms_block(x_all[b * 128:(b + 1) * 128, :], xbuf[pb], ("xbuf", pb), "xbuf%d" % pb, hbuf[pb], ("hbuf", pb), junk, 4 * pb, gbc)

    def s1_t16(b, xbuf=xbuf, hbuf=hbuf, hTblk=hTblk):
        pb = b % 2
        hb, htb = hbuf[pb], hTblk[pb]
        bA, bB = (0, 1) if pb == 0 else (2, 3)
        transposes16(hb, ("hbuf", pb), bA, bB)
        P.add("act", lambda e: e.copy(out=htb[:, 0:1024], in_=psb[bA][:, 0:1024]), reads=[("ps", bA)], writes=[("htb", pb, 0)])
        P.add("dve", lambda e: e.tensor_copy(out=htb[:, 1024:2048], in_=psb[bB][:, 0:1024]), reads=[("ps", bB)], writes=[("htb", pb, 1)])

    def s1_mm(b, hTblk=hTblk, kvf=kvf, kb16=kb16):
        pb = b % 2
        htb3 = v3(hTblk[pb], 16)
        bKV, bKI = (4, 5) if pb == 0 else (6, 7)
        for c in range(16):
            P.add("pe", lambda e, c=c: e.matmul(ps[bKV][:, 0:512], lhsT=htb3[:, c, :], rhs=Wk3[:, c, 0:512], start=(c == 0), stop=(c == 15)),
                  reads=[("htb", pb, c // 8), "Wkvki"], writes=[("ps", bKV)])
            P.add("pe", lambda e, c=c: e.matmul(ps[bKI][:, 0:128], lhsT=htb3[:, c, :], rhs=Wk3[:, c, 512:640], start=(c == 0), stop=(c == 15)),
                  reads=[("htb", pb, c // 8), "Wkvki"], writes=[("ps", bKI)])
        kf = kvf[pb]
        k16 = kb16[pb]
        P.add("dve", lambda e: e.tensor_copy(out=kf[:, 0:512], in_=ps[bKV][:, 0:512]), reads=[("ps", bKV)], writes=[("kvf", pb, 0)])
        P.add("act", lambda e: e.copy(out=kf[:, 512:640], in_=ps[bKI][:, 0:128]), reads=[("ps", bKI)], writes=[("kvf", pb, 1)])
        P.add("dve", lambda e: e.tensor_copy(out=k16[:, 0:256], in_=ps[bKV][:, 0:256]), reads=[("ps", bKV)], writes=[("kb16", pb)])
        P.add("act", lambda e: e.copy(out=k16[:, 256:384], in_=ps[bKI][:, 0:128]), reads=[("ps", bKI)], writes=[("kb16", pb)])
        P.add("dve", lambda e: e.tensor_copy(out=v3a[:, b, :], in_=ps[bKV][:, 256:512]), reads=[("ps", bKV)], writes=[("v_all", b)])
        rows = slice(b * 128, (b + 1) * 128)
        P.add("pool", lambda e: e.dma_start(out=knew[rows, :], in_=kf[:, 0:256]), reads=[("kvf", pb, 0)], writes=[("o_k", pb)], slot="o_k%d" % pb)
        P.add("pool", lambda e: e.dma_start(out=vnew[rows, :], in_=kf[:, 256:512]), reads=[("kvf", pb, 0)], writes=[("o_v", pb)], slot="o_v%d" % pb)
        P.add("pool", lambda e: e.dma_start(out=kinew[rows, :], in_=kf[:, 512:640]), reads=[("kvf", pb, 1)], writes=[("o_ki", pb)], slot="o_ki%d" % pb)

    def s1_tk(b, kb16=kb16):
        pb = b % 2
        k16 = kb16[pb]
        bKI = 5 if pb == 0 else 7
        for j in range(3):
            P.add("pe", lambda e, j=j: e.transpose(out=psb[bKI][:, 256 + j * 128:256 + (j + 1) * 128], in_=k16[:, j * 128:(j + 1) * 128], identity=ident),
                  reads=[("kb16", pb), "ident"], writes=[("ps", bKI)])
        P.add("act", lambda e: e.copy(out=kT3[:, :, b * 128:(b + 1) * 128], in_=v3(psb[bKI][:, 256:512], 2)), reads=[("ps", bKI)], writes=[("kT", b)])
        P.add("act", lambda e: e.copy(out=kiT_all[:, b * 128:(b + 1) * 128], in_=psb[bKI][:, 512:640]), reads=[("ps", bKI)], writes=[("kiT", b)])

    s1_front(0)
    s1_front(1)
    s1_t16(0)
    s1_t16(1)
    for b in range(NB):
        if b + 2 < NB:
            s1_front(b + 2)
        s1_mm(b)
        if b + 2 < NB:
            s1_t16(b + 2)
        s1_tk(b)

    if upto <= 1:
        return finish()
    P.barrier()
    S2 = 140 * KB
    xbuf = [buf(S2 + i * 8 * KB, 8 * KB, F32) for i in range(2)]
    hbuf = [buf(S2 + 16 * KB + i * 4 * KB, 4 * KB) for i in range(2)]
    junk = buf(S2 + 24 * KB, 4 * KB)
    wbuf = [buf(S2 + 28 * KB + i * 4 * KB, 4 * KB) for i in range(3)]
    wwi = buf(S2 + 40 * KB, 512)

    def own_pass(xbuf, hbuf, junk):
        def front(i):
            pb = i % 2
            rms_block(x_own[i * 128:(i + 1) * 128, :], xbuf[pb], ("xbuf", pb), "xbuf%d" % pb, hbuf[pb], ("hbuf", pb), junk, 4 * pb, gbc)

        def back(i):
            pb = i % 2
            hb = hbuf[pb]
            bA, bB = (0, 1) if pb == 0 else (2, 3)
            transposes16(hb, ("hbuf", pb), bA, bB)
            P.add("act", lambda e, bA=bA, i=i: e.copy(out=hT3[:, 0:8, i * 128:(i + 1) * 128], in_=v3(psb[bA][:, 0:1024], 8)),
                  reads=[("ps", bA)], writes=[("hT", i, 0)])
            P.add("dve", lambda e, bB=bB, i=i: e.tensor_copy(out=hT3[:, 8:16, i * 128:(i + 1) * 128], in_=v3(psb[bB][:, 0:1024], 8)),
                  reads=[("ps", bB)], writes=[("hT", i, 1)])

        front(0)
        for i in range(OWN):
            if i + 1 < OWN:
                front(i + 1)
            back(i)

    own_pass(xbuf, hbuf, junk)

    hT_reads = [("hT", i, h) for i in range(OWN) for h in range(2)]
    fm_state = {"n": 0}

    fm_pref = {}

    def fm_load(cb):
        if cb in fm_pref:
            return fm_pref.pop(cb)
        ws = fm_state.get("nl", 0) % 3
        fm_state["nl"] = fm_state.get("nl", 0) + 1
        wsl = wbuf[ws]
        P.add("pool", lambda e: e.dma_start(out=wsl, in_=wfm_d[cb]), writes=[("wbuf", ws)], slot="wbuf%d" % ws)
        return ws

    def fm_prefetch(cb):
        if cb < N_FM and cb not in fm_pref:
            fm_pref[cb] = fm_load(cb)

    def proj_fm(cb, scol, evac, pa=None):
        n = fm_state["n"]
        fm_state["n"] += 1
        ws = fm_load(cb)
        wsl = wbuf[ws]
        w3 = v3(wsl, 16)
        if pa is None:
            pa = 2 * (n % 3)
        for c in range(16):
            for hf in range(2):
                P.add("pe", lambda e, c=c, hf=hf: e.matmul(ps[pa + hf][:, 0:512], lhsT=w3[:, c, :], rhs=hT3[:, c, hf * 512:(hf + 1) * 512],
                                                          start=(c == 0), stop=(c == 15)),
                      reads=[("wbuf", ws)] + hT_reads, writes=[("ps", pa + hf)])
            P.add("pe", lambda e, c=c: e.matmul(ps[6][:, scol:scol + 1], lhsT=w3[:, c, :], rhs=hsT[:, c:c + 1],
                                                start=(c == 0), stop=(c == 15)),
                  reads=[("wbuf", ws), "hsT"], writes=[("ps", 6)])
        evac(pa)

    def evac_copy(dst3, h):
        def f(pa):
            P.add("act", lambda e: e.copy(out=dst3[:, h, 0:512], in_=ps[pa][:, 0:512]), reads=[("ps", pa)], writes=[(id(dst3), h, 0)])
            P.add("dve", lambda e: e.tensor_copy(out=dst3[:, h, 512:1024], in_=ps[pa + 1][:, 0:512]), reads=[("ps", pa + 1)],
                  writes=[(id(dst3), h, 1)])
        return f

    for h in range(8):
        proj_fm(FM_Q + h, h, evac_copy(qT3, h))
    for h in range(16):
        proj_fm(FM_QI + h, 8 + h, evac_copy(qiT3, h))
    P.add("dve", lambda e: e.tensor_copy(out=sm[:, 0:24], in_=ps[6][:, 0:24]), reads=[("ps", 6)], writes=["sm_q"])
    P.add("pool", lambda e: e.dma_start(out=wwi, in_=wwi_d), writes=["wwi"], slot="wwi")
    wwi3 = v3(wwi, 16)
    for i in range(OWN):
        for c in range(16):
            P.add("pe", lambda e, i=i, c=c: e.matmul(ps[7][:, i * 16:(i + 1) * 16], lhsT=hT3[:, c, i * 128:(i + 1) * 128], rhs=wwi3[:, c, :],
                                                    start=(c == 0), stop=(c == 15)),
                  reads=["wwi"] + hT_reads, writes=[("ps", 7)])
    P.add("act", lambda e: e.copy(out=wabs, in_=ps[7][:, 0:128]), reads=[("ps", 7)], writes=["wabs"])
    P.add("act", lambda e: e.activation(out=wsgn, in_=ps[7][:, 0:128], func=AF.Sign), reads=[("ps", 7)], writes=["wsgn"])
    for c in range(16):
        P.add("pe", lambda e, c=c: e.matmul(ps[7][0:16, 128:129], lhsT=wwi3[:, c, :], rhs=hsT[:, c:c + 1], start=(c == 0), stop=(c == 15)),
              reads=["wwi", "hsT"], writes=[("ps", 7)])
    P.add("dve", lambda e: e.tensor_copy(out=sm[0:16, 120:121], in_=ps[7][0:16, 128:129]), reads=[("ps", 7)], writes=["sm_w"])
    if debug:
        P.add("sp", lambda e: e.dma_start(out=dbg["qT"], in_=qT), reads=[(id(qT3), h, j) for h in range(8) for j in range(2)], writes=["dbg_dq"], slot="dbg0")
        P.add("sp", lambda e: e.dma_start(out=dbg["qiT"], in_=qiT), reads=[(id(qiT3), h, j) for h in range(16) for j in range(2)], writes=["dbg_dqi"], slot="dbg1")
        P.add("sp", lambda e: e.dma_start(out=dbg["kT"], in_=kT_all), reads=[("kT", b) for b in range(NB)], writes=["dbg_dk"], slot="dbg2")
        P.add("sp", lambda e: e.dma_start(out=dbg["kiT"], in_=kiT_all), reads=[("kiT", b) for b in range(NB)], writes=["dbg_dki"], slot="dbg3")
        P.add("sp", lambda e: e.dma_start(out=dbg["wabs"], in_=wabs), reads=["wabs"], writes=["dbg_dwa"], slot="dbg4")

    if upto <= 2:
        return finish()
    P.barrier()
    acc = buf(108 * KB, 16 * KB, F32)
    negm = buf(124 * KB, 8 * KB)
    jnk3 = buf(132 * KB, 8 * KB)
    S3 = 156 * KB
    Rb = [buf(S3 + i * 2 * KB, 2 * KB) for i in range(3)]
    dsg = [buf(S3 + 16 * KB + i * 4 * KB, 4 * KB) for i in range(2)]
    PT = [buf(S3 + 6 * KB + i * KB, KB) for i in range(3)]
    pen = buf(S3 + 9 * KB, 2 * KB, F32)
    tmn = buf(S3 + 11 * KB, 2 * KB, F32)
    rdens = [buf(S3 + 13 * KB, 2 * KB, F32), buf(S3 + 41 * KB, 2 * KB, F32)]
    bs = buf(S3 + 15 * KB, 512, F32)
    HK = bs[:, 0:NIT + 1]
    TC = bs[:, 32:32 + NIT + 2]
    cnt = bs[:, 64:65]
    tmpc = bs[:, 65:66]
    rmax = bs[:, 66:67]
    rmin = bs[:, 67:68]
    rmin2 = bs[:, 68:69]
    lo0 = bs[:, 69:70]
    w0 = bs[:, 70:71]
    thr = bs[:, 72:80]
    NTC = bs[:, 80:80 + NIT + 2]
    sA = bs[:, 100:101]
    cnt2 = bs[:, 101:102]
    SCALE = float(128 ** -0.5)
    rcount = {"s": 0, "st": 0, "pt": 0}
    accs = [acc, buf(S3 + 25 * KB, 16 * KB, F32)]

    def idx_prep(i):
        dg3 = v3(dsg[i % 2], 16)
        for h in range(16):
            P.add("pool", lambda e, h=h: e.tensor_scalar(out=dg3[:, h, :], in0=identf, scalar1=wabs[:, i * 16 + h:i * 16 + h + 1], scalar2=0.0,
                                                        op0=ALU.mult, op1=ALU.add),
                  reads=["identf", "wabs"], writes=[("dsg", i % 2)])

    def idx_group(i, kg):
        ac = accs[i % 2]
        dg3 = v3(dsg[i % 2], 16)
        tok = slice(i * 128, (i + 1) * 128)
        cols = slice(kg * 512, (kg + 1) * 512)

        def emit_s(j):
            n = rcount["s"]
            rcount["s"] += 1
            pa = 2 * (n % 2)
            rb = Rb[n % 3]
            for u in range(2):
                h = 2 * j + u
                P.add("pe", lambda e, h=h, u=u: e.matmul(ps[pa + u][:, 0:512], lhsT=qiT3[:, h, tok], rhs=kiT_all[:, cols], start=True, stop=True),
                      reads=["qiT", "kiT"], writes=[("ps", pa + u)])
            P.add("act", lambda e: e.activation(out=rb, in_=psall[:, pa * 512:(pa + 2) * 512], func=AF.Relu),
                  reads=[("ps", pa), ("ps", pa + 1)], writes=[("Rb", n % 3)])
            return rb, n % 3

        def emit_acc(j, rbn):
            rb, rn = rbn
            for u in range(2):
                h = 2 * j + u
                P.add("pe", lambda e, h=h, u=u: e.matmul(ps[4][:, 0:512], lhsT=dg3[:, h, :], rhs=rb[:, u * 512:(u + 1) * 512], start=(h == 0), stop=(h == 15)),
                      reads=[("Rb", rn), ("dsg", i % 2)], writes=[("ps", 4)])

        prev = emit_s(0)
        for j in range(8):
            nxt = emit_s(j + 1) if j < 7 else None
            emit_acc(j, prev)
            prev = nxt
        P.add("dve", lambda e: e.tensor_copy(out=ac[:, cols], in_=ps[4][:, 0:512]), reads=[("ps", 4)], writes=[("acc", i % 2, kg)])

    def bis_setup(i):
        ac = accs[i % 2]
        span = 512 * (i + 1)
        accr = [("acc", i % 2, kg) for kg in range(i + 1)]
        last = slice(i * 512, (i + 1) * 512)
        lres = ("acc", i % 2, i)
        P.add("dve", lambda e: e.tensor_scalar(out=pen, in0=iota_f, scalar1=qrel[:, i:i + 1], scalar2=CBIG, op0=ALU.is_gt, op1=ALU.mult),
              reads=["iota", "qrel"], writes=["pen"])
        P.add("dve", lambda e: e.tensor_tensor(out=tmn, in0=ac[:, last], in1=pen, op=ALU.add), reads=["pen", lres], writes=["tmn"])
        P.add("dve", lambda e: e.tensor_reduce(out=rmin, in_=tmn, axis=AX.X, op=ALU.min), reads=["tmn"], writes=["rmin"])
        P.add("dve", lambda e: e.tensor_tensor(out=ac[:, last], in0=ac[:, last], in1=pen, op=ALU.subtract), reads=["pen", lres], writes=[lres])
        P.add("dve", lambda e: e.tensor_reduce(out=rmax, in_=ac[:, 0:span], axis=AX.X, op=ALU.max), reads=accr, writes=["rmax"])
        if i > 0:
            P.add("dve", lambda e: e.tensor_reduce(out=rmin2, in_=ac[:, 0:i * 512], axis=AX.X, op=ALU.min), reads=accr, writes=["rmin2"])
            P.add("dve", lambda e: e.tensor_tensor(out=rmin, in0=rmin, in1=rmin2, op=ALU.min), reads=["rmin", "rmin2"], writes=["rmin"])
        P.add("dve", lambda e: e.tensor_tensor(out=w0, in0=rmax, in1=rmin, op=ALU.subtract), reads=["rmax", "rmin"], writes=["w0"])
        P.add("dve", lambda e: e.tensor_scalar(out=w0, in0=w0, scalar1=float(2.0 ** -10), scalar2=1e-3, op0=ALU.mult, op1=ALU.add), reads=["w0"], writes=["w0b"])
        P.add("dve", lambda e: e.tensor_tensor(out=lo0, in0=rmin, in1=w0, op=ALU.subtract), reads=["rmin", "w0b"], writes=["lo0"])
        P.add("dve", lambda e: e.tensor_tensor(out=w0, in0=rmax, in1=lo0, op=ALU.subtract), reads=["rmax", "lo0"], writes=["w0c"])
        P.add("dve", lambda e: e.tensor_scalar(out=HK, in0=pow2[:, 0:NIT + 1], scalar1=w0, scalar2=None, op0=ALU.mult), reads=["pow2", "w0c"], writes=["HK"])
        P.add("dve", lambda e: e.tensor_tensor(out=TC[:, 0:1], in0=lo0, in1=HK[:, 0:1], op=ALU.add), reads=["lo0", "HK"], writes=[("TC", 0)])

    def bis_iter(i, k):
        ac = accs[i % 2]
        span = 512 * (i + 1)
        accr = [("acc", i % 2, kg) for kg in range(i + 1)]
        P.add("dve", lambda e: e.tensor_scalar(out=jnk3[:, 0:span], in0=ac[:, 0:span], scalar1=TC[:, k:k + 1], scalar2=None,
                                              op0=ALU.is_ge, op1=ALU.add, accum_out=cnt),
              reads=accr + [("TC", k)], writes=["jnk3", "cnt"])
        P.add("dve", lambda e: e.scalar_tensor_tensor(out=tmpc, in0=cnt, scalar=TOPK - 0.5, in1=HK[:, k:k + 1], op0=ALU.is_ge, op1=ALU.mult),
              reads=["cnt", "HK"], writes=["tmpc"])
        P.add("dve", lambda e: e.scalar_tensor_tensor(out=TC[:, k + 1:k + 2], in0=tmpc, scalar=HK[:, k + 1:k + 2], in1=TC[:, k:k + 1],
                                                      op0=ALU.subtract, op1=ALU.add),
              reads=["tmpc", "HK", ("TC", k)], writes=[("TC", k + 1)])

    def bis_final(i):
        ac = accs[i % 2]
        span = 512 * (i + 1)
        accr = [("acc", i % 2, kg) for kg in range(i + 1)]
        P.add("dve", lambda e: e.tensor_tensor(out=thr[:, i:i + 1], in0=TC[:, NIT:NIT + 1], in1=HK[:, NIT:NIT + 1], op=ALU.subtract),
              reads=[("TC", NIT), "HK"], writes=[("thr", i)])
        P.add("dve", lambda e: e.tensor_scalar(out=negm[:, 0:span], in0=ac[:, 0:span], scalar1=thr[:, i:i + 1], scalar2=1.0,
                                              op0=ALU.is_ge, op1=ALU.subtract),
              reads=accr + [("thr", i)], writes=["negm"])
        if debug:
            P.add("sp", lambda e: e.dma_start(out=dbg["acc"][:, i * 4096:i * 4096 + span], in_=ac[:, 0:span]), reads=accr, writes=["dbg_dacc"], slot="dbg5")

    def attention(i, mid_cb=None):
        tok = slice(i * 128, (i + 1) * 128)
        nkb = 4 * (i + 1)
        for g in range(2):
            qrhs = qT3[:, 4 * g:4 * g + 4, tok]

            def emit_st(kb, g=g, qrhs=qrhs):
                n = rcount["st"]
                rcount["st"] += 1
                bk = 1 + 2 * (n % 2)
                P.add("pe", lambda e: e.matmul(ps[bk][:, 0:512], lhsT=kT3[:, g, kb * 128:(kb + 1) * 128], rhs=qrhs, start=True, stop=False),
                      reads=["qT", "kT"], writes=[("ps", bk)])
                P.add("pe", lambda e: e.matmul(ps[bk][:, 0:512], lhsT=negm[:, kb * 128:(kb + 1) * 128], rhs=identB4, start=False, stop=True),
                      reads=["negm", "identB4"], writes=[("ps", bk)])
                return bk

            bo, bd = (5, 6) if g == 0 else (7, 2)

            def emit_pv(kb, bk, g=g, nkb=nkb, bo=bo, bd=bd):
                n = rcount["pt"]
                rcount["pt"] += 1
                pt = PT[n % 3]
                P.add("act", lambda e: e.activation(out=pt, in_=ps[bk][:, 0:512], func=AF.Exp, scale=SCALE), reads=[("ps", bk)], writes=[("PT", n % 3)])
                P.add("pe", lambda e: e.matmul(ps[bo][:, 0:512], lhsT=v3a[:, kb, g * 128:(g + 1) * 128], rhs=pt, start=(kb == 0), stop=(kb == nkb - 1)),
                      reads=[("PT", n % 3), "v_all"], writes=[("ps", bo)])
                P.add("pe", lambda e: e.matmul(ps[bd][:, 0:512], lhsT=ones_bf, rhs=pt, start=(kb == 0), stop=(kb == nkb - 1)),
                      reads=[("PT", n % 3), "ones"], writes=[("ps", bd)])

            prev = None
            for kb in range(nkb):
                bk = emit_st(kb)
                if prev is not None:
                    emit_pv(*prev)
                prev = (kb, bk)
            emit_pv(*prev)
            if mid_cb is not None:
                mid_cb()
            rden = rdens[g]
            P.add("dve", lambda e, bd=bd, rden=rden: e.reciprocal(out=rden, in_=ps[bd][:, 0:512]), reads=[("ps", bd)], writes=[("rden", g)])
            P.add("dve", lambda e, g=g, bo=bo, rden=rden: e.tensor_tensor(out=oaT3[:, 4 * g:4 * g + 4, tok], in0=v3(ps[bo][:, 0:512], 4), in1=v3(rden, 4), op=ALU.mult),
                  reads=[("ps", bo), ("rden", g)], writes=[("oaT", i, g)])

    kd = {}
    setup_done = set()
    PRE_IT = 4

    def ensure_setup(i):
        if i not in setup_done:
            bis_setup(i)
            setup_done.add(i)
            kd[i] = 0

    def emit_iters(i, upto_k):
        while kd[i] < upto_k:
            bis_iter(i, kd[i])
            kd[i] += 1

    idx_prep(0)
    idx_group(0, 0)
    for i in range(OWN):
        ensure_setup(i)
        if i + 1 < OWN:
            idx_prep(i + 1)
            ngr = i + 2
            k0 = kd[i]
            for kg in range(ngr):
                idx_group(i + 1, kg)
                emit_iters(i, k0 + ((NIT - k0) * (kg + 1)) // ngr)
        emit_iters(i, NIT)
        bis_final(i)
        if i + 1 < OWN:
            ensure_setup(i + 1)
            attention(i, mid_cb=lambda i=i: emit_iters(i + 1, kd[i + 1] + PRE_IT // 2))
        else:
            attention(i)
    if debug:
        P.add("sp", lambda e: e.dma_start(out=dbg["thr"], in_=thr), reads=[("thr", i) for i in range(OWN)], writes=["dbg_dthr"], slot="dbg6")
        P.add("sp", lambda e: e.dma_start(out=dbg["oaT"], in_=oaT), reads=[("oaT", i, g) for i in range(OWN) for g in range(2)], writes=["dbg_doa"], slot="dbg7")

    if upto <= 3:
        return finish()

    P.barrier()
    oaTs = sm[:, 112:120]
    B3 = 20 * KB
    idxp_i = buf(B3, 64, I32)
    idx8 = buf(B3 + 64, 64, I32)
    idx16 = buf(B3 + 128, 64, I32)
    pgf = buf(B3 + 192, 64, F32)
    ones_f = buf(B3 + 512, 512, F32)
    scb = buf(B3 + 1024, 1024, F32)
    sc = scb[:, 0:129]
    vld = buf(B3 + 2048, 1024, F32)[:, 0:129]
    kib = [buf(B3 + 4 * KB + i * 4 * KB, 4 * KB) for i in range(2)] + [buf(172 * KB + i * 4 * KB, 4 * KB) for i in range(6)]
    Vall = [buf(68 * KB + i * 4 * KB, 4 * KB) for i in range(16)]
    kidT = [buf(B3 + 12 * KB + i * KB, KB) for i in range(2)]
    Rs = [buf(B3 + 14 * KB + i * 2 * KB, KB) for i in range(2)]
    Kc = [buf(B3 + 18 * KB + i * 4 * KB, 4 * KB) for i in range(2)] + [buf(132 * KB + i * 4 * KB, 4 * KB) for i in range(2)]
    KTb = [buf(B3 + 26 * KB + i * 4 * KB, 4 * KB) for i in range(2)]
    Kx = buf(B3 + 34 * KB, 512)
    Vx = buf(B3 + 35 * KB, 512)
    Psb = buf(B3 + 36 * KB, 4224, F32)
    Pbf = buf(B3 + 41 * KB, 2112)
    Pred = buf(B3 + 44 * KB, 64, F32)
    smx = buf(B3 + 45 * KB, 256, F32)
    osb = buf(B3 + 46 * KB, 1024, F32)
    qiTs_bf = smb[:, 56:72]
    qTs_bf = smb[:, 48:56]
    w_s = smb[0:16, 74:75]
    P.add("dve", lambda e: e.tensor_copy(out=w_s, in_=sm[0:16, 120:121]), reads=["sm_w"], writes=["w_s"])
    IOA = bass.IndirectOffsetOnAxis
    P.add("sp", lambda e: e.dma_start(out=idxp_i[:, 0:1], in_=ptab.rearrange("o p -> p o")), writes=["idxp"], slot="s_idx")
    P.add("dve", lambda e: e.tensor_copy(out=pgf[:, 0:1], in_=idxp_i[:, 0:1]), reads=["idxp"], writes=["pgf0"])
    P.add("dve", lambda e: e.tensor_scalar(out=pgf[:, 1:2], in0=pgf[:, 0:1], scalar1=8.0, scalar2=None, op0=ALU.mult), reads=["pgf0"], writes=["pgf1"])
    P.add("dve", lambda e: e.tensor_scalar(out=pgf[:, 2:3], in0=pgf[:, 0:1], scalar1=16.0, scalar2=None, op0=ALU.mult), reads=["pgf0"], writes=["pgf2"])
    P.add("dve", lambda e: e.tensor_scalar(out=idx8, in0=iota_f[:, 0:16], scalar1=pgf[:, 1:2], scalar2=None, op0=ALU.add), reads=["pgf1", "iota"], writes=["idx8"])
    P.add("dve", lambda e: e.tensor_scalar(out=idx16, in0=iota_f[:, 0:16], scalar1=pgf[:, 2:3], scalar2=None, op0=ALU.add), reads=["pgf2", "iota"], writes=["idx16"])
    P.add("pool", lambda e: e.memset(ones_f, 1.0), writes=["ones_f"])
    P.add("dve", lambda e: e.tensor_copy(out=qiTs_bf, in_=sm[:, 8:24]), reads=["sm_q"], writes=["qis_bf"])
    P.add("dve", lambda e: e.tensor_copy(out=qTs_bf, in_=sm[:, 0:8]), reads=["sm_q"], writes=["qs_bf"])
    for ch in range(8):
        P.add("pool", lambda e, ch=ch: e.indirect_dma_start(out=kib[ch], out_offset=None, in_=pool_ki8, in_offset=IOA(ap=idx8[:, ch:ch + 1], axis=0)),
              reads=["idx8"], writes=[("kib", ch)], slot="s_kib%d" % ch)
    for ch in range(16):
        P.add("pool", lambda e, ch=ch: e.indirect_dma_start(out=Vall[ch], out_offset=None, in_=pool_v16, in_offset=IOA(ap=idx16[:, ch:ch + 1], axis=0)),
              reads=["idx16"], writes=[("Vall", ch)], slot="s_v%d" % ch)

    def ix_T(g):
        ch, grp = g // 4, g % 4
        tb_ = g % 2
        for j in range(4):
            pos_l = 4 * grp + j
            P.add("pe", lambda e, j=j, pos_l=pos_l: e.transpose(out=psb[tb_][:, j * 128:(j + 1) * 128], in_=kib[ch][:, pos_l * 128:(pos_l + 1) * 128], identity=ident),
                  reads=[("kib", ch), "ident"], writes=[("ps", tb_)])
        P.add("act", lambda e: e.copy(out=kidT[g % 2], in_=psb[tb_][:, 0:512]), reads=[("ps", tb_)], writes=[("kidT", g % 2)])

    def ix_S(g):
        tb_ = g % 2
        P.add("pe", lambda e: e.matmul(ps[2 + tb_][0:16, 0:512], lhsT=qiTs_bf, rhs=kidT[g % 2], start=True, stop=True),
              reads=[("kidT", g % 2), "qis_bf"], writes=[("ps", 2 + tb_)])
        P.add("act", lambda e: e.activation(out=Rs[g % 2][0:16, :], in_=ps[2 + tb_][0:16, 0:512], func=AF.Relu), reads=[("ps", 2 + tb_)], writes=[("Rs", g % 2)])

    def ix_SC(g):
        for j in range(4):
            pos = g * 4 + j
            P.add("pe", lambda e, j=j, pos=pos: e.matmul(ps[4][:, pos:pos + 1], lhsT=Rs[g % 2][0:16, j * 128:(j + 1) * 128], rhs=w_s, start=True, stop=True),
                  reads=[("Rs", g % 2), "w_s"], writes=[("ps", 4)])

    for step in range(32 + 2):
        if step < 32:
            ix_T(step)
        if 1 <= step <= 32:
            ix_S(step - 1)
        if step >= 2:
            ix_SC(step - 2)
    P.add("pe", lambda e: e.transpose(out=ps[5][:, 0:1], in_=ksv[0:1, 512:640], identity=identf[0:1, 0:1]), reads=["ksv", "identf"], writes=[("ps", 5)])
    P.add("act", lambda e: e.copy(out=smb[:, 72:73], in_=ps[5][:, 0:1]), reads=[("ps", 5)], writes=["kisT"])
    P.add("pe", lambda e: e.matmul(ps[5][0:16, 8:9], lhsT=qiTs_bf, rhs=smb[:, 72:73], start=True, stop=True), reads=["kisT", "qis_bf"], writes=[("ps", 5)])
    P.add("act", lambda e: e.activation(out=smb[0:16, 76:77], in_=ps[5][0:16, 8:9], func=AF.Relu), reads=[("ps", 5)], writes=["Rn"])
    P.add("pe", lambda e: e.matmul(ps[5][0:1, 16:17], lhsT=smb[0:16, 76:77], rhs=w_s, start=True, stop=True), reads=["Rn", "w_s"], writes=[("ps", 5)])
    P.add("pool", lambda e: e.memset(scb[:, 128:129], -CBIG), writes=["sc_x"])
    P.add("dve", lambda e: e.tensor_copy(out=scb[0:1, 128:129], in_=ps[5][0:1, 16:17]), reads=[("ps", 5), "sc_x"], writes=["sc_x"])
    P.add("dve", lambda e: e.tensor_copy(out=scb[:, 0:128], in_=ps[4][:, 0:128]), reads=[("ps", 4)], writes=["sc_m"])
    pm = smx[:, 2:4]
    P.add("dve", lambda e: e.tensor_reduce(out=smx[:, 2:3], in_=sc, axis=AX.X, op=ALU.max), reads=["sc_x", "sc_m"], writes=["pm0"])
    P.add("dve", lambda e: e.tensor_reduce(out=smx[:, 3:4], in_=scb[:, 0:128], axis=AX.X, op=ALU.min, negate=True), reads=["sc_m"], writes=["pm1"])
    P.add("pe", lambda e: e.transpose(out=ps[5][0:2, 32:160], in_=pm, identity=identf), reads=["pm0", "pm1", "identf"], writes=[("ps", 5)])
    P.add("dve", lambda e: e.tensor_reduce(out=smx[0:2, 4:5], in_=ps[5][0:2, 32:160], axis=AX.X, op=ALU.max), reads=[("ps", 5)], writes=["g2"])
    P.add("dve", lambda e: e.tensor_scalar(out=smx[0:2, 6:8], in0=identf[0:2, 0:2], scalar1=smx[0:2, 4:5], scalar2=None, op0=ALU.mult),
          reads=["g2", "identf"], writes=["g2d"])
    P.add("pe", lambda e: e.matmul(ps[5][:, 200:202], lhsT=ones_f[0:2, :], rhs=smx[0:2, 6:8], start=True, stop=True), reads=["g2d", "ones_f"], writes=[("ps", 5)])
    P.add("dve", lambda e: e.tensor_copy(out=smx[:, 8:10], in_=ps[5][:, 200:202]), reads=[("ps", 5)], writes=["gmm"])
    gmax = smx[:, 8:9]
    ngmin = smx[:, 9:10]
    P.add("dve", lambda e: e.tensor_tensor(out=w0, in0=gmax, in1=ngmin, op=ALU.add), reads=["gmm"], writes=["w0"])
    P.add("dve", lambda e: e.tensor_scalar(out=w0, in0=w0, scalar1=float(2.0 ** -10), scalar2=1e-3, op0=ALU.mult, op1=ALU.add), reads=["w0"], writes=["w0b"])
    P.add("dve", lambda e: e.tensor_tensor(out=lo0, in0=ngmin, in1=w0, op=ALU.add), reads=["gmm", "w0b"], writes=["lo0n"])
    P.add("dve", lambda e: e.tensor_scalar(out=lo0, in0=lo0, scalar1=-1.0, scalar2=None, op0=ALU.mult), reads=["lo0n"], writes=["lo0"])
    P.add("dve", lambda e: e.tensor_tensor(out=w0, in0=gmax, in1=lo0, op=ALU.subtract), reads=["gmm", "lo0"], writes=["w0c"])
    P.add("dve", lambda e: e.tensor_scalar(out=HK, in0=pow2[:, 0:NIT + 1], scalar1=w0, scalar2=None, op0=ALU.mult), reads=["pow2", "w0c"], writes=["HK"])
    P.add("dve", lambda e: e.tensor_tensor(out=TC[:, 0:1], in0=lo0, in1=HK[:, 0:1], op=ALU.add), reads=["lo0", "HK"], writes=[("TC", 0)])
    for k in range(NIT):
        P.add("dve", lambda e, k=k: e.tensor_scalar(out=vld, in0=sc, scalar1=TC[:, k:k + 1], scalar2=None, op0=ALU.is_ge, op1=ALU.add, accum_out=cnt),
              reads=["sc_x", "sc_m", ("TC", k)], writes=["vld", "cnt"])
        P.add("pe", lambda e: e.matmul(ps[6][:, 0:1], lhsT=ones_f, rhs=cnt, start=True, stop=True), reads=["cnt", "ones_f"], writes=[("ps", 6)])
        P.add("dve", lambda e, k=k: e.scalar_tensor_tensor(out=tmpc, in0=ps[6][:, 0:1], scalar=TOPK - 0.5, in1=HK[:, k:k + 1], op0=ALU.is_ge, op1=ALU.mult),
              reads=[("ps", 6), "HK"], writes=["tmpc"])
        P.add("dve", lambda e, k=k: e.scalar_tensor_tensor(out=TC[:, k + 1:k + 2], in0=tmpc, scalar=HK[:, k + 1:k + 2], in1=TC[:, k:k + 1],
                                                           op0=ALU.subtract, op1=ALU.add),
              reads=["tmpc", "HK", ("TC", k)], writes=[("TC", k + 1)])
    thr_s = smx[:, 12:13]
    P.add("dve", lambda e: e.tensor_tensor(out=thr_s, in0=TC[:, NIT:NIT + 1], in1=HK[:, NIT:NIT + 1], op=ALU.subtract), reads=[("TC", NIT), "HK"], writes=["thr_s"])
    P.add("dve", lambda e: e.tensor_scalar(out=vld, in0=sc, scalar1=thr_s, scalar2=None, op0=ALU.is_ge), reads=["sc_x", "sc_m", "thr_s"], writes=["vld"])
    P.add("pool", lambda e: e.memset(Kx, 0.0), writes=["Kx"])
    P.add("pool", lambda e: e.memset(Vx, 0.0), writes=["Vx"])
    P.add("dve", lambda e: e.tensor_copy(out=Kx[0:1, :], in_=ksv[0:1, 0:256]), reads=["ksv", "Kx"], writes=["Kx"])
    P.add("dve", lambda e: e.tensor_copy(out=Vx[0:1, :], in_=ksv[0:1, 256:512]), reads=["ksv", "Vx"], writes=["Vx"])

    def s_col(pos):
        return (4 + pos // 64, (pos % 64) * 8) if pos < 128 else (6, 0)

    def k_T(ch):
        npos = 8 if ch < 16 else 1
        if ch < 16:
            kc_ = Kc[ch % 4]
            kres = ("Kc", ch % 4)
        else:
            kc_ = Kx
            kres = "Kx"
        ktb = KTb[ch % 2]
        ba = 2 * (ch % 2)
        for t in range(2 * npos):
            bk = ba + t // 8
            P.add("pe", lambda e, t=t, bk=bk: e.transpose(out=psb[bk][:, (t % 8) * 128:(t % 8 + 1) * 128], in_=kc_[:, t * 128:(t + 1) * 128], identity=ident),
                  reads=[kres, "ident"], writes=[("ps", bk)])
        nb_ = (2 * npos + 7) // 8
        for hb in range(nb_):
            ncol = min(8, 2 * npos - 8 * hb) * 128
            if hb == 0:
                P.add("act", lambda e, ncol=ncol: e.copy(out=ktb[:, 0:ncol], in_=psb[ba][:, 0:ncol]), reads=[("ps", ba)], writes=[("KTb", ch % 2, 0)])
            else:
                P.add("dve", lambda e, ncol=ncol: e.tensor_copy(out=ktb[:, 1024:1024 + ncol], in_=psb[ba + 1][:, 0:ncol]), reads=[("ps", ba + 1)], writes=[("KTb", ch % 2, 1)])

    def k_MM(ch):
        npos = 8 if ch < 16 else 1
        ktb = KTb[ch % 2]
        for p in range(npos):
            pos = ch * 8 + p
            bk, col = s_col(pos)
            for g in range(2):
                t = 2 * p + g
                P.add("pe", lambda e, t=t, g=g, bk=bk, col=col: e.matmul(ps[bk][:, col + 4 * g:col + 4 * g + 4], lhsT=ktb[:, t * 128:(t + 1) * 128],
                                                                     rhs=qTs_bf[:, 4 * g:4 * g + 4], start=True, stop=True),
                      reads=[("KTb", ch % 2, t // 8), "qs_bf"], writes=[("ps", bk)])

    def k_gather(ch):
        if ch < 16:
            P.add("pool", lambda e: e.indirect_dma_start(out=Kc[ch % 4], out_offset=None, in_=pool_k16, in_offset=IOA(ap=idx16[:, ch:ch + 1], axis=0)),
                  reads=["idx16"], writes=[("Kc", ch % 4)], slot="s_kc%d" % (ch % 4))

    for ch in range(3):
        k_gather(ch)
    k_T(0)
    for ch in range(17):
        k_gather(ch + 3)
        if ch + 1 < 17:
            k_T(ch + 1)
        k_MM(ch)
    P.add("act", lambda e: e.activation(out=Psb[:, 0:512], in_=ps[4][:, 0:512], func=AF.Exp, scale=SCALE), reads=[("ps", 4)], writes=["Psb0"])
    P.add("act", lambda e: e.activation(out=Psb[:, 512:1024], in_=ps[5][:, 0:512], func=AF.Exp, scale=SCALE), reads=[("ps", 5)], writes=["Psb1"])
    P.add("act", lambda e: e.activation(out=Psb[:, 1024:1032], in_=ps[6][:, 0:8], func=AF.Exp, scale=SCALE), reads=[("ps", 6)], writes=["Psb2"])
    P3 = Psb[:, 0:1032].rearrange("p (s h) -> p s h", h=8)
    Pb3 = Pbf[:, 0:1032].rearrange("p (s h) -> p s h", h=8)
    P.add("dve", lambda e: e.tensor_tensor(out=Pb3, in0=P3, in1=vld.unsqueeze(2).to_broadcast([128, 129, 8]), op=ALU.mult),
          reads=["Psb0", "Psb1", "Psb2", "vld"], writes=["Pbf"])
    P.add("dve", lambda e: e.tensor_reduce(out=Pred[:, 0:8], in_=Pbf[:, 0:1032].rearrange("p (s h) -> p h s", h=8), axis=AX.X, op=ALU.add),
          reads=["Pbf"], writes=["Pred"])
    for ch in range(17):
        npos = 8 if ch < 16 else 1
        vc_ = Vall[ch] if ch < 16 else Vx
        vres = ("Vall", ch) if ch < 16 else "Vx"
        for p in range(npos):
            pos = ch * 8 + p
            for g in range(2):
                P.add("pe", lambda e, vc_=vc_, p=p, g=g, pos=pos: e.matmul(ps[g][0:4, 0:128], lhsT=Pbf[:, pos * 8 + 4 * g:pos * 8 + 4 * g + 4],
                                                                       rhs=vc_[:, p * 256 + g * 128:p * 256 + (g + 1) * 128],
                                                                       start=(pos == 0), stop=(pos == 128)),
                      reads=[vres, "Pbf"], writes=[("ps", g)])
    for g in range(2):
        P.add("pe", lambda e, g=g: e.matmul(ps[2][0:4, g:g + 1], lhsT=Pred[:, 4 * g:4 * g + 4], rhs=ones_f[:, 0:1], start=True, stop=True),
              reads=["Pred", "ones_f"], writes=[("ps", 2)])
    P.add("dve", lambda e: e.reciprocal(out=smx[0:4, 16:18], in_=ps[2][0:4, 0:2]), reads=[("ps", 2)], writes=["rden_s"])
    for g in range(2):
        P.add("dve", lambda e, g=g: e.tensor_scalar(out=osb[0:4, g * 128:(g + 1) * 128], in0=ps[g][0:4, 0:128], scalar1=smx[0:4, 16 + g:17 + g], scalar2=None,
                                                   op0=ALU.mult), reads=[("ps", g), "rden_s"], writes=[("osb", g)])
        P.add("pe", lambda e, g=g: e.transpose(out=ps[3][:, 4 * g:4 * g + 4], in_=osb[0:4, g * 128:(g + 1) * 128], identity=identf[0:4, 0:4]),
              reads=[("osb", g), "identf"], writes=[("ps", 3)])
    P.add("dve", lambda e: e.tensor_copy(out=oaTs, in_=ps[3][:, 0:8]), reads=[("ps", 3)], writes=["oaTs"])

    P.barrier()
    S4 = 20 * KB
    xbuf = [buf(S4 + i * 8 * KB, 8 * KB, F32) for i in range(2)]
    hbuf = [buf(S4 + 16 * KB + i * 4 * KB, 4 * KB) for i in range(2)]
    junk = buf(S4 + 24 * KB, 4 * KB)
    wvbuf = buf(52 * KB, 32 * KB)
    wv3 = v3(wvbuf, 16)
    bspbc = buf(200 * KB, 4 * KB, F32)
    for q4 in range(8):
        P.add("pool", lambda e, q4=q4: e.dma_start(out=wv3[:, q4 * 2:(q4 + 1) * 2, :], in_=v3(wvb_d, 16)[:, q4 * 2:(q4 + 1) * 2, :]),
              writes=["wvbuf"], slot="wvb%d" % (q4 % 4))
    P.add("sp", lambda e: e.dma_start(out=bspbc, in_=bsp_d.partition_broadcast(128)), writes=["bspbc"], slot="c_bsp")
    P.add("pool", lambda e: e.dma_start(out=WsT, in_=wspT_d), writes=["WsT"], slot="c_wsp")
    P.add("pool", lambda e: e.affine_select(out=v3(WsT, 8), in_=v3(WsT, 8), pattern=[[0, 8], [1, 128]], compare_op=ALU.is_ge,
                                            fill=0.0, base=0, channel_multiplier=-1), reads=["WsT"], writes=["WsT"])
    own_pass(xbuf, hbuf, junk)

    P.barrier()
    uT = buf(20 * KB, 16 * KB)
    obT = buf(36 * KB, 16 * KB)
    wbuf = [buf(84 * KB + i * 4 * KB, 4 * KB) for i in range(3)]
    wpbuf = [buf(96 * KB + i * 2 * KB, 2 * KB) for i in range(2)]
    szb = [buf(100 * KB + i * KB, KB) for i in range(2)]
    ftmp = buf(102 * KB, 4 * KB, F32)
    ftmp2 = buf(188 * KB, 4 * KB, F32)
    gvb = buf(192 * KB, 4 * KB, F32)
    vnb = [buf(196 * KB + i * 2 * KB, 2 * KB) for i in range(2)]
    uT3 = v3(uT, 8)
    obT3 = v3(obT, 8)
    P.add("sp", lambda e: e.dma_start(out=gbc[:, 0:1024], in_=ln_g_d.partition_broadcast(128)), writes=["gbc"], slot="c_gbc")
    P.add("sp", lambda e: e.dma_start(out=gbc[:, 1024:2048], in_=ln_b_d.partition_broadcast(128)), writes=["gbc"], slot="c_gbc")
    fm_state["n"] = 0

    def evac_act(dst3, func, tag):
        def mk(cb):
            def f(pa):
                for hf in range(2):
                    P.add("act", lambda e, hf=hf: e.activation(out=dst3[:, cb, hf * 512:(hf + 1) * 512], in_=ps[pa + hf][:, 0:512], func=func),
                          reads=[("ps", pa + hf)], writes=[(tag, cb, hf)])
            return f
        return mk

    oa_all = [("oaT", i, g) for i in range(OWN) for g in range(2)]

    def evac_mul_act(dst3, func, dres):
        def mk(cb):
            def f(pa):
                for hf in range(2):
                    sz = szb[hf]
                    P.add("act", lambda e, hf=hf, sz=sz: e.activation(out=sz, in_=ps[pa + hf][:, 0:512], func=func),
                          reads=[("ps", pa + hf)], writes=[("szb", hf)])
                    P.add("dve", lambda e, hf=hf, sz=sz: e.tensor_tensor(out=dst3[:, cb, hf * 512:(hf + 1) * 512],
                                                                       in0=dst3[:, cb, hf * 512:(hf + 1) * 512], in1=sz, op=ALU.mult),
                          reads=[("szb", hf)] + dres, writes=[(id(dst3), "z", cb, hf)])
            return f
        return mk

    def layernorm(src, dst, sidx, nparts, src_res, dst_res, tmp, jk):
        s1 = stat[0:nparts, sidx:sidx + 1]
        s2 = stat[0:nparts, sidx + 1:sidx + 2]
        mu = stat[0:nparts, sidx + 2:sidx + 3]
        var = stat[0:nparts, sidx + 3:sidx + 4]
        sd = stat[0:nparts, sidx + 4:sidx + 5]
        rs = stat[0:nparts, sidx + 5:sidx + 6]
        P.add("dve", lambda e: e.reduce_sum(out=s1, in_=src, axis=AX.X), reads=[src_res], writes=[("st", sidx)])
        P.add("dve", lambda e: e.tensor_scalar(out=mu, in0=s1, scalar1=1.0 / 1024, scalar2=None, op0=ALU.mult), reads=[("st", sidx)], writes=[("st", sidx + 2)])
        P.add("dve", lambda e: e.tensor_scalar(out=tmp, in0=src, scalar1=mu, scalar2=None, op0=ALU.subtract), reads=[src_res, ("st", sidx + 2)], writes=[(dst_res, "t")])
        P.add("dve", lambda e: e.scalar_tensor_tensor(out=jk[0:nparts, 0:1024], in0=tmp, scalar=1.0, in1=tmp, op0=ALU.mult, op1=ALU.mult, accum_out=s2),
              reads=[(dst_res, "t")], writes=["junk", ("st", sidx + 1)])
        P.add("dve", lambda e: e.tensor_scalar(out=sd, in0=s2, scalar1=1.0 / 1024, scalar2=1e-5, op0=ALU.mult, op1=ALU.add),
              reads=[("st", sidx + 1)], writes=[("st", sidx + 4)])
        P.add("pool", lambda e: e.tensor_tensor(out=rs, in0=sd, in1=epsc[0:nparts, 2:3], op=ALU.pow), reads=[("st", sidx + 4), "epsc"], writes=[("st", sidx + 5)])
        P.add("dve", lambda e: e.scalar_tensor_tensor(out=tmp, in0=tmp, scalar=rs, in1=gbc[0:nparts, 0:1024], op0=ALU.mult, op1=ALU.mult),
              reads=[(dst_res, "t"), ("st", sidx + 5), "gbc"], writes=[(dst_res, "t")])
        P.add("dve", lambda e: e.tensor_tensor(out=dst, in0=tmp, in1=gbc[0:nparts, 1024:2048], op=ALU.add), reads=[(dst_res, "t"), "gbc"], writes=[dst_res])

    junk = buf(106 * KB, 2 * KB)
    def vb_block(i):
        tok = slice(i * 128, (i + 1) * 128)
        for hf in range(2):
            bk = 4 + hf
            for c in range(16):
                P.add("pe", lambda e, c=c, hf=hf, bk=bk: e.matmul(ps[bk][:, 0:512], lhsT=hT3[:, c, tok], rhs=wv3[:, c, hf * 512:(hf + 1) * 512],
                                                                start=(c == 0), stop=(c == 15)),
                      reads=["wvbuf"] + hT_reads, writes=[("ps", bk)])
            P.add("act", lambda e, hf=hf, bk=bk: e.activation(out=gvb[:, hf * 512:(hf + 1) * 512], in_=ps[bk][:, 0:512], func=AF.Gelu),
                  reads=[("ps", bk)], writes=["gvb"])

    def vb_ln(i):
        vn = vnb[i % 2]
        layernorm(gvb, vn, 16, 128, "gvb", ("vn", i % 2), ftmp, junk)

    def vb_back(i):
        tok = slice(i * 128, (i + 1) * 128)
        vn = vnb[i % 2]
        for hh in range(2):
            gs = slice(4 * hh, 4 * hh + 4)
            for g in range(4 * hh, 4 * hh + 4):
                P.add("pe", lambda e, g=g: e.matmul(ps[7][:, (g % 4) * 128:(g % 4 + 1) * 128], lhsT=vn[:, g * 128:(g + 1) * 128],
                                                   rhs=v3(WsT, 8)[:, g, :], start=True, stop=True),
                      reads=[("vn", i % 2), "WsT"], writes=[("ps", 7)])
            P.add("dve", lambda e, hh=hh, gs=gs: e.tensor_tensor(out=obT3[:, gs, tok], in0=v3(ps[7][:, 0:512], 4),
                                                                in1=v3(bspbc[:, hh * 512:(hh + 1) * 512], 4), op=ALU.add),
                  reads=[("ps", 7), "bspbc"], writes=[("obT", i, hh)])

    mk = evac_act(uT3, AF.Gelu, "uT")
    for cb in range(8):
        proj_fm(FM_U + cb, 32 + cb, mk(cb), pa=2 * (cb % 2))
        fm_prefetch(FM_U + cb + 1 if cb < 7 else FM_ZA)
        vb_block(cb)
        if cb >= 1:
            vb_back(cb - 1)
        vb_ln(cb)
    vb_back(7)
    uT_all = [("uT", cb, hf) for cb in range(8) for hf in range(2)]
    P.add("act", lambda e: e.activation(out=sm[:, 32:40], in_=ps[6][:, 32:40], func=AF.Gelu), reads=[("ps", 6)], writes=["uTs"])
    for cb in range(8):
        P.add("dve", lambda e, cb=cb: e.tensor_tensor(out=obT3[:, cb, :], in0=obT3[:, cb, :], in1=uT3[:, cb, :], op=ALU.mult),
              reads=[("obT", i, cb // 4) for i in range(OWN)] + [("uT", cb, 0), ("uT", cb, 1)], writes=[("obT", i, cb // 4) for i in range(OWN)])
    mk = evac_mul_act(oaT3, AF.Silu, oa_all)
    for cb in range(8):
        proj_fm(FM_ZA + cb, 24 + cb, mk(cb))
    oaz_all = [(id(oaT3), "z", cb, hf) for cb in range(8) for hf in range(2)]
    P.add("act", lambda e: e.activation(out=sm[:, 24:32], in_=ps[6][:, 24:32], func=AF.Silu), reads=[("ps", 6)], writes=["zaTs"])
    P.add("dve", lambda e: e.tensor_tensor(out=oazTs, in0=oaTs, in1=sm[:, 24:32], op=ALU.mult), reads=["zaTs", "oaTs"], writes=["srhs"])

    ob_all = [("obT", i, hh) for i in range(OWN) for hh in range(2)]
    for hf in range(2):
        for c in range(16):
            P.add("pe", lambda e, c=c, hf=hf: e.matmul(ps[4 + hf][0:1, 0:512], lhsT=hsT[:, c:c + 1], rhs=wv3[:, c, hf * 512:(hf + 1) * 512],
                                                      start=(c == 0), stop=(c == 15)),
                  reads=["wvbuf", "hsT"], writes=[("ps", 4 + hf)])
        P.add("act", lambda e, hf=hf: e.activation(out=gvb[0:1, hf * 512:(hf + 1) * 512], in_=ps[4 + hf][0:1, 0:512], func=AF.Gelu),
              reads=[("ps", 4 + hf)], writes=["gvb"])
    vns = ftmp2[0:1, :]
    layernorm(gvb[0:1, :], vns, 24, 1, "gvb", "vns", ftmp[0:1, :], junk)
    P.add("sp", lambda e: e.dma_start(out=gvs_o, in_=vns), reads=["vns", ("gt", 1, 0), ("gt", 1, 1)], writes=["o_gvs"], slot="o_gvs")

    P.add("sp", lambda e: e.dma_start(out=gvb[0:1, :], in_=ws00_d), reads=["vns"], writes=["gvb"], slot="c_ws00")
    P.add("sp", lambda e: e.dma_start(out=ftmp[0:1, :], in_=bs0_d), reads=["vns"], writes=[("vns", "t")], slot="c_bs0")
    P.add("dve", lambda e: e.tensor_tensor(out=gvb[0:1, :], in0=vns, in1=gvb[0:1, :], op=ALU.mult), reads=["vns", "gvb"], writes=["gvb"])
    P.add("dve", lambda e: e.tensor_tensor(out=gvb[0:1, :], in0=gvb[0:1, :], in1=ftmp[0:1, :], op=ALU.add), reads=["gvb", ("vns", "t")], writes=["gvb"])
    for g in range(8):
        P.add("pe", lambda e, g=g: e.transpose(out=ps[4][:, g:g + 1], in_=gvb[0:1, g * 128:(g + 1) * 128], identity=identf[0:1, 0:1]),
              reads=["gvb", "identf"], writes=[("ps", 4)])
    P.add("dve", lambda e: e.tensor_tensor(out=sm[:, 32:40], in0=ps[4][:, 0:8], in1=sm[:, 32:40], op=ALU.mult), reads=[("ps", 4), "uTs"], writes=["obTs"])
    wo = [buf(20 * KB + gidx * 16 * KB, 16 * KB) for gidx in range(4)]

    pending_wo = []

    def load_wo(gi, extra, defer=False):
        w3 = v3(wo[gi], 16)
        for q4 in range(4):
            def emit(q4=q4):
                P.add("pool", lambda e: e.dma_start(out=w3[:, q4 * 4:(q4 + 1) * 4, :], in_=v3(wout_d[gi], 16)[:, q4 * 4:(q4 + 1) * 4, :]),
                      writes=[("wo", gi)] + extra, slot="wo%d_%d" % (gi, q4))
            if defer:
                pending_wo.append(emit)
            else:
                emit()

    load_wo(0, uT_all, defer=True)
    load_wo(2, ["wvbuf"], defer=True)
    load_wo(3, ["wvbuf"], defer=True)
    mk = evac_mul_act(obT3, AF.Silu, ob_all)
    for cb in range(8):
        proj_fm(FM_ZB + cb, 40 + cb, mk(cb))
    obz_all = [(id(obT3), "z", cb, hf) for cb in range(8) for hf in range(2)]
    P.add("act", lambda e: e.activation(out=sm[:, 40:48], in_=ps[6][:, 40:48], func=AF.Silu), reads=[("ps", 6)], writes=["zbTs"])
    P.add("dve", lambda e: e.tensor_tensor(out=obzTs, in0=sm[:, 32:40], in1=sm[:, 40:48], op=ALU.mult), reads=["zbTs", "obTs"], writes=["srhs"])
    if debug:
        P.add("sp", lambda e: e.dma_start(out=dbg["obT"], in_=obT), reads=obz_all, writes=["dbg_dob"], slot="dbg8")

    wp_n = {"n": 0}

    def proj_branch(wd, cb, src3, sres, bank, scol, srhs):
        n = wp_n["n"]
        wp_n["n"] += 1
        wsl = wpbuf[n % 2]
        w3 = v3(wsl, 8)
        P.add("pool", lambda e: e.dma_start(out=wsl, in_=wd[cb]), writes=[("wpbuf", n % 2)], slot="wpbuf%d" % (n % 2))
        for c in range(8):
            for hf in range(2):
                P.add("pe", lambda e, c=c, hf=hf: e.matmul(ps[bank + hf][:, 0:512], lhsT=w3[:, c, :], rhs=src3[:, c, hf * 512:(hf + 1) * 512],
                                                          start=(c == 0), stop=(c == 7)),
                      reads=[("wpbuf", n % 2)] + sres, writes=[("ps", bank + hf)])
            P.add("pe", lambda e, c=c: e.matmul(ps[7][:, scol:scol + 1], lhsT=w3[:, c, :], rhs=srhs[:, c:c + 1], start=(c == 0), stop=(c == 7)),
                  reads=[("wpbuf", n % 2), "srhs"], writes=[("ps", 7)])


    for cb in range(16):
        sg = [None, None]
        for br in range(2):
            gt = ftmp if br == 0 else ftmp2

            def ev_gate(pa, gt=gt, br=br):
                for hf in range(2):
                    P.add("act", lambda e, hf=hf: e.activation(out=gt[:, hf * 512:(hf + 1) * 512], in_=ps[pa + hf][:, 0:512], func=AF.Sigmoid),
                          reads=[("ps", pa + hf)], writes=[("gt", br, hf)])
                sg[br] = pa
            proj_fm(FM_G + 2 * cb + br, (48 + cb if br == 0 else 64 + cb), ev_gate, pa=2 * br)
            if pending_wo:
                pending_wo.pop(0)()
            if br == 0:
                proj_branch(wpa_d, cb, oaT3, oaz_all, 4, cb, oazTs)
            else:
                proj_branch(wpb_d, cb, obT3, obz_all, 4, 16 + cb, obzTs)
            for hf in range(2):
                P.add("dve", lambda e, hf=hf, gt=gt: e.tensor_tensor(out=gt[:, hf * 512:(hf + 1) * 512], in0=ps[4 + hf][:, 0:512],
                                                                   in1=gt[:, hf * 512:(hf + 1) * 512], op=ALU.mult),
                      reads=[("ps", 4 + hf), ("gt", br, hf)], writes=[("gt", br, hf)])
        for hf in range(2):
            P.add("dve", lambda e, hf=hf, cb=cb: e.tensor_tensor(out=mixT3[:, cb, hf * 512:(hf + 1) * 512], in0=ftmp[:, hf * 512:(hf + 1) * 512],
                                                               in1=ftmp2[:, hf * 512:(hf + 1) * 512], op=ALU.add),
                  reads=[("gt", 0, hf), ("gt", 1, hf)], writes=[("mixT", cb, hf)])
    mix_all = [("mixT", cb, hf) for cb in range(16) for hf in range(2)]
    if debug:
        P.add("sp", lambda e: e.dma_start(out=dbg["mixT"], in_=mixT), reads=mix_all, writes=["dbg_dmx"], slot="dbg9")
    P.add("act", lambda e: e.activation(out=sm[:, 48:80], in_=ps[6][:, 48:80], func=AF.Sigmoid),
          reads=[("ps", 6)], writes=["sm_g"])
    P.add("dve", lambda e: e.tensor_tensor(out=sm[:, 80:112], in0=ps[7][:, 0:32], in1=sm[:, 48:80], op=ALU.mult),
          reads=[("ps", 7), "sm_g"], writes=["sm_y"])
    P.add("dve", lambda e: e.tensor_tensor(out=mixTs, in0=sm[:, 80:96], in1=sm[:, 96:112], op=ALU.add), reads=["sm_y"], writes=["mixTs"])

    if upto <= 4:
        return finish()
    P.barrier()
    xbuf = [buf(84 * KB + i * 8 * KB, 8 * KB, F32) for i in range(2)]
    xo = [buf(100 * KB + i * 8 * KB, 8 * KB, F32) for i in range(2)]
    junk = buf(116 * KB, 4 * KB)
    xs2 = buf(120 * KB, 8 * KB, F32, parts=1)
    xos = buf(128 * KB, 8 * KB, F32, parts=1)
    P.add("sp", lambda e: e.dma_start(out=gbc, in_=g_f_d.partition_broadcast(128)), writes=["gbc"], slot="c_gbc")
    load_wo(1, [])

    def final_block(src_ap, xb, xres, xslot, xo_t, lhs_fn, lhs_reads, banks, out_ap, sidx, nparts, oslot):
        P.add("sp", lambda e: e.dma_start(out=xb, in_=src_ap), writes=[xres], slot=xslot)
        for gi in range(4):
            w3 = v3(wo[gi], 16)
            for c in range(16):
                P.add("pe", lambda e, gi=gi, c=c, w3=w3: e.matmul(ps[banks[gi]][0:nparts, 0:512], lhsT=lhs_fn(c), rhs=w3[:, c, :],
                                                                start=(c == 0), stop=(c == 15)),
                      reads=[("wo", gi)] + lhs_reads, writes=[("ps", banks[gi])])
            P.add("dve", lambda e, gi=gi: e.tensor_tensor(out=xo_t[:, gi * 512:(gi + 1) * 512], in0=ps[banks[gi]][0:nparts, 0:512],
                                                         in1=xb[:, gi * 512:(gi + 1) * 512], op=ALU.add),
                  reads=[("ps", banks[gi]), xres], writes=[(xres, "xo", gi)])
        ss = stat[0:nparts, sidx:sidx + 1]
        sd = stat[0:nparts, sidx + 1:sidx + 2]
        rs = stat[0:nparts, sidx + 2:sidx + 3]
        xor = [(xres, "xo", gi) for gi in range(4)]
        P.add("act", lambda e: e.activation(out=junk[0:nparts, :], in_=xo_t, func=AF.Square, accum_out=ss), reads=xor, writes=["junk", ("st", sidx)])
        P.add("act", lambda e: e.activation(out=sd, in_=ss, func=AF.Sqrt, scale=1.0 / D, bias=epsc[0:nparts, 0:1]),
              reads=[("st", sidx), "epsc"], writes=[("st", sidx + 1)])
        P.add("dve", lambda e: e.reciprocal(out=rs, in_=sd), reads=[("st", sidx + 1)], writes=[("st", sidx + 2)])
        P.add("dve", lambda e: e.scalar_tensor_tensor(out=xb, in0=xo_t, scalar=rs, in1=gbc[0:nparts, :], op0=ALU.mult, op1=ALU.mult),
              reads=xor + [("st", sidx + 2), "gbc"], writes=[xres])
        P.add("pool", lambda e: e.dma_start(out=out_ap, in_=xb), reads=[xres], writes=[("o_y", oslot)], slot="o_y%s" % oslot)

    for i in range(OWN):
        pb = i % 2
        tok = slice(i * 128, (i + 1) * 128)
        banks = [0, 1, 2, 3] if pb == 0 else [4, 5, 6, 7]
        final_block(x_own[i * 128:(i + 1) * 128, :], xbuf[pb], ("xbuf", pb), "xbuf%d" % pb, xo[pb],
                    lambda c, tok=tok: mixT3[:, c, tok], mix_all, banks, y_own[i * 128:(i + 1) * 128, :], 4 * pb, 128, pb)
    final_block(x_s, xs2, "xs2", "xs2", xos, lambda c: mixTs[:, c:c + 1], ["mixTs"], [0, 1, 2, 3], ys_o, 8, 1, "s")

    return finish()


def own_blocks(core):
    j = core % 4
    return sorted([j, 7 - j, 8 + j, 15 - j, 16 + j, 23 - j, 24 + j, 31 - j])


def prep_shared(inputs):
    w_in = np.asarray(inputs["w_in"])[0]

    def chunked(cols):
        n = cols.shape[1]
        return np.ascontiguousarray(cols.reshape(16, 128, n).transpose(1, 0, 2))

    sh = {}
    kvki = np.concatenate([w_in[:, C_K:C_K + 256], w_in[:, C_V:C_V + 256], w_in[:, C_KI:C_KI + 128]], axis=1)
    sh["wkvki"] = chunked(kvki).reshape(128, -1)
    cbs = []
    for base, n in ((C_Q, 8), (C_QI, 16), (C_ZA, 8), (C_U, 8), (C_ZB, 8)):
        for j in range(n):
            cbs.append(w_in[:, base + j * 128: base + (j + 1) * 128])
    for j in range(16):
        cbs.append(w_in[:, C_GA + j * 128:C_GA + (j + 1) * 128])
        cbs.append(w_in[:, C_GB + j * 128:C_GB + (j + 1) * 128])
    sh["wfm"] = np.stack([chunked(c).reshape(128, -1) for c in cbs])
    sh["wwi"] = chunked(w_in[:, C_WI:C_WI + 16]).reshape(128, -1)
    sh["wvb"] = chunked(w_in[:, C_VB:C_VB + 1024]).reshape(128, -1)
    wpa = np.asarray(inputs["w_proj_a"])[0]
    wpb = np.asarray(inputs["w_proj_b"])[0]

    def chunk8(cols):
        return np.ascontiguousarray(cols.reshape(8, 128, 128).transpose(1, 0, 2)).reshape(128, -1)

    sh["wpa"] = np.stack([chunk8(wpa[:, j * 128:(j + 1) * 128]) for j in range(16)])
    sh["wpb"] = np.stack([chunk8(wpb[:, j * 128:(j + 1) * 128]) for j in range(16)])
    wout = np.asarray(inputs["w_out"])[0]
    sh["wout"] = np.stack([chunked(wout[:, j * 512:(j + 1) * 512]).reshape(128, -1) for j in range(4)])
    ws = np.asarray(inputs["w_spatial"])[0]
    sh["wspT"] = np.ascontiguousarray(ws.transpose(2, 0, 1)).reshape(128, -1)
    bsp = np.asarray(inputs["b_spatial"])[0]
    sh["bsp"] = np.ascontiguousarray(bsp.reshape(1, 1024))
    sh["ws00"] = np.ascontiguousarray(np.repeat(ws[:, 0, 0], 128).reshape(1, 1024))
    sh["bs0"] = np.ascontiguousarray(np.repeat(bsp[:, 0], 128).reshape(1, 1024))
    sh["pool_ki8"] = np.ascontiguousarray(np.asarray(inputs["cache_k_idx"])[0].reshape(1280 * 8, 2048))
    sh["pool_k16"] = np.ascontiguousarray(np.asarray(inputs["cache_k"])[0].reshape(1280 * 16, 2048))
    sh["pool_v16"] = np.ascontiguousarray(np.asarray(inputs["cache_v"])[0].reshape(1280 * 16, 2048))
    sh["g_in"] = np.ascontiguousarray(np.asarray(inputs["norm_in_g"]).reshape(1, D))
    sh["g_f"] = np.ascontiguousarray(np.asarray(inputs["norm_f_g"]).reshape(1, D))
    sh["ln_g"] = np.ascontiguousarray(np.asarray(inputs["ln_g"]).reshape(1, 1024))
    sh["ln_b"] = np.ascontiguousarray(np.asarray(inputs["ln_b"]).reshape(1, 1024))
    return {k: np.ascontiguousarray(v, dtype=v.dtype) for k, v in sh.items()}


def make_in_maps(inputs):
    sh = prep_shared(inputs)
    xp = np.asarray(inputs["x_prompt"])
    xs = np.asarray(inputs["x_sample"])
    pt = np.asarray(inputs["page_table"]).astype(np.int32)
    maps = []
    for c in range(NCORES):
        b = c // 4
        ob = own_blocks(c)
        m = dict(sh)
        m["x_all"] = np.ascontiguousarray(xp[b])
        m["x_own"] = np.ascontiguousarray(np.concatenate([xp[b, blk * 128:(blk + 1) * 128] for blk in ob], axis=0))
        t = np.arange(128, dtype=np.float32)[:, None]
        m["qrel"] = np.ascontiguousarray(
            np.concatenate([(ob[i] * 128 - 512 * i) + t for i in range(OWN)], axis=1).astype(np.float32))
        m["x_s"] = np.ascontiguousarray(xs[c].reshape(1, D))
        m["ptab"] = np.ascontiguousarray(pt[c].reshape(1, 128))
        maps.append(m)
    return maps


_CACHE = {}


def kernel(**inputs):
    if "nc" not in _CACHE:
        _CACHE["nc"] = build_program(debug=False)[0]
    nc = _CACHE["nc"]
    maps = make_in_maps(inputs)
    res = run_bass_kernel_spmd(nc, maps, core_ids=list(range(NCORES)))
    r = res.results
    y_prompt = np.zeros((2, SEQ, D), np.float32)
    for c in range(NCORES):
        b = c // 4
        for i, blk in enumerate(own_blocks(c)):
            y_prompt[b, blk * 128:(blk + 1) * 128] = r[c]["y_own"][i * 128:(i + 1) * 128]
    y_sample = np.stack([r[c]["ys"].reshape(1, D) for c in range(NCORES)]).astype(np.float32)
    nk = np.stack([r[4 * b]["knew"].reshape(SEQ, 2, 128) for b in range(2)])[None].astype(np.float32)
    nv = np.stack([r[4 * b]["vnew"].reshape(SEQ, 2, 128) for b in range(2)])[None].astype(np.float32)
    nki = np.stack([r[4 * b]["kinew"].reshape(SEQ, 128) for b in range(2)])[None].astype(np.float32)
    ks = np.stack([r[c]["ks"].reshape(1, 2, 128) for c in range(NCORES)])[None].astype(np.float32)
    vs = np.stack([r[c]["vs"].reshape(1, 2, 128) for c in range(NCORES)])[None].astype(np.float32)
    kis = np.stack([r[c]["kis"].reshape(1, 128) for c in range(NCORES)])[None].astype(np.float32)
    gvs = np.stack([r[c]["gvs"].reshape(1, 1024) for c in range(NCORES)])[None].astype(np.float32)
    return (y_prompt, y_sample, nk, nv, nki, ks, vs, kis, gvs)
```
